# Optimizing a Trainium2 kernel written in Bass

```python
import math
import jax, jax.numpy as jnp
from jax import lax
import numpy as np

D_MODEL = 1024
BATCH = 16
SEQ = 256
DEPTH = 2
DEC_BATCH = 8
DEC_SEQ = 1024
PAST_LEN = 512

GRID_W = 64
N_EVEN = (DEPTH + 1) // 2
N_ODD = DEPTH // 2
EPS = 1e-6
FOURIER_GROUPS = 4
FOURIER_GROUP_W = D_MODEL // 16
FOURIER_W = FOURIER_GROUPS * FOURIER_GROUP_W
HEAD_DIM = 64
N_Q_HEADS = (D_MODEL - FOURIER_W) // HEAD_DIM
N_KV_HEADS = N_Q_HEADS // 3
GQA_GROUP = N_Q_HEADS // N_KV_HEADS
Q_W = N_Q_HEADS * HEAD_DIM
KV_W = N_KV_HEADS * HEAD_DIM
EVEN_IN = FOURIER_W + Q_W + 2 * KV_W
EVEN_MIX = FOURIER_W + Q_W
Q_BLOCK = 128
ROPE_THETA = 10000.0
AX_PAIRS = HEAD_DIM // 4
CONV_W = 4
LRU_W = D_MODEL // 2
LRU_BLOCKS = 8
LRU_BLOCK_W = LRU_W // LRU_BLOCKS
LRU_C = 8.0
SSD_INNER = D_MODEL
SSD_HEAD_P = 64
SSD_HEADS = SSD_INNER // SSD_HEAD_P
SSD_GROUPS = 2
SSD_STATE = 64
SSD_CHUNK = 128
SSD_CONV_CH = SSD_INNER + 2 * SSD_GROUPS * SSD_STATE
ODD_IN = 2 * LRU_W + SSD_INNER + SSD_CONV_CH + 2 * SSD_HEADS
ODD_MIX = LRU_W + SSD_INNER
D_FF = -(-(8 * D_MODEL) // (3 * 256)) * 256

kernel_name = "hybrid_dit_fourier_gqa_rglru_ssd_step"


def rmsnorm(x, g):
    xf = x.astype(jnp.float32)
    y = xf * lax.rsqrt(jnp.mean(xf * xf, axis=-1, keepdims=True) + EPS)
    return (y * g.astype(jnp.float32)).astype(x.dtype)


def modulation(cond, w, b):
    m = jax.nn.silu(cond) @ w + b
    return [t[:, None, :] for t in jnp.split(m, 6, axis=-1)]


def centred_dwconv(x, w, b):
    y = lax.conv_general_dilated(
        x, w.astype(x.dtype)[:, None, :], window_strides=(1,),
        padding=[((CONV_W - 1) // 2, CONV_W // 2)],
        dimension_numbers=("NWC", "WIO", "NWC"), feature_group_count=x.shape[-1])
    return y + b.astype(x.dtype)


def swiglu(h, w1, w3, w2):
    return (jax.nn.silu(h @ w1) * (h @ w3)) @ w2


def fourier_mix(f):
    B, S, _ = f.shape
    fg = f.astype(jnp.float32).reshape(B, S, FOURIER_GROUPS, FOURIER_GROUP_W)
    out = jnp.fft.fft2(fg, axes=(1, 3), norm="ortho").real
    return out.reshape(B, S, FOURIER_W).astype(f.dtype)


def axial_rope_tables(S):
    rows = S // GRID_W
    row = jnp.repeat(jnp.arange(rows, dtype=jnp.float32), GRID_W)
    col = jnp.tile(jnp.arange(GRID_W, dtype=jnp.float32), rows)
    freqs = ROPE_THETA ** (-jnp.arange(AX_PAIRS, dtype=jnp.float32) / AX_PAIRS)
    ang = jnp.stack([row[:, None] * freqs, col[:, None] * freqs], axis=1)
    return jnp.cos(ang), jnp.sin(ang)


def apply_axial_rope(x, cos, sin):
    B, S, H, _ = x.shape
    xa = x.astype(jnp.float32).reshape(B, S, H, 2, 2, AX_PAIRS)
    x1, x2 = xa[..., 0, :], xa[..., 1, :]
    c = cos[None, :, None]
    s = sin[None, :, None]
    out = jnp.stack([x1 * c - x2 * s, x2 * c + x1 * s], axis=-2)
    return out.reshape(x.shape).astype(x.dtype)


def block_attention(q, k, v):
    B, S, _, _ = q.shape
    nb = S // Q_BLOCK
    qb = q.reshape(B, nb, Q_BLOCK, N_KV_HEADS, GQA_GROUP, HEAD_DIM).transpose(1, 0, 2, 3, 4, 5)
    kf = k.astype(jnp.float32)
    vf = v.astype(jnp.float32)
    scale = HEAD_DIM ** -0.5

    def one_block(qblk):
        s = jnp.einsum("bqhgd,bkhd->bhgqk", qblk.astype(jnp.float32), kf) * scale
        p = jax.nn.softmax(s, axis=-1)
        return jnp.einsum("bhgqk,bkhd->bqhgd", p, vf)

    o = lax.map(one_block, qb)
    return o.transpose(1, 0, 2, 3, 4, 5).reshape(B, S, Q_W).astype(q.dtype)


def even_mixer(h, w_in, q_norm, k_norm, w_out, ctx_kv):
    B, S, _ = h.shape
    p = h @ w_in
    f, q, k, v = jnp.split(p, [FOURIER_W, FOURIER_W + Q_W, FOURIER_W + Q_W + KV_W], axis=-1)
    fo = fourier_mix(f)
    q = rmsnorm(q.reshape(B, S, N_Q_HEADS, HEAD_DIM), q_norm)
    k = rmsnorm(k.reshape(B, S, N_KV_HEADS, HEAD_DIM), k_norm)
    v = v.reshape(B, S, N_KV_HEADS, HEAD_DIM)
    if ctx_kv is None:
        ao = block_attention(q, k, v)
        kv_out = (k, v)
    else:
        cos, sin = axial_rope_tables(S)
        q = apply_axial_rope(q, cos, sin)
        kl = apply_axial_rope(k, cos, sin)
        ck, cv = ctx_kv
        keys = jnp.concatenate([kl, ck.astype(kl.dtype)], axis=1)
        vals = jnp.concatenate([v, cv.astype(v.dtype)], axis=1)
        ao = block_attention(q, keys, vals)
        kv_out = None
    return jnp.concatenate([fo, ao], axis=-1) @ w_out, kv_out


def linear_scan(a, b, h0, reverse):
    if reverse:
        a = jnp.flip(a, axis=1)
        b = jnp.flip(b, axis=1)
    b = b.at[:, 0].add(a[:, 0] * h0)

    def combine(l, r):
        return (l[0] * r[0], r[0] * l[1] + r[1])

    _, h = lax.associative_scan(combine, (a, b), axis=1)
    final = h[:, -1]
    if reverse:
        h = jnp.flip(h, axis=1)
    return h, final


def rglru_direction(xc, wa, ba, wx, bx, lam, h0, reverse):
    B, S, _ = xc.shape
    xf = xc.astype(jnp.float32)
    xb = xf.reshape(B, S, LRU_BLOCKS, LRU_BLOCK_W)
    r = jax.nn.sigmoid(jnp.einsum("bshi,hij->bshj", xb, wa.astype(jnp.float32)).reshape(B, S, LRU_W)
                       + ba.astype(jnp.float32))
    i = jax.nn.sigmoid(jnp.einsum("bshi,hij->bshj", xb, wx.astype(jnp.float32)).reshape(B, S, LRU_W)
                       + bx.astype(jnp.float32))
    log_a = -LRU_C * r * jax.nn.softplus(-lam.astype(jnp.float32))
    a = jnp.exp(log_a)
    b = jnp.sqrt(-jnp.expm1(2.0 * log_a)) * (i * xf)
    return linear_scan(a, b, h0.astype(jnp.float32), reverse)


def ssd_direction(x, dt, A, Bm, Cm, h0, reverse):
    if reverse:
        x, dt, Bm, Cm = (jnp.flip(t, axis=1) for t in (x, dt, Bm, Cm))
    Bsz, L = x.shape[:2]
    nc = L // SSD_CHUNK
    Q = SSD_CHUNK
    G = SSD_GROUPS
    Hg = SSD_HEADS // G
    xc = x.reshape(Bsz, nc, Q, G, Hg, SSD_HEAD_P)
    dtc = dt.reshape(Bsz, nc, Q, G, Hg)
    Bc = Bm.reshape(Bsz, nc, Q, G, SSD_STATE)
    Cc = Cm.reshape(Bsz, nc, Q, G, SSD_STATE)
    acs = jnp.cumsum(dtc * A.reshape(G, Hg), axis=2)
    diff = acs[:, :, :, None] - acs[:, :, None, :]
    causal = jnp.tril(jnp.ones((Q, Q), dtype=bool))[:, :, None, None]
    decay = jnp.exp(jnp.where(causal, diff, -jnp.inf))
    cb = jnp.einsum("bcqgn,bckgn->bcqkg", Cc, Bc)
    y_diag = jnp.einsum("bcqkg,bcqkgh,bckgh,bckghp->bcqghp", cb, decay, dtc, xc)
    decay_end = jnp.exp(acs[:, :, -1:] - acs)
    states = jnp.einsum("bckgn,bckgh,bckghp->bcghpn", Bc, decay_end * dtc, xc)
    chunk_decay = jnp.exp(acs[:, :, -1])

    def step(h_prev, inp):
        st, dec = inp
        return dec[..., None, None] * h_prev + st, h_prev

    h0g = h0.reshape(Bsz, G, Hg, SSD_HEAD_P, SSD_STATE)
    h_final, h_in = lax.scan(step, h0g, (jnp.moveaxis(states, 1, 0), jnp.moveaxis(chunk_decay, 1, 0)))
    h_in = jnp.moveaxis(h_in, 0, 1)
    y_off = jnp.einsum("bcqgn,bcghpn,bcqgh->bcqghp", Cc, h_in, jnp.exp(acs))
    y = (y_diag + y_off).reshape(Bsz, L, SSD_HEADS, SSD_HEAD_P)
    if reverse:
        y = jnp.flip(y, axis=1)
    return y, h_final.reshape(Bsz, SSD_HEADS, SSD_HEAD_P, SSD_STATE)


def odd_mixer(h, w_in, conv_lru_w, conv_lru_b, lru_wa, lru_ba, lru_wx, lru_bx, lru_lambda,
              conv_ssd_w, conv_ssd_b, ssd_dt_bias, ssd_a_log, ssd_d, ssd_norm, w_out,
              lru_h0, ssd_h0):
    B, S, _ = h.shape
    f32 = jnp.float32
    p = h @ w_in
    g, xl, z, xbc, dt = jnp.split(
        p, [LRU_W, 2 * LRU_W, 2 * LRU_W + SSD_INNER, 2 * LRU_W + SSD_INNER + SSD_CONV_CH], axis=-1)
    xc = centred_dwconv(xl, conv_lru_w, conv_lru_b)
    y_f, hl_f = rglru_direction(xc, lru_wa[0], lru_ba[0], lru_wx[0], lru_bx[0], lru_lambda[0],
                                lru_h0[:, 0], False)
    y_b, hl_b = rglru_direction(xc, lru_wa[1], lru_ba[1], lru_wx[1], lru_bx[1], lru_lambda[1],
                                lru_h0[:, 1], True)
    y_lru = ((y_f + y_b) * jax.nn.gelu(g.astype(f32))).astype(h.dtype)
    xbc = jax.nn.silu(centred_dwconv(xbc, conv_ssd_w, conv_ssd_b)).astype(f32)
    xs, Bm, Cm = jnp.split(xbc, [SSD_INNER, SSD_INNER + SSD_GROUPS * SSD_STATE], axis=-1)
    xs = xs.reshape(B, S, SSD_HEADS, SSD_HEAD_P)
    Bm = Bm.reshape(B, S, SSD_GROUPS, SSD_STATE)
    Cm = Cm.reshape(B, S, SSD_GROUPS, SSD_STATE)
    dt = dt.astype(f32)
    dt_f = jax.nn.softplus(dt[..., :SSD_HEADS] + ssd_dt_bias[0].astype(f32))
    dt_b = jax.nn.softplus(dt[..., SSD_HEADS:] + ssd_dt_bias[1].astype(f32))
    A_f = -jnp.exp(ssd_a_log[0].astype(f32))
    A_b = -jnp.exp(ssd_a_log[1].astype(f32))
    ys_f, hs_f = ssd_direction(xs, dt_f, A_f, Bm, Cm, ssd_h0[:, 0].astype(f32), False)
    ys_b, hs_b = ssd_direction(xs, dt_b, A_b, Bm, Cm, ssd_h0[:, 1].astype(f32), True)
    y = ys_f + ys_b + ssd_d.astype(f32)[:, None] * xs
    y = (y.reshape(B, S, SSD_INNER) * jax.nn.silu(z.astype(f32)))
    y_ssd = rmsnorm(y, ssd_norm).astype(h.dtype)
    out = jnp.concatenate([y_lru, y_ssd], axis=-1) @ w_out
    return out, jnp.stack([hl_f, hl_b], axis=1), jnp.stack([hs_f, hs_b], axis=1)


def run_trunk(x, cond, p, ctx_k, ctx_v, ctx_lru, ctx_ssd, is_ctx):
    B = x.shape[0]
    new_k, new_v, new_lru, new_ssd = [], [], [], []
    for l in range(DEPTH):
        sm, cm, gm, sf, cf, gf = modulation(cond, p["w_ada"][l], p["b_ada"][l])
        hmix = rmsnorm(x, p["norm_mix"][l]) * (1 + cm) + sm
        j = l // 2
        if l % 2 == 0:
            kv = None if is_ctx else (ctx_k[:, j], ctx_v[:, j])
            out, kv_new = even_mixer(hmix, p["w_in_even"][j], p["q_norm"][j], p["k_norm"][j],
                                     p["w_out_even"][j], kv)
            if is_ctx:
                new_k.append(kv_new[0])
                new_v.append(kv_new[1])
        else:
            if is_ctx:
                h0l = jnp.zeros((B, 2, LRU_W), jnp.float32)
                h0s = jnp.zeros((B, 2, SSD_HEADS, SSD_HEAD_P, SSD_STATE), jnp.float32)
            else:
                h0l = ctx_lru[:, j]
                h0s = ctx_ssd[:, j]
            out, sl, ss = odd_mixer(
                hmix, p["w_in_odd"][j], p["conv_lru_w"][j], p["conv_lru_b"][j],
                p["lru_wa"][j], p["lru_ba"][j], p["lru_wx"][j], p["lru_bx"][j], p["lru_lambda"][j],
                p["conv_ssd_w"][j], p["conv_ssd_b"][j], p["ssd_dt_bias"][j], p["ssd_a_log"][j],
                p["ssd_d"][j], p["ssd_norm"][j], p["w_out_odd"][j], h0l, h0s)
            if is_ctx:
                new_lru.append(sl.astype(x.dtype))
                new_ssd.append(ss.astype(x.dtype))
        x = x + gm * out
        hff = rmsnorm(x, p["norm_ffn"][l]) * (1 + cf) + sf
        x = x + gf * swiglu(hff, p["ffn_w1"][l], p["ffn_w3"][l], p["ffn_w2"][l])
    return x, new_k, new_v, new_lru, new_ssd


def setup_inputs(seed: int = 0) -> dict:
    key = jax.random.key(seed)
    ks = iter(jax.random.split(key, 48))
    f32 = jnp.float32

    def nrm(shape, scale=1.0):
        return jax.random.normal(next(ks), shape, f32) * scale

    def gain(shape):
        return 1.0 + 0.02 * jax.random.normal(next(ks), shape, f32)

    u = jax.random.uniform(next(ks), (N_ODD, 2, LRU_W), f32,
                           minval=0.9 ** (1.0 / LRU_C), maxval=0.999 ** (1.0 / LRU_C))
    lru_lambda = jnp.log(u) - jnp.log1p(-u)
    dt0 = jnp.exp(jax.random.uniform(next(ks), (N_ODD, 2, SSD_HEADS), f32,
                                     minval=math.log(1e-3), maxval=math.log(1e-1)))
    ssd_dt_bias = dt0 + jnp.log(-jnp.expm1(-dt0))
    ssd_a_log = jnp.log(jax.random.uniform(next(ks), (N_ODD, 2, SSD_HEADS), f32, minval=1.0, maxval=16.0))

    return {
        "x_prompt": nrm((BATCH, SEQ, D_MODEL)),
        "x_sample": nrm((DEC_BATCH, DEC_SEQ, D_MODEL)),
        "c": nrm((DEC_BATCH, D_MODEL)),
        "cache_k": nrm((DEC_BATCH, N_EVEN, PAST_LEN, N_KV_HEADS, HEAD_DIM)),
        "cache_v": nrm((DEC_BATCH, N_EVEN, PAST_LEN, N_KV_HEADS, HEAD_DIM)),
        "state_lru": nrm((DEC_BATCH, N_ODD, 2, LRU_W), 0.5),
        "state_ssd": nrm((DEC_BATCH, N_ODD, 2, SSD_HEADS, SSD_HEAD_P, SSD_STATE), 0.5),
        "c_ctx": nrm((D_MODEL,)),
        "w_ada": nrm((DEPTH, D_MODEL, 6 * D_MODEL), 0.5 * D_MODEL ** -0.5),
        "b_ada": nrm((DEPTH, 6 * D_MODEL), 0.02),
        "norm_mix": gain((DEPTH, D_MODEL)),
        "norm_ffn": gain((DEPTH, D_MODEL)),
        "w_in_even": nrm((N_EVEN, D_MODEL, EVEN_IN), D_MODEL ** -0.5),
        "q_norm": gain((N_EVEN, HEAD_DIM)),
        "k_norm": gain((N_EVEN, HEAD_DIM)),
        "w_out_even": nrm((N_EVEN, EVEN_MIX, D_MODEL), EVEN_MIX ** -0.5),
        "w_in_odd": nrm((N_ODD, D_MODEL, ODD_IN), D_MODEL ** -0.5),
        "conv_lru_w": nrm((N_ODD, CONV_W, LRU_W), CONV_W ** -0.5),
        "conv_lru_b": nrm((N_ODD, LRU_W), 0.02),
        "lru_wa": nrm((N_ODD, 2, LRU_BLOCKS, LRU_BLOCK_W, LRU_BLOCK_W), LRU_BLOCK_W ** -0.5),
        "lru_ba": nrm((N_ODD, 2, LRU_W), 0.02),
        "lru_wx": nrm((N_ODD, 2, LRU_BLOCKS, LRU_BLOCK_W, LRU_BLOCK_W), LRU_BLOCK_W ** -0.5),
        "lru_bx": nrm((N_ODD, 2, LRU_W), 0.02),
        "lru_lambda": lru_lambda,
        "conv_ssd_w": nrm((N_ODD, CONV_W, SSD_CONV_CH), CONV_W ** -0.5),
        "conv_ssd_b": nrm((N_ODD, SSD_CONV_CH), 0.02),
        "ssd_dt_bias": ssd_dt_bias,
        "ssd_a_log": ssd_a_log,
        "ssd_d": gain((N_ODD, SSD_HEADS)),
        "ssd_norm": gain((N_ODD, SSD_INNER)),
        "w_out_odd": nrm((N_ODD, ODD_MIX, D_MODEL), ODD_MIX ** -0.5),
        "ffn_w1": nrm((DEPTH, D_MODEL, D_FF), D_MODEL ** -0.5),
        "ffn_w3": nrm((DEPTH, D_MODEL, D_FF), D_MODEL ** -0.5),
        "ffn_w2": nrm((DEPTH, D_FF, D_MODEL), D_FF ** -0.5),
    }


def reference(x_prompt, x_sample, c, cache_k, cache_v, state_lru, state_ssd, c_ctx,
              w_ada, b_ada, norm_mix, norm_ffn, w_in_even, q_norm, k_norm, w_out_even,
              w_in_odd, conv_lru_w, conv_lru_b, lru_wa, lru_ba, lru_wx, lru_bx, lru_lambda,
              conv_ssd_w, conv_ssd_b, ssd_dt_bias, ssd_a_log, ssd_d, ssd_norm, w_out_odd,
              ffn_w1, ffn_w3, ffn_w2):
    p = dict(w_ada=w_ada, b_ada=b_ada, norm_mix=norm_mix, norm_ffn=norm_ffn,
             w_in_even=w_in_even, q_norm=q_norm, k_norm=k_norm, w_out_even=w_out_even,
             w_in_odd=w_in_odd, conv_lru_w=conv_lru_w, conv_lru_b=conv_lru_b,
             lru_wa=lru_wa, lru_ba=lru_ba, lru_wx=lru_wx, lru_bx=lru_bx, lru_lambda=lru_lambda,
             conv_ssd_w=conv_ssd_w, conv_ssd_b=conv_ssd_b, ssd_dt_bias=ssd_dt_bias,
             ssd_a_log=ssd_a_log, ssd_d=ssd_d, ssd_norm=ssd_norm, w_out_odd=w_out_odd,
             ffn_w1=ffn_w1, ffn_w3=ffn_w3, ffn_w2=ffn_w2)
    y_prompt, ks_, vs_, lrus_, ssds_ = run_trunk(
        x_prompt, c_ctx[None, :], p, None, None, None, None, True)
    new_k = jnp.stack(ks_, axis=1)
    new_v = jnp.stack(vs_, axis=1)
    new_lru = jnp.stack(lrus_, axis=1)
    new_ssd = jnp.stack(ssds_, axis=1)
    y_sample, _, _, _, _ = run_trunk(
        x_sample, c, p, cache_k, cache_v, state_lru, state_ssd, False)
    return (y_prompt, y_sample, new_k, new_v, new_lru, new_ssd)
```

```python
import numpy as np
from contextlib import ExitStack
import concourse.bass as bass
import concourse.mybir as mybir
from concourse.bass_utils import run_bass_kernel_spmd

F32 = mybir.dt.float32
BF16 = mybir.dt.bfloat16
AF = mybir.ActivationFunctionType
ALU = mybir.AluOpType

ENGS = ("pe", "act", "dve", "pool", "sp")
NDMASEM = 6
NCORES = 8
NRUN = 8
EPS = 1e-6
NLAYERS = 2
STAGE = 99
SUB = 0


class Reg:
    __slots__ = ("name", "lw", "rd", "excl")

    def __init__(self, name, inherit=(), excl=False):
        self.name = name
        self.lw = None
        self.rd = list(inherit)
        self.excl = excl


class Op:
    __slots__ = ("eng", "fn", "deps", "needed", "dma", "sem", "val", "gidx", "thr")

    def __init__(self, eng, fn, dma):
        self.eng = eng
        self.fn = fn
        self.dma = dma
        self.deps = []
        self.needed = False
        self.sem = None
        self.val = 0
        self.thr = None


class Sched:
    def __init__(self, nc):
        self.nc = nc
        self.ops = {e: [] for e in ENGS}
        self.all = []
        self.ndma = {e: 0 for e in ENGS}
        self.dmaops = {e: [] for e in ENGS}

    def add(self, eng, fn, reads=(), writes=(), dma=False):
        op = Op(eng, fn, dma)
        deps = {}
        for r in reads:
            w = r.lw
            if w is not None and (w.dma or dma or w.eng != eng or eng != "pe"):
                deps[id(w)] = w
            if r.excl:
                for q in r.rd:
                    if q.eng != eng:
                        deps[id(q)] = q
        for t in writes:
            w = t.lw
            if w is not None and (w.dma or dma or w.eng != eng or eng != "pe"):
                deps[id(w)] = w
            for q in t.rd:
                if q.dma or dma or q.eng != eng or eng != "pe":
                    deps[id(q)] = q
        op.deps = list(deps.values())
        for r in reads:
            if not dma:
                r.rd = [q for q in r.rd if q.dma or q.eng != eng]
            r.rd.append(op)
        for t in writes:
            t.lw = op
            t.rd = []
        if dma:
            i = self.ndma[eng]
            self.ndma[eng] += 1
            if i >= NDMASEM:
                op.thr = self.dmaops[eng][i - NDMASEM]
            self.dmaops[eng].append(op)
        op.gidx = len(self.all)
        self.all.append(op)
        self.ops[eng].append(op)
        return op

    def emit(self, stack):
        nc = self.nc
        for op in self.all:
            for d in op.deps:
                d.needed = True
        esem = {e: stack.enter_context(nc.semaphore("s_" + e)) for e in ENGS if e != "sp"}
        dsem = {e: [stack.enter_context(nc.semaphore("d_%s%d" % (e, i))) for i in range(NDMASEM)]
                for e in ENGS if self.ndma[e] > 0}
        for e in ENGS:
            cnt = 0
            i = 0
            for op in self.ops[e]:
                if op.dma:
                    op.sem = dsem[e][i % NDMASEM]
                    op.val = 16 * (i // NDMASEM + 1)
                    i += 1
                elif op.needed:
                    cnt += 1
                    op.sem = esem[e]
                    op.val = cnt
        block = stack.enter_context(nc.Block())
        engh = {"pe": block.tensor, "act": block.scalar, "dve": block.vector,
                "pool": block.gpsimd, "sp": block.sync}
        for e in ENGS:
            ops = self.ops[e]
            if not ops:
                continue

            def body(eng, ops=ops, e=e):
                waited = {}
                for op in ops:
                    ds = list(op.deps)
                    if op.thr is not None:
                        ds.append(op.thr)
                    for d in ds:
                        k = id(d.sem)
                        if waited.get(k, 0) >= d.val:
                            continue
                        waited[k] = d.val
                        eng.wait_ge(d.sem, d.val)
                    ins = op.fn(eng)
                    if op.dma:
                        ins.then_inc(op.sem, 16)
                    elif op.needed:
                        ins.then_inc(op.sem, 1)
                for d in self.dmaops[e][-NDMASEM:]:
                    if waited.get(id(d.sem), 0) < d.val:
                        waited[id(d.sem)] = d.val
                        eng.wait_ge(d.sem, d.val)

            engh[e](body)


def _prune(ops):
    best = {}
    out = []
    for o in ops:
        if o.dma:
            out.append(o)
        else:
            b = best.get(o.eng)
            if b is None or o.gidx > b.gidx:
                best[o.eng] = o
    return out + list(best.values())


class TV:
    def __init__(self, ap, name, off, nw, inherit):
        self.ap = ap
        self.name = name
        self.off = off
        self.nw = nw
        self.inherit = inherit
        self.regs = {}

    def __getitem__(self, k):
        return self.ap[k]

    def r(self, key=0):
        g = self.regs.get(key)
        if g is None:
            g = Reg("%s.%s" % (self.name, key), self.inherit)
            self.regs[key] = g
        return g

    def rs(self, keys):
        return [self.r(k) for k in keys]


class Arena:
    def __init__(self, nc, stack, nwords):
        self.t = stack.enter_context(nc.sbuf_tensor("arena", [128, nwords], F32))
        self.n = nwords
        self.live = []
        self.dead = []
        self.peak = 0

    def alloc(self, name, shape, dt):
        n = int(np.prod(shape[1:]))
        nw = n if dt == F32 else (n + 1) // 2
        nw = (nw + 7) // 8 * 8
        off = 0
        for tv in sorted(self.live, key=lambda t: t.off):
            if tv.off - off >= nw:
                break
            off = max(off, tv.off + tv.nw)
        assert off + nw <= self.n, "SBUF arena overflow allocating %s (%d words at %d)" % (name, nw, off)
        self.peak = max(self.peak, off + nw)
        pend = []
        keep = []
        for (o, w, ops) in self.dead:
            if o < off + nw and off < o + w:
                pend += ops
                if not (off <= o and o + w <= off + nw):
                    keep.append((o, w, ops))
            else:
                keep.append((o, w, ops))
        self.dead = keep
        v = self.t[0:shape[0], off:off + nw]
        if dt != F32:
            v = v.bitcast(dt)
        v = v[:, 0:n]
        if len(shape) == 3:
            v = v.rearrange("p (a b) -> p a b", a=shape[1])
        elif len(shape) == 4:
            v = v.rearrange("p (a b c) -> p a b c", a=shape[1], b=shape[2])
        tv = TV(v, name, off, nw, _prune(pend))
        self.live.append(tv)
        return tv

    def free(self, *tvs):
        for tv in tvs:
            ops = list(tv.inherit)
            for g in tv.regs.values():
                if g.lw is not None:
                    ops.append(g.lw)
                ops += g.rd
            self.dead.append((tv.off, tv.nw, _prune(ops)))
            self.live.remove(tv)


D = 1024
NTOK = 1536
TB = 512
PERM = [0, 3, 1, 4, 2, 5, 6, 9, 7, 10, 8, 11]
DFF = 2816
R_BADA, R_CCTX, R_CI, R_NMIX, R_NFFN = 0, 96, 104, 112, 128
R_CLW, R_CLB, R_BA, R_BX, R_LAM, R_CSW, R_CSB, R_QN, R_KN, R_LRU0 = 144, 160, 164, 172, 180, 188, 228, 238, 239, 240
B_NORM, B_D, B_DTB, B_ALOG, NBV = 0, 1024, 1040, 1072, 1104


def _consts():
    c = {}
    ident = np.eye(128, dtype=np.float32)
    onesm = np.full((128, 128), 1.0 / 1024, np.float32)
    o64 = np.zeros((128, 128), np.float32)
    o64[:64, :64] = 1.0 / 64
    o64[64:, 64:] = 1.0 / 64
    R = np.zeros((128, 128), np.float32)
    for h in range(2):
        for ax in range(2):
            for i in range(16):
                p1 = h * 64 + ax * 32 + i
                p2 = p1 + 16
                R[p1, p2] = -1.0
                R[p2, p1] = 1.0
    j = np.arange(128)
    utri = (j[:, None] <= j[None, :]).astype(np.float32)
    ltri = (j[:, None] >= j[None, :]).astype(np.float32)
    nmf = np.where(j[:, None] < j[None, :], -32768.0, 0.0).astype(np.float32)
    nmb = np.where(j[:, None] > j[None, :], -32768.0, 0.0).astype(np.float32)
    onesf = np.ones((128, 128), np.float32)
    c["cf32"] = np.stack([ident, R.T.copy(), utri, ltri, onesf])
    a64 = 2 * np.pi * np.outer(np.arange(64), np.arange(64)) / 64
    c64 = np.zeros((2, 128, 128), np.float32)
    for g in range(2):
        c64[0, g * 64:(g + 1) * 64, g * 64:(g + 1) * 64] = np.cos(a64) / 8
        c64[1, g * 64:(g + 1) * 64, g * 64:(g + 1) * 64] = -np.sin(a64) / 8
    c["cb16"] = np.stack([onesm, o64, nmf, nmb, ident, c64[0], c64[1], utri, ltri])
    for S in (256, 1024):
        a = 2 * np.pi * (np.outer(np.arange(S), np.arange(S)) % S) / S
        c["dft%d" % S] = np.stack([np.cos(a), np.sin(a)]).astype(np.float32) / np.sqrt(S)
    s = np.arange(1024)
    row = (s // 64).astype(np.float32)
    col = (s % 64).astype(np.float32)
    freqs = (10000.0 ** (-np.arange(16, dtype=np.float32) / 16)).astype(np.float32)
    ang = np.zeros((64, 1024), np.float32)
    for d in range(64):
        ax = d // 32
        i = d % 16
        ang[d] = (row if ax == 0 else col) * freqs[i]
    ang = np.concatenate([ang, ang], 0)
    c["rope"] = np.stack([np.cos(ang), np.sin(ang)]).astype(np.float32)
    return c


def build_program():
    nc = bass.Bass("TRN2", target_bir_lowering=False)
    S = Sched(nc)

    def din(name, shape):
        return nc.dram_tensor(name, list(shape), F32, kind="ExternalInput").ap()

    def dout(name, shape):
        return nc.dram_tensor(name, list(shape), F32, kind="ExternalOutput").ap()

    d_x = din("x", [NTOK, D])
    d_vecs = din("vecs", [256, 128])
    d_bvec = din("bvec", [1, NBV])
    d_wada = din("w_ada", [2, D, 6 * D])
    d_wie = din("w_in_even", [D, 1536])
    d_woe = din("w_out_even", [D, D])
    d_wio = din("w_in_odd", [D, 3360])
    d_woo = din("w_out_odd", [1536, D])
    d_w1 = din("ffn_w1", [2, D, DFF])
    d_w3 = din("ffn_w3", [2, D, DFF])
    d_w2 = din("ffn_w2", [2, DFF, D])
    d_lrubd = din("lru_bd", [16, 128, 128])
    d_ck = din("cache_k", [512, 256])
    d_cv = din("cache_v", [512, 256])
    d_sssd = din("state_ssd", [2, 1024, 64])
    d_cf32 = din("cf32", [5, 128, 128])
    d_cb16 = din("cb16", [9, 128, 128])
    d_dft256 = din("dft256", [2, 256, 256])
    d_dft1024 = din("dft1024", [2, 1024, 1024])
    d_rope = din("rope", [2, 128, 1024])
    o_y = dout("y", [NTOK, D])
    o_k = dout("newk", [512, 256])
    o_v = dout("newv", [512, 256])
    o_lru = dout("newlru", [16, 128])
    o_ssd = dout("newssd", [2, 2, 1024, 64])

    with ExitStack() as st:
        A = Arena(nc, st, 53100)
        psb = []
        for i in range(8):
            t = st.enter_context(nc.psum_tensor("ps%d" % i, [128, 512], F32))
            psb.append((t, Reg("ps%d" % i, excl=True)))

        def PS(i):
            return psb[i]

        class Rot:
            def __init__(self, banks):
                self.b = list(banks)
                self.i = 0

            def next(self):
                r = psb[self.b[self.i % len(self.b)]]
                self.i += 1
                return r

        def dma(q, out, in_, reads=(), writes=()):
            return S.add(q, lambda e: e.dma_start(out=out, in_=in_), reads=reads, writes=writes, dma=True)

        def mm(out, lhsT, rhs, start, stop, reads, writes):
            return S.add("pe", lambda e: e.matmul(out, lhsT, rhs, start=start, stop=stop), reads=reads, writes=writes)

        NSLOT = 4
        slots = [A.alloc("wslot%d" % i, [128, 4096], BF16) for i in range(NSLOT)]
        wplan = []
        wstate = {"issued": 0, "got": 0}

        def wplan_add(ap, a, b):
            wplan.append((ap, a, b))

        def kp(ap2d):
            return ap2d.rearrange("(k p) n -> p k n", p=128)

        def wview(i):
            ap, a, b = wplan[i]
            sl = slots[i % NSLOT]
            return sl[:, 0:a * b].rearrange("p (a b) -> p a b", a=a), sl.r()

        def wget():
            i = wstate["got"]
            wstate["got"] += 1
            while wstate["issued"] < min(len(wplan), i + NSLOT - 1):
                j = wstate["issued"]
                v, rg = wview(j)
                dma("pool", v, wplan[j][0], writes=[rg])
                wstate["issued"] += 1
            return wview(i)

        def plan_ada(l):
            for t in range(12):
                wplan_add(kp(d_wada[l])[:, :, t * 512:(t + 1) * 512], 8, 512)

        def plan_ada_tiles(l, ts):
            for t in ts:
                wplan_add(kp(d_wada[l])[:, :, t * 512:(t + 1) * 512], 8, 512)

        def plan_ffn(l):
            for jt in range(6):
                n = 512 if jt < 5 else 256
                wplan_add(kp(d_w1[l])[:, :, jt * 512:jt * 512 + n], 8, n)
                wplan_add(kp(d_w3[l])[:, :, jt * 512:jt * 512 + n], 8, n)
            for o in range(8):
                wplan_add(kp(d_w2[l])[:, :, o * 128:(o + 1) * 128], 22, 128)

        plan_ada_tiles(0, range(0, 4))
        for t in range(3):
            wplan_add(kp(d_wie)[:, :, t * 512:(t + 1) * 512], 8, 512)
        plan_ada_tiles(0, range(4, 12))
        if NLAYERS > 1:
            plan_ada_tiles(1, range(12))
        for sb in range(2):
            for m in range(2):
                wplan_add(kp(d_dft1024[m])[:, :, sb * 512:(sb + 1) * 512], 8, 512)
        for t in range(2):
            wplan_add(kp(d_woe)[:, :, t * 512:(t + 1) * 512], 8, 512)
        plan_ffn(0)
        if NLAYERS > 1:
            wio = kp(d_wio)
            for t in range(2):
                wplan_add(wio[:, :, t * 512:(t + 1) * 512], 8, 512)
            for grp in range(2):
                wplan_add(wio[:, :, 2048:2560], 8, 512)
                wplan_add(wio[:, :, 2560:3072], 8, 512)
                wplan_add(wio[:, :, 3072:3328], 8, 256)
                wplan_add(wio[:, :, 1024:1536], 8, 512)
                wplan_add(wio[:, :, 1536:2048], 8, 512)
                wplan_add(wio[:, :, 3328:3360], 8, 32)
            wstate["woo"] = len(wplan)
            for t in range(4):
                wplan_add(kp(d_woo)[:, :, t * 256:(t + 1) * 256], 12, 256)
            plan_ffn(1)

        xT = A.alloc("xT", [128, 8, NTOK], F32)
        hT = A.alloc("hT", [128, 8, NTOK], BF16)
        cf32 = A.alloc("cf32", [128, 5, 128], F32)
        cb16 = A.alloc("cb16", [128, 9, 128], BF16)
        vecT = A.alloc("vecT", [128, 256], F32)
        bv = A.alloc("bv", [128, NBV], F32)
        mod = A.alloc("mod", [128, 2, 48, 2], F32)
        A1 = A.alloc("A1", [128, 2, 2, 8, 2], F32) if False else A.alloc("A1", [128, 64], F32)
        ident = cf32[:, 0, :]
        RT = cf32[:, 1, :]
        utri = cf32[:, 2, :]
        ltri = cf32[:, 3, :]
        onesf = cf32[:, 4, :]
        onesm = cb16[:, 0, :]
        o64 = cb16[:, 1, :]
        identb = cb16[:, 4, :]

        def A1v(l, kind, c, j):
            o = ((l * 2 + kind) * 8 + c) * 2 + j
            return A1[:, o:o + 1]

        def modv(l, q, c, j):
            return mod[:, l, q * 8 + c, j:j + 1]

        def xr(b):
            return xT.rs(range(4 * b, 4 * b + 4))

        dma("sp", cf32[:], d_cf32.rearrange("c p n -> p c n"), writes=[cf32.r()])
        dma("pool", cb16[:], d_cb16.rearrange("c p n -> p c n"), writes=[cb16.r()])
        vraw = A.alloc("vraw", [128, 2, 128], F32)
        dma("sp", vraw[:], d_vecs.rearrange("(a p) n -> p a n", p=128), writes=[vraw.r()])
        dma("sp", bv[:], d_bvec.partition_broadcast(128), writes=[bv.r()])
        for a in range(2):
            p, pr = PS(a)
            S.add("pe", lambda e, p=p, a=a: e.transpose(p[:, 0:128], vraw[:, a, :], ident),
                  reads=[vraw.r(), cf32.r()], writes=[pr])
            S.add("dve", lambda e, p=p, a=a: e.tensor_copy(out=vecT[:, a * 128:(a + 1) * 128], in_=p[:, 0:128]),
                  reads=[pr], writes=[vecT.r()])
        A.free(vraw)

        xst = [A.alloc("xst%d" % i, [128, D], F32) for i in range(2)]
        rot = Rot([2, 3, 4, 5, 6, 7])
        for t in range(12):
            xs_ = xst[t % 2]
            dma("sp", xs_[:], d_x[t * 128:(t + 1) * 128, :], writes=[xs_.r()])
            for hf in range(2):
                p, pr = rot.next()

                def tr(e, p=p, xs_=xs_, hf=hf):
                    for c in range(4):
                        ins = e.transpose(p[:, c * 128:(c + 1) * 128], xs_[:, (hf * 4 + c) * 128:(hf * 4 + c + 1) * 128], ident)
                    return ins
                S.add("pe", tr, reads=[xs_.r(), cf32.r()], writes=[pr])
                eng = "act" if hf == 0 else "dve"
                outv = xT[:, hf * 4:hf * 4 + 4, t * 128:(t + 1) * 128]
                inv = p[:].rearrange("p (c n) -> p c n", c=4)
                if eng == "act":
                    S.add("act", lambda e, o=outv, i=inv: e.activation(out=o, in_=i, func=AF.Copy), reads=[pr], writes=[xT.r(t)])
                else:
                    S.add("dve", lambda e, o=outv, i=inv: e.tensor_copy(out=o, in_=i), reads=[pr], writes=[xT.r(t)])
        A.free(*xst)

        scT = A.alloc("scT", [128, 8, 2], BF16)
        S.add("act", lambda e: e.activation(out=scT[:].rearrange("p k j -> p j k"),
                                            in_=vecT[:, R_CCTX:R_CCTX + 16].rearrange("p (j k) -> p j k", j=2), func=AF.Silu),
              reads=[vecT.r()], writes=[scT.r()])

        def ada_tile(l, t, bank):
            p, pr = PS(bank) if isinstance(bank, int) else bank
            wt, wr = wget()

            def f(e):
                for oc4 in range(4):
                    for k in range(8):
                        ins = e.matmul(p[:, oc4 * 2:oc4 * 2 + 2], wt[:, k, oc4 * 128:(oc4 + 1) * 128], scT[:, k, :],
                                       start=(k == 0), stop=(k == 7))
                return ins
            S.add("pe", f, reads=[wr, scT.r()], writes=[pr])
            r0 = R_BADA + l * 48 + t * 4
            S.add("dve", lambda e: e.tensor_tensor(out=mod[:, l, t * 4:(t + 1) * 4, :], in0=p[:, 0:8].rearrange("p (c j) -> p c j", j=2),
                                                   in1=vecT[:, r0:r0 + 4].unsqueeze(2).to_broadcast([128, 4, 2]),
                                                   op=ALU.add), reads=[pr, vecT.r()], writes=[mod.r()])

        def ada_finish(l, kind):
            q, rbase = ((1, R_NMIX), (4, R_NFFN))[kind]
            o = (l * 2 + kind) * 16
            S.add("dve", lambda e: e.scalar_tensor_tensor(
                out=A1[:, o:o + 16].rearrange("p (c j) -> p c j", j=2), in0=mod[:, l, q * 8:(q + 1) * 8, :], scalar=1.0,
                in1=vecT[:, rbase + l * 8:rbase + (l + 1) * 8].unsqueeze(2).to_broadcast([128, 8, 2]),
                op0=ALU.add, op1=ALU.mult), reads=[mod.r(), vecT.r()], writes=[A1.r()])

        def norm_mod(l, kind):
            qs = 0 if kind == 0 else 3
            sq = [A.alloc("nsq%d" % i, [128, TB], BF16) for i in range(4)]
            rs = [A.alloc("nrs%d" % i, [128, TB], F32) for i in range(3)]
            tt = [A.alloc("ntt%d" % i, [128, TB], F32) for i in range(3)]
            rot = Rot([6, 7])
            cnt = [0, 0]
            pbank = {}

            def stA(b):
                blk = slice(b * TB, (b + 1) * TB)
                p, pr = rot.next()
                pbank[b] = (p, pr)
                for c in range(8):
                    s_ = sq[cnt[0] % 4]
                    cnt[0] += 1
                    if c % 2 == 0:
                        S.add("act", lambda e, s_=s_, c=c: e.activation(out=s_[:], in_=xT[:, c, blk], func=AF.Square), reads=xr(b), writes=[s_.r()])
                    else:
                        S.add("dve", lambda e, s_=s_, c=c: e.tensor_tensor(out=s_[:], in0=xT[:, c, blk], in1=xT[:, c, blk], op=ALU.mult), reads=xr(b), writes=[s_.r()])
                    mm(p[:], onesm, s_[:], c == 0, c == 7, [s_.r(), cb16.r()], [pr])

            def stB(b):
                p, pr = pbank[b]
                r_ = rs[b]
                S.add("act", lambda e: e.activation(out=r_[:], in_=p[:], func=AF.Ln, bias=EPS_AP[:, 0:1], scale=1.0), reads=[pr, EPS_AP.r()], writes=[r_.r()])
                S.add("act", lambda e: e.activation(out=r_[:], in_=r_[:], func=AF.Exp, scale=-0.5), reads=[r_.r()], writes=[r_.r()])

            def stC(b):
                j = 0 if b == 0 else 1
                blk = slice(b * TB, (b + 1) * TB)
                r_ = rs[b]
                for c in range(8):
                    t_ = tt[cnt[1] % 3]
                    cnt[1] += 1
                    S.add("dve", lambda e, t_=t_, c=c: e.tensor_tensor(out=t_[:], in0=xT[:, c, blk], in1=r_[:], op=ALU.mult),
                          reads=xr(b) + [r_.r()], writes=[t_.r()])
                    S.add("act", lambda e, t_=t_, c=c: e.activation(out=hT[:, c, blk], in_=t_[:], func=AF.Identity,
                                                                    bias=modv(l, qs, c, j), scale=A1v(l, kind, c, j)),
                          reads=[t_.r(), mod.r(), A1.r()], writes=[hT.r(b)])

            stA(0)
            stA(1)
            stB(0)
            stC(0)
            stA(2)
            stB(1)
            stC(1)
            stB(2)
            stC(2)
            A.free(*sq, *rs, *tt)

        EPS_AP = A.alloc("eps", [128, 2], F32)
        S.add("pool", lambda e: e.memset(EPS_AP[:], EPS), writes=[EPS_AP.r()])
        ONE_AP = A.alloc("one", [128, 2], F32)
        S.add("pool", lambda e: e.memset(ONE_AP[:], 1.0), writes=[ONE_AP.r()])

        def resid(p, pr, l, q, o, b):
            j = 0 if b == 0 else 1
            blk = slice(b * TB, (b + 1) * TB)
            S.add("dve", lambda e: e.scalar_tensor_tensor(out=xT[:, o, blk], in0=p[:], scalar=modv(l, q, o, j), in1=xT[:, o, blk],
                                                          op0=ALU.mult, op1=ALU.add),
                  reads=[pr, mod.r()] + xr(b), writes=xr(b))

        def ffn(l):
            norm_mod(l, 1)
            actT = A.alloc("actT", [128, 22, NTOK], BF16)
            sl = [A.alloc("fsl%d" % i, [128, TB], BF16) for i in range(3)]
            rot1 = Rot([0, 1, 2, 3])
            n = 0
            for jt in range(6):
                w1, w1r = wget()
                w3, w3r = wget()
                ncs = 4 if jt < 5 else 2
                for cc in range(ncs):
                    jj = jt * 4 + cc
                    for b in range(3):
                        blk = slice(b * TB, (b + 1) * TB)
                        p1, p1r = rot1.next()
                        p3, p3r = rot1.next()

                        def f(e, w=w1, p=p1, cc=cc, blk=blk):
                            for k in range(8):
                                ins = e.matmul(p[:], w[:, k, cc * 128:(cc + 1) * 128], hT[:, k, blk], start=(k == 0), stop=(k == 7))
                            return ins
                        S.add("pe", f, reads=[w1r, hT.r(b)], writes=[p1r])

                        def f3(e, w=w3, p=p3, cc=cc, blk=blk):
                            for k in range(8):
                                ins = e.matmul(p[:], w[:, k, cc * 128:(cc + 1) * 128], hT[:, k, blk], start=(k == 0), stop=(k == 7))
                            return ins
                        S.add("pe", f3, reads=[w3r, hT.r(b)], writes=[p3r])
                        s_ = sl[n % 3]
                        n += 1
                        S.add("act", lambda e, s_=s_, p=p1: e.activation(out=s_[:], in_=p[:], func=AF.Silu), reads=[p1r], writes=[s_.r()])
                        S.add("dve", lambda e, s_=s_, p=p3, jj=jj, blk=blk: e.tensor_tensor(out=actT[:, jj, blk], in0=p[:], in1=s_[:], op=ALU.mult),
                              reads=[p3r, s_.r()], writes=[actT.r(b)])
            A.free(*sl)
            rot2 = Rot([4, 5, 6, 7])
            for o in range(8):
                w2, w2r = wget()
                for b in range(3):
                    blk = slice(b * TB, (b + 1) * TB)
                    p, pr = rot2.next()

                    def f(e, w=w2, p=p, blk=blk):
                        for jj in range(22):
                            ins = e.matmul(p[:], w[:, jj, :], actT[:, jj, blk], start=(jj == 0), stop=(jj == 21))
                        return ins
                    S.add("pe", f, reads=[w2r, actT.r(b)], writes=[pr])
                    resid(p, pr, l, 5, o, b)
            A.free(actT)

        def even_layer(l):
            norm_mod(l, 0)
            qT = A.alloc("qT", [128, 6, NTOK], BF16)
            kT = A.alloc("kT", [128, 2, 2048], BF16)
            vaug = A.alloc("vaug", [128, 16, 4, 128], BF16)
            ftok = A.alloc("ftok", [128, 12, 256], BF16)
            rope = A.alloc("rope", [128, 2, 1024], F32)
            dft256 = A.alloc("dft256", [128, 2, 2, 256], BF16)
            knew = A.alloc("knew", [128, 4, 256], F32)
            vnew = A.alloc("vnew", [128, 4, 256], F32)
            dma("sp", rope[:], d_rope.rearrange("c p n -> p c n"), writes=[rope.r()])
            for m in range(2):
                dma("pool", dft256[:, m, :, :], d_dft256[m].rearrange("(t p) n -> p t n", p=128), writes=[dft256.r()])
            S.add("pool", lambda e: e.memset(vaug[:], 1.0), writes=vaug.rs(range(16)))
            cvv = d_cv.rearrange("(t p) (j d) -> p t j d", p=128, j=4)
            for par in range(2):
                for tt_ in range(4):
                    dma("pool", vaug[:, 12 + tt_, par::2, par * 64:par * 64 + 64], cvv[:, tt_, par::2, :], writes=[vaug.r(12 + tt_)])
            ckt = A.alloc("ckt", [128, 4, 256], F32)
            dma("sp", ckt[:], d_ck.rearrange("(t p) f -> p t f", p=128), writes=[ckt.r()])
            rotc = Rot([4, 5])
            for kc in range(2):
                p, pr = rotc.next()

                def f(e, p=p, kc=kc):
                    for tt_ in range(4):
                        ins = e.transpose(p[:, tt_ * 128:(tt_ + 1) * 128], ckt[:, tt_, kc * 128:(kc + 1) * 128], ident)
                    return ins
                S.add("pe", f, reads=[ckt.r(), cf32.r()], writes=[pr])
                S.add("dve", lambda e, p=p, kc=kc: e.tensor_copy(out=kT[:, kc, 1536:2048], in_=p[:]), reads=[pr], writes=[kT.r(3)])
            A.free(ckt)

            if STAGE < 3:
                return
            sqn = [A.alloc("sqn%d" % i, [128, TB], BF16) for i in range(3)]
            rsn = [A.alloc("rsn%d" % i, [128, TB], F32) for i in range(3)]
            qn = [A.alloc("qn%d" % i, [128, TB], F32) for i in range(3)]
            t1 = [A.alloc("rt1%d" % i, [128, TB], F32) for i in range(2)]
            t2 = [A.alloc("rt2%d" % i, [128, TB], F32) for i in range(2)]
            rotA = Rot([0, 1, 6])
            rotB = Rot([2, 3])
            rotC = Rot([4, 5, 7])
            qunits = [(wt_i, cc, b) for wt_i in range(2) for cc in range(4) for b in range(3)]
            ust = {}
            wcur = {}

            def stageA(u):
                wt_i, cc, b = qunits[u]
                if (wt_i,) not in wcur:
                    wcur[(wt_i,)] = wget()
                w, wr = wcur[(wt_i,)]
                blk = slice(b * TB, (b + 1) * TB)
                pa, par_ = rotA.next()

                def f(e):
                    for k in range(8):
                        ins = e.matmul(pa[:], w[:, k, cc * 128:(cc + 1) * 128], hT[:, k, blk], start=(k == 0), stop=(k == 7))
                    return ins
                S.add("pe", f, reads=[wr, hT.r(b)], writes=[par_])
                s_ = sqn[u % 3]
                S.add("act", lambda e: e.activation(out=s_[:], in_=pa[:], func=AF.Square), reads=[par_], writes=[s_.r()])
                ust[u] = (pa, par_, s_)

            def stageB(u):
                wt_i, cc, b = qunits[u]
                qc = wt_i * 4 + cc
                isk = qc >= 6
                gain = vecT[:, R_KN:R_KN + 1] if isk else vecT[:, R_QN:R_QN + 1]
                pa, par_, s_ = ust[u]
                r_ = rsn[u % 3]
                q_ = qn[u % 3]
                pb, pbr = rotB.next()
                mm(pb[:], o64, s_[:], True, True, [s_.r(), cb16.r()], [pbr])
                S.add("act", lambda e: e.activation(out=r_[:], in_=pb[:], func=AF.Ln, bias=EPS_AP[:, 0:1], scale=1.0),
                      reads=[pbr, EPS_AP.r()], writes=[r_.r()])
                S.add("act", lambda e: e.activation(out=r_[:], in_=r_[:], func=AF.Exp, scale=-0.5), reads=[r_.r()], writes=[r_.r()])
                if b == 0 and not isk:
                    S.add("dve", lambda e: e.scalar_tensor_tensor(out=qT[:, qc, 0:TB], in0=pa[:], scalar=gain, in1=r_[:], op0=ALU.mult, op1=ALU.mult),
                          reads=[par_, r_.r(), vecT.r()], writes=[qT.r(0)])
                else:
                    S.add("dve", lambda e: e.scalar_tensor_tensor(out=q_[:], in0=pa[:], scalar=gain, in1=r_[:], op0=ALU.mult, op1=ALU.mult),
                          reads=[par_, r_.r(), vecT.r()], writes=[q_.r()])
                ust[u] = (q_,)

            def stageC(u):
                wt_i, cc, b = qunits[u]
                qc = wt_i * 4 + cc
                isk = qc >= 6
                (q_,) = ust.pop(u)
                blk = slice(b * TB, (b + 1) * TB)
                if isk:
                    dst, dreg = kT[:, qc - 6, b * TB:(b + 1) * TB], kT.r(b)
                else:
                    dst, dreg = qT[:, qc, blk], qT.r(b)
                if b == 0:
                    if isk:
                        S.add("act", lambda e: e.activation(out=dst, in_=q_[:], func=AF.Copy), reads=[q_.r()], writes=[dreg])
                        pc, pcr = rotC.next()

                        def ftr(e):
                            for tt_ in range(4):
                                ins = e.transpose(pc[:, tt_ * 128:(tt_ + 1) * 128], q_[:, tt_ * 128:(tt_ + 1) * 128], ident)
                            return ins
                        S.add("pe", ftr, reads=[q_.r(), cf32.r()], writes=[pcr])
                        kc = qc - 6
                        S.add("act", lambda e: e.activation(out=knew[:, :, kc * 128:(kc + 1) * 128], in_=pc[:].rearrange("p (t f) -> p t f", t=4), func=AF.Copy),
                              reads=[pcr], writes=[knew.r()])
                else:
                    a_ = t1[u % 2]
                    b_ = t2[u % 2]
                    pc, pcr = rotC.next()
                    mm(pc[:], RT, q_[:], True, True, [q_.r(), cf32.r()], [pcr])
                    rsl = slice((b - 1) * TB, b * TB)
                    S.add("dve", lambda e: e.tensor_tensor(out=a_[:], in0=q_[:], in1=rope[:, 0, rsl], op=ALU.mult), reads=[q_.r(), rope.r()], writes=[a_.r()])
                    S.add("dve", lambda e: e.tensor_tensor(out=b_[:], in0=pc[:], in1=rope[:, 1, rsl], op=ALU.mult), reads=[pcr, rope.r()], writes=[b_.r()])
                    S.add("dve", lambda e: e.tensor_tensor(out=dst, in0=a_[:], in1=b_[:], op=ALU.add), reads=[a_.r(), b_.r()], writes=[dreg])

            NU = len(qunits)
            for step in range(NU + 2):
                if step < NU:
                    stageA(step)
                if 0 <= step - 1 < NU:
                    stageB(step - 1)
                if 0 <= step - 2 < NU:
                    stageC(step - 2)
            dma("sp", o_k.rearrange("(t p) f -> p t f", p=128), knew[:], reads=[knew.r()])
            A.free(*sqn, *rsn, *qn, *t1, *t2)

            if STAGE < 4:
                return
            w, wr = wget()
            rotA = Rot([0, 1, 2, 3])
            for t in range(12):
                b = t // 4
                p, pr = rotA.next()

                def f(e, p=p, t=t, w=w):
                    for k in range(8):
                        ins = e.matmul(p[:], hT[:, k, t * 128:(t + 1) * 128], w[:, k, :], start=(k == 0), stop=(k == 7))
                    return ins
                S.add("pe", f, reads=[wr, hT.r(b)], writes=[pr])
                if not (SUB & 1):
                    S.add("act", lambda e, p=p, t=t: e.activation(out=ftok[:, t, :], in_=p[:, 0:256], func=AF.Copy), reads=[pr], writes=[ftok.r(t)])
                pv = p[:, 256:512].rearrange("p (j d) -> p j d", j=4)
                for par in range(2 if not (SUB & 2) else 0):
                    S.add("dve", lambda e, pv=pv, t=t, par=par: e.tensor_copy(out=vaug[:, t, par::2, par * 64:par * 64 + 64], in_=pv[:, par::2, :]),
                          reads=[pr], writes=[vaug.r(t)])
                if t < 4 and not (SUB & 4):
                    S.add("act", lambda e, p=p, t=t: e.activation(out=vnew[:, t, :], in_=p[:, 256:512], func=AF.Copy), reads=[pr], writes=[vnew.r()])
            dma("sp", o_v.rearrange("(t p) f -> p t f", p=128), vnew[:], reads=[vnew.r()])
            for t in range(4, 12):
                ada_tile(l, t, 4 + t % 4)
            ada_finish(l, 1)

            if STAGE < 5:
                return
            pT = [A.alloc("pT%d" % i, [128, TB], BF16) for i in range(6)]
            rtmp = [A.alloc("rtmp%d" % i, [128, TB], F32) for i in range(2)]
            evb = [A.alloc("evb%d" % i, [128, TB], F32) for i in range(2)]
            rotS = Rot([0, 1, 2, 3, 4, 5])
            rotO = Rot([6, 7])
            units = []
            for sidx in range(2):
                units.append((sidx * 256, 256, 0, [(sidx * 256 + kt * 128, 2 * sidx + kt) for kt in range(2)]))
            skeys = [(512 + kt * 128, 4 + kt) for kt in range(8)] + [(1536 + kt * 128, 12 + kt) for kt in range(4)]
            for qb in range(2):
                units.append((512 + qb * 512, 512, 1 + qb, skeys))
            n = 0
            nh = 0
            ada1_next = [0]
            LAG = 2
            for (q0, N, b, keys) in units:
                nk = len(keys)
                groups = [(qc, ki) for qc in range(6) for ki in range(nk)]
                pts = {}

                def emit_scores(gi, q0=q0, N=N, b=b, keys=keys, groups=groups, pts=pts):
                    nonlocal n
                    qc, ki = groups[gi]
                    kc = qc // 3
                    k0, vt = keys[ki]
                    kreg = kT.r(0 if k0 < 512 else (1 if k0 < 1024 else (2 if k0 < 1536 else 3)))
                    banks = []
                    for hh in range(2):
                        lo, hi = hh * 64, hh * 64 + 64
                        sp_, spr = rotS.next()
                        mm(sp_[:, 0:N], kT[lo:hi, kc, k0:k0 + 128], qT[lo:hi, qc, q0:q0 + N], True, True, [kreg, qT.r(b)], [spr])
                        banks.append((sp_, spr))
                    lst = []
                    for (sp_, spr) in banks:
                        pt = pT[n % 6]
                        n += 1
                        S.add("act", lambda e, pt=pt, sp_=sp_, N=N: e.activation(out=pt[:, 0:N], in_=sp_[:, 0:N], func=AF.Exp, scale=0.125),
                              reads=[spr], writes=[pt.r()])
                        lst.append(pt)
                    pts[gi] = lst

                for gi in range(min(LAG, len(groups))):
                    emit_scores(gi)
                accs = None
                for gi, (qc, ki) in enumerate(groups):
                    if ki == 0:
                        accs = [rotO.next(), rotO.next()]
                    k0, vt = keys[ki]
                    lst = pts.pop(gi)
                    for hh in range(2):
                        j = (qc // 3) * 2 + hh
                        acc, accr = accs[hh]
                        mm(acc[:, 0:N], vaug[:, vt, j, :], lst[hh][:, 0:N], ki == 0, ki == nk - 1, [vaug.r(vt), lst[hh].r()], [accr])
                    if gi + LAG < len(groups):
                        emit_scores(gi + LAG)
                    if NLAYERS > 1 and N == 512 and gi % 12 == 6 and ada1_next[0] < 12:
                        ada_tile(1, ada1_next[0], rotS.next())
                        ada1_next[0] += 1
                    if ki == nk - 1:
                        for hh in range(2):
                            acc, accr = accs[hh]
                            lo, hi = hh * 64, hh * 64 + 64
                            olo, ohi = (1 - hh) * 64, (1 - hh) * 64 + 64
                            rt = rtmp[nh % 2]
                            ev = evb[nh % 2]
                            nh += 1
                            S.add("dve", lambda e, ev=ev, acc=acc, N=N: e.tensor_copy(out=ev[:, 0:N], in_=acc[:, 0:N]), reads=[accr], writes=[ev.r()])
                            S.add("dve", lambda e, rt=rt, ev=ev, N=N, lo=lo, hi=hi, olo=olo, ohi=ohi: e.reciprocal(out=rt[lo:hi, 0:N], in_=ev[olo:ohi, 0:N]),
                                  reads=[ev.r()], writes=[rt.r()])
                            S.add("dve", lambda e, rt=rt, ev=ev, N=N, lo=lo, hi=hi, qc=qc, q0=q0: e.tensor_tensor(
                                out=hT[lo:hi, 2 + qc, q0:q0 + N], in0=ev[lo:hi, 0:N], in1=rt[lo:hi, 0:N], op=ALU.mult),
                                reads=[ev.r(), rt.r()], writes=[hT.r(b)])
            if NLAYERS > 1:
                while ada1_next[0] < 12:
                    ada_tile(1, ada1_next[0], rotS.next())
                    ada1_next[0] += 1
                ada_finish(1, 0)
                ada_finish(1, 1)
            A.free(*pT, *rtmp, *evb, qT, kT, vaug, rope, knew, vnew)

            if STAGE < 6:
                return
            uv = [A.alloc("uv%d" % i, [128, TB], BF16) for i in range(4)]
            rotU = Rot([0, 1, 2, 3])
            rotF = Rot([4, 5, 6, 7])
            c64c = cb16[:, 5, :]
            c64s = cb16[:, 6, :]
            n = 0

            def stageB(u, v, N, cch, q0, b):
                p, pr = rotF.next()
                mm(p[:, 0:N], c64c, u[:, 0:N], True, False, [cb16.r(), u.r()], [pr])
                mm(p[:, 0:N], c64s, v[:, 0:N], False, True, [cb16.r(), v.r()], [pr])
                S.add("act", lambda e: e.activation(out=hT[:, cch, q0:q0 + N], in_=p[:, 0:N], func=AF.Copy), reads=[pr], writes=[hT.r(b)])

            for sidx in range(2):
                for cch in range(2):
                    pair = []
                    for m in range(2):
                        p, pr = rotU.next()

                        def f(e, p=p, m=m, sidx=sidx, cch=cch):
                            for st_ in range(2):
                                ins = e.matmul(p[:, 0:256], ftok[:, 2 * sidx + st_, cch * 128:(cch + 1) * 128], dft256[:, m, st_, :],
                                               start=(st_ == 0), stop=(st_ == 1))
                            return ins
                        S.add("pe", f, reads=[ftok.r(2 * sidx), ftok.r(2 * sidx + 1), dft256.r()], writes=[pr])
                        u = uv[n % 4]
                        n += 1
                        S.add("dve", lambda e, u=u, p=p: e.tensor_copy(out=u[:, 0:256], in_=p[:, 0:256]), reads=[pr], writes=[u.r()])
                        pair.append(u)
                    stageB(pair[0], pair[1], 256, cch, sidx * 256, 0)
            for sb in range(2):
                wc, wcr = wget()
                wsn, wsr = wget()
                for cch in range(2):
                    pair = []
                    for m, (wm, wmr) in enumerate(((wc, wcr), (wsn, wsr))):
                        p, pr = rotU.next()

                        def f(e, p=p, wm=wm, cch=cch):
                            for st_ in range(8):
                                ins = e.matmul(p[:], ftok[:, 4 + st_, cch * 128:(cch + 1) * 128], wm[:, st_, :], start=(st_ == 0), stop=(st_ == 7))
                            return ins
                        S.add("pe", f, reads=ftok.rs(range(4, 12)) + [wmr], writes=[pr])
                        u = uv[n % 4]
                        n += 1
                        S.add("dve", lambda e, u=u, p=p: e.tensor_copy(out=u[:], in_=p[:]), reads=[pr], writes=[u.r()])
                        pair.append(u)
                    stageB(pair[0], pair[1], 512, cch, 512 + sb * 512, 1 + sb)
            A.free(*uv, ftok, dft256)

            if STAGE < 7:
                return
            rotW = Rot([0, 1, 2, 3])
            for t in range(2):
                w, wr = wget()
                for oc in range(4):
                    o = t * 4 + oc
                    for b in range(3):
                        blk = slice(b * TB, (b + 1) * TB)
                        p, pr = rotW.next()

                        def f(e, p=p, w=w, oc=oc, blk=blk):
                            for k in range(8):
                                ins = e.matmul(p[:], w[:, k, oc * 128:(oc + 1) * 128], hT[:, k, blk], start=(k == 0), stop=(k == 7))
                            return ins
                        S.add("pe", f, reads=[wr, hT.r(b)], writes=[pr])
                        resid(p, pr, l, 2, o, b)


        def rev(v):
            (ps_, pn), (st_, n) = v.ap
            return bass.AP(v.tensor, v.offset + (n - 1) * st_, [[ps_, pn], [-st_, n]])

        SEQS = [(0, 256), (256, 512), (512, 1536)]

        def conv4(eng, xp, yp, n, wcol, bcol):
            lo, hi = 1, n - 2
            S.add(eng, lambda e: e.tensor_scalar(out=yp[:, lo:hi], in0=xp[:, lo:hi], scalar1=vecT[:, wcol(1):wcol(1) + 1],
                                                 scalar2=vecT[:, bcol:bcol + 1], op0=ALU.mult, op1=ALU.add),
                  reads=[xp.r(), vecT.r()], writes=[yp.r()])
            for j, sh in ((0, -1), (2, 1), (3, 2)):
                S.add(eng, lambda e, j=j, sh=sh: e.scalar_tensor_tensor(out=yp[:, lo:hi], in0=xp[:, lo + sh:hi + sh], scalar=vecT[:, wcol(j):wcol(j) + 1],
                                                                        in1=yp[:, lo:hi], op0=ALU.mult, op1=ALU.add),
                      reads=[xp.r(), yp.r(), vecT.r()], writes=[yp.r()])

        def odd_layer(l):
            norm_mod(l, 0)
            yl = A.alloc("yl", [128, 4, NTOK], BF16)
            gg = A.alloc("gg", [128, 4, NTOK], BF16)
            xc = A.alloc("xc", [128, 4, NTOK], F32)
            bdw = A.alloc("bdw", [128, 16, 128], BF16)
            fin = A.alloc("fin", [128, 16], F32)
            clT = A.alloc("clT", [128, 8], F32)
            dma("pool", bdw[:], d_lrubd.rearrange("c p n -> p c n"), writes=[bdw.r()])
            S.add("act", lambda e: e.activation(out=clT[:], in_=vecT[:, R_LAM:R_LAM + 8], func=AF.Exp, scale=-1.0), reads=[vecT.r()], writes=[clT.r()])
            S.add("act", lambda e: e.activation(out=clT[:], in_=clT[:], func=AF.Ln, bias=ONE_AP[:, 0:1], scale=1.0), reads=[clT.r(), ONE_AP.r()], writes=[clT.r()])
            S.add("dve", lambda e: e.tensor_scalar(out=clT[:], in0=clT[:], scalar1=-8.0, scalar2=0.0, op0=ALU.mult, op1=ALU.add), reads=[clT.r()], writes=[clT.r()])
            NP = 1544
            POFF = [1, 259, 517]
            xpad = [A.alloc("xpad%d" % i, [128, NP], F32) for i in range(2)]
            ypad = [A.alloc("ypad%d" % i, [128, NP], F32) for i in range(2)]
            for xp in xpad:
                S.add("pool", lambda e, xp=xp: e.memset(xp[:], 0.0), writes=[xp.r()])
            rotA = Rot([0, 1, 2, 3])
            wg, wgr = wget()
            for c in range(4):
                for b in range(3):
                    blk = slice(b * TB, (b + 1) * TB)
                    p, pr = rotA.next()

                    def f(e, p=p, c=c, blk=blk):
                        for k in range(8):
                            ins = e.matmul(p[:], wg[:, k, c * 128:(c + 1) * 128], hT[:, k, blk], start=(k == 0), stop=(k == 7))
                        return ins
                    S.add("pe", f, reads=[wgr, hT.r(b)], writes=[pr])
                    S.add("act", lambda e, p=p, c=c, blk=blk: e.activation(out=gg[:, c, blk], in_=p[:], func=AF.Gelu_apprx_tanh), reads=[pr], writes=[gg.r(c)])
            wx, wxr = wget()
            for c in range(4):
                xp = xpad[c % 2]
                yp = ypad[c % 2]
                for b in range(3):
                    blk = slice(b * TB, (b + 1) * TB)
                    p, pr = rotA.next()

                    def f(e, p=p, c=c, blk=blk):
                        for k in range(8):
                            ins = e.matmul(p[:], wx[:, k, c * 128:(c + 1) * 128], hT[:, k, blk], start=(k == 0), stop=(k == 7))
                        return ins
                    S.add("pe", f, reads=[wxr, hT.r(b)], writes=[pr])
                    if b == 0:
                        ov = xp[:, 1:1 + 2 * 258].rearrange("p (a b) -> p a b", b=258)[:, :, 0:256]
                        S.add("act", lambda e, p=p, ov=ov: e.activation(out=ov, in_=p[:].rearrange("p (a b) -> p a b", a=2), func=AF.Copy), reads=[pr], writes=[xp.r()])
                    else:
                        o0 = 517 + (b - 1) * 512
                        S.add("act", lambda e, p=p, xp=xp, o0=o0: e.activation(out=xp[:, o0:o0 + 512], in_=p[:], func=AF.Copy), reads=[pr], writes=[xp.r()])
                conv4("dve", xp, yp, NP, lambda j, c=c: R_CLW + j * 4 + c, R_CLB + c)
                for si, (s0, s1) in enumerate(SEQS):
                    S.add("act", lambda e, yp=yp, c=c, s0=s0, s1=s1, si=si: e.activation(out=xc[:, c, s0:s1], in_=yp[:, POFF[si]:POFF[si] + s1 - s0], func=AF.Copy),
                          reads=[yp.r()], writes=[xc.r(c)])
            A.free(*xpad, *ypad)
            xcb = A.alloc("xcb", [128, NTOK], BF16)
            ra_ = [A.alloc("lra%d" % i, [128, NTOK], F32) for i in range(2)]
            ig_ = [A.alloc("lig%d" % i, [128, NTOK], F32) for i in range(2)]
            tq_ = [A.alloc("ltq0", [128, NTOK], F32)] * 2
            hd = [A.alloc("lh0", [128, NTOK], F32), ig_[1]]
            rotL = Rot([4, 5, 6, 7])
            xcbs = [xcb, xcb]

            def lruA(u):
                c, d = u // 2, u % 2
                ra, ig = ra_[d], ig_[d]
                xb = xcbs[c % 2]
                if d == 0:
                    S.add("act", lambda e: e.activation(out=xb[:], in_=xc[:, c, :], func=AF.Copy), reads=[xc.r(c)], writes=[xb.r()])
                for b in range(3):
                    blk = slice(b * TB, (b + 1) * TB)
                    for gate, dstT, brow in ((0, ra, R_BA), (1, ig, R_BX)):
                        p, pr = rotL.next()
                        mm(p[:], bdw[:, (gate * 2 + d) * 4 + c, :], xb[:, blk], True, True, [bdw.r(), xb.r()], [pr])
                        col = brow + d * 4 + c
                        S.add("act", lambda e, p=p, dstT=dstT, blk=blk, col=col: e.activation(out=dstT[:, blk], in_=p[:], func=AF.Sigmoid,
                                                                                               bias=vecT[:, col:col + 1], scale=1.0),
                              reads=[pr, vecT.r()], writes=[dstT.r()])

            def lruB(u):
                c, d = u // 2, u % 2
                ra, ig, tq = ra_[d], ig_[d], tq_[d]
                S.add("act", lambda e: e.activation(out=ra[:], in_=ra[:], func=AF.Exp, scale=clT[:, d * 4 + c:d * 4 + c + 1]),
                      reads=[ra.r(), clT.r()], writes=[ra.r()])
                S.add("act", lambda e: e.activation(out=tq[:], in_=ra[:], func=AF.Square), reads=[ra.r()], writes=[tq.r()])
                S.add("dve", lambda e: e.tensor_scalar(out=tq[:], in0=tq[:], scalar1=-1.0, scalar2=1.0, op0=ALU.mult, op1=ALU.add), reads=[tq.r()], writes=[tq.r()])
                S.add("act", lambda e: e.activation(out=tq[:], in_=tq[:], func=AF.Sqrt), reads=[tq.r()], writes=[tq.r()])
                S.add("dve", lambda e: e.tensor_tensor(out=ig[:], in0=ig[:], in1=xc[:, c, :], op=ALU.mult), reads=[ig.r(), xc.r(c)], writes=[ig.r()])
                S.add("dve", lambda e: e.tensor_tensor(out=tq[:], in0=tq[:], in1=ig[:], op=ALU.mult), reads=[tq.r(), ig.r()], writes=[tq.r()])
                h_ = hd[d]
                for si, (s0, s1) in enumerate(SEQS):
                    init = 0.0 if si < 2 else vecT[:, R_LRU0 + d * 4 + c:R_LRU0 + d * 4 + c + 1]
                    if d == 0:
                        S.add("dve", lambda e, s0=s0, s1=s1, init=init: e.tensor_tensor_scan(
                            out=h_[:, s0:s1], data0=ra[:, s0:s1], data1=tq[:, s0:s1], initial=init, op0=ALU.mult, op1=ALU.add),
                            reads=[ra.r(), tq.r(), vecT.r()], writes=[h_.r()])
                    else:
                        S.add("dve", lambda e, s0=s0, s1=s1, init=init: e.tensor_tensor_scan(
                            out=rev(h_[:, s0:s1]), data0=rev(ra[:, s0:s1]), data1=rev(tq[:, s0:s1]), initial=init, op0=ALU.mult, op1=ALU.add),
                            reads=[ra.r(), tq.r(), vecT.r()], writes=[h_.r()])
                    if si < 2:
                        pos = s1 - 1 if d == 0 else s0
                        col = (si * 2 + d) * 4 + c
                        S.add("dve", lambda e, pos=pos, col=col: e.tensor_copy(out=fin[:, col:col + 1], in_=h_[:, pos:pos + 1]),
                              reads=[h_.r()], writes=[fin.r()])
                if d == 1:
                    S.add("dve", lambda e: e.tensor_tensor(out=hd[0][:], in0=hd[0][:], in1=hd[1][:], op=ALU.add), reads=[hd[0].r(), hd[1].r()], writes=[hd[0].r()])
                    S.add("dve", lambda e: e.tensor_tensor(out=yl[:, c, :], in0=hd[0][:], in1=gg[:, c, :], op=ALU.mult),
                          reads=[hd[0].r(), gg.r(c)], writes=[yl.r()])

            lruA(0)
            for u in range(8):
                if u + 1 < 8:
                    lruA(u + 1)
                lruB(u)
            p, pr = PS(4)
            S.add("pe", lambda e: e.transpose(p[0:16, 0:128], fin[:, 0:16], ident), reads=[fin.r(), cf32.r()], writes=[pr])
            fino = A.alloc("fino", [16, 128], F32)
            S.add("dve", lambda e: e.tensor_copy(out=fino[:], in_=p[0:16, 0:128]), reads=[pr], writes=[fino.r()])
            dma("sp", o_lru, fino[:], reads=[fino.r()])
            A.free(gg, xc, bdw, clT, xcb, *ra_, *ig_, tq_[0], hd[0], fin, fino)
            if STAGE < 13:
                for _ in range(12):
                    wget()

            aneg = A.alloc("aneg", [128, 32], F32)
            S.add("act", lambda e: e.activation(out=aneg[:], in_=bv[:, B_ALOG:B_ALOG + 32], func=AF.Exp), reads=[bv.r()], writes=[aneg.r()])
            S.add("dve", lambda e: e.tensor_scalar(out=aneg[:], in0=aneg[:], scalar1=-1.0, scalar2=0.0, op0=ALU.mult, op1=ALU.add), reads=[aneg.r()], writes=[aneg.r()])
            nmk = [cb16[:, 2, :], cb16[:, 3, :]]
            tri = [cb16[:, 7, :], cb16[:, 8, :]]
            def ssd_group(grp):
                tiles = list(range(0, 4)) if grp == 0 else list(range(4, 12))
                nt = len(tiles)
                t0 = tiles[0]
                ntk = nt * 128
                blocks = [0] if grp == 0 else [1, 2]
                seqs = [[0, 1], [2, 3]] if grp == 0 else [list(range(8))]
                if grp == 0:
                    NPg = 520
                    poff = [1, 259]
                    slen = 256
                else:
                    NPg = 1032
                    poff = [1]
                    slen = 1024
                xs_tok = A.alloc("xs_tok", [128, nt, 1024], BF16)
                z_tok = A.alloc("z_tok", [128, nt, 1024], BF16)
                B_tok = A.alloc("B_tok", [128, nt, 128], BF16)
                BT = A.alloc("BT", [128, ntk], BF16)
                CT = A.alloc("CT", [128, ntk], BF16)
                xpad = [A.alloc("sxp%d" % i, [128, NPg], F32) for i in range(3)]
                ypad = [A.alloc("syp%d" % i, [128, NPg], F32) for i in range(3)]
                xsf = [A.alloc("xsf%d" % i, [128, ntk], F32) for i in range(3)]
                for xp in xpad:
                    S.add("pool", lambda e, xp=xp: e.memset(xp[:], 0.0), writes=[xp.r()])
                rotA = Rot([0, 1, 2, 3])
                rotT = Rot([4, 5, 6, 7])
                wc1 = {}

                def c1A(ch):
                    wt_i, cc = ch // 4, ch % 4
                    if wt_i not in wc1:
                        wc1[wt_i] = wget()
                    w, wr = wc1[wt_i]
                    xp = xpad[ch % 3]
                    for b in blocks:
                        blk = slice(b * TB, (b + 1) * TB)
                        p, pr = rotA.next()

                        def f(e, p=p, blk=blk):
                            for k in range(8):
                                ins = e.matmul(p[:], w[:, k, cc * 128:(cc + 1) * 128], hT[:, k, blk], start=(k == 0), stop=(k == 7))
                            return ins
                        S.add("pe", f, reads=[wr, hT.r(b)], writes=[pr])
                        if grp == 0:
                            ov = xp[:, 1:1 + 2 * 258].rearrange("p (a b) -> p a b", b=258)[:, :, 0:256]
                            S.add("act", lambda e, p=p, ov=ov: e.activation(out=ov, in_=p[:].rearrange("p (a b) -> p a b", a=2), func=AF.Copy), reads=[pr], writes=[xp.r()])
                        else:
                            o0 = 1 + (b - 1) * 512
                            S.add("act", lambda e, p=p, o0=o0: e.activation(out=xp[:, o0:o0 + 512], in_=p[:], func=AF.Copy), reads=[pr], writes=[xp.r()])

                def c1B(ch):
                    xp = xpad[ch % 3]
                    yp = ypad[ch % 3]
                    xf = xsf[ch % 3]
                    conv4("dve", xp, yp, NPg, lambda j: R_CSW + j * 10 + ch, R_CSB + ch)
                    for si, po in enumerate(poff):
                        if ch <= 8:
                            dst, dreg = xf[:, si * slen:(si + 1) * slen], xf.r()
                        else:
                            dst, dreg = CT[:, si * slen:(si + 1) * slen], CT.r()
                        S.add("act", lambda e, po=po, dst=dst: e.activation(out=dst, in_=yp[:, po:po + slen], func=AF.Silu), reads=[yp.r()], writes=[dreg])
                    if ch == 8:
                        S.add("act", lambda e: e.activation(out=BT[:], in_=xf[:], func=AF.Copy), reads=[xf.r()], writes=[BT.r()])

                def c1C(ch):
                    xf = xsf[ch % 3]
                    if ch <= 8:
                        for t4 in range(0, nt, 4):
                            p, pr = rotT.next()

                            def ftr(e, p=p, t4=t4):
                                for q in range(4):
                                    ins = e.transpose(p[:, q * 128:(q + 1) * 128], xf[:, (t4 + q) * 128:(t4 + q + 1) * 128], ident)
                                return ins
                            S.add("pe", ftr, reads=[xf.r(), cf32.r()], writes=[pr])
                            pv = p[:].rearrange("p (q n) -> p q n", q=4)
                            if ch < 8:
                                S.add("dve", lambda e, pv=pv, t4=t4: e.tensor_copy(out=xs_tok[:, t4:t4 + 4, ch * 128:(ch + 1) * 128], in_=pv),
                                      reads=[pr], writes=[xs_tok.r()])
                            else:
                                S.add("dve", lambda e, pv=pv, t4=t4: e.tensor_copy(out=B_tok[:, t4:t4 + 4, :], in_=pv), reads=[pr], writes=[B_tok.r()])

                for step in range(12):
                    if step < 10:
                        c1A(step)
                    if 0 <= step - 1 < 10:
                        c1B(step - 1)
                    if 0 <= step - 2 < 10:
                        c1C(step - 2)
                A.free(*xpad, *ypad, *xsf)
                if SUB == 1:
                    return
                sm = lambda name, w=32: A.alloc(name, [128, nt, w], F32)
                dtr = sm("dtr")
                dta = sm("dta")
                aal = sm("aal")
                stats = sm("stats", 64)
                nacs = sm("nacs")
                eacs = sm("eacs")
                wdec = sm("wdec")
                cdec = sm("cdec")
                cdsel = A.alloc("cdsel", [128, nt, 2, 8], F32)
                wz0, wz0r = wget()
                wz1, wz1r = wget()
                for ti in range(nt):
                    tg = t0 + ti
                    b = tg // 4
                    for hf, (wz, wzr) in enumerate(((wz0, wz0r), (wz1, wz1r))):
                        p, pr = rotA.next()

                        def f(e, p=p, wz=wz, tg=tg):
                            for k in range(8):
                                ins = e.matmul(p[:], hT[:, k, tg * 128:(tg + 1) * 128], wz[:, k, :], start=(k == 0), stop=(k == 7))
                            return ins
                        S.add("pe", f, reads=[wzr, hT.r(b)], writes=[pr])
                        S.add("act", lambda e, p=p, ti=ti, hf=hf: e.activation(out=z_tok[:, ti, hf * 512:(hf + 1) * 512], in_=p[:], func=AF.Silu),
                              reads=[pr], writes=[z_tok.r()])
                wd, wdr = wget()
                for ti in range(nt):
                    tg = t0 + ti
                    b = tg // 4
                    p, pr = rotA.next()

                    def f(e, p=p, tg=tg):
                        for k in range(8):
                            ins = e.matmul(p[:, 0:32], hT[:, k, tg * 128:(tg + 1) * 128], wd[:, k, :], start=(k == 0), stop=(k == 7))
                        return ins
                    S.add("pe", f, reads=[wdr, hT.r(b)], writes=[pr])
                    S.add("dve", lambda e, p=p, ti=ti: e.tensor_tensor(out=dtr[:, ti, :], in0=p[:, 0:32], in1=bv[:, B_DTB:B_DTB + 32], op=ALU.add),
                          reads=[pr, bv.r()], writes=[dtr.r()])
                S.add("act", lambda e: e.activation(out=dtr[:], in_=dtr[:], func=AF.Exp), reads=[dtr.r()], writes=[dtr.r()])
                S.add("act", lambda e: e.activation(out=dta[:], in_=dtr[:], func=AF.Ln, bias=ONE_AP[:, 0:1], scale=1.0), reads=[dtr.r(), ONE_AP.r()], writes=[dta.r()])
                S.add("dve", lambda e: e.tensor_tensor(out=aal[:], in0=dta[:], in1=aneg[:].unsqueeze(1).to_broadcast([128, nt, 32]), op=ALU.mult),
                      reads=[dta.r(), aneg.r()], writes=[aal.r()])
                ahi = A.alloc("ahi", [128, nt, 32], BF16)
                alo = A.alloc("alo", [128, nt, 32], BF16)
                S.add("act", lambda e: e.activation(out=ahi[:], in_=aal[:], func=AF.Copy), reads=[aal.r()], writes=[ahi.r()])
                S.add("dve", lambda e: e.tensor_tensor(out=alo[:], in0=aal[:], in1=ahi[:], op=ALU.subtract), reads=[aal.r(), ahi.r()], writes=[alo.r()])
                for ti in range(nt):
                    p, pr = rotT.next()
                    mm(p[:, 0:16], utri, aal[:, ti, 0:16], True, True, [cf32.r(), aal.r()], [pr])
                    mm(p[:, 16:32], ltri, aal[:, ti, 16:32], True, True, [cf32.r(), aal.r()], [pr])
                    mm(p[:, 32:64], onesf, aal[:, ti, :], True, True, [cf32.r(), aal.r()], [pr])
                    S.add("dve", lambda e, p=p, ti=ti: e.tensor_copy(out=stats[:, ti, :], in_=p[:, 0:64]), reads=[pr], writes=[stats.r()])
                S.add("act", lambda e: e.activation(out=nacs[:], in_=dta[:], func=AF.Ln), reads=[dta.r()], writes=[nacs.r()])
                S.add("dve", lambda e: e.tensor_tensor(out=nacs[:], in0=nacs[:], in1=stats[:, :, 0:32], op=ALU.subtract),
                      reads=[stats.r(), nacs.r()], writes=[nacs.r()])
                S.add("act", lambda e: e.activation(out=eacs[:], in_=stats[:, :, 0:32], func=AF.Exp), reads=[stats.r()], writes=[eacs.r()])
                S.add("dve", lambda e: e.tensor_tensor(out=wdec[:], in0=stats[:, :, 32:64], in1=stats[:, :, 0:32], op=ALU.subtract), reads=[stats.r()], writes=[wdec.r()])
                S.add("act", lambda e: e.activation(out=wdec[:], in_=wdec[:], func=AF.Exp), reads=[wdec.r()], writes=[wdec.r()])
                S.add("dve", lambda e: e.tensor_tensor(out=wdec[:], in0=wdec[:], in1=dta[:], op=ALU.mult), reads=[wdec.r(), dta.r()], writes=[wdec.r()])
                S.add("act", lambda e: e.activation(out=cdec[:], in_=stats[:, :, 32:64], func=AF.Exp), reads=[stats.r()], writes=[cdec.r()])
                for g in range(2):
                    src = cdec[g * 64:(g + 1) * 64, :, :].rearrange("p t (d g h) -> p t d g h", d=2, g=2)[:, :, :, g, :]
                    S.add("dve", lambda e, g=g, src=src: e.tensor_copy(out=cdsel[g * 64:(g + 1) * 64, :, :, :], in_=src), reads=[cdec.r()], writes=[cdsel.r()])
                A.free(dtr, cdec, stats, aal)
                if SUB == 2:
                    return
                Sst = [A.alloc("Sst%d" % d, [128, 512], F32) for d in range(2)]
                hinb = A.alloc("hinb", [128, nt, 512], BF16)
                hinf = A.alloc("hinf", [128, 512], BF16)
                xw = [A.alloc("xw%d" % i, [128, 1024], BF16) for i in range(2)]
                stmp = A.alloc("stmp", [128, 512], F32)
                stg = A.alloc("stg", [128, 8, 64], F32)
                rotP = Rot([5, 6])
                nxw = [0]

                def init_state(d):
                    St = Sst[d]
                    if grp == 0:
                        S.add("pool", lambda e: e.memset(St[:], 0.0), writes=[St.r()])
                    else:
                        dma("sp", stg[:], d_sssd[d].rearrange("(a p) n -> p a n", p=128), writes=[stg.r()])
                        for g in range(2):
                            p, pr = rotP.next()

                            def f(e, p=p, g=g):
                                for a4 in range(4):
                                    ins = e.transpose(p[0:64, a4 * 128:(a4 + 1) * 128], stg[:, g * 4 + a4, :], ident)
                                return ins
                            S.add("pe", f, reads=[stg.r(), cf32.r()], writes=[pr])
                            S.add("dve", lambda e, p=p, g=g: e.tensor_copy(out=St[g * 64:(g + 1) * 64, :], in_=p[0:64, :]), reads=[pr], writes=[St.r()])

                def out_state(d, sidx):
                    St = Sst[d]
                    for g in range(2):
                        p, pr = rotP.next()

                        def f(e, p=p, g=g):
                            for s4 in range(4):
                                ins = e.transpose(p[:, s4 * 64:(s4 + 1) * 64], St[g * 64:(g + 1) * 64, s4 * 128:(s4 + 1) * 128], ident[g * 64:(g + 1) * 64, g * 64:(g + 1) * 64])
                            return ins
                        S.add("pe", f, reads=[St.r(), cf32.r()], writes=[pr])
                        S.add("dve", lambda e, p=p, g=g: e.tensor_copy(out=stg[:, g * 4:(g + 1) * 4, :], in_=p[:, 0:256].rearrange("p (a n) -> p a n", a=4)), reads=[pr], writes=[stg.r()])
                    dma("sp", o_ssd[sidx, d].rearrange("(a p) n -> p a n", p=128), stg[:], reads=[stg.r()])

                def state_update(d, ti):
                    St = Sst[d]
                    x_ = xw[nxw[0] % 2]
                    nxw[0] += 1
                    S.add("pool", lambda e, x_=x_, ti=ti, d=d: e.tensor_tensor(
                        out=x_[:].rearrange("p (h q) -> p h q", h=16), in0=xs_tok[:, ti, :].rearrange("p (h q) -> p h q", h=16),
                        in1=wdec[:, ti, d * 16:(d + 1) * 16].unsqueeze(2).to_broadcast([128, 16, 64]), op=ALU.mult),
                        reads=[xs_tok.r(), wdec.r()], writes=[x_.r()])
                    p, pr = rotP.next()
                    for g in range(2):
                        mm(p[g * 64:(g + 1) * 64, :], B_tok[:, ti, g * 64:(g + 1) * 64], x_[:, g * 512:(g + 1) * 512], True, True, [B_tok.r(), x_.r()], [pr])
                    S.add("dve", lambda e, ti=ti, d=d: e.tensor_tensor(
                        out=stmp[:].rearrange("p (h q) -> p h q", h=8), in0=St[:].rearrange("p (h q) -> p h q", h=8),
                        in1=cdsel[:, ti, d, :].unsqueeze(2).to_broadcast([128, 8, 64]), op=ALU.mult),
                        reads=[St.r(), cdsel.r()], writes=[stmp.r()])
                    S.add("dve", lambda e, p=p: e.tensor_tensor(out=St[:], in0=p[:], in1=stmp[:], op=ALU.add), reads=[pr, stmp.r()], writes=[St.r()])

                for sidx, sq_ in enumerate(seqs):
                    init_state(1)
                    for ti in reversed(sq_):
                        S.add("act", lambda e, ti=ti: e.activation(out=hinb[:, ti, :], in_=Sst[1][:], func=AF.Copy), reads=[Sst[1].r()], writes=[hinb.r()])
                        if SUB != 5:
                            state_update(1, ti)
                    if grp == 0 and SUB != 4:
                        out_state(1, sidx)
                if SUB in (3, 4, 5):
                    return
                cbs = [A.alloc("cbs%d" % i, [128, 2, 128], F32) for i in range(2)]
                decb = [A.alloc("decb%d" % i, [128, 2, 128], F32) for i in range(3)]
                MTb = [A.alloc("MTb%d" % i, [128, 2, 128], BF16) for i in range(3)]
                yacc = A.alloc("yacc", [128, 1024], F32)
                ytmp = A.alloc("ytmp", [128, 1024], F32)
                ssq = A.alloc("ssq", [128, 2], F32)
                rotD = Rot([1, 2, 3, 4])
                nm_ = 0
                tiles_seq = []
                for sidx, sq_ in enumerate(seqs):
                    for i_, ti in enumerate(sq_):
                        tiles_seq.append((sidx, ti, i_ == 0, i_ == len(sq_) - 1))

                def make_tail(sidx, ti, first, last):
                    tg = t0 + ti
                    tcol = slice(ti * 128, (ti + 1) * 128)
                    th = []
                    if first:
                        th.append(lambda: init_state(0))
                    th.append(lambda: S.add("act", lambda e: e.activation(out=hinf[:], in_=Sst[0][:], func=AF.Copy), reads=[Sst[0].r()], writes=[hinf.r()]))
                    for d in range(2):
                        for g in range(2):
                            def yo(d=d, g=g):
                                hin = hinf[:] if d == 0 else hinb[:, ti, :]
                                hreg = hinf.r() if d == 0 else hinb.r()
                                p, pr = rotP.next()
                                mm(p[:], CT[g * 64:(g + 1) * 64, tcol], hin[g * 64:(g + 1) * 64, :], True, True, [CT.r(), hreg], [pr])
                                ea = eacs[:, ti, d * 16 + g * 8:d * 16 + g * 8 + 8].unsqueeze(2).to_broadcast([128, 8, 64])
                                pv = p[:].rearrange("p (h q) -> p h q", h=8)
                                tsl = ytmp[:, g * 512:(g + 1) * 512].rearrange("p (h q) -> p h q", h=8)
                                S.add("dve", lambda e: e.tensor_tensor(out=tsl, in0=pv, in1=ea, op=ALU.mult), reads=[pr, eacs.r()], writes=[ytmp.r()])
                                S.add("dve", lambda e: e.tensor_tensor(out=yacc[:, g * 512:(g + 1) * 512], in0=yacc[:, g * 512:(g + 1) * 512],
                                                                       in1=ytmp[:, g * 512:(g + 1) * 512], op=ALU.add),
                                      reads=[yacc.r(), ytmp.r()], writes=[yacc.r()])
                            th.append(yo)
                    th.append(lambda: S.add("dve", lambda e: e.tensor_tensor(
                        out=ytmp[:].rearrange("p (h q) -> p h q", h=16), in0=xs_tok[:, ti, :].rearrange("p (h q) -> p h q", h=16),
                        in1=bv[:, B_D:B_D + 16].unsqueeze(2).to_broadcast([128, 16, 64]), op=ALU.mult),
                        reads=[xs_tok.r(), bv.r(), yacc.r()], writes=[ytmp.r()]))
                    th.append(lambda: S.add("dve", lambda e: e.tensor_tensor(out=yacc[:], in0=yacc[:], in1=ytmp[:], op=ALU.add), reads=[yacc.r(), ytmp.r()], writes=[yacc.r()]))
                    th.append(lambda: S.add("dve", lambda e: e.tensor_tensor(out=yacc[:], in0=yacc[:], in1=z_tok[:, ti, :], op=ALU.mult), reads=[yacc.r(), z_tok.r()], writes=[yacc.r()]))
                    th.append(lambda: S.add("act", lambda e: e.activation(out=ytmp[:], in_=yacc[:], func=AF.Square, accum_out=ssq[:, 0:1]), reads=[yacc.r()], writes=[ytmp.r(), ssq.r()]))
                    th.append(lambda: S.add("act", lambda e: e.activation(out=ssq[:, 0:1], in_=ssq[:, 0:1], func=AF.Ln, bias=EPS_AP[:, 0:1], scale=1.0 / 1024), reads=[ssq.r(), EPS_AP.r()], writes=[ssq.r()]))
                    th.append(lambda: S.add("act", lambda e: e.activation(out=ssq[:, 0:1], in_=ssq[:, 0:1], func=AF.Exp, scale=-0.5), reads=[ssq.r()], writes=[ssq.r()]))
                    th.append(lambda: S.add("dve", lambda e: e.scalar_tensor_tensor(out=ytmp[:], in0=yacc[:], scalar=ssq[:, 0:1], in1=bv[:, B_NORM:B_NORM + 1024], op0=ALU.mult, op1=ALU.mult),
                                            reads=[yacc.r(), ssq.r(), bv.r()], writes=[ytmp.r()]))
                    for hf in range(2):
                        def trp(hf=hf):
                            p, pr = PS(7) if hf == 0 else rotP.next()

                            def ftr(e):
                                for c in range(4):
                                    ins = e.transpose(p[:, c * 128:(c + 1) * 128], ytmp[:, (hf * 4 + c) * 128:(hf * 4 + c + 1) * 128], ident)
                                return ins
                            S.add("pe", ftr, reads=[ytmp.r(), cf32.r()], writes=[pr])
                            outv = hT[:, hf * 4:hf * 4 + 4, tg * 128:(tg + 1) * 128]
                            S.add("act", lambda e: e.activation(out=outv, in_=p[:].rearrange("p (c n) -> p c n", c=4), func=AF.Copy),
                                  reads=[pr], writes=[hT.r(tg // 4)])
                        th.append(trp)
                    th.append(lambda: state_update(0, ti))
                    if last and grp == 0:
                        th.append(lambda: out_state(0, sidx))
                    return th

                pending = []
                for (sidx, ti, first, last) in tiles_seq:
                    tcol = slice(ti * 128, (ti + 1) * 128)
                    cb_ = cbs[ti % 2]
                    for g in range(2):
                        p, pr = PS(7) if g == 0 else rotP.next()
                        mm(p[:, 0:128], BT[g * 64:(g + 1) * 64, tcol], CT[g * 64:(g + 1) * 64, tcol], True, True, [BT.r(), CT.r()], [pr])
                        S.add("act", lambda e, p=p, cb_=cb_, g=g: e.activation(out=cb_[:, g, :], in_=p[:, 0:128], func=AF.Copy), reads=[pr], writes=[cb_.r()])
                    ydp, ydr = PS(0)
                    munits = [(hp, d) for hp in range(8) for d in range(2)]
                    mts = {}

                    def emit_pdc(u, ti=ti, cb_=cb_, mts=mts, munits=munits):
                        nonlocal nm_
                        hp, d = munits[u]
                        g = hp // 4
                        pd, pdr = rotD.next()

                        def f(e, pd=pd, ti=ti, hp=hp, d=d):
                            for i_ in range(2):
                                hc = d * 16 + 2 * hp + i_
                                o_ = pd[:, i_ * 128:(i_ + 1) * 128]
                                e.matmul(o_, ahi[:, ti, hc:hc + 1].to_broadcast([128, 128]), tri[d], start=True, stop=False)
                                e.matmul(o_, alo[:, ti, hc:hc + 1].to_broadcast([128, 128]), tri[d], start=False, stop=False)
                                ins = e.matmul(o_, nmk[d], identb, start=False, stop=True)
                            return ins
                        S.add("pe", f, reads=[ahi.r(), alo.r(), cb16.r()], writes=[pdr])
                        dc = decb[nm_ % 3]
                        mt = MTb[nm_ % 3]
                        nm_ += 1
                        for i_ in range(2):
                            hc = d * 16 + 2 * hp + i_
                            S.add("act", lambda e, pd=pd, dc=dc, ti=ti, hc=hc, i_=i_: e.activation(out=dc[:, i_, :], in_=pd[:, i_ * 128:(i_ + 1) * 128], func=AF.Exp,
                                                                                                     bias=nacs[:, ti, hc:hc + 1], scale=1.0),
                                  reads=[pdr, nacs.r()], writes=[dc.r()])
                        S.add("pool", lambda e, dc=dc, mt=mt, g=g, cb_=cb_: e.tensor_tensor(
                            out=mt[:], in0=dc[:], in1=cb_[:, g, :].unsqueeze(1).to_broadcast([128, 2, 128]), op=ALU.mult),
                            reads=[dc.r(), cb_.r()], writes=[mt.r()])
                        mts[u] = mt

                    LAm = 3
                    for u in range(LAm):
                        emit_pdc(u)
                    for u, (hp, d) in enumerate(munits):
                        mt = mts.pop(u)

                        def fy(e, mt=mt, hp=hp, d=d, u=u, ti=ti):
                            for i_ in range(2):
                                h = 2 * hp + i_
                                ins = e.matmul(ydp[:, (h % 8) * 64:(h % 8 + 1) * 64], mt[:, i_, :], xs_tok[:, ti, h * 64:(h + 1) * 64],
                                               start=(u % 8 == 0 and i_ == 0), stop=(d == 1), skip_group_check=True)
                            return ins
                        S.add("pe", fy, reads=[mt.r(), xs_tok.r()], writes=[ydr])
                        if u + LAm < len(munits):
                            emit_pdc(u + LAm)
                        if u < 7:
                            for _ in range(4):
                                if pending:
                                    pending.pop(0)()
                        if u % 8 == 7:
                            while pending:
                                pending.pop(0)()
                            g = u // 8
                            S.add("act", lambda e, g=g: e.activation(out=yacc[:, g * 512:(g + 1) * 512], in_=ydp[:], func=AF.Copy), reads=[ydr], writes=[yacc.r()])
                    pending = make_tail(sidx, ti, first, last)
                while pending:
                    pending.pop(0)()
                A.free(xs_tok, z_tok, B_tok, BT, CT, dta, ahi, alo, nacs, eacs, wdec, cdsel, *Sst, hinb, hinf, *xw, stmp, stg,
                       *cbs, *decb, *MTb, yacc, ytmp, ssq)
            for grp in range((2 if SUB == 0 else 1) if STAGE >= 13 else 0):
                ssd_group(grp if SUB < 10 else 1)
            A.free(aneg)
            while wstate["got"] < wstate["woo"]:
                wget()
            rotW = Rot([0, 1, 2, 3])
            for t in range(4):
                w, wr = wget()
                for oc in range(2):
                    o = t * 2 + oc
                    for b in range(3):
                        blk = slice(b * TB, (b + 1) * TB)
                        p, pr = rotW.next()

                        def f(e, p=p, w=w, oc=oc, blk=blk):
                            for k in range(12):
                                rhs = yl[:, k, blk] if k < 4 else hT[:, k - 4, blk]
                                ins = e.matmul(p[:], w[:, k, oc * 128:(oc + 1) * 128], rhs, start=(k == 0), stop=(k == 11))
                            return ins
                        S.add("pe", f, reads=[wr, hT.r(b), yl.r()], writes=[pr])
                        resid(p, pr, l, 2, o, b)
            A.free(yl)

        if STAGE >= 1:
            for t in range(4):
                ada_tile(0, t, 0)
            ada_finish(0, 0)
        if STAGE >= 2:
            even_layer(0)
        if STAGE >= 10:
            ffn(0)
        if NLAYERS > 1:
            odd_layer(1)
            if STAGE >= 20:
                ffn(1)

        yst = [A.alloc("yst%d" % i, [128, D], F32) for i in range(2)]
        rot = Rot([0, 1, 2, 3, 4, 5, 6, 7])
        for t in range(12):
            ys = yst[t % 2]
            for hf in range(2):
                p, pr = rot.next()

                def tr(e, p=p, t=t, hf=hf):
                    for c in range(4):
                        ins = e.transpose(p[:, c * 128:(c + 1) * 128], xT[:, hf * 4 + c, t * 128:(t + 1) * 128], ident)
                    return ins
                S.add("pe", tr, reads=[xT.r(t), cf32.r()], writes=[pr])
                if hf == 0:
                    S.add("act", lambda e, ys=ys, p=p: e.activation(out=ys[:, 0:512], in_=p[:], func=AF.Copy), reads=[pr], writes=[ys.r()])
                else:
                    S.add("dve", lambda e, ys=ys, p=p: e.tensor_copy(out=ys[:, 512:1024], in_=p[:]), reads=[pr], writes=[ys.r()])
            dma("sp", o_y[t * 128:(t + 1) * 128, :], ys[:], reads=[ys.r()])
        assert STAGE < 99 or wstate["got"] == len(wplan), (wstate, len(wplan))
        S.emit(st)
    return nc


_CACHE = {}


def _prep_shared(inp):
    f = np.float32
    sh = {}
    sh["w_ada"] = np.ascontiguousarray(inp["w_ada"], f)
    wie = np.asarray(inp["w_in_even"][0], f)
    fcols = wie[:, 0:256]
    q = wie[:, 256:1024].reshape(D, 12, 64)[:, PERM, :].reshape(D, 768)
    k = wie[:, 1024:1280]
    v = wie[:, 1280:1536]
    sh["w_in_even"] = np.ascontiguousarray(np.concatenate([q, k, fcols, v], 1))
    woe = np.asarray(inp["w_out_even"][0], f)
    att = woe[256:].reshape(12, 64, D)[PERM].reshape(768, D)
    sh["w_out_even"] = np.ascontiguousarray(np.concatenate([woe[:256], att], 0))
    sh["w_in_odd"] = np.ascontiguousarray(inp["w_in_odd"][0], f)
    sh["w_out_odd"] = np.ascontiguousarray(inp["w_out_odd"][0], f)
    sh["ffn_w1"] = np.ascontiguousarray(inp["ffn_w1"], f)
    sh["ffn_w3"] = np.ascontiguousarray(inp["ffn_w3"], f)
    sh["ffn_w2"] = np.ascontiguousarray(inp["ffn_w2"], f)
    bd = np.zeros((2, 2, 4, 128, 128), f)
    for gi, wname in enumerate(("lru_wa", "lru_wx")):
        wsrc = np.asarray(inp[wname][0], f)
        for d in range(2):
            for blk in range(8):
                c, hlf = blk // 2, blk % 2
                bd[gi, d, c, hlf * 64:(hlf + 1) * 64, hlf * 64:(hlf + 1) * 64] = wsrc[d, blk]
    sh["lru_bd"] = bd.reshape(16, 128, 128)
    bvec = np.zeros((1, NBV), f)
    bvec[0, B_NORM:B_NORM + 1024] = inp["ssd_norm"][0]
    bvec[0, B_D:B_D + 16] = inp["ssd_d"][0]
    bvec[0, B_DTB:B_DTB + 32] = np.asarray(inp["ssd_dt_bias"][0]).reshape(32)
    bvec[0, B_ALOG:B_ALOG + 32] = np.asarray(inp["ssd_a_log"][0]).reshape(32)
    sh["bvec"] = bvec
    sh.update(_consts())
    return sh


def _vecs(inp, i):
    f = np.float32
    v = np.zeros((256, 128), f)
    v[R_BADA:R_BADA + 96] = np.asarray(inp["b_ada"], f).reshape(96, 128)
    v[R_CCTX:R_CCTX + 8] = np.asarray(inp["c_ctx"], f).reshape(8, 128)
    v[R_CI:R_CI + 8] = np.asarray(inp["c"][i], f).reshape(8, 128)
    v[R_NMIX:R_NMIX + 16] = np.asarray(inp["norm_mix"], f).reshape(16, 128)
    v[R_NFFN:R_NFFN + 16] = np.asarray(inp["norm_ffn"], f).reshape(16, 128)
    v[R_CLW:R_CLW + 16] = np.asarray(inp["conv_lru_w"][0], f).reshape(16, 128)
    v[R_CLB:R_CLB + 4] = np.asarray(inp["conv_lru_b"][0], f).reshape(4, 128)
    v[R_BA:R_BA + 8] = np.asarray(inp["lru_ba"][0], f).reshape(8, 128)
    v[R_BX:R_BX + 8] = np.asarray(inp["lru_bx"][0], f).reshape(8, 128)
    v[R_LAM:R_LAM + 8] = np.asarray(inp["lru_lambda"][0], f).reshape(8, 128)
    v[R_CSW:R_CSW + 40] = np.asarray(inp["conv_ssd_w"][0], f).reshape(40, 128)
    v[R_CSB:R_CSB + 10] = np.asarray(inp["conv_ssd_b"][0], f).reshape(10, 128)
    v[R_QN] = np.tile(np.asarray(inp["q_norm"][0], f), 2)
    v[R_KN] = np.tile(np.asarray(inp["k_norm"][0], f), 2)
    v[R_LRU0:R_LRU0 + 8] = np.asarray(inp["state_lru"][i, 0], f).reshape(8, 128)
    return v


def kernel(**inp):
    inp = {k: np.asarray(v) for k, v in inp.items()}
    if "nc" not in _CACHE:
        _CACHE["nc"] = build_program()
    nc = _CACHE["nc"]
    sh = _prep_shared(inp)
    in_maps = []
    for i in range(NCORES):
        m = dict(sh)
        m["x"] = np.ascontiguousarray(np.concatenate(
            [inp["x_prompt"][2 * i], inp["x_prompt"][2 * i + 1], inp["x_sample"][i]], 0), np.float32)
        m["vecs"] = _vecs(inp, i)
        m["cache_k"] = np.ascontiguousarray(inp["cache_k"][i, 0].reshape(512, 256), np.float32)
        m["cache_v"] = np.ascontiguousarray(inp["cache_v"][i, 0].reshape(512, 256), np.float32)
        m["state_ssd"] = np.ascontiguousarray(inp["state_ssd"][i, 0].reshape(2, 1024, 64), np.float32)
        in_maps.append(m)
    res = run_bass_kernel_spmd(nc, in_maps[:NRUN], core_ids=list(range(NRUN)))
    R = res.results
    y_prompt = np.zeros((16, 256, D), np.float32)
    y_sample = np.zeros((8, 1024, D), np.float32)
    new_k = np.zeros((16, 1, 256, 4, 64), np.float32)
    new_v = np.zeros((16, 1, 256, 4, 64), np.float32)
    new_lru = np.zeros((16, 1, 2, 512), np.float32)
    new_ssd = np.zeros((16, 1, 2, 16, 64, 64), np.float32)
    for i in range(NRUN):
        y = np.asarray(R[i]["y"])
        y_prompt[2 * i] = y[0:256]
        y_prompt[2 * i + 1] = y[256:512]
        y_sample[i] = y[512:]
        nk = np.asarray(R[i]["newk"]).reshape(2, 256, 4, 64)
        nv = np.asarray(R[i]["newv"]).reshape(2, 256, 4, 64)
        new_k[2 * i:2 * i + 2, 0] = nk
        new_v[2 * i:2 * i + 2, 0] = nv
        new_lru[2 * i:2 * i + 2, 0] = np.asarray(R[i]["newlru"]).reshape(2, 2, 512)
        new_ssd[2 * i:2 * i + 2, 0] = np.asarray(R[i]["newssd"]).reshape(2, 2, 16, 64, 64)
    return (y_prompt, y_sample, new_k, new_v, new_lru, new_ssd)
```

```python
import numpy as np
from contextlib import ExitStack
import concourse.bass as bass
import concourse.mybir as mybir
from concourse.bass_utils import run_bass_kernel_spmd

F32 = mybir.dt.float32
BF16 = mybir.dt.bfloat16
AF = mybir.ActivationFunctionType
ALU = mybir.AluOpType

ENGS = ("pe", "act", "dve", "pool", "sp")
NDMASEM = 6
NCORES = 8
NRUN = 8
EPS = 1e-6
NLAYERS = 2
STAGE = 99
SUB = 0


class Reg:
    __slots__ = ("name", "lw", "rd", "excl")

    def __init__(self, name, inherit=(), excl=False):
        self.name = name
        self.lw = None
        self.rd = list(inherit)
        self.excl = excl


class Op:
    __slots__ = ("eng", "fn", "deps", "needed", "dma", "sem", "val", "gidx", "thr")

    def __init__(self, eng, fn, dma):
        self.eng = eng
        self.fn = fn
        self.dma = dma
        self.deps = []
        self.needed = False
        self.sem = None
        self.val = 0
        self.thr = None


class Sched:
    def __init__(self, nc):
        self.nc = nc
        self.ops = {e: [] for e in ENGS}
        self.all = []
        self.ndma = {e: 0 for e in ENGS}
        self.dmaops = {e: [] for e in ENGS}

    def add(self, eng, fn, reads=(), writes=(), dma=False):
        op = Op(eng, fn, dma)
        deps = {}
        for r in reads:
            w = r.lw
            if w is not None and (w.dma or dma or w.eng != eng or eng != "pe"):
                deps[id(w)] = w
            if r.excl:
                for q in r.rd:
                    if q.eng != eng:
                        deps[id(q)] = q
        for t in writes:
            w = t.lw
            if w is not None and (w.dma or dma or w.eng != eng or eng != "pe"):
                deps[id(w)] = w
            for q in t.rd:
                if q.dma or dma or q.eng != eng or eng != "pe":
                    deps[id(q)] = q
        op.deps = list(deps.values())
        for r in reads:
            if not dma:
                r.rd = [q for q in r.rd if q.dma or q.eng != eng]
            r.rd.append(op)
        for t in writes:
            t.lw = op
            t.rd = []
        if dma:
            i = self.ndma[eng]
            self.ndma[eng] += 1
            if i >= NDMASEM:
                op.thr = self.dmaops[eng][i - NDMASEM]
            self.dmaops[eng].append(op)
        op.gidx = len(self.all)
        self.all.append(op)
        self.ops[eng].append(op)
        return op

    def emit(self, stack):
        nc = self.nc
        for op in self.all:
            for d in op.deps:
                d.needed = True
        esem = {e: stack.enter_context(nc.semaphore("s_" + e)) for e in ENGS if e != "sp"}
        dsem = {e: [stack.enter_context(nc.semaphore("d_%s%d" % (e, i))) for i in range(NDMASEM)]
                for e in ENGS if self.ndma[e] > 0}
        for e in ENGS:
            cnt = 0
            i = 0
            for op in self.ops[e]:
                if op.dma:
                    op.sem = dsem[e][i % NDMASEM]
                    op.val = 16 * (i // NDMASEM + 1)
                    i += 1
                elif op.needed:
                    cnt += 1
                    op.sem = esem[e]
                    op.val = cnt
        block = stack.enter_context(nc.Block())
        engh = {"pe": block.tensor, "act": block.scalar, "dve": block.vector,
                "pool": block.gpsimd, "sp": block.sync}
        for e in ENGS:
            ops = self.ops[e]
            if not ops:
                continue

            def body(eng, ops=ops, e=e):
                waited = {}
                for op in ops:
                    ds = list(op.deps)
                    if op.thr is not None:
                        ds.append(op.thr)
                    for d in ds:
                        k = id(d.sem)
                        if waited.get(k, 0) >= d.val:
                            continue
                        waited[k] = d.val
                        eng.wait_ge(d.sem, d.val)
                    ins = op.fn(eng)
                    if op.dma:
                        ins.then_inc(op.sem, 16)
                    elif op.needed:
                        ins.then_inc(op.sem, 1)
                for d in self.dmaops[e][-NDMASEM:]:
                    if waited.get(id(d.sem), 0) < d.val:
                        waited[id(d.sem)] = d.val
                        eng.wait_ge(d.sem, d.val)

            engh[e](body)


def _prune(ops):
    best = {}
    out = []
    for o in ops:
        if o.dma:
            out.append(o)
        else:
            b = best.get(o.eng)
            if b is None or o.gidx > b.gidx:
                best[o.eng] = o
    return out + list(best.values())


class TV:
    def __init__(self, ap, name, off, nw, inherit):
        self.ap = ap
        self.name = name
        self.off = off
        self.nw = nw
        self.inherit = inherit
        self.regs = {}

    def __getitem__(self, k):
        return self.ap[k]

    def r(self, key=0):
        g = self.regs.get(key)
        if g is None:
            g = Reg("%s.%s" % (self.name, key), self.inherit)
            self.regs[key] = g
        return g

    def rs(self, keys):
        return [self.r(k) for k in keys]


class Arena:
    def __init__(self, nc, stack, nwords):
        self.t = stack.enter_context(nc.sbuf_tensor("arena", [128, nwords], F32))
        self.n = nwords
        self.live = []
        self.dead = []
        self.peak = 0

    def alloc(self, name, shape, dt):
        n = int(np.prod(shape[1:]))
        nw = n if dt == F32 else (n + 1) // 2
        nw = (nw + 7) // 8 * 8
        off = 0
        for tv in sorted(self.live, key=lambda t: t.off):
            if tv.off - off >= nw:
                break
            off = max(off, tv.off + tv.nw)
        assert off + nw <= self.n, "SBUF arena overflow allocating %s (%d words at %d)" % (name, nw, off)
        self.peak = max(self.peak, off + nw)
        pend = []
        keep = []
        for (o, w, ops) in self.dead:
            if o < off + nw and off < o + w:
                pend += ops
                if not (off <= o and o + w <= off + nw):
                    keep.append((o, w, ops))
            else:
                keep.append((o, w, ops))
        self.dead = keep
        v = self.t[0:shape[0], off:off + nw]
        if dt != F32:
            v = v.bitcast(dt)
        v = v[:, 0:n]
        if len(shape) == 3:
            v = v.rearrange("p (a b) -> p a b", a=shape[1])
        elif len(shape) == 4:
            v = v.rearrange("p (a b c) -> p a b c", a=shape[1], b=shape[2])
        tv = TV(v, name, off, nw, _prune(pend))
        self.live.append(tv)
        return tv

    def free(self, *tvs):
        for tv in tvs:
            ops = list(tv.inherit)
            for g in tv.regs.values():
                if g.lw is not None:
                    ops.append(g.lw)
                ops += g.rd
            self.dead.append((tv.off, tv.nw, _prune(ops)))
            self.live.remove(tv)


D = 1024
NTOK = 1536
TB = 512
PERM = [0, 3, 1, 4, 2, 5, 6, 9, 7, 10, 8, 11]
DFF = 2816
R_BADA, R_CCTX, R_CI, R_NMIX, R_NFFN = 0, 96, 104, 112, 128
R_CLW, R_CLB, R_BA, R_BX, R_LAM, R_CSW, R_CSB, R_QN, R_KN, R_LRU0 = 144, 160, 164, 172, 180, 188, 228, 238, 239, 240
B_NORM, B_D, B_DTB, B_ALOG, NBV = 0, 1024, 1040, 1072, 1104


def _consts():
    c = {}
    ident = np.eye(128, dtype=np.float32)
    onesm = np.full((128, 128), 1.0 / 1024, np.float32)
    o64 = np.zeros((128, 128), np.float32)
    o64[:64, :64] = 1.0 / 64
    o64[64:, 64:] = 1.0 / 64
    R = np.zeros((128, 128), np.float32)
    for h in range(2):
        for ax in range(2):
            for i in range(16):
                p1 = h * 64 + ax * 32 + i
                p2 = p1 + 16
                R[p1, p2] = -1.0
                R[p2, p1] = 1.0
    j = np.arange(128)
    utri = (j[:, None] <= j[None, :]).astype(np.float32)
    ltri = (j[:, None] >= j[None, :]).astype(np.float32)
    nmf = np.where(j[:, None] < j[None, :], -32768.0, 0.0).astype(np.float32)
    nmb = np.where(j[:, None] > j[None, :], -32768.0, 0.0).astype(np.float32)
    onesf = np.ones((128, 128), np.float32)
    c["cf32"] = np.stack([ident, R.T.copy(), utri, ltri, onesf])
    a64 = 2 * np.pi * np.outer(np.arange(64), np.arange(64)) / 64
    c64 = np.zeros((2, 128, 128), np.float32)
    for g in range(2):
        c64[0, g * 64:(g + 1) * 64, g * 64:(g + 1) * 64] = np.cos(a64) / 8
        c64[1, g * 64:(g + 1) * 64, g * 64:(g + 1) * 64] = -np.sin(a64) / 8
    c["cb16"] = np.stack([onesm, o64, nmf, nmb, ident, c64[0], c64[1], utri, ltri])
    for S in (256, 1024):
        a = 2 * np.pi * (np.outer(np.arange(S), np.arange(S)) % S) / S
        c["dft%d" % S] = np.stack([np.cos(a), np.sin(a)]).astype(np.float32) / np.sqrt(S)
    s = np.arange(1024)
    row = (s // 64).astype(np.float32)
    col = (s % 64).astype(np.float32)
    freqs = (10000.0 ** (-np.arange(16, dtype=np.float32) / 16)).astype(np.float32)
    ang = np.zeros((64, 1024), np.float32)
    for d in range(64):
        ax = d // 32
        i = d % 16
        ang[d] = (row if ax == 0 else col) * freqs[i]
    ang = np.concatenate([ang, ang], 0)
    c["rope"] = np.stack([np.cos(ang), np.sin(ang)]).astype(np.float32)
    return c


def build_program():
    nc = bass.Bass("TRN2", target_bir_lowering=False)
    S = Sched(nc)

    def din(name, shape):
        return nc.dram_tensor(name, list(shape), F32, kind="ExternalInput").ap()

    def dout(name, shape):
        return nc.dram_tensor(name, list(shape), F32, kind="ExternalOutput").ap()

    d_x = din("x", [NTOK, D])
    d_vecs = din("vecs", [256, 128])
    d_bvec = din("bvec", [1, NBV])
    d_wada = din("w_ada", [2, D, 6 * D])
    d_wie = din("w_in_even", [D, 1536])
    d_woe = din("w_out_even", [D, D])
    d_wio = din("w_in_odd", [D, 3360])
    d_woo = din("w_out_odd", [1536, D])
    d_w1 = din("ffn_w1", [2, D, DFF])
    d_w3 = din("ffn_w3", [2, D, DFF])
    d_w2 = din("ffn_w2", [2, DFF, D])
    d_lrubd = din("lru_bd", [16, 128, 128])
    d_ck = din("cache_k", [512, 256])
    d_cv = din("cache_v", [512, 256])
    d_sssd = din("state_ssd", [2, 1024, 64])
    d_cf32 = din("cf32", [5, 128, 128])
    d_cb16 = din("cb16", [9, 128, 128])
    d_dft256 = din("dft256", [2, 256, 256])
    d_dft1024 = din("dft1024", [2, 1024, 1024])
    d_rope = din("rope", [2, 128, 1024])
    o_y = dout("y", [NTOK, D])
    o_k = dout("newk", [512, 256])
    o_v = dout("newv", [512, 256])
    o_lru = dout("newlru", [16, 128])
    o_ssd = dout("newssd", [2, 2, 1024, 64])

    with ExitStack() as st:
        A = Arena(nc, st, 53100)
        psb = []
        for i in range(8):
            t = st.enter_context(nc.psum_tensor("ps%d" % i, [128, 512], F32))
            psb.append((t, Reg("ps%d" % i, excl=True)))

        def PS(i):
            return psb[i]

        class Rot:
            def __init__(self, banks):
                self.b = list(banks)
                self.i = 0

            def next(self):
                r = psb[self.b[self.i % len(self.b)]]
                self.i += 1
                return r

        def dma(q, out, in_, reads=(), writes=()):
            return S.add(q, lambda e: e.dma_start(out=out, in_=in_), reads=reads, writes=writes, dma=True)

        def mm(out, lhsT, rhs, start, stop, reads, writes):
            return S.add("pe", lambda e: e.matmul(out, lhsT, rhs, start=start, stop=stop), reads=reads, writes=writes)

        NSLOT = 4
        slots = [A.alloc("wslot%d" % i, [128, 4096], BF16) for i in range(NSLOT)]
        wplan = []
        wstate = {"issued": 0, "got": 0}

        def wplan_add(ap, a, b):
            wplan.append((ap, a, b))

        def kp(ap2d):
            return ap2d.rearrange("(k p) n -> p k n", p=128)

        def wview(i):
            ap, a, b = wplan[i]
            sl = slots[i % NSLOT]
            return sl[:, 0:a * b].rearrange("p (a b) -> p a b", a=a), sl.r()

        def wget():
            i = wstate["got"]
            wstate["got"] += 1
            while wstate["issued"] < min(len(wplan), i + NSLOT - 1):
                j = wstate["issued"]
                v, rg = wview(j)
                dma("pool", v, wplan[j][0], writes=[rg])
                wstate["issued"] += 1
            return wview(i)

        def plan_ada(l):
            for t in range(12):
                wplan_add(kp(d_wada[l])[:, :, t * 512:(t + 1) * 512], 8, 512)

        def plan_ada_tiles(l, ts):
            for t in ts:
                wplan_add(kp(d_wada[l])[:, :, t * 512:(t + 1) * 512], 8, 512)

        def plan_ffn(l):
            for jt in range(6):
                n = 512 if jt < 5 else 256
                wplan_add(kp(d_w1[l])[:, :, jt * 512:jt * 512 + n], 8, n)
                wplan_add(kp(d_w3[l])[:, :, jt * 512:jt * 512 + n], 8, n)
            for o in range(8):
                wplan_add(kp(d_w2[l])[:, :, o * 128:(o + 1) * 128], 22, 128)

        plan_ada_tiles(0, range(0, 4))
        for t in range(3):
            wplan_add(kp(d_wie)[:, :, t * 512:(t + 1) * 512], 8, 512)
        plan_ada_tiles(0, range(4, 12))
        if NLAYERS > 1:
            plan_ada_tiles(1, range(12))
        for sb in range(2):
            for m in range(2):
                wplan_add(kp(d_dft1024[m])[:, :, sb * 512:(sb + 1) * 512], 8, 512)
        for t in range(2):
            wplan_add(kp(d_woe)[:, :, t * 512:(t + 1) * 512], 8, 512)
        plan_ffn(0)
        if NLAYERS > 1:
            wio = kp(d_wio)
            for t in range(2):
                wplan_add(wio[:, :, t * 512:(t + 1) * 512], 8, 512)
            for grp in range(2):
                wplan_add(wio[:, :, 2048:2560], 8, 512)
                wplan_add(wio[:, :, 2560:3072], 8, 512)
                wplan_add(wio[:, :, 3072:3328], 8, 256)
                wplan_add(wio[:, :, 1024:1536], 8, 512)
                wplan_add(wio[:, :, 1536:2048], 8, 512)
                wplan_add(wio[:, :, 3328:3360], 8, 32)
            wstate["woo"] = len(wplan)
            for t in range(4):
                wplan_add(kp(d_woo)[:, :, t * 256:(t + 1) * 256], 12, 256)
            plan_ffn(1)

        xT = A.alloc("xT", [128, 8, NTOK], F32)
        hT = A.alloc("hT", [128, 8, NTOK], BF16)
        cf32 = A.alloc("cf32", [128, 5, 128], F32)
        cb16 = A.alloc("cb16", [128, 9, 128], BF16)
        vecT = A.alloc("vecT", [128, 256], F32)
        bv = A.alloc("bv", [128, NBV], F32)
        mod = A.alloc("mod", [128, 2, 48, 2], F32)
        A1 = A.alloc("A1", [128, 2, 2, 8, 2], F32) if False else A.alloc("A1", [128, 64], F32)
        ident = cf32[:, 0, :]
        RT = cf32[:, 1, :]
        utri = cf32[:, 2, :]
        ltri = cf32[:, 3, :]
        onesf = cf32[:, 4, :]
        onesm = cb16[:, 0, :]
        o64 = cb16[:, 1, :]
        identb = cb16[:, 4, :]

        def A1v(l, kind, c, j):
            o = ((l * 2 + kind) * 8 + c) * 2 + j
            return A1[:, o:o + 1]

        def modv(l, q, c, j):
            return mod[:, l, q * 8 + c, j:j + 1]

        def xr(b):
            return xT.rs(range(4 * b, 4 * b + 4))

        dma("sp", cf32[:], d_cf32.rearrange("c p n -> p c n"), writes=[cf32.r()])
        dma("pool", cb16[:], d_cb16.rearrange("c p n -> p c n"), writes=[cb16.r()])
        vraw = A.alloc("vraw", [128, 2, 128], F32)
        dma("sp", vraw[:], d_vecs.rearrange("(a p) n -> p a n", p=128), writes=[vraw.r()])
        dma("sp", bv[:], d_bvec.partition_broadcast(128), writes=[bv.r()])
        for a in range(2):
            p, pr = PS(a)
            S.add("pe", lambda e, p=p, a=a: e.transpose(p[:, 0:128], vraw[:, a, :], ident),
                  reads=[vraw.r(), cf32.r()], writes=[pr])
            S.add("dve", lambda e, p=p, a=a: e.tensor_copy(out=vecT[:, a * 128:(a + 1) * 128], in_=p[:, 0:128]),
                  reads=[pr], writes=[vecT.r()])
        A.free(vraw)

        xst = [A.alloc("xst%d" % i, [128, D], F32) for i in range(2)]
        rot = Rot([2, 3, 4, 5, 6, 7])
        for t in range(12):
            xs_ = xst[t % 2]
            dma("sp", xs_[:], d_x[t * 128:(t + 1) * 128, :], writes=[xs_.r()])
            for hf in range(2):
                p, pr = rot.next()

                def tr(e, p=p, xs_=xs_, hf=hf):
                    for c in range(4):
                        ins = e.transpose(p[:, c * 128:(c + 1) * 128], xs_[:, (hf * 4 + c) * 128:(hf * 4 + c + 1) * 128], ident)
                    return ins
                S.add("pe", tr, reads=[xs_.r(), cf32.r()], writes=[pr])
                eng = "act" if hf == 0 else "dve"
                outv = xT[:, hf * 4:hf * 4 + 4, t * 128:(t + 1) * 128]
                inv = p[:].rearrange("p (c n) -> p c n", c=4)
                if eng == "act":
                    S.add("act", lambda e, o=outv, i=inv: e.activation(out=o, in_=i, func=AF.Copy), reads=[pr], writes=[xT.r(t)])
                else:
                    S.add("dve", lambda e, o=outv, i=inv: e.tensor_copy(out=o, in_=i), reads=[pr], writes=[xT.r(t)])
        A.free(*xst)

        scT = A.alloc("scT", [128, 8, 2], BF16)
        S.add("act", lambda e: e.activation(out=scT[:].rearrange("p k j -> p j k"),
                                            in_=vecT[:, R_CCTX:R_CCTX + 16].rearrange("p (j k) -> p j k", j=2), func=AF.Silu),
              reads=[vecT.r()], writes=[scT.r()])

        def ada_tile(l, t, bank):
            p, pr = PS(bank) if isinstance(bank, int) else bank
            wt, wr = wget()

            def f(e):
                for oc4 in range(4):
                    for k in range(8):
                        ins = e.matmul(p[:, oc4 * 2:oc4 * 2 + 2], wt[:, k, oc4 * 128:(oc4 + 1) * 128], scT[:, k, :],
                                       start=(k == 0), stop=(k == 7))
                return ins
            S.add("pe", f, reads=[wr, scT.r()], writes=[pr])
            r0 = R_BADA + l * 48 + t * 4
            S.add("dve", lambda e: e.tensor_tensor(out=mod[:, l, t * 4:(t + 1) * 4, :], in0=p[:, 0:8].rearrange("p (c j) -> p c j", j=2),
                                                   in1=vecT[:, r0:r0 + 4].unsqueeze(2).to_broadcast([128, 4, 2]),
                                                   op=ALU.add), reads=[pr, vecT.r()], writes=[mod.r()])

        def ada_finish(l, kind):
            q, rbase = ((1, R_NMIX), (4, R_NFFN))[kind]
            o = (l * 2 + kind) * 16
            S.add("dve", lambda e: e.scalar_tensor_tensor(
                out=A1[:, o:o + 16].rearrange("p (c j) -> p c j", j=2), in0=mod[:, l, q * 8:(q + 1) * 8, :], scalar=1.0,
                in1=vecT[:, rbase + l * 8:rbase + (l + 1) * 8].unsqueeze(2).to_broadcast([128, 8, 2]),
                op0=ALU.add, op1=ALU.mult), reads=[mod.r(), vecT.r()], writes=[A1.r()])

        def norm_mod(l, kind):
            qs = 0 if kind == 0 else 3
            sq = [A.alloc("nsq%d" % i, [128, TB], BF16) for i in range(4)]
            rs = [A.alloc("nrs%d" % i, [128, TB], F32) for i in range(3)]
            tt = [A.alloc("ntt%d" % i, [128, TB], F32) for i in range(3)]
            rot = Rot([6, 7])
            cnt = [0, 0]
            pbank = {}

            def stA(b):
                blk = slice(b * TB, (b + 1) * TB)
                p, pr = rot.next()
                pbank[b] = (p, pr)
                for c in range(8):
                    s_ = sq[cnt[0] % 4]
                    cnt[0] += 1
                    if c % 2 == 0:
                        S.add("act", lambda e, s_=s_, c=c: e.activation(out=s_[:], in_=xT[:, c, blk], func=AF.Square), reads=xr(b), writes=[s_.r()])
                    else:
                        S.add("dve", lambda e, s_=s_, c=c: e.tensor_tensor(out=s_[:], in0=xT[:, c, blk], in1=xT[:, c, blk], op=ALU.mult), reads=xr(b), writes=[s_.r()])
                    mm(p[:], onesm, s_[:], c == 0, c == 7, [s_.r(), cb16.r()], [pr])

            def stB(b):
                p, pr = pbank[b]
                r_ = rs[b]
                S.add("act", lambda e: e.activation(out=r_[:], in_=p[:], func=AF.Ln, bias=EPS_AP[:, 0:1], scale=1.0), reads=[pr, EPS_AP.r()], writes=[r_.r()])
                S.add("act", lambda e: e.activation(out=r_[:], in_=r_[:], func=AF.Exp, scale=-0.5), reads=[r_.r()], writes=[r_.r()])

            def stC(b):
                j = 0 if b == 0 else 1
                blk = slice(b * TB, (b + 1) * TB)
                r_ = rs[b]
                for c in range(8):
                    t_ = tt[cnt[1] % 3]
                    cnt[1] += 1
                    S.add("dve", lambda e, t_=t_, c=c: e.tensor_tensor(out=t_[:], in0=xT[:, c, blk], in1=r_[:], op=ALU.mult),
                          reads=xr(b) + [r_.r()], writes=[t_.r()])
                    S.add("act", lambda e, t_=t_, c=c: e.activation(out=hT[:, c, blk], in_=t_[:], func=AF.Identity,
                                                                    bias=modv(l, qs, c, j), scale=A1v(l, kind, c, j)),
                          reads=[t_.r(), mod.r(), A1.r()], writes=[hT.r(b)])

            stA(0)
            stA(1)
            stB(0)
            stC(0)
            stA(2)
            stB(1)
            stC(1)
            stB(2)
            stC(2)
            A.free(*sq, *rs, *tt)

        EPS_AP = A.alloc("eps", [128, 2], F32)
        S.add("pool", lambda e: e.memset(EPS_AP[:], EPS), writes=[EPS_AP.r()])
        ONE_AP = A.alloc("one", [128, 2], F32)
        S.add("pool", lambda e: e.memset(ONE_AP[:], 1.0), writes=[ONE_AP.r()])

        def resid(p, pr, l, q, o, b):
            j = 0 if b == 0 else 1
            blk = slice(b * TB, (b + 1) * TB)
            S.add("dve", lambda e: e.scalar_tensor_tensor(out=xT[:, o, blk], in0=p[:], scalar=modv(l, q, o, j), in1=xT[:, o, blk],
                                                          op0=ALU.mult, op1=ALU.add),
                  reads=[pr, mod.r()] + xr(b), writes=xr(b))

        def ffn(l):
            norm_mod(l, 1)
            actT = A.alloc("actT", [128, 22, NTOK], BF16)
            sl = [A.alloc("fsl%d" % i, [128, TB], BF16) for i in range(3)]
            rot1 = Rot([0, 1, 2, 3])
            n = 0
            for jt in range(6):
                w1, w1r = wget()
                w3, w3r = wget()
                ncs = 4 if jt < 5 else 2
                for cc in range(ncs):
                    jj = jt * 4 + cc
                    for b in range(3):
                        blk = slice(b * TB, (b + 1) * TB)
                        p1, p1r = rot1.next()
                        p3, p3r = rot1.next()

                        def f(e, w=w1, p=p1, cc=cc, blk=blk):
                            for k in range(8):
                                ins = e.matmul(p[:], w[:, k, cc * 128:(cc + 1) * 128], hT[:, k, blk], start=(k == 0), stop=(k == 7))
                            return ins
                        S.add("pe", f, reads=[w1r, hT.r(b)], writes=[p1r])

                        def f3(e, w=w3, p=p3, cc=cc, blk=blk):
                            for k in range(8):
                                ins = e.matmul(p[:], w[:, k, cc * 128:(cc + 1) * 128], hT[:, k, blk], start=(k == 0), stop=(k == 7))
                            return ins
                        S.add("pe", f3, reads=[w3r, hT.r(b)], writes=[p3r])
                        s_ = sl[n % 3]
                        n += 1
                        S.add("act", lambda e, s_=s_, p=p1: e.activation(out=s_[:], in_=p[:], func=AF.Silu), reads=[p1r], writes=[s_.r()])
                        S.add("dve", lambda e, s_=s_, p=p3, jj=jj, blk=blk: e.tensor_tensor(out=actT[:, jj, blk], in0=p[:], in1=s_[:], op=ALU.mult),
                              reads=[p3r, s_.r()], writes=[actT.r(b)])
            A.free(*sl)
            rot2 = Rot([4, 5, 6, 7])
            for o in range(8):
                w2, w2r = wget()
                for b in range(3):
                    blk = slice(b * TB, (b + 1) * TB)
                    p, pr = rot2.next()

                    def f(e, w=w2, p=p, blk=blk):
                        for jj in range(22):
                            ins = e.matmul(p[:], w[:, jj, :], actT[:, jj, blk], start=(jj == 0), stop=(jj == 21))
                        return ins
                    S.add("pe", f, reads=[w2r, actT.r(b)], writes=[pr])
                    resid(p, pr, l, 5, o, b)
            A.free(actT)

        def even_layer(l):
            norm_mod(l, 0)
            qT = A.alloc("qT", [128, 6, NTOK], BF16)
            kT = A.alloc("kT", [128, 2, 2048], BF16)
            vaug = A.alloc("vaug", [128, 16, 4, 128], BF16)
            ftok = A.alloc("ftok", [128, 12, 256], BF16)
            rope = A.alloc("rope", [128, 2, 1024], F32)
            dft256 = A.alloc("dft256", [128, 2, 2, 256], BF16)
            knew = A.alloc("knew", [128, 4, 256], F32)
            vnew = A.alloc("vnew", [128, 4, 256], F32)
            dma("sp", rope[:], d_rope.rearrange("c p n -> p c n"), writes=[rope.r()])
            for m in range(2):
                dma("pool", dft256[:, m, :, :], d_dft256[m].rearrange("(t p) n -> p t n", p=128), writes=[dft256.r()])
            S.add("pool", lambda e: e.memset(vaug[:], 1.0), writes=vaug.rs(range(16)))
            cvv = d_cv.rearrange("(t p) (j d) -> p t j d", p=128, j=4)
            for par in range(2):
                for tt_ in range(4):
                    dma("pool", vaug[:, 12 + tt_, par::2, par * 64:par * 64 + 64], cvv[:, tt_, par::2, :], writes=[vaug.r(12 + tt_)])
            ckt = A.alloc("ckt", [128, 4, 256], F32)
            dma("sp", ckt[:], d_ck.rearrange("(t p) f -> p t f", p=128), writes=[ckt.r()])
            rotc = Rot([4, 5])
            for kc in range(2):
                p, pr = rotc.next()

                def f(e, p=p, kc=kc):
                    for tt_ in range(4):
                        ins = e.transpose(p[:, tt_ * 128:(tt_ + 1) * 128], ckt[:, tt_, kc * 128:(kc + 1) * 128], ident)
                    return ins
                S.add("pe", f, reads=[ckt.r(), cf32.r()], writes=[pr])
                S.add("dve", lambda e, p=p, kc=kc: e.tensor_copy(out=kT[:, kc, 1536:2048], in_=p[:]), reads=[pr], writes=[kT.r(3)])
            A.free(ckt)

            if STAGE < 3:
                return
            sqn = [A.alloc("sqn%d" % i, [128, TB], BF16) for i in range(3)]
            rsn = [A.alloc("rsn%d" % i, [128, TB], F32) for i in range(3)]
            qn = [A.alloc("qn%d" % i, [128, TB], F32) for i in range(3)]
            t1 = [A.alloc("rt1%d" % i, [128, TB], F32) for i in range(2)]
            t2 = [A.alloc("rt2%d" % i, [128, TB], F32) for i in range(2)]
            rotA = Rot([0, 1, 6])
            rotB = Rot([2, 3])
            rotC = Rot([4, 5, 7])
            qunits = [(wt_i, cc, b) for wt_i in range(2) for cc in range(4) for b in range(3)]
            ust = {}
            wcur = {}

            def stageA(u):
                wt_i, cc, b = qunits[u]
                if (wt_i,) not in wcur:
                    wcur[(wt_i,)] = wget()
                w, wr = wcur[(wt_i,)]
                blk = slice(b * TB, (b + 1) * TB)
                pa, par_ = rotA.next()

                def f(e):
                    for k in range(8):
                        ins = e.matmul(pa[:], w[:, k, cc * 128:(cc + 1) * 128], hT[:, k, blk], start=(k == 0), stop=(k == 7))
                    return ins
                S.add("pe", f, reads=[wr, hT.r(b)], writes=[par_])
                s_ = sqn[u % 3]
                S.add("act", lambda e: e.activation(out=s_[:], in_=pa[:], func=AF.Square), reads=[par_], writes=[s_.r()])
                ust[u] = (pa, par_, s_)

            def stageB(u):
                wt_i, cc, b = qunits[u]
                qc = wt_i * 4 + cc
                isk = qc >= 6
                gain = vecT[:, R_KN:R_KN + 1] if isk else vecT[:, R_QN:R_QN + 1]
                pa, par_, s_ = ust[u]
                r_ = rsn[u % 3]
                q_ = qn[u % 3]
                pb, pbr = rotB.next()
                mm(pb[:], o64, s_[:], True, True, [s_.r(), cb16.r()], [pbr])
                S.add("act", lambda e: e.activation(out=r_[:], in_=pb[:], func=AF.Ln, bias=EPS_AP[:, 0:1], scale=1.0),
                      reads=[pbr, EPS_AP.r()], writes=[r_.r()])
                S.add("act", lambda e: e.activation(out=r_[:], in_=r_[:], func=AF.Exp, scale=-0.5), reads=[r_.r()], writes=[r_.r()])
                if b == 0 and not isk:
                    S.add("dve", lambda e: e.scalar_tensor_tensor(out=qT[:, qc, 0:TB], in0=pa[:], scalar=gain, in1=r_[:], op0=ALU.mult, op1=ALU.mult),
                          reads=[par_, r_.r(), vecT.r()], writes=[qT.r(0)])
                else:
                    S.add("dve", lambda e: e.scalar_tensor_tensor(out=q_[:], in0=pa[:], scalar=gain, in1=r_[:], op0=ALU.mult, op1=ALU.mult),
                          reads=[par_, r_.r(), vecT.r()], writes=[q_.r()])
                ust[u] = (q_,)

            def stageC(u):
                wt_i, cc, b = qunits[u]
                qc = wt_i * 4 + cc
                isk = qc >= 6
                (q_,) = ust.pop(u)
                blk = slice(b * TB, (b + 1) * TB)
                if isk:
                    dst, dreg = kT[:, qc - 6, b * TB:(b + 1) * TB], kT.r(b)
                else:
                    dst, dreg = qT[:, qc, blk], qT.r(b)
                if b == 0:
                    if isk:
                        S.add("act", lambda e: e.activation(out=dst, in_=q_[:], func=AF.Copy), reads=[q_.r()], writes=[dreg])
                        pc, pcr = rotC.next()

                        def ftr(e):
                            for tt_ in range(4):
                                ins = e.transpose(pc[:, tt_ * 128:(tt_ + 1) * 128], q_[:, tt_ * 128:(tt_ + 1) * 128], ident)
                            return ins
                        S.add("pe", ftr, reads=[q_.r(), cf32.r()], writes=[pcr])
                        kc = qc - 6
                        S.add("act", lambda e: e.activation(out=knew[:, :, kc * 128:(kc + 1) * 128], in_=pc[:].rearrange("p (t f) -> p t f", t=4), func=AF.Copy),
                              reads=[pcr], writes=[knew.r()])
                else:
                    a_ = t1[u % 2]
                    b_ = t2[u % 2]
                    pc, pcr = rotC.next()
                    mm(pc[:], RT, q_[:], True, True, [q_.r(), cf32.r()], [pcr])
                    rsl = slice((b - 1) * TB, b * TB)
                    S.add("dve", lambda e: e.tensor_tensor(out=a_[:], in0=q_[:], in1=rope[:, 0, rsl], op=ALU.mult), reads=[q_.r(), rope.r()], writes=[a_.r()])
                    S.add("dve", lambda e: e.tensor_tensor(out=b_[:], in0=pc[:], in1=rope[:, 1, rsl], op=ALU.mult), reads=[pcr, rope.r()], writes=[b_.r()])
                    S.add("dve", lambda e: e.tensor_tensor(out=dst, in0=a_[:], in1=b_[:], op=ALU.add), reads=[a_.r(), b_.r()], writes=[dreg])

            NU = len(qunits)
            for step in range(NU + 2):
                if step < NU:
                    stageA(step)
                if 0 <= step - 1 < NU:
                    stageB(step - 1)
                if 0 <= step - 2 < NU:
                    stageC(step - 2)
            dma("sp", o_k.rearrange("(t p) f -> p t f", p=128), knew[:], reads=[knew.r()])
            A.free(*sqn, *rsn, *qn, *t1, *t2)

            if STAGE < 4:
                return
            w, wr = wget()
            rotA = Rot([0, 1, 2, 3])
            for t in range(12):
                b = t // 4
                p, pr = rotA.next()

                def f(e, p=p, t=t, w=w):
                    for k in range(8):
                        ins = e.matmul(p[:], hT[:, k, t * 128:(t + 1) * 128], w[:, k, :], start=(k == 0), stop=(k == 7))
                    return ins
                S.add("pe", f, reads=[wr, hT.r(b)], writes=[pr])
                if not (SUB & 1):
                    S.add("act", lambda e, p=p, t=t: e.activation(out=ftok[:, t, :], in_=p[:, 0:256], func=AF.Copy), reads=[pr], writes=[ftok.r(t)])
                pv = p[:, 256:512].rearrange("p (j d) -> p j d", j=4)
                for par in range(2 if not (SUB & 2) else 0):
                    S.add("dve", lambda e, pv=pv, t=t, par=par: e.tensor_copy(out=vaug[:, t, par::2, par * 64:par * 64 + 64], in_=pv[:, par::2, :]),
                          reads=[pr], writes=[vaug.r(t)])
                if t < 4 and not (SUB & 4):
                    S.add("act", lambda e, p=p, t=t: e.activation(out=vnew[:, t, :], in_=p[:, 256:512], func=AF.Copy), reads=[pr], writes=[vnew.r()])
            dma("sp", o_v.rearrange("(t p) f -> p t f", p=128), vnew[:], reads=[vnew.r()])
            for t in range(4, 12):
                ada_tile(l, t, 4 + t % 4)
            ada_finish(l, 1)

            if STAGE < 5:
                return
            pT = [A.alloc("pT%d" % i, [128, TB], BF16) for i in range(6)]
            rtmp = [A.alloc("rtmp%d" % i, [128, TB], F32) for i in range(2)]
            evb = [A.alloc("evb%d" % i, [128, TB], F32) for i in range(2)]
            rotS = Rot([0, 1, 2, 3, 4, 5])
            rotO = Rot([6, 7])
            units = []
            for sidx in range(2):
                units.append((sidx * 256, 256, 0, [(sidx * 256 + kt * 128, 2 * sidx + kt) for kt in range(2)]))
            skeys = [(512 + kt * 128, 4 + kt) for kt in range(8)] + [(1536 + kt * 128, 12 + kt) for kt in range(4)]
            for qb in range(2):
                units.append((512 + qb * 512, 512, 1 + qb, skeys))
            n = 0
            nh = 0
            ada1_next = [0]
            LAG = 2
            for (q0, N, b, keys) in units:
                nk = len(keys)
                groups = [(qc, ki) for qc in range(6) for ki in range(nk)]
                pts = {}

                def emit_scores(gi, q0=q0, N=N, b=b, keys=keys, groups=groups, pts=pts):
                    nonlocal n
                    qc, ki = groups[gi]
                    kc = qc // 3
                    k0, vt = keys[ki]
                    kreg = kT.r(0 if k0 < 512 else (1 if k0 < 1024 else (2 if k0 < 1536 else 3)))
                    banks = []
                    for hh in range(2):
                        lo, hi = hh * 64, hh * 64 + 64
                        sp_, spr = rotS.next()
                        mm(sp_[:, 0:N], kT[lo:hi, kc, k0:k0 + 128], qT[lo:hi, qc, q0:q0 + N], True, True, [kreg, qT.r(b)], [spr])
                        banks.append((sp_, spr))
                    lst = []
                    for (sp_, spr) in banks:
                        pt = pT[n % 6]
                        n += 1
                        S.add("act", lambda e, pt=pt, sp_=sp_, N=N: e.activation(out=pt[:, 0:N], in_=sp_[:, 0:N], func=AF.Exp, scale=0.125),
                              reads=[spr], writes=[pt.r()])
                        lst.append(pt)
                    pts[gi] = lst

                for gi in range(min(LAG, len(groups))):
                    emit_scores(gi)
                accs = None
                for gi, (qc, ki) in enumerate(groups):
                    if ki == 0:
                        accs = [rotO.next(), rotO.next()]
                    k0, vt = keys[ki]
                    lst = pts.pop(gi)
                    for hh in range(2):
                        j = (qc // 3) * 2 + hh
                        acc, accr = accs[hh]
                        mm(acc[:, 0:N], vaug[:, vt, j, :], lst[hh][:, 0:N], ki == 0, ki == nk - 1, [vaug.r(vt), lst[hh].r()], [accr])
                    if gi + LAG < len(groups):
                        emit_scores(gi + LAG)
                    if NLAYERS > 1 and N == 512 and gi % 12 == 6 and ada1_next[0] < 12:
                        ada_tile(1, ada1_next[0], rotS.next())
                        ada1_next[0] += 1
                    if ki == nk - 1:
                        for hh in range(2):
                            acc, accr = accs[hh]
                            lo, hi = hh * 64, hh * 64 + 64
                            olo, ohi = (1 - hh) * 64, (1 - hh) * 64 + 64
                            rt = rtmp[nh % 2]
                            ev = evb[nh % 2]
                            nh += 1
                            S.add("dve", lambda e, ev=ev, acc=acc, N=N: e.tensor_copy(out=ev[:, 0:N], in_=acc[:, 0:N]), reads=[accr], writes=[ev.r()])
                            S.add("dve", lambda e, rt=rt, ev=ev, N=N, lo=lo, hi=hi, olo=olo, ohi=ohi: e.reciprocal(out=rt[lo:hi, 0:N], in_=ev[olo:ohi, 0:N]),
                                  reads=[ev.r()], writes=[rt.r()])
                            S.add("dve", lambda e, rt=rt, ev=ev, N=N, lo=lo, hi=hi, qc=qc, q0=q0: e.tensor_tensor(
                                out=hT[lo:hi, 2 + qc, q0:q0 + N], in0=ev[lo:hi, 0:N], in1=rt[lo:hi, 0:N], op=ALU.mult),
                                reads=[ev.r(), rt.r()], writes=[hT.r(b)])
            if NLAYERS > 1:
                while ada1_next[0] < 12:
                    ada_tile(1, ada1_next[0], rotS.next())
                    ada1_next[0] += 1
                ada_finish(1, 0)
                ada_finish(1, 1)
            A.free(*pT, *rtmp, *evb, qT, kT, vaug, rope, knew, vnew)

            if STAGE < 6:
                return
            uv = [A.alloc("uv%d" % i, [128, TB], BF16) for i in range(4)]
            rotU = Rot([0, 1, 2, 3])
            rotF = Rot([4, 5, 6, 7])
            c64c = cb16[:, 5, :]
            c64s = cb16[:, 6, :]
            n = 0

            def stageB(u, v, N, cch, q0, b):
                p, pr = rotF.next()
                mm(p[:, 0:N], c64c, u[:, 0:N], True, False, [cb16.r(), u.r()], [pr])
                mm(p[:, 0:N], c64s, v[:, 0:N], False, True, [cb16.r(), v.r()], [pr])
                S.add("act", lambda e: e.activation(out=hT[:, cch, q0:q0 + N], in_=p[:, 0:N], func=AF.Copy), reads=[pr], writes=[hT.r(b)])

            for sidx in range(2):
                for cch in range(2):
                    pair = []
                    for m in range(2):
                        p, pr = rotU.next()

                        def f(e, p=p, m=m, sidx=sidx, cch=cch):
                            for st_ in range(2):
                                ins = e.matmul(p[:, 0:256], ftok[:, 2 * sidx + st_, cch * 128:(cch + 1) * 128], dft256[:, m, st_, :],
                                               start=(st_ == 0), stop=(st_ == 1))
                            return ins
                        S.add("pe", f, reads=[ftok.r(2 * sidx), ftok.r(2 * sidx + 1), dft256.r()], writes=[pr])
                        u = uv[n % 4]
                        n += 1
                        S.add("dve", lambda e, u=u, p=p: e.tensor_copy(out=u[:, 0:256], in_=p[:, 0:256]), reads=[pr], writes=[u.r()])
                        pair.append(u)
                    stageB(pair[0], pair[1], 256, cch, sidx * 256, 0)
            for sb in range(2):
                wc, wcr = wget()
                wsn, wsr = wget()
                for cch in range(2):
                    pair = []
                    for m, (wm, wmr) in enumerate(((wc, wcr), (wsn, wsr))):
                        p, pr = rotU.next()

                        def f(e, p=p, wm=wm, cch=cch):
                            for st_ in range(8):
                                ins = e.matmul(p[:], ftok[:, 4 + st_, cch * 128:(cch + 1) * 128], wm[:, st_, :], start=(st_ == 0), stop=(st_ == 7))
                            return ins
                        S.add("pe", f, reads=ftok.rs(range(4, 12)) + [wmr], writes=[pr])
                        u = uv[n % 4]
                        n += 1
                        S.add("dve", lambda e, u=u, p=p: e.tensor_copy(out=u[:], in_=p[:]), reads=[pr], writes=[u.r()])
                        pair.append(u)
                    stageB(pair[0], pair[1], 512, cch, 512 + sb * 512, 1 + sb)
            A.free(*uv, ftok, dft256)

            if STAGE < 7:
                return
            rotW = Rot([0, 1, 2, 3])
            for t in range(2):
                w, wr = wget()
                for oc in range(4):
                    o = t * 4 + oc
                    for b in range(3):
                        blk = slice(b * TB, (b + 1) * TB)
                        p, pr = rotW.next()

                        def f(e, p=p, w=w, oc=oc, blk=blk):
                            for k in range(8):
                                ins = e.matmul(p[:], w[:, k, oc * 128:(oc + 1) * 128], hT[:, k, blk], start=(k == 0), stop=(k == 7))
                            return ins
                        S.add("pe", f, reads=[wr, hT.r(b)], writes=[pr])
                        resid(p, pr, l, 2, o, b)


        def rev(v):
            (ps_, pn), (st_, n) = v.ap
            return bass.AP(v.tensor, v.offset + (n - 1) * st_, [[ps_, pn], [-st_, n]])

        SEQS = [(0, 256), (256, 512), (512, 1536)]

        def conv4(eng, xp, yp, n, wcol, bcol):
            lo, hi = 1, n - 2
            S.add(eng, lambda e: e.tensor_scalar(out=yp[:, lo:hi], in0=xp[:, lo:hi], scalar1=vecT[:, wcol(1):wcol(1) + 1],
                                                 scalar2=vecT[:, bcol:bcol + 1], op0=ALU.mult, op1=ALU.add),
                  reads=[xp.r(), vecT.r()], writes=[yp.r()])
            for j, sh in ((0, -1), (2, 1), (3, 2)):
                S.add(eng, lambda e, j=j, sh=sh: e.scalar_tensor_tensor(out=yp[:, lo:hi], in0=xp[:, lo + sh:hi + sh], scalar=vecT[:, wcol(j):wcol(j) + 1],
                                                                        in1=yp[:, lo:hi], op0=ALU.mult, op1=ALU.add),
                      reads=[xp.r(), yp.r(), vecT.r()], writes=[yp.r()])

        def odd_layer(l):
            norm_mod(l, 0)
            yl = A.alloc("yl", [128, 4, NTOK], BF16)
            gg = A.alloc("gg", [128, 4, NTOK], BF16)
            xc = A.alloc("xc", [128, 4, NTOK], F32)
            bdw = A.alloc("bdw", [128, 16, 128], BF16)
            fin = A.alloc("fin", [128, 16], F32)
            clT = A.alloc("clT", [128, 8], F32)
            dma("pool", bdw[:], d_lrubd.rearrange("c p n -> p c n"), writes=[bdw.r()])
            S.add("act", lambda e: e.activation(out=clT[:], in_=vecT[:, R_LAM:R_LAM + 8], func=AF.Exp, scale=-1.0), reads=[vecT.r()], writes=[clT.r()])
            S.add("act", lambda e: e.activation(out=clT[:], in_=clT[:], func=AF.Ln, bias=ONE_AP[:, 0:1], scale=1.0), reads=[clT.r(), ONE_AP.r()], writes=[clT.r()])
            S.add("dve", lambda e: e.tensor_scalar(out=clT[:], in0=clT[:], scalar1=-8.0, scalar2=0.0, op0=ALU.mult, op1=ALU.add), reads=[clT.r()], writes=[clT.r()])
            NP = 1544
            POFF = [1, 259, 517]
            xpad = [A.alloc("xpad%d" % i, [128, NP], F32) for i in range(2)]
            ypad = [A.alloc("ypad%d" % i, [128, NP], F32) for i in range(2)]
            for xp in xpad:
                S.add("pool", lambda e, xp=xp: e.memset(xp[:], 0.0), writes=[xp.r()])
            rotA = Rot([0, 1, 2, 3])
            wg, wgr = wget()
            for c in range(4):
                for b in range(3):
                    blk = slice(b * TB, (b + 1) * TB)
                    p, pr = rotA.next()

                    def f(e, p=p, c=c, blk=blk):
                        for k in range(8):
                            ins = e.matmul(p[:], wg[:, k, c * 128:(c + 1) * 128], hT[:, k, blk], start=(k == 0), stop=(k == 7))
                        return ins
                    S.add("pe", f, reads=[wgr, hT.r(b)], writes=[pr])
                    S.add("act", lambda e, p=p, c=c, blk=blk: e.activation(out=gg[:, c, blk], in_=p[:], func=AF.Gelu_apprx_tanh), reads=[pr], writes=[gg.r(c)])
            wx, wxr = wget()
            for c in range(4):
                xp = xpad[c % 2]
                yp = ypad[c % 2]
                for b in range(3):
                    blk = slice(b * TB, (b + 1) * TB)
                    p, pr = rotA.next()

                    def f(e, p=p, c=c, blk=blk):
                        for k in range(8):
                            ins = e.matmul(p[:], wx[:, k, c * 128:(c + 1) * 128], hT[:, k, blk], start=(k == 0), stop=(k == 7))
                        return ins
                    S.add("pe", f, reads=[wxr, hT.r(b)], writes=[pr])
                    if b == 0:
                        ov = xp[:, 1:1 + 2 * 258].rearrange("p (a b) -> p a b", b=258)[:, :, 0:256]
                        S.add("act", lambda e, p=p, ov=ov: e.activation(out=ov, in_=p[:].rearrange("p (a b) -> p a b", a=2), func=AF.Copy), reads=[pr], writes=[xp.r()])
                    else:
                        o0 = 517 + (b - 1) * 512
                        S.add("act", lambda e, p=p, xp=xp, o0=o0: e.activation(out=xp[:, o0:o0 + 512], in_=p[:], func=AF.Copy), reads=[pr], writes=[xp.r()])
                conv4("dve", xp, yp, NP, lambda j, c=c: R_CLW + j * 4 + c, R_CLB + c)
                for si, (s0, s1) in enumerate(SEQS):
                    S.add("act", lambda e, yp=yp, c=c, s0=s0, s1=s1, si=si: e.activation(out=xc[:, c, s0:s1], in_=yp[:, POFF[si]:POFF[si] + s1 - s0], func=AF.Copy),
                          reads=[yp.r()], writes=[xc.r(c)])
            A.free(*xpad, *ypad)
            xcb = A.alloc("xcb", [128, NTOK], BF16)
            ra_ = [A.alloc("lra%d" % i, [128, NTOK], F32) for i in range(2)]
            ig_ = [A.alloc("lig%d" % i, [128, NTOK], F32) for i in range(2)]
            tq_ = [A.alloc("ltq0", [128, NTOK], F32)] * 2
            hd = [A.alloc("lh0", [128, NTOK], F32), ig_[1]]
            rotL = Rot([4, 5, 6, 7])
            xcbs = [xcb, xcb]

            def lruA(u):
                c, d = u // 2, u % 2
                ra, ig = ra_[d], ig_[d]
                xb = xcbs[c % 2]
                if d == 0:
                    S.add("act", lambda e: e.activation(out=xb[:], in_=xc[:, c, :], func=AF.Copy), reads=[xc.r(c)], writes=[xb.r()])
                for b in range(3):
                    blk = slice(b * TB, (b + 1) * TB)
                    for gate, dstT, brow in ((0, ra, R_BA), (1, ig, R_BX)):
                        p, pr = rotL.next()
                        mm(p[:], bdw[:, (gate * 2 + d) * 4 + c, :], xb[:, blk], True, True, [bdw.r(), xb.r()], [pr])
                        col = brow + d * 4 + c
                        S.add("act", lambda e, p=p, dstT=dstT, blk=blk, col=col: e.activation(out=dstT[:, blk], in_=p[:], func=AF.Sigmoid,
                                                                                               bias=vecT[:, col:col + 1], scale=1.0),
                              reads=[pr, vecT.r()], writes=[dstT.r()])

            def lruB(u):
                c, d = u // 2, u % 2
                ra, ig, tq = ra_[d], ig_[d], tq_[d]
                S.add("act", lambda e: e.activation(out=ra[:], in_=ra[:], func=AF.Exp, scale=clT[:, d * 4 + c:d * 4 + c + 1]),
                      reads=[ra.r(), clT.r()], writes=[ra.r()])
                S.add("act", lambda e: e.activation(out=tq[:], in_=ra[:], func=AF.Square), reads=[ra.r()], writes=[tq.r()])
                S.add("dve", lambda e: e.tensor_scalar(out=tq[:], in0=tq[:], scalar1=-1.0, scalar2=1.0, op0=ALU.mult, op1=ALU.add), reads=[tq.r()], writes=[tq.r()])
                S.add("act", lambda e: e.activation(out=tq[:], in_=tq[:], func=AF.Sqrt), reads=[tq.r()], writes=[tq.r()])
                S.add("dve", lambda e: e.tensor_tensor(out=ig[:], in0=ig[:], in1=xc[:, c, :], op=ALU.mult), reads=[ig.r(), xc.r(c)], writes=[ig.r()])
                S.add("dve", lambda e: e.tensor_tensor(out=tq[:], in0=tq[:], in1=ig[:], op=ALU.mult), reads=[tq.r(), ig.r()], writes=[tq.r()])
                h_ = hd[d]
                for si, (s0, s1) in enumerate(SEQS):
                    init = 0.0 if si < 2 else vecT[:, R_LRU0 + d * 4 + c:R_LRU0 + d * 4 + c + 1]
                    if d == 0:
                        S.add("dve", lambda e, s0=s0, s1=s1, init=init: e.tensor_tensor_scan(
                            out=h_[:, s0:s1], data0=ra[:, s0:s1], data1=tq[:, s0:s1], initial=init, op0=ALU.mult, op1=ALU.add),
                            reads=[ra.r(), tq.r(), vecT.r()], writes=[h_.r()])
                    else:
                        S.add("dve", lambda e, s0=s0, s1=s1, init=init: e.tensor_tensor_scan(
                            out=rev(h_[:, s0:s1]), data0=rev(ra[:, s0:s1]), data1=rev(tq[:, s0:s1]), initial=init, op0=ALU.mult, op1=ALU.add),
                            reads=[ra.r(), tq.r(), vecT.r()], writes=[h_.r()])
                    if si < 2:
                        pos = s1 - 1 if d == 0 else s0
                        col = (si * 2 + d) * 4 + c
                        S.add("dve", lambda e, pos=pos, col=col: e.tensor_copy(out=fin[:, col:col + 1], in_=h_[:, pos:pos + 1]),
                              reads=[h_.r()], writes=[fin.r()])
                if d == 1:
                    S.add("dve", lambda e: e.tensor_tensor(out=hd[0][:], in0=hd[0][:], in1=hd[1][:], op=ALU.add), reads=[hd[0].r(), hd[1].r()], writes=[hd[0].r()])
                    S.add("dve", lambda e: e.tensor_tensor(out=yl[:, c, :], in0=hd[0][:], in1=gg[:, c, :], op=ALU.mult),
                          reads=[hd[0].r(), gg.r(c)], writes=[yl.r()])

            lruA(0)
            for u in range(8):
                if u + 1 < 8:
                    lruA(u + 1)
                lruB(u)
            p, pr = PS(4)
            S.add("pe", lambda e: e.transpose(p[0:16, 0:128], fin[:, 0:16], ident), reads=[fin.r(), cf32.r()], writes=[pr])
            fino = A.alloc("fino", [16, 128], F32)
            S.add("dve", lambda e: e.tensor_copy(out=fino[:], in_=p[0:16, 0:128]), reads=[pr], writes=[fino.r()])
            dma("sp", o_lru, fino[:], reads=[fino.r()])
            A.free(gg, xc, bdw, clT, xcb, *ra_, *ig_, tq_[0], hd[0], fin, fino)
            if STAGE < 13:
                for _ in range(12):
                    wget()

            aneg = A.alloc("aneg", [128, 32], F32)
            S.add("act", lambda e: e.activation(out=aneg[:], in_=bv[:, B_ALOG:B_ALOG + 32], func=AF.Exp), reads=[bv.r()], writes=[aneg.r()])
            S.add("dve", lambda e: e.tensor_scalar(out=aneg[:], in0=aneg[:], scalar1=-1.0, scalar2=0.0, op0=ALU.mult, op1=ALU.add), reads=[aneg.r()], writes=[aneg.r()])
            nmk = [cb16[:, 2, :], cb16[:, 3, :]]
            tri = [cb16[:, 7, :], cb16[:, 8, :]]
            def ssd_group(grp):
                tiles = list(range(0, 4)) if grp == 0 else list(range(4, 12))
                nt = len(tiles)
                t0 = tiles[0]
                ntk = nt * 128
                blocks = [0] if grp == 0 else [1, 2]
                seqs = [[0, 1], [2, 3]] if grp == 0 else [list(range(8))]
                if grp == 0:
                    NPg = 520
                    poff = [1, 259]
                    slen = 256
                else:
                    NPg = 1032
                    poff = [1]
                    slen = 1024
                xs_tok = A.alloc("xs_tok", [128, nt, 1024], BF16)
                z_tok = A.alloc("z_tok", [128, nt, 1024], BF16)
                B_tok = A.alloc("B_tok", [128, nt, 128], BF16)
                BT = A.alloc("BT", [128, ntk], BF16)
                CT = A.alloc("CT", [128, ntk], BF16)
                xpad = [A.alloc("sxp%d" % i, [128, NPg], F32) for i in range(3)]
                ypad = [A.alloc("syp%d" % i, [128, NPg], F32) for i in range(3)]
                xsf = [A.alloc("xsf%d" % i, [128, ntk], F32) for i in range(3)]
                for xp in xpad:
                    S.add("pool", lambda e, xp=xp: e.memset(xp[:], 0.0), writes=[xp.r()])
                rotA = Rot([0, 1, 2, 3])
                rotT = Rot([4, 5, 6, 7])
                wc1 = {}

                def c1A(ch):
                    wt_i, cc = ch // 4, ch % 4
                    if wt_i not in wc1:
                        wc1[wt_i] = wget()
                    w, wr = wc1[wt_i]
                    xp = xpad[ch % 3]
                    for b in blocks:
                        blk = slice(b * TB, (b + 1) * TB)
                        p, pr = rotA.next()

                        def f(e, p=p, blk=blk):
                            for k in range(8):
                                ins = e.matmul(p[:], w[:, k, cc * 128:(cc + 1) * 128], hT[:, k, blk], start=(k == 0), stop=(k == 7))
                            return ins
                        S.add("pe", f, reads=[wr, hT.r(b)], writes=[pr])
                        if grp == 0:
                            ov = xp[:, 1:1 + 2 * 258].rearrange("p (a b) -> p a b", b=258)[:, :, 0:256]
                            S.add("act", lambda e, p=p, ov=ov: e.activation(out=ov, in_=p[:].rearrange("p (a b) -> p a b", a=2), func=AF.Copy), reads=[pr], writes=[xp.r()])
                        else:
                            o0 = 1 + (b - 1) * 512
                            S.add("act", lambda e, p=p, o0=o0: e.activation(out=xp[:, o0:o0 + 512], in_=p[:], func=AF.Copy), reads=[pr], writes=[xp.r()])

                def c1B(ch):
                    xp = xpad[ch % 3]
                    yp = ypad[ch % 3]
                    xf = xsf[ch % 3]
                    conv4("dve", xp, yp, NPg, lambda j: R_CSW + j * 10 + ch, R_CSB + ch)
                    for si, po in enumerate(poff):
                        if ch <= 8:
                            dst, dreg = xf[:, si * slen:(si + 1) * slen], xf.r()
                        else:
                            dst, dreg = CT[:, si * slen:(si + 1) * slen], CT.r()
                        S.add("act", lambda e, po=po, dst=dst: e.activation(out=dst, in_=yp[:, po:po + slen], func=AF.Silu), reads=[yp.r()], writes=[dreg])
                    if ch == 8:
                        S.add("act", lambda e: e.activation(out=BT[:], in_=xf[:], func=AF.Copy), reads=[xf.r()], writes=[BT.r()])

                def c1C(ch):
                    xf = xsf[ch % 3]
                    if ch <= 8:
                        for t4 in range(0, nt, 4):
                            p, pr = rotT.next()

                            def ftr(e, p=p, t4=t4):
                                for q in range(4):
                                    ins = e.transpose(p[:, q * 128:(q + 1) * 128], xf[:, (t4 + q) * 128:(t4 + q + 1) * 128], ident)
                                return ins
                            S.add("pe", ftr, reads=[xf.r(), cf32.r()], writes=[pr])
                            pv = p[:].rearrange("p (q n) -> p q n", q=4)
                            if ch < 8:
                                S.add("dve", lambda e, pv=pv, t4=t4: e.tensor_copy(out=xs_tok[:, t4:t4 + 4, ch * 128:(ch + 1) * 128], in_=pv),
                                      reads=[pr], writes=[xs_tok.r()])
                            else:
                                S.add("dve", lambda e, pv=pv, t4=t4: e.tensor_copy(out=B_tok[:, t4:t4 + 4, :], in_=pv), reads=[pr], writes=[B_tok.r()])

                for step in range(12):
                    if step < 10:
                        c1A(step)
                    if 0 <= step - 1 < 10:
                        c1B(step - 1)
                    if 0 <= step - 2 < 10:
                        c1C(step - 2)
                A.free(*xpad, *ypad, *xsf)
                if SUB == 1:
                    return
                sm = lambda name, w=32: A.alloc(name, [128, nt, w], F32)
                dtr = sm("dtr")
                dta = sm("dta")
                aal = sm("aal")
                stats = sm("stats", 64)
                nacs = sm("nacs")
                eacs = sm("eacs")
                wdec = sm("wdec")
                cdec = sm("cdec")
                cdsel = A.alloc("cdsel", [128, nt, 2, 8], F32)
                wz0, wz0r = wget()
                wz1, wz1r = wget()
                for ti in range(nt):
                    tg = t0 + ti
                    b = tg // 4
                    for hf, (wz, wzr) in enumerate(((wz0, wz0r), (wz1, wz1r))):
                        p, pr = rotA.next()

                        def f(e, p=p, wz=wz, tg=tg):
                            for k in range(8):
                                ins = e.matmul(p[:], hT[:, k, tg * 128:(tg + 1) * 128], wz[:, k, :], start=(k == 0), stop=(k == 7))
                            return ins
                        S.add("pe", f, reads=[wzr, hT.r(b)], writes=[pr])
                        S.add("act", lambda e, p=p, ti=ti, hf=hf: e.activation(out=z_tok[:, ti, hf * 512:(hf + 1) * 512], in_=p[:], func=AF.Silu),
                              reads=[pr], writes=[z_tok.r()])
                wd, wdr = wget()
                for ti in range(nt):
                    tg = t0 + ti
                    b = tg // 4
                    p, pr = rotA.next()

                    def f(e, p=p, tg=tg):
                        for k in range(8):
                            ins = e.matmul(p[:, 0:32], hT[:, k, tg * 128:(tg + 1) * 128], wd[:, k, :], start=(k == 0), stop=(k == 7))
                        return ins
                    S.add("pe", f, reads=[wdr, hT.r(b)], writes=[pr])
                    S.add("dve", lambda e, p=p, ti=ti: e.tensor_tensor(out=dtr[:, ti, :], in0=p[:, 0:32], in1=bv[:, B_DTB:B_DTB + 32], op=ALU.add),
                          reads=[pr, bv.r()], writes=[dtr.r()])
                S.add("act", lambda e: e.activation(out=dtr[:], in_=dtr[:], func=AF.Exp), reads=[dtr.r()], writes=[dtr.r()])
                S.add("act", lambda e: e.activation(out=dta[:], in_=dtr[:], func=AF.Ln, bias=ONE_AP[:, 0:1], scale=1.0), reads=[dtr.r(), ONE_AP.r()], writes=[dta.r()])
                S.add("dve", lambda e: e.tensor_tensor(out=aal[:], in0=dta[:], in1=aneg[:].unsqueeze(1).to_broadcast([128, nt, 32]), op=ALU.mult),
                      reads=[dta.r(), aneg.r()], writes=[aal.r()])
                ahi = A.alloc("ahi", [128, nt, 32], BF16)
                alo = A.alloc("alo", [128, nt, 32], BF16)
                S.add("act", lambda e: e.activation(out=ahi[:], in_=aal[:], func=AF.Copy), reads=[aal.r()], writes=[ahi.r()])
                S.add("dve", lambda e: e.tensor_tensor(out=alo[:], in0=aal[:], in1=ahi[:], op=ALU.subtract), reads=[aal.r(), ahi.r()], writes=[alo.r()])
                for ti in range(nt):
                    p, pr = rotT.next()
                    mm(p[:, 0:16], utri, aal[:, ti, 0:16], True, True, [cf32.r(), aal.r()], [pr])
                    mm(p[:, 16:32], ltri, aal[:, ti, 16:32], True, True, [cf32.r(), aal.r()], [pr])
                    mm(p[:, 32:64], onesf, aal[:, ti, :], True, True, [cf32.r(), aal.r()], [pr])
                    S.add("dve", lambda e, p=p, ti=ti: e.tensor_copy(out=stats[:, ti, :], in_=p[:, 0:64]), reads=[pr], writes=[stats.r()])
                S.add("act", lambda e: e.activation(out=nacs[:], in_=dta[:], func=AF.Ln), reads=[dta.r()], writes=[nacs.r()])
                S.add("dve", lambda e: e.tensor_tensor(out=nacs[:], in0=nacs[:], in1=stats[:, :, 0:32], op=ALU.subtract),
                      reads=[stats.r(), nacs.r()], writes=[nacs.r()])
                S.add("act", lambda e: e.activation(out=eacs[:], in_=stats[:, :, 0:32], func=AF.Exp), reads=[stats.r()], writes=[eacs.r()])
                S.add("dve", lambda e: e.tensor_tensor(out=wdec[:], in0=stats[:, :, 32:64], in1=stats[:, :, 0:32], op=ALU.subtract), reads=[stats.r()], writes=[wdec.r()])
                S.add("act", lambda e: e.activation(out=wdec[:], in_=wdec[:], func=AF.Exp), reads=[wdec.r()], writes=[wdec.r()])
                S.add("dve", lambda e: e.tensor_tensor(out=wdec[:], in0=wdec[:], in1=dta[:], op=ALU.mult), reads=[wdec.r(), dta.r()], writes=[wdec.r()])
                S.add("act", lambda e: e.activation(out=cdec[:], in_=stats[:, :, 32:64], func=AF.Exp), reads=[stats.r()], writes=[cdec.r()])
                for g in range(2):
                    src = cdec[g * 64:(g + 1) * 64, :, :].rearrange("p t (d g h) -> p t d g h", d=2, g=2)[:, :, :, g, :]
                    S.add("dve", lambda e, g=g, src=src: e.tensor_copy(out=cdsel[g * 64:(g + 1) * 64, :, :, :], in_=src), reads=[cdec.r()], writes=[cdsel.r()])
                A.free(dtr, cdec, stats, aal)
                if SUB == 2:
                    return
                Sst = [A.alloc("Sst%d" % d, [128, 512], F32) for d in range(2)]
                hinb = A.alloc("hinb", [128, nt, 512], BF16)
                hinf = A.alloc("hinf", [128, 512], BF16)
                xw = [A.alloc("xw0", [128, 1024], BF16)] * 2
                stmp = A.alloc("stmp", [128, 512], F32)

                class _Stg:
                    def __getitem__(self, k):
                        return stmp[:].rearrange("p (a n) -> p a n", a=8)[k]

                    def r(self):
                        return stmp.r()
                stg = _Stg()
                rotP = Rot([5, 6])
                nxw = [0]

                def init_state(d):
                    St = Sst[d]
                    if grp == 0:
                        S.add("pool", lambda e: e.memset(St[:], 0.0), writes=[St.r()])
                    else:
                        dma("sp", stg[:], d_sssd[d].rearrange("(a p) n -> p a n", p=128), writes=[stg.r()])
                        for g in range(2):
                            p, pr = rotP.next()

                            def f(e, p=p, g=g):
                                for a4 in range(4):
                                    ins = e.transpose(p[0:64, a4 * 128:(a4 + 1) * 128], stg[:, g * 4 + a4, :], ident)
                                return ins
                            S.add("pe", f, reads=[stg.r(), cf32.r()], writes=[pr])
                            S.add("dve", lambda e, p=p, g=g: e.tensor_copy(out=St[g * 64:(g + 1) * 64, :], in_=p[0:64, :]), reads=[pr], writes=[St.r()])

                def out_state(d, sidx):
                    St = Sst[d]
                    for g in range(2):
                        p, pr = rotP.next()

                        def f(e, p=p, g=g):
                            for s4 in range(4):
                                ins = e.transpose(p[:, s4 * 64:(s4 + 1) * 64], St[g * 64:(g + 1) * 64, s4 * 128:(s4 + 1) * 128], ident[g * 64:(g + 1) * 64, g * 64:(g + 1) * 64])
                            return ins
                        S.add("pe", f, reads=[St.r(), cf32.r()], writes=[pr])
                        S.add("dve", lambda e, p=p, g=g: e.tensor_copy(out=stg[:, g * 4:(g + 1) * 4, :], in_=p[:, 0:256].rearrange("p (a n) -> p a n", a=4)), reads=[pr], writes=[stg.r()])
                    dma("sp", o_ssd[sidx, d].rearrange("(a p) n -> p a n", p=128), stg[:], reads=[stg.r()])

                def state_update(d, ti):
                    St = Sst[d]
                    x_ = xw[nxw[0] % 2]
                    nxw[0] += 1
                    S.add("pool", lambda e, x_=x_, ti=ti, d=d: e.tensor_tensor(
                        out=x_[:].rearrange("p (h q) -> p h q", h=16), in0=xs_tok[:, ti, :].rearrange("p (h q) -> p h q", h=16),
                        in1=wdec[:, ti, d * 16:(d + 1) * 16].unsqueeze(2).to_broadcast([128, 16, 64]), op=ALU.mult),
                        reads=[xs_tok.r(), wdec.r()], writes=[x_.r()])
                    p, pr = rotP.next()
                    for g in range(2):
                        mm(p[g * 64:(g + 1) * 64, :], B_tok[:, ti, g * 64:(g + 1) * 64], x_[:, g * 512:(g + 1) * 512], True, True, [B_tok.r(), x_.r()], [pr])
                    S.add("dve", lambda e, ti=ti, d=d: e.tensor_tensor(
                        out=stmp[:].rearrange("p (h q) -> p h q", h=8), in0=St[:].rearrange("p (h q) -> p h q", h=8),
                        in1=cdsel[:, ti, d, :].unsqueeze(2).to_broadcast([128, 8, 64]), op=ALU.mult),
                        reads=[St.r(), cdsel.r()], writes=[stmp.r()])
                    S.add("dve", lambda e, p=p: e.tensor_tensor(out=St[:], in0=p[:], in1=stmp[:], op=ALU.add), reads=[pr, stmp.r()], writes=[St.r()])

                for sidx, sq_ in enumerate(seqs):
                    init_state(1)
                    for ti in reversed(sq_):
                        S.add("act", lambda e, ti=ti: e.activation(out=hinb[:, ti, :], in_=Sst[1][:], func=AF.Copy), reads=[Sst[1].r()], writes=[hinb.r()])
                        if SUB != 5:
                            state_update(1, ti)
                    if grp == 0 and SUB != 4:
                        out_state(1, sidx)
                if SUB in (3, 4, 5):
                    return
                cbs = [A.alloc("cbs%d" % i, [128, 2, 128], F32) for i in range(2)]
                decb = [A.alloc("decb%d" % i, [128, 2, 128], F32) for i in range(3)]
                MTb = [A.alloc("MTb%d" % i, [128, 2, 128], BF16) for i in range(3)]
                yaccs = [A.alloc("yacc%d" % i, [128, 1024], F32) for i in range(2)]
                ytmp = A.alloc("ytmp", [128, 1024], F32)
                ssq = A.alloc("ssq", [128, 2], F32)
                rotD = Rot([1, 2, 3, 4])
                nm_ = 0
                tiles_seq = []
                for sidx, sq_ in enumerate(seqs):
                    for i_, ti in enumerate(sq_):
                        tiles_seq.append((sidx, ti, i_ == 0, i_ == len(sq_) - 1))

                def make_tail(sidx, ti, first, last, yacc):
                    tg = t0 + ti
                    tcol = slice(ti * 128, (ti + 1) * 128)
                    th = []
                    if first:
                        th.append(lambda: init_state(0))
                    th.append(lambda: S.add("act", lambda e: e.activation(out=hinf[:], in_=Sst[0][:], func=AF.Copy), reads=[Sst[0].r()], writes=[hinf.r()]))
                    for d in range(2):
                        for g in range(2):
                            def yo(d=d, g=g):
                                hin = hinf[:] if d == 0 else hinb[:, ti, :]
                                hreg = hinf.r() if d == 0 else hinb.r()
                                p, pr = rotP.next()
                                mm(p[:], CT[g * 64:(g + 1) * 64, tcol], hin[g * 64:(g + 1) * 64, :], True, True, [CT.r(), hreg], [pr])
                                ea = eacs[:, ti, d * 16 + g * 8:d * 16 + g * 8 + 8].unsqueeze(2).to_broadcast([128, 8, 64])
                                pv = p[:].rearrange("p (h q) -> p h q", h=8)
                                tsl = ytmp[:, g * 512:(g + 1) * 512].rearrange("p (h q) -> p h q", h=8)
                                S.add("dve", lambda e: e.tensor_tensor(out=tsl, in0=pv, in1=ea, op=ALU.mult), reads=[pr, eacs.r()], writes=[ytmp.r()])
                                S.add("dve", lambda e: e.tensor_tensor(out=yacc[:, g * 512:(g + 1) * 512], in0=yacc[:, g * 512:(g + 1) * 512],
                                                                       in1=ytmp[:, g * 512:(g + 1) * 512], op=ALU.add),
                                      reads=[yacc.r(), ytmp.r()], writes=[yacc.r()])
                            th.append(yo)
                    th.append(lambda: S.add("dve", lambda e: e.tensor_tensor(
                        out=ytmp[:].rearrange("p (h q) -> p h q", h=16), in0=xs_tok[:, ti, :].rearrange("p (h q) -> p h q", h=16),
                        in1=bv[:, B_D:B_D + 16].unsqueeze(2).to_broadcast([128, 16, 64]), op=ALU.mult),
                        reads=[xs_tok.r(), bv.r(), yacc.r()], writes=[ytmp.r()]))
                    th.append(lambda: S.add("dve", lambda e: e.tensor_tensor(out=yacc[:], in0=yacc[:], in1=ytmp[:], op=ALU.add), reads=[yacc.r(), ytmp.r()], writes=[yacc.r()]))
                    th.append(lambda: S.add("dve", lambda e: e.tensor_tensor(out=yacc[:], in0=yacc[:], in1=z_tok[:, ti, :], op=ALU.mult), reads=[yacc.r(), z_tok.r()], writes=[yacc.r()]))
                    th.append(lambda: S.add("act", lambda e: e.activation(out=ytmp[:], in_=yacc[:], func=AF.Square, accum_out=ssq[:, 0:1]), reads=[yacc.r()], writes=[ytmp.r(), ssq.r()]))
                    th.append(lambda: S.add("act", lambda e: e.activation(out=ssq[:, 0:1], in_=ssq[:, 0:1], func=AF.Ln, bias=EPS_AP[:, 0:1], scale=1.0 / 1024), reads=[ssq.r(), EPS_AP.r()], writes=[ssq.r()]))
                    th.append(lambda: S.add("act", lambda e: e.activation(out=ssq[:, 0:1], in_=ssq[:, 0:1], func=AF.Exp, scale=-0.5), reads=[ssq.r()], writes=[ssq.r()]))
                    th.append(lambda: S.add("dve", lambda e: e.scalar_tensor_tensor(out=ytmp[:], in0=yacc[:], scalar=ssq[:, 0:1], in1=bv[:, B_NORM:B_NORM + 1024], op0=ALU.mult, op1=ALU.mult),
                                            reads=[yacc.r(), ssq.r(), bv.r()], writes=[ytmp.r()]))
                    for hf in range(2):
                        def trp(hf=hf):
                            p, pr = PS(7) if hf == 0 else rotP.next()

                            def ftr(e):
                                for c in range(4):
                                    ins = e.transpose(p[:, c * 128:(c + 1) * 128], ytmp[:, (hf * 4 + c) * 128:(hf * 4 + c + 1) * 128], ident)
                                return ins
                            S.add("pe", ftr, reads=[ytmp.r(), cf32.r()], writes=[pr])
                            outv = hT[:, hf * 4:hf * 4 + 4, tg * 128:(tg + 1) * 128]
                            S.add("act", lambda e: e.activation(out=outv, in_=p[:].rearrange("p (c n) -> p c n", c=4), func=AF.Copy),
                                  reads=[pr], writes=[hT.r(tg // 4)])
                        th.append(trp)
                    th.append(lambda: state_update(0, ti))
                    if last and grp == 0:
                        th.append(lambda: out_state(0, sidx))
                    return th

                pending = []
                for tix, (sidx, ti, first, last) in enumerate(tiles_seq):
                    yacc = yaccs[tix % 2]
                    tcol = slice(ti * 128, (ti + 1) * 128)
                    cb_ = cbs[ti % 2]
                    for g in range(2):
                        p, pr = PS(7) if g == 0 else rotP.next()
                        mm(p[:, 0:128], BT[g * 64:(g + 1) * 64, tcol], CT[g * 64:(g + 1) * 64, tcol], True, True, [BT.r(), CT.r()], [pr])
                        S.add("act", lambda e, p=p, cb_=cb_, g=g: e.activation(out=cb_[:, g, :], in_=p[:, 0:128], func=AF.Copy), reads=[pr], writes=[cb_.r()])
                    ydp, ydr = PS(0)
                    munits = [(hp, d) for hp in range(8) for d in range(2)]
                    mts = {}

                    def emit_pdc(u, ti=ti, cb_=cb_, mts=mts, munits=munits):
                        nonlocal nm_
                        hp, d = munits[u]
                        g = hp // 4
                        pd, pdr = rotD.next()

                        def f(e, pd=pd, ti=ti, hp=hp, d=d):
                            for i_ in range(2):
                                hc = d * 16 + 2 * hp + i_
                                o_ = pd[:, i_ * 128:(i_ + 1) * 128]
                                e.matmul(o_, ahi[:, ti, hc:hc + 1].to_broadcast([128, 128]), tri[d], start=True, stop=False)
                                e.matmul(o_, alo[:, ti, hc:hc + 1].to_broadcast([128, 128]), tri[d], start=False, stop=False)
                                ins = e.matmul(o_, nmk[d], identb, start=False, stop=True)
                            return ins
                        S.add("pe", f, reads=[ahi.r(), alo.r(), cb16.r()], writes=[pdr])
                        dc = decb[nm_ % 3]
                        mt = MTb[nm_ % 3]
                        nm_ += 1
                        for i_ in range(2):
                            hc = d * 16 + 2 * hp + i_
                            S.add("act", lambda e, pd=pd, dc=dc, ti=ti, hc=hc, i_=i_: e.activation(out=dc[:, i_, :], in_=pd[:, i_ * 128:(i_ + 1) * 128], func=AF.Exp,
                                                                                                     bias=nacs[:, ti, hc:hc + 1], scale=1.0),
                                  reads=[pdr, nacs.r()], writes=[dc.r()])
                        S.add("pool" if u % 4 == 3 else "dve", lambda e, dc=dc, mt=mt, g=g, cb_=cb_: e.tensor_tensor(
                            out=mt[:], in0=dc[:], in1=cb_[:, g, :].unsqueeze(1).to_broadcast([128, 2, 128]), op=ALU.mult),
                            reads=[dc.r(), cb_.r()], writes=[mt.r()])
                        mts[u] = mt

                    LAm = 3
                    for u in range(LAm):
                        emit_pdc(u)
                    for u, (hp, d) in enumerate(munits):
                        mt = mts.pop(u)

                        def fy(e, mt=mt, hp=hp, d=d, u=u, ti=ti):
                            for i_ in range(2):
                                h = 2 * hp + i_
                                ins = e.matmul(ydp[:, (h % 8) * 64:(h % 8 + 1) * 64], mt[:, i_, :], xs_tok[:, ti, h * 64:(h + 1) * 64],
                                               start=(u % 8 == 0 and i_ == 0), stop=(d == 1), skip_group_check=True)
                            return ins
                        S.add("pe", fy, reads=[mt.r(), xs_tok.r()], writes=[ydr])
                        if u + LAm < len(munits):
                            emit_pdc(u + LAm)
                        for _ in range(2):
                            if pending:
                                pending.pop(0)()
                        if u % 8 == 7:
                            g = u // 8
                            S.add("act", lambda e, g=g, yacc=yacc: e.activation(out=yacc[:, g * 512:(g + 1) * 512], in_=ydp[:], func=AF.Copy), reads=[ydr], writes=[yacc.r()])
                        if u == 15:
                            while pending:
                                pending.pop(0)()
                    pending = pending + make_tail(sidx, ti, first, last, yacc)
                while pending:
                    pending.pop(0)()
                A.free(xs_tok, z_tok, B_tok, BT, CT, dta, ahi, alo, nacs, eacs, wdec, cdsel, *Sst, hinb, hinf, xw[0], stmp,
                       *cbs, *decb, *MTb, *yaccs, ytmp, ssq)
            for grp in range((2 if SUB == 0 else 1) if STAGE >= 13 else 0):
                ssd_group(grp if SUB < 10 else 1)
            A.free(aneg)
            while wstate["got"] < wstate["woo"]:
                wget()
            rotW = Rot([0, 1, 2, 3])
            for t in range(4):
                w, wr = wget()
                for oc in range(2):
                    o = t * 2 + oc
                    for b in range(3):
                        blk = slice(b * TB, (b + 1) * TB)
                        p, pr = rotW.next()

                        def f(e, p=p, w=w, oc=oc, blk=blk):
                            for k in range(12):
                                rhs = yl[:, k, blk] if k < 4 else hT[:, k - 4, blk]
                                ins = e.matmul(p[:], w[:, k, oc * 128:(oc + 1) * 128], rhs, start=(k == 0), stop=(k == 11))
                            return ins
                        S.add("pe", f, reads=[wr, hT.r(b), yl.r()], writes=[pr])
                        resid(p, pr, l, 2, o, b)
            A.free(yl)

        if STAGE >= 1:
            for t in range(4):
                ada_tile(0, t, 0)
            ada_finish(0, 0)
        if STAGE >= 2:
            even_layer(0)
        if STAGE >= 10:
            ffn(0)
        if NLAYERS > 1:
            odd_layer(1)
            if STAGE >= 20:
                ffn(1)

        yst = [A.alloc("yst%d" % i, [128, D], F32) for i in range(2)]
        rot = Rot([0, 1, 2, 3, 4, 5, 6, 7])
        for t in range(12):
            ys = yst[t % 2]
            for hf in range(2):
                p, pr = rot.next()

                def tr(e, p=p, t=t, hf=hf):
                    for c in range(4):
                        ins = e.transpose(p[:, c * 128:(c + 1) * 128], xT[:, hf * 4 + c, t * 128:(t + 1) * 128], ident)
                    return ins
                S.add("pe", tr, reads=[xT.r(t), cf32.r()], writes=[pr])
                if hf == 0:
                    S.add("act", lambda e, ys=ys, p=p: e.activation(out=ys[:, 0:512], in_=p[:], func=AF.Copy), reads=[pr], writes=[ys.r()])
                else:
                    S.add("dve", lambda e, ys=ys, p=p: e.tensor_copy(out=ys[:, 512:1024], in_=p[:]), reads=[pr], writes=[ys.r()])
            dma("sp", o_y[t * 128:(t + 1) * 128, :], ys[:], reads=[ys.r()])
        assert STAGE < 99 or wstate["got"] == len(wplan), (wstate, len(wplan))
        S.emit(st)
    return nc


_CACHE = {}


def _prep_shared(inp):
    f = np.float32
    sh = {}
    sh["w_ada"] = np.ascontiguousarray(inp["w_ada"], f)
    wie = np.asarray(inp["w_in_even"][0], f)
    fcols = wie[:, 0:256]
    q = wie[:, 256:1024].reshape(D, 12, 64)[:, PERM, :].reshape(D, 768)
    k = wie[:, 1024:1280]
    v = wie[:, 1280:1536]
    sh["w_in_even"] = np.ascontiguousarray(np.concatenate([q, k, fcols, v], 1))
    woe = np.asarray(inp["w_out_even"][0], f)
    att = woe[256:].reshape(12, 64, D)[PERM].reshape(768, D)
    sh["w_out_even"] = np.ascontiguousarray(np.concatenate([woe[:256], att], 0))
    sh["w_in_odd"] = np.ascontiguousarray(inp["w_in_odd"][0], f)
    sh["w_out_odd"] = np.ascontiguousarray(inp["w_out_odd"][0], f)
    sh["ffn_w1"] = np.ascontiguousarray(inp["ffn_w1"], f)
    sh["ffn_w3"] = np.ascontiguousarray(inp["ffn_w3"], f)
    sh["ffn_w2"] = np.ascontiguousarray(inp["ffn_w2"], f)
    bd = np.zeros((2, 2, 4, 128, 128), f)
    for gi, wname in enumerate(("lru_wa", "lru_wx")):
        wsrc = np.asarray(inp[wname][0], f)
        for d in range(2):
            for blk in range(8):
                c, hlf = blk // 2, blk % 2
                bd[gi, d, c, hlf * 64:(hlf + 1) * 64, hlf * 64:(hlf + 1) * 64] = wsrc[d, blk]
    sh["lru_bd"] = bd.reshape(16, 128, 128)
    bvec = np.zeros((1, NBV), f)
    bvec[0, B_NORM:B_NORM + 1024] = inp["ssd_norm"][0]
    bvec[0, B_D:B_D + 16] = inp["ssd_d"][0]
    bvec[0, B_DTB:B_DTB + 32] = np.asarray(inp["ssd_dt_bias"][0]).reshape(32)
    bvec[0, B_ALOG:B_ALOG + 32] = np.asarray(inp["ssd_a_log"][0]).reshape(32)
    sh["bvec"] = bvec
    sh.update(_consts())
    return sh


def _vecs(inp, i):
    f = np.float32
    v = np.zeros((256, 128), f)
    v[R_BADA:R_BADA + 96] = np.asarray(inp["b_ada"], f).reshape(96, 128)
    v[R_CCTX:R_CCTX + 8] = np.asarray(inp["c_ctx"], f).reshape(8, 128)
    v[R_CI:R_CI + 8] = np.asarray(inp["c"][i], f).reshape(8, 128)
    v[R_NMIX:R_NMIX + 16] = np.asarray(inp["norm_mix"], f).reshape(16, 128)
    v[R_NFFN:R_NFFN + 16] = np.asarray(inp["norm_ffn"], f).reshape(16, 128)
    v[R_CLW:R_CLW + 16] = np.asarray(inp["conv_lru_w"][0], f).reshape(16, 128)
    v[R_CLB:R_CLB + 4] = np.asarray(inp["conv_lru_b"][0], f).reshape(4, 128)
    v[R_BA:R_BA + 8] = np.asarray(inp["lru_ba"][0], f).reshape(8, 128)
    v[R_BX:R_BX + 8] = np.asarray(inp["lru_bx"][0], f).reshape(8, 128)
    v[R_LAM:R_LAM + 8] = np.asarray(inp["lru_lambda"][0], f).reshape(8, 128)
    v[R_CSW:R_CSW + 40] = np.asarray(inp["conv_ssd_w"][0], f).reshape(40, 128)
    v[R_CSB:R_CSB + 10] = np.asarray(inp["conv_ssd_b"][0], f).reshape(10, 128)
    v[R_QN] = np.tile(np.asarray(inp["q_norm"][0], f), 2)
    v[R_KN] = np.tile(np.asarray(inp["k_norm"][0], f), 2)
    v[R_LRU0:R_LRU0 + 8] = np.asarray(inp["state_lru"][i, 0], f).reshape(8, 128)
    return v


def kernel(**inp):
    inp = {k: np.asarray(v) for k, v in inp.items()}
    if "nc" not in _CACHE:
        _CACHE["nc"] = build_program()
    nc = _CACHE["nc"]
    sh = _prep_shared(inp)
    in_maps = []
    for i in range(NCORES):
        m = dict(sh)
        m["x"] = np.ascontiguousarray(np.concatenate(
            [inp["x_prompt"][2 * i], inp["x_prompt"][2 * i + 1], inp["x_sample"][i]], 0), np.float32)
        m["vecs"] = _vecs(inp, i)
        m["cache_k"] = np.ascontiguousarray(inp["cache_k"][i, 0].reshape(512, 256), np.float32)
        m["cache_v"] = np.ascontiguousarray(inp["cache_v"][i, 0].reshape(512, 256), np.float32)
        m["state_ssd"] = np.ascontiguousarray(inp["state_ssd"][i, 0].reshape(2, 1024, 64), np.float32)
        in_maps.append(m)
    res = run_bass_kernel_spmd(nc, in_maps[:NRUN], core_ids=list(range(NRUN)))
    R = res.results
    y_prompt = np.zeros((16, 256, D), np.float32)
    y_sample = np.zeros((8, 1024, D), np.float32)
    new_k = np.zeros((16, 1, 256, 4, 64), np.float32)
    new_v = np.zeros((16, 1, 256, 4, 64), np.float32)
    new_lru = np.zeros((16, 1, 2, 512), np.float32)
    new_ssd = np.zeros((16, 1, 2, 16, 64, 64), np.float32)
    for i in range(NRUN):
        y = np.asarray(R[i]["y"])
        y_prompt[2 * i] = y[0:256]
        y_prompt[2 * i + 1] = y[256:512]
        y_sample[i] = y[512:]
        nk = np.asarray(R[i]["newk"]).reshape(2, 256, 4, 64)
        nv = np.asarray(R[i]["newv"]).reshape(2, 256, 4, 64)
        new_k[2 * i:2 * i + 2, 0] = nk
        new_v[2 * i:2 * i + 2, 0] = nv
        new_lru[2 * i:2 * i + 2, 0] = np.asarray(R[i]["newlru"]).reshape(2, 2, 512)
        new_ssd[2 * i:2 * i + 2, 0] = np.asarray(R[i]["newssd"]).reshape(2, 2, 16, 64, 64)
    return (y_prompt, y_sample, new_k, new_v, new_lru, new_ssd)
```

```python
import numpy as np
from contextlib import ExitStack
import concourse.bass as bass
import concourse.mybir as mybir
from concourse.bass_utils import run_bass_kernel_spmd

F32 = mybir.dt.float32
BF16 = mybir.dt.bfloat16
AF = mybir.ActivationFunctionType
ALU = mybir.AluOpType

ENGS = ("pe", "act", "dve", "pool", "sp")
NDMASEM = 6
NCORES = 8
NRUN = 8
EPS = 1e-6
NLAYERS = 2
STAGE = 99
SUB = 0


class Reg:
    __slots__ = ("name", "lw", "rd", "excl")

    def __init__(self, name, inherit=(), excl=False):
        self.name = name
        self.lw = None
        self.rd = list(inherit)
        self.excl = excl


class Op:
    __slots__ = ("eng", "fn", "deps", "needed", "dma", "sem", "val", "gidx", "thr")

    def __init__(self, eng, fn, dma):
        self.eng = eng
        self.fn = fn
        self.dma = dma
        self.deps = []
        self.needed = False
        self.sem = None
        self.val = 0
        self.thr = None


class Sched:
    def __init__(self, nc):
        self.nc = nc
        self.ops = {e: [] for e in ENGS}
        self.all = []
        self.ndma = {e: 0 for e in ENGS}
        self.dmaops = {e: [] for e in ENGS}

    def add(self, eng, fn, reads=(), writes=(), dma=False):
        op = Op(eng, fn, dma)
        deps = {}
        for r in reads:
            w = r.lw
            if w is not None and (w.dma or dma or w.eng != eng or eng != "pe"):
                deps[id(w)] = w
            if r.excl:
                for q in r.rd:
                    if q.eng != eng:
                        deps[id(q)] = q
        for t in writes:
            w = t.lw
            if w is not None and (w.dma or dma or w.eng != eng or eng != "pe"):
                deps[id(w)] = w
            for q in t.rd:
                if q.dma or dma or q.eng != eng or eng != "pe":
                    deps[id(q)] = q
        op.deps = list(deps.values())
        for r in reads:
            if not dma:
                r.rd = [q for q in r.rd if q.dma or q.eng != eng]
            r.rd.append(op)
        for t in writes:
            t.lw = op
            t.rd = []
        if dma:
            i = self.ndma[eng]
            self.ndma[eng] += 1
            if i >= NDMASEM:
                op.thr = self.dmaops[eng][i - NDMASEM]
            self.dmaops[eng].append(op)
        op.gidx = len(self.all)
        self.all.append(op)
        self.ops[eng].append(op)
        return op

    def emit(self, stack):
        nc = self.nc
        for op in self.all:
            for d in op.deps:
                d.needed = True
        esem = {e: stack.enter_context(nc.semaphore("s_" + e)) for e in ENGS if e != "sp"}
        dsem = {e: [stack.enter_context(nc.semaphore("d_%s%d" % (e, i))) for i in range(NDMASEM)]
                for e in ENGS if self.ndma[e] > 0}
        for e in ENGS:
            cnt = 0
            i = 0
            for op in self.ops[e]:
                if op.dma:
                    op.sem = dsem[e][i % NDMASEM]
                    op.val = 16 * (i // NDMASEM + 1)
                    i += 1
                elif op.needed:
                    cnt += 1
                    op.sem = esem[e]
                    op.val = cnt
        block = stack.enter_context(nc.Block())
        engh = {"pe": block.tensor, "act": block.scalar, "dve": block.vector,
                "pool": block.gpsimd, "sp": block.sync}
        for e in ENGS:
            ops = self.ops[e]
            if not ops:
                continue

            def body(eng, ops=ops, e=e):
                waited = {}
                for op in ops:
                    ds = list(op.deps)
                    if op.thr is not None:
                        ds.append(op.thr)
                    for d in ds:
                        k = id(d.sem)
                        if waited.get(k, 0) >= d.val:
                            continue
                        waited[k] = d.val
                        eng.wait_ge(d.sem, d.val)
                    ins = op.fn(eng)
                    if op.dma:
                        ins.then_inc(op.sem, 16)
                    elif op.needed:
                        ins.then_inc(op.sem, 1)
                for d in self.dmaops[e][-NDMASEM:]:
                    if waited.get(id(d.sem), 0) < d.val:
                        waited[id(d.sem)] = d.val
                        eng.wait_ge(d.sem, d.val)

            engh[e](body)


def _prune(ops):
    best = {}
    out = []
    for o in ops:
        if o.dma:
            out.append(o)
        else:
            b = best.get(o.eng)
            if b is None or o.gidx > b.gidx:
                best[o.eng] = o
    return out + list(best.values())


class TV:
    def __init__(self, ap, name, off, nw, inherit):
        self.ap = ap
        self.name = name
        self.off = off
        self.nw = nw
        self.inherit = inherit
        self.regs = {}

    def __getitem__(self, k):
        return self.ap[k]

    def r(self, key=0):
        g = self.regs.get(key)
        if g is None:
            g = Reg("%s.%s" % (self.name, key), self.inherit)
            self.regs[key] = g
        return g

    def rs(self, keys):
        return [self.r(k) for k in keys]


class Arena:
    def __init__(self, nc, stack, nwords):
        self.t = stack.enter_context(nc.sbuf_tensor("arena", [128, nwords], F32))
        self.n = nwords
        self.live = []
        self.dead = []
        self.peak = 0

    def alloc(self, name, shape, dt):
        n = int(np.prod(shape[1:]))
        nw = n if dt == F32 else (n + 1) // 2
        nw = (nw + 7) // 8 * 8
        off = 0
        for tv in sorted(self.live, key=lambda t: t.off):
            if tv.off - off >= nw:
                break
            off = max(off, tv.off + tv.nw)
        assert off + nw <= self.n, "SBUF arena overflow allocating %s (%d words at %d)" % (name, nw, off)
        self.peak = max(self.peak, off + nw)
        pend = []
        keep = []
        for (o, w, ops) in self.dead:
            if o < off + nw and off < o + w:
                pend += ops
                if not (off <= o and o + w <= off + nw):
                    keep.append((o, w, ops))
            else:
                keep.append((o, w, ops))
        self.dead = keep
        v = self.t[0:shape[0], off:off + nw]
        if dt != F32:
            v = v.bitcast(dt)
        v = v[:, 0:n]
        if len(shape) == 3:
            v = v.rearrange("p (a b) -> p a b", a=shape[1])
        elif len(shape) == 4:
            v = v.rearrange("p (a b c) -> p a b c", a=shape[1], b=shape[2])
        tv = TV(v, name, off, nw, _prune(pend))
        self.live.append(tv)
        return tv

    def free(self, *tvs):
        for tv in tvs:
            ops = list(tv.inherit)
            for g in tv.regs.values():
                if g.lw is not None:
                    ops.append(g.lw)
                ops += g.rd
            self.dead.append((tv.off, tv.nw, _prune(ops)))
            self.live.remove(tv)


D = 1024
NTOK = 1536
TB = 512
PERM = [0, 3, 1, 4, 2, 5, 6, 9, 7, 10, 8, 11]
DFF = 2816
R_BADA, R_CCTX, R_CI, R_NMIX, R_NFFN = 0, 96, 104, 112, 128
R_CLW, R_CLB, R_BA, R_BX, R_LAM, R_CSW, R_CSB, R_QN, R_KN, R_LRU0 = 144, 160, 164, 172, 180, 188, 228, 238, 239, 240
B_NORM, B_D, B_DTB, B_ALOG, NBV = 0, 1024, 1040, 1072, 1104


def _consts():
    c = {}
    ident = np.eye(128, dtype=np.float32)
    onesm = np.full((128, 128), 1.0 / 1024, np.float32)
    o64 = np.zeros((128, 128), np.float32)
    o64[:64, :64] = 1.0 / 64
    o64[64:, 64:] = 1.0 / 64
    R = np.zeros((128, 128), np.float32)
    for h in range(2):
        for ax in range(2):
            for i in range(16):
                p1 = h * 64 + ax * 32 + i
                p2 = p1 + 16
                R[p1, p2] = -1.0
                R[p2, p1] = 1.0
    j = np.arange(128)
    utri = (j[:, None] <= j[None, :]).astype(np.float32)
    ltri = (j[:, None] >= j[None, :]).astype(np.float32)
    nmf = np.where(j[:, None] < j[None, :], -32768.0, 0.0).astype(np.float32)
    nmb = np.where(j[:, None] > j[None, :], -32768.0, 0.0).astype(np.float32)
    onesf = np.ones((128, 128), np.float32)
    c["cf32"] = np.stack([ident, R.T.copy(), utri, ltri, onesf])
    a64 = 2 * np.pi * np.outer(np.arange(64), np.arange(64)) / 64
    c64 = np.zeros((2, 128, 128), np.float32)
    for g in range(2):
        c64[0, g * 64:(g + 1) * 64, g * 64:(g + 1) * 64] = np.cos(a64) / 8
        c64[1, g * 64:(g + 1) * 64, g * 64:(g + 1) * 64] = -np.sin(a64) / 8
    c["cb16"] = np.stack([onesm, o64, nmf, nmb, ident, c64[0], c64[1], utri, ltri])
    for S in (256, 1024):
        a = 2 * np.pi * (np.outer(np.arange(S), np.arange(S)) % S) / S
        c["dft%d" % S] = np.stack([np.cos(a), np.sin(a)]).astype(np.float32) / np.sqrt(S)
    s = np.arange(1024)
    row = (s // 64).astype(np.float32)
    col = (s % 64).astype(np.float32)
    freqs = (10000.0 ** (-np.arange(16, dtype=np.float32) / 16)).astype(np.float32)
    ang = np.zeros((64, 1024), np.float32)
    for d in range(64):
        ax = d // 32
        i = d % 16
        ang[d] = (row if ax == 0 else col) * freqs[i]
    ang = np.concatenate([ang, ang], 0)
    c["rope"] = np.stack([np.cos(ang), np.sin(ang)]).astype(np.float32)
    return c


def build_program():
    nc = bass.Bass("TRN2", target_bir_lowering=False)
    S = Sched(nc)

    def din(name, shape):
        return nc.dram_tensor(name, list(shape), F32, kind="ExternalInput").ap()

    def dout(name, shape):
        return nc.dram_tensor(name, list(shape), F32, kind="ExternalOutput").ap()

    d_x = din("x", [NTOK, D])
    d_vecs = din("vecs", [256, 128])
    d_bvec = din("bvec", [1, NBV])
    d_wada = din("w_ada", [2, D, 6 * D])
    d_wie = din("w_in_even", [D, 1536])
    d_woe = din("w_out_even", [D, D])
    d_wio = din("w_in_odd", [D, 3360])
    d_woo = din("w_out_odd", [1536, D])
    d_w1 = din("ffn_w1", [2, D, DFF])
    d_w3 = din("ffn_w3", [2, D, DFF])
    d_w2 = din("ffn_w2", [2, DFF, D])
    d_lrubd = din("lru_bd", [16, 128, 128])
    d_ck = din("cache_k", [512, 256])
    d_cv = din("cache_v", [512, 256])
    d_sssd = din("state_ssd", [2, 1024, 64])
    d_cf32 = din("cf32", [5, 128, 128])
    d_cb16 = din("cb16", [9, 128, 128])
    d_dft256 = din("dft256", [2, 256, 256])
    d_dft1024 = din("dft1024", [2, 1024, 1024])
    d_rope = din("rope", [2, 128, 1024])
    o_y = dout("y", [NTOK, D])
    o_k = dout("newk", [512, 256])
    o_v = dout("newv", [512, 256])
    o_lru = dout("newlru", [16, 128])
    o_ssd = dout("newssd", [2, 2, 1024, 64])

    with ExitStack() as st:
        A = Arena(nc, st, 53100)
        psb = []
        for i in range(8):
            t = st.enter_context(nc.psum_tensor("ps%d" % i, [128, 512], F32))
            psb.append((t, Reg("ps%d" % i, excl=True)))

        def PS(i):
            return psb[i]

        class Rot:
            def __init__(self, banks):
                self.b = list(banks)
                self.i = 0

            def next(self):
                r = psb[self.b[self.i % len(self.b)]]
                self.i += 1
                return r

        def dma(q, out, in_, reads=(), writes=()):
            return S.add(q, lambda e: e.dma_start(out=out, in_=in_), reads=reads, writes=writes, dma=True)

        def mm(out, lhsT, rhs, start, stop, reads, writes):
            return S.add("pe", lambda e: e.matmul(out, lhsT, rhs, start=start, stop=stop), reads=reads, writes=writes)

        NSLOT = 4
        slots = [A.alloc("wslot%d" % i, [128, 4096], BF16) for i in range(NSLOT)]
        wplan = []
        wstate = {"issued": 0, "got": 0}

        def wplan_add(ap, a, b):
            wplan.append((ap, a, b))

        def kp(ap2d):
            return ap2d.rearrange("(k p) n -> p k n", p=128)

        def wview(i):
            ap, a, b = wplan[i]
            sl = slots[i % NSLOT]
            return sl[:, 0:a * b].rearrange("p (a b) -> p a b", a=a), sl.r()

        def wget():
            i = wstate["got"]
            wstate["got"] += 1
            while wstate["issued"] < min(len(wplan), i + NSLOT - 1):
                j = wstate["issued"]
                v, rg = wview(j)
                dma("pool", v, wplan[j][0], writes=[rg])
                wstate["issued"] += 1
            return wview(i)

        def plan_ada(l):
            for t in range(12):
                wplan_add(kp(d_wada[l])[:, :, t * 512:(t + 1) * 512], 8, 512)

        def plan_ada_tiles(l, ts):
            for t in ts:
                wplan_add(kp(d_wada[l])[:, :, t * 512:(t + 1) * 512], 8, 512)

        def plan_ffn(l):
            for jt in range(6):
                n = 512 if jt < 5 else 256
                wplan_add(kp(d_w1[l])[:, :, jt * 512:jt * 512 + n], 8, n)
                wplan_add(kp(d_w3[l])[:, :, jt * 512:jt * 512 + n], 8, n)
            for o in range(8):
                wplan_add(kp(d_w2[l])[:, :, o * 128:(o + 1) * 128], 22, 128)

        plan_ada_tiles(0, range(0, 4))
        for t in range(3):
            wplan_add(kp(d_wie)[:, :, t * 512:(t + 1) * 512], 8, 512)
        plan_ada_tiles(0, range(4, 12))
        if NLAYERS > 1:
            plan_ada_tiles(1, range(12))
        for sb in range(2):
            for m in range(2):
                wplan_add(kp(d_dft1024[m])[:, :, sb * 512:(sb + 1) * 512], 8, 512)
        for t in range(2):
            wplan_add(kp(d_woe)[:, :, t * 512:(t + 1) * 512], 8, 512)
        plan_ffn(0)
        if NLAYERS > 1:
            wio = kp(d_wio)
            for t in range(2):
                wplan_add(wio[:, :, t * 512:(t + 1) * 512], 8, 512)
            for grp in range(2):
                wplan_add(wio[:, :, 2048:2560], 8, 512)
                wplan_add(wio[:, :, 2560:3072], 8, 512)
                wplan_add(wio[:, :, 3072:3328], 8, 256)
                wplan_add(wio[:, :, 1024:1536], 8, 512)
                wplan_add(wio[:, :, 1536:2048], 8, 512)
                wplan_add(wio[:, :, 3328:3360], 8, 32)
            wstate["woo"] = len(wplan)
            for t in range(4):
                wplan_add(kp(d_woo)[:, :, t * 256:(t + 1) * 256], 12, 256)
            plan_ffn(1)

        xT = A.alloc("xT", [128, 8, NTOK], F32)
        hT = A.alloc("hT", [128, 8, NTOK], BF16)
        cf32 = A.alloc("cf32", [128, 5, 128], F32)
        cb16 = A.alloc("cb16", [128, 9, 128], BF16)
        vecT = A.alloc("vecT", [128, 256], F32)
        bv = A.alloc("bv", [128, NBV], F32)
        mod = A.alloc("mod", [128, 2, 48, 2], F32)
        A1 = A.alloc("A1", [128, 2, 2, 8, 2], F32) if False else A.alloc("A1", [128, 64], F32)
        ident = cf32[:, 0, :]
        RT = cf32[:, 1, :]
        utri = cf32[:, 2, :]
        ltri = cf32[:, 3, :]
        onesf = cf32[:, 4, :]
        onesm = cb16[:, 0, :]
        o64 = cb16[:, 1, :]
        identb = cb16[:, 4, :]

        def A1v(l, kind, c, j):
            o = ((l * 2 + kind) * 8 + c) * 2 + j
            return A1[:, o:o + 1]

        def modv(l, q, c, j):
            return mod[:, l, q * 8 + c, j:j + 1]

        def xr(b):
            return xT.rs(range(4 * b, 4 * b + 4))

        dma("sp", cf32[:], d_cf32.rearrange("c p n -> p c n"), writes=[cf32.r()])
        dma("pool", cb16[:], d_cb16.rearrange("c p n -> p c n"), writes=[cb16.r()])
        vraw = A.alloc("vraw", [128, 2, 128], F32)
        dma("sp", vraw[:], d_vecs.rearrange("(a p) n -> p a n", p=128), writes=[vraw.r()])
        dma("sp", bv[:], d_bvec.partition_broadcast(128), writes=[bv.r()])
        for a in range(2):
            p, pr = PS(a)
            S.add("pe", lambda e, p=p, a=a: e.transpose(p[:, 0:128], vraw[:, a, :], ident),
                  reads=[vraw.r(), cf32.r()], writes=[pr])
            S.add("dve", lambda e, p=p, a=a: e.tensor_copy(out=vecT[:, a * 128:(a + 1) * 128], in_=p[:, 0:128]),
                  reads=[pr], writes=[vecT.r()])
        A.free(vraw)

        xst = [A.alloc("xst%d" % i, [128, D], F32) for i in range(2)]
        rot = Rot([2, 3, 4, 5, 6, 7])
        for t in range(12):
            xs_ = xst[t % 2]
            dma("sp", xs_[:], d_x[t * 128:(t + 1) * 128, :], writes=[xs_.r()])
            for hf in range(2):
                p, pr = rot.next()

                def tr(e, p=p, xs_=xs_, hf=hf):
                    for c in range(4):
                        ins = e.transpose(p[:, c * 128:(c + 1) * 128], xs_[:, (hf * 4 + c) * 128:(hf * 4 + c + 1) * 128], ident)
                    return ins
                S.add("pe", tr, reads=[xs_.r(), cf32.r()], writes=[pr])
                eng = "act" if hf == 0 else "dve"
                outv = xT[:, hf * 4:hf * 4 + 4, t * 128:(t + 1) * 128]
                inv = p[:].rearrange("p (c n) -> p c n", c=4)
                if eng == "act":
                    S.add("act", lambda e, o=outv, i=inv: e.activation(out=o, in_=i, func=AF.Copy), reads=[pr], writes=[xT.r(t)])
                else:
                    S.add("dve", lambda e, o=outv, i=inv: e.tensor_copy(out=o, in_=i), reads=[pr], writes=[xT.r(t)])
        A.free(*xst)

        scT = A.alloc("scT", [128, 8, 2], BF16)
        S.add("act", lambda e: e.activation(out=scT[:].rearrange("p k j -> p j k"),
                                            in_=vecT[:, R_CCTX:R_CCTX + 16].rearrange("p (j k) -> p j k", j=2), func=AF.Silu),
              reads=[vecT.r()], writes=[scT.r()])

        def ada_tile(l, t, bank):
            p, pr = PS(bank) if isinstance(bank, int) else bank
            wt, wr = wget()

            def f(e):
                for oc4 in range(4):
                    for k in range(8):
                        ins = e.matmul(p[:, oc4 * 2:oc4 * 2 + 2], wt[:, k, oc4 * 128:(oc4 + 1) * 128], scT[:, k, :],
                                       start=(k == 0), stop=(k == 7))
                return ins
            S.add("pe", f, reads=[wr, scT.r()], writes=[pr])
            r0 = R_BADA + l * 48 + t * 4
            S.add("dve", lambda e: e.tensor_tensor(out=mod[:, l, t * 4:(t + 1) * 4, :], in0=p[:, 0:8].rearrange("p (c j) -> p c j", j=2),
                                                   in1=vecT[:, r0:r0 + 4].unsqueeze(2).to_broadcast([128, 4, 2]),
                                                   op=ALU.add), reads=[pr, vecT.r()], writes=[mod.r()])

        def ada_finish(l, kind):
            q, rbase = ((1, R_NMIX), (4, R_NFFN))[kind]
            o = (l * 2 + kind) * 16
            S.add("dve", lambda e: e.scalar_tensor_tensor(
                out=A1[:, o:o + 16].rearrange("p (c j) -> p c j", j=2), in0=mod[:, l, q * 8:(q + 1) * 8, :], scalar=1.0,
                in1=vecT[:, rbase + l * 8:rbase + (l + 1) * 8].unsqueeze(2).to_broadcast([128, 8, 2]),
                op0=ALU.add, op1=ALU.mult), reads=[mod.r(), vecT.r()], writes=[A1.r()])

        def norm_mod(l, kind):
            qs = 0 if kind == 0 else 3
            sq = [A.alloc("nsq%d" % i, [128, TB], BF16) for i in range(4)]
            rs = [A.alloc("nrs%d" % i, [128, TB], F32) for i in range(3)]
            tt = [A.alloc("ntt%d" % i, [128, TB], F32) for i in range(3)]
            rot = Rot([6, 7])
            cnt = [0, 0]
            pbank = {}

            def stA(b):
                blk = slice(b * TB, (b + 1) * TB)
                p, pr = rot.next()
                pbank[b] = (p, pr)
                for c in range(8):
                    s_ = sq[cnt[0] % 4]
                    cnt[0] += 1
                    if c % 2 == 0:
                        S.add("act", lambda e, s_=s_, c=c: e.activation(out=s_[:], in_=xT[:, c, blk], func=AF.Square), reads=xr(b), writes=[s_.r()])
                    else:
                        S.add("dve", lambda e, s_=s_, c=c: e.tensor_tensor(out=s_[:], in0=xT[:, c, blk], in1=xT[:, c, blk], op=ALU.mult), reads=xr(b), writes=[s_.r()])
                    mm(p[:], onesm, s_[:], c == 0, c == 7, [s_.r(), cb16.r()], [pr])

            def stB(b):
                p, pr = pbank[b]
                r_ = rs[b]
                S.add("act", lambda e: e.activation(out=r_[:], in_=p[:], func=AF.Ln, bias=EPS_AP[:, 0:1], scale=1.0), reads=[pr, EPS_AP.r()], writes=[r_.r()])
                S.add("act", lambda e: e.activation(out=r_[:], in_=r_[:], func=AF.Exp, scale=-0.5), reads=[r_.r()], writes=[r_.r()])

            def stC(b):
                j = 0 if b == 0 else 1
                blk = slice(b * TB, (b + 1) * TB)
                r_ = rs[b]
                for c in range(8):
                    t_ = tt[cnt[1] % 3]
                    cnt[1] += 1
                    S.add("dve", lambda e, t_=t_, c=c: e.tensor_tensor(out=t_[:], in0=xT[:, c, blk], in1=r_[:], op=ALU.mult),
                          reads=xr(b) + [r_.r()], writes=[t_.r()])
                    S.add("act", lambda e, t_=t_, c=c: e.activation(out=hT[:, c, blk], in_=t_[:], func=AF.Identity,
                                                                    bias=modv(l, qs, c, j), scale=A1v(l, kind, c, j)),
                          reads=[t_.r(), mod.r(), A1.r()], writes=[hT.r(b)])

            stA(0)
            stA(1)
            stB(0)
            stC(0)
            stA(2)
            stB(1)
            stC(1)
            stB(2)
            stC(2)
            A.free(*sq, *rs, *tt)

        EPS_AP = A.alloc("eps", [128, 2], F32)
        S.add("pool", lambda e: e.memset(EPS_AP[:], EPS), writes=[EPS_AP.r()])
        ONE_AP = A.alloc("one", [128, 2], F32)
        S.add("pool", lambda e: e.memset(ONE_AP[:], 1.0), writes=[ONE_AP.r()])

        def resid(p, pr, l, q, o, b):
            j = 0 if b == 0 else 1
            blk = slice(b * TB, (b + 1) * TB)
            S.add("dve", lambda e: e.scalar_tensor_tensor(out=xT[:, o, blk], in0=p[:], scalar=modv(l, q, o, j), in1=xT[:, o, blk],
                                                          op0=ALU.mult, op1=ALU.add),
                  reads=[pr, mod.r()] + xr(b), writes=xr(b))

        def ffn(l):
            norm_mod(l, 1)
            actT = A.alloc("actT", [128, 22, NTOK], BF16)
            sl = [A.alloc("fsl%d" % i, [128, TB], BF16) for i in range(3)]
            rot1 = Rot([0, 1, 2, 3])
            n = 0
            for jt in range(6):
                w1, w1r = wget()
                w3, w3r = wget()
                ncs = 4 if jt < 5 else 2
                for cc in range(ncs):
                    jj = jt * 4 + cc
                    for b in range(3):
                        blk = slice(b * TB, (b + 1) * TB)
                        p1, p1r = rot1.next()
                        p3, p3r = rot1.next()

                        def f(e, w=w1, p=p1, cc=cc, blk=blk):
                            for k in range(8):
                                ins = e.matmul(p[:], w[:, k, cc * 128:(cc + 1) * 128], hT[:, k, blk], start=(k == 0), stop=(k == 7))
                            return ins
                        S.add("pe", f, reads=[w1r, hT.r(b)], writes=[p1r])

                        def f3(e, w=w3, p=p3, cc=cc, blk=blk):
                            for k in range(8):
                                ins = e.matmul(p[:], w[:, k, cc * 128:(cc + 1) * 128], hT[:, k, blk], start=(k == 0), stop=(k == 7))
                            return ins
                        S.add("pe", f3, reads=[w3r, hT.r(b)], writes=[p3r])
                        s_ = sl[n % 3]
                        n += 1
                        S.add("act", lambda e, s_=s_, p=p1: e.activation(out=s_[:], in_=p[:], func=AF.Silu), reads=[p1r], writes=[s_.r()])
                        S.add("dve", lambda e, s_=s_, p=p3, jj=jj, blk=blk: e.tensor_tensor(out=actT[:, jj, blk], in0=p[:], in1=s_[:], op=ALU.mult),
                              reads=[p3r, s_.r()], writes=[actT.r(b)])
            A.free(*sl)
            rot2 = Rot([4, 5, 6, 7])
            for o in range(8):
                w2, w2r = wget()
                for b in range(3):
                    blk = slice(b * TB, (b + 1) * TB)
                    p, pr = rot2.next()

                    def f(e, w=w2, p=p, blk=blk):
                        for jj in range(22):
                            ins = e.matmul(p[:], w[:, jj, :], actT[:, jj, blk], start=(jj == 0), stop=(jj == 21))
                        return ins
                    S.add("pe", f, reads=[w2r, actT.r(b)], writes=[pr])
                    resid(p, pr, l, 5, o, b)
            A.free(actT)

        def even_layer(l):
            norm_mod(l, 0)
            qT = A.alloc("qT", [128, 6, NTOK], BF16)
            kT = A.alloc("kT", [128, 2, 2048], BF16)
            vaug = A.alloc("vaug", [128, 16, 4, 128], BF16)
            ftok = A.alloc("ftok", [128, 12, 256], BF16)
            rope = A.alloc("rope", [128, 2, 1024], F32)
            dft256 = A.alloc("dft256", [128, 2, 2, 256], BF16)
            knew = A.alloc("knew", [128, 4, 256], F32)
            vnew = A.alloc("vnew", [128, 4, 256], F32)
            dma("sp", rope[:], d_rope.rearrange("c p n -> p c n"), writes=[rope.r()])
            for m in range(2):
                dma("pool", dft256[:, m, :, :], d_dft256[m].rearrange("(t p) n -> p t n", p=128), writes=[dft256.r()])
            S.add("pool", lambda e: e.memset(vaug[:], 1.0), writes=vaug.rs(range(16)))
            cvv = d_cv.rearrange("(t p) (j d) -> p t j d", p=128, j=4)
            for par in range(2):
                for tt_ in range(4):
                    dma("pool", vaug[:, 12 + tt_, par::2, par * 64:par * 64 + 64], cvv[:, tt_, par::2, :], writes=[vaug.r(12 + tt_)])
            ckt = A.alloc("ckt", [128, 4, 256], F32)
            dma("sp", ckt[:], d_ck.rearrange("(t p) f -> p t f", p=128), writes=[ckt.r()])
            rotc = Rot([4, 5])
            for kc in range(2):
                p, pr = rotc.next()

                def f(e, p=p, kc=kc):
                    for tt_ in range(4):
                        ins = e.transpose(p[:, tt_ * 128:(tt_ + 1) * 128], ckt[:, tt_, kc * 128:(kc + 1) * 128], ident)
                    return ins
                S.add("pe", f, reads=[ckt.r(), cf32.r()], writes=[pr])
                S.add("dve", lambda e, p=p, kc=kc: e.tensor_copy(out=kT[:, kc, 1536:2048], in_=p[:]), reads=[pr], writes=[kT.r(3)])
            A.free(ckt)

            if STAGE < 3:
                return
            sqn = [A.alloc("sqn%d" % i, [128, TB], BF16) for i in range(3)]
            rsn = [A.alloc("rsn%d" % i, [128, TB], F32) for i in range(3)]
            qn = [A.alloc("qn%d" % i, [128, TB], F32) for i in range(3)]
            t1 = [A.alloc("rt1%d" % i, [128, TB], F32) for i in range(2)]
            t2 = [A.alloc("rt2%d" % i, [128, TB], F32) for i in range(2)]
            rotA = Rot([0, 1, 6])
            rotB = Rot([2, 3])
            rotC = Rot([4, 5, 7])
            qunits = [(wt_i, cc, b) for wt_i in range(2) for cc in range(4) for b in range(3)]
            ust = {}
            wcur = {}

            def stageA(u):
                wt_i, cc, b = qunits[u]
                if (wt_i,) not in wcur:
                    wcur[(wt_i,)] = wget()
                w, wr = wcur[(wt_i,)]
                blk = slice(b * TB, (b + 1) * TB)
                pa, par_ = rotA.next()

                def f(e):
                    for k in range(8):
                        ins = e.matmul(pa[:], w[:, k, cc * 128:(cc + 1) * 128], hT[:, k, blk], start=(k == 0), stop=(k == 7))
                    return ins
                S.add("pe", f, reads=[wr, hT.r(b)], writes=[par_])
                s_ = sqn[u % 3]
                S.add("act", lambda e: e.activation(out=s_[:], in_=pa[:], func=AF.Square), reads=[par_], writes=[s_.r()])
                ust[u] = (pa, par_, s_)

            def stageB(u):
                wt_i, cc, b = qunits[u]
                qc = wt_i * 4 + cc
                isk = qc >= 6
                gain = vecT[:, R_KN:R_KN + 1] if isk else vecT[:, R_QN:R_QN + 1]
                pa, par_, s_ = ust[u]
                r_ = rsn[u % 3]
                q_ = qn[u % 3]
                pb, pbr = rotB.next()
                mm(pb[:], o64, s_[:], True, True, [s_.r(), cb16.r()], [pbr])
                S.add("act", lambda e: e.activation(out=r_[:], in_=pb[:], func=AF.Ln, bias=EPS_AP[:, 0:1], scale=1.0),
                      reads=[pbr, EPS_AP.r()], writes=[r_.r()])
                S.add("act", lambda e: e.activation(out=r_[:], in_=r_[:], func=AF.Exp, scale=-0.5), reads=[r_.r()], writes=[r_.r()])
                if b == 0 and not isk:
                    S.add("dve", lambda e: e.scalar_tensor_tensor(out=qT[:, qc, 0:TB], in0=pa[:], scalar=gain, in1=r_[:], op0=ALU.mult, op1=ALU.mult),
                          reads=[par_, r_.r(), vecT.r()], writes=[qT.r(0)])
                else:
                    S.add("dve", lambda e: e.scalar_tensor_tensor(out=q_[:], in0=pa[:], scalar=gain, in1=r_[:], op0=ALU.mult, op1=ALU.mult),
                          reads=[par_, r_.r(), vecT.r()], writes=[q_.r()])
                ust[u] = (q_,)

            def stageC(u):
                wt_i, cc, b = qunits[u]
                qc = wt_i * 4 + cc
                isk = qc >= 6
                (q_,) = ust.pop(u)
                blk = slice(b * TB, (b + 1) * TB)
                if isk:
                    dst, dreg = kT[:, qc - 6, b * TB:(b + 1) * TB], kT.r(b)
                else:
                    dst, dreg = qT[:, qc, blk], qT.r(b)
                if b == 0:
                    if isk:
                        S.add("act", lambda e: e.activation(out=dst, in_=q_[:], func=AF.Copy), reads=[q_.r()], writes=[dreg])
                        pc, pcr = rotC.next()

                        def ftr(e):
                            for tt_ in range(4):
                                ins = e.transpose(pc[:, tt_ * 128:(tt_ + 1) * 128], q_[:, tt_ * 128:(tt_ + 1) * 128], ident)
                            return ins
                        S.add("pe", ftr, reads=[q_.r(), cf32.r()], writes=[pcr])
                        kc = qc - 6
                        S.add("act", lambda e: e.activation(out=knew[:, :, kc * 128:(kc + 1) * 128], in_=pc[:].rearrange("p (t f) -> p t f", t=4), func=AF.Copy),
                              reads=[pcr], writes=[knew.r()])
                else:
                    a_ = t1[u % 2]
                    b_ = t2[u % 2]
                    pc, pcr = rotC.next()
                    mm(pc[:], RT, q_[:], True, True, [q_.r(), cf32.r()], [pcr])
                    rsl = slice((b - 1) * TB, b * TB)
                    S.add("dve", lambda e: e.tensor_tensor(out=a_[:], in0=q_[:], in1=rope[:, 0, rsl], op=ALU.mult), reads=[q_.r(), rope.r()], writes=[a_.r()])
                    S.add("dve", lambda e: e.tensor_tensor(out=b_[:], in0=pc[:], in1=rope[:, 1, rsl], op=ALU.mult), reads=[pcr, rope.r()], writes=[b_.r()])
                    S.add("dve", lambda e: e.tensor_tensor(out=dst, in0=a_[:], in1=b_[:], op=ALU.add), reads=[a_.r(), b_.r()], writes=[dreg])

            NU = len(qunits)
            for step in range(NU + 2):
                if step < NU:
                    stageA(step)
                if 0 <= step - 1 < NU:
                    stageB(step - 1)
                if 0 <= step - 2 < NU:
                    stageC(step - 2)
            dma("sp", o_k.rearrange("(t p) f -> p t f", p=128), knew[:], reads=[knew.r()])
            A.free(*sqn, *rsn, *qn, *t1, *t2)

            if STAGE < 4:
                return
            w, wr = wget()
            rotA = Rot([0, 1, 2, 3])
            for t in range(12):
                b = t // 4
                p, pr = rotA.next()

                def f(e, p=p, t=t, w=w):
                    for k in range(8):
                        ins = e.matmul(p[:], hT[:, k, t * 128:(t + 1) * 128], w[:, k, :], start=(k == 0), stop=(k == 7))
                    return ins
                S.add("pe", f, reads=[wr, hT.r(b)], writes=[pr])
                if not (SUB & 1):
                    S.add("act", lambda e, p=p, t=t: e.activation(out=ftok[:, t, :], in_=p[:, 0:256], func=AF.Copy), reads=[pr], writes=[ftok.r(t)])
                pv = p[:, 256:512].rearrange("p (j d) -> p j d", j=4)
                for par in range(2 if not (SUB & 2) else 0):
                    S.add("dve", lambda e, pv=pv, t=t, par=par: e.tensor_copy(out=vaug[:, t, par::2, par * 64:par * 64 + 64], in_=pv[:, par::2, :]),
                          reads=[pr], writes=[vaug.r(t)])
                if t < 4 and not (SUB & 4):
                    S.add("act", lambda e, p=p, t=t: e.activation(out=vnew[:, t, :], in_=p[:, 256:512], func=AF.Copy), reads=[pr], writes=[vnew.r()])
            dma("sp", o_v.rearrange("(t p) f -> p t f", p=128), vnew[:], reads=[vnew.r()])
            for t in range(4, 12):
                ada_tile(l, t, 4 + t % 4)
            ada_finish(l, 1)

            if STAGE < 5:
                return
            pT = [A.alloc("pT%d" % i, [128, TB], BF16) for i in range(6)]
            rtmp = [A.alloc("rtmp%d" % i, [128, TB], F32) for i in range(2)]
            evb = [A.alloc("evb%d" % i, [128, TB], F32) for i in range(2)]
            rotS = Rot([0, 1, 2, 3, 4, 5])
            rotO = Rot([6, 7])
            units = []
            for sidx in range(2):
                units.append((sidx * 256, 256, 0, [(sidx * 256 + kt * 128, 2 * sidx + kt) for kt in range(2)]))
            skeys = [(512 + kt * 128, 4 + kt) for kt in range(8)] + [(1536 + kt * 128, 12 + kt) for kt in range(4)]
            for qb in range(2):
                units.append((512 + qb * 512, 512, 1 + qb, skeys))
            n = 0
            nh = 0
            ada1_next = [0]
            LAG = 2
            for (q0, N, b, keys) in units:
                nk = len(keys)
                groups = [(qc, ki) for qc in range(6) for ki in range(nk)]
                pts = {}

                def emit_scores(gi, q0=q0, N=N, b=b, keys=keys, groups=groups, pts=pts):
                    nonlocal n
                    qc, ki = groups[gi]
                    kc = qc // 3
                    k0, vt = keys[ki]
                    kreg = kT.r(0 if k0 < 512 else (1 if k0 < 1024 else (2 if k0 < 1536 else 3)))
                    banks = []
                    for hh in range(2):
                        lo, hi = hh * 64, hh * 64 + 64
                        sp_, spr = rotS.next()
                        mm(sp_[:, 0:N], kT[lo:hi, kc, k0:k0 + 128], qT[lo:hi, qc, q0:q0 + N], True, True, [kreg, qT.r(b)], [spr])
                        banks.append((sp_, spr))
                    lst = []
                    for (sp_, spr) in banks:
                        pt = pT[n % 6]
                        n += 1
                        S.add("act", lambda e, pt=pt, sp_=sp_, N=N: e.activation(out=pt[:, 0:N], in_=sp_[:, 0:N], func=AF.Exp, scale=0.125),
                              reads=[spr], writes=[pt.r()])
                        lst.append(pt)
                    pts[gi] = lst

                for gi in range(min(LAG, len(groups))):
                    emit_scores(gi)
                accs = None
                for gi, (qc, ki) in enumerate(groups):
                    if ki == 0:
                        accs = [rotO.next(), rotO.next()]
                    k0, vt = keys[ki]
                    lst = pts.pop(gi)
                    for hh in range(2):
                        j = (qc // 3) * 2 + hh
                        acc, accr = accs[hh]
                        mm(acc[:, 0:N], vaug[:, vt, j, :], lst[hh][:, 0:N], ki == 0, ki == nk - 1, [vaug.r(vt), lst[hh].r()], [accr])
                    if gi + LAG < len(groups):
                        emit_scores(gi + LAG)
                    if NLAYERS > 1 and N == 512 and gi % 12 == 6 and ada1_next[0] < 12:
                        ada_tile(1, ada1_next[0], rotS.next())
                        ada1_next[0] += 1
                    if ki == nk - 1:
                        for hh in range(2):
                            acc, accr = accs[hh]
                            lo, hi = hh * 64, hh * 64 + 64
                            olo, ohi = (1 - hh) * 64, (1 - hh) * 64 + 64
                            rt = rtmp[nh % 2]
                            ev = evb[nh % 2]
                            nh += 1
                            S.add("dve", lambda e, ev=ev, acc=acc, N=N: e.tensor_copy(out=ev[:, 0:N], in_=acc[:, 0:N]), reads=[accr], writes=[ev.r()])
                            S.add("dve", lambda e, rt=rt, ev=ev, N=N, lo=lo, hi=hi, olo=olo, ohi=ohi: e.reciprocal(out=rt[lo:hi, 0:N], in_=ev[olo:ohi, 0:N]),
                                  reads=[ev.r()], writes=[rt.r()])
                            S.add("dve", lambda e, rt=rt, ev=ev, N=N, lo=lo, hi=hi, qc=qc, q0=q0: e.tensor_tensor(
                                out=hT[lo:hi, 2 + qc, q0:q0 + N], in0=ev[lo:hi, 0:N], in1=rt[lo:hi, 0:N], op=ALU.mult),
                                reads=[ev.r(), rt.r()], writes=[hT.r(b)])
            if NLAYERS > 1:
                while ada1_next[0] < 12:
                    ada_tile(1, ada1_next[0], rotS.next())
                    ada1_next[0] += 1
                ada_finish(1, 0)
                ada_finish(1, 1)
            A.free(*pT, *rtmp, *evb, qT, kT, vaug, rope, knew, vnew)

            if STAGE < 6:
                return
            uv = [A.alloc("uv%d" % i, [128, TB], BF16) for i in range(4)]
            rotU = Rot([0, 1, 2, 3])
            rotF = Rot([4, 5, 6, 7])
            c64c = cb16[:, 5, :]
            c64s = cb16[:, 6, :]
            n = 0

            def stageB(u, v, N, cch, q0, b):
                p, pr = rotF.next()
                mm(p[:, 0:N], c64c, u[:, 0:N], True, False, [cb16.r(), u.r()], [pr])
                mm(p[:, 0:N], c64s, v[:, 0:N], False, True, [cb16.r(), v.r()], [pr])
                S.add("act", lambda e: e.activation(out=hT[:, cch, q0:q0 + N], in_=p[:, 0:N], func=AF.Copy), reads=[pr], writes=[hT.r(b)])

            for sidx in range(2):
                for cch in range(2):
                    pair = []
                    for m in range(2):
                        p, pr = rotU.next()

                        def f(e, p=p, m=m, sidx=sidx, cch=cch):
                            for st_ in range(2):
                                ins = e.matmul(p[:, 0:256], ftok[:, 2 * sidx + st_, cch * 128:(cch + 1) * 128], dft256[:, m, st_, :],
                                               start=(st_ == 0), stop=(st_ == 1))
                            return ins
                        S.add("pe", f, reads=[ftok.r(2 * sidx), ftok.r(2 * sidx + 1), dft256.r()], writes=[pr])
                        u = uv[n % 4]
                        n += 1
                        S.add("dve", lambda e, u=u, p=p: e.tensor_copy(out=u[:, 0:256], in_=p[:, 0:256]), reads=[pr], writes=[u.r()])
                        pair.append(u)
                    stageB(pair[0], pair[1], 256, cch, sidx * 256, 0)
            for sb in range(2):
                wc, wcr = wget()
                wsn, wsr = wget()
                for cch in range(2):
                    pair = []
                    for m, (wm, wmr) in enumerate(((wc, wcr), (wsn, wsr))):
                        p, pr = rotU.next()

                        def f(e, p=p, wm=wm, cch=cch):
                            for st_ in range(8):
                                ins = e.matmul(p[:], ftok[:, 4 + st_, cch * 128:(cch + 1) * 128], wm[:, st_, :], start=(st_ == 0), stop=(st_ == 7))
                            return ins
                        S.add("pe", f, reads=ftok.rs(range(4, 12)) + [wmr], writes=[pr])
                        u = uv[n % 4]
                        n += 1
                        S.add("dve", lambda e, u=u, p=p: e.tensor_copy(out=u[:], in_=p[:]), reads=[pr], writes=[u.r()])
                        pair.append(u)
                    stageB(pair[0], pair[1], 512, cch, 512 + sb * 512, 1 + sb)
            A.free(*uv, ftok, dft256)

            if STAGE < 7:
                return
            rotW = Rot([0, 1, 2, 3])
            for t in range(2):
                w, wr = wget()
                for oc in range(4):
                    o = t * 4 + oc
                    for b in range(3):
                        blk = slice(b * TB, (b + 1) * TB)
                        p, pr = rotW.next()

                        def f(e, p=p, w=w, oc=oc, blk=blk):
                            for k in range(8):
                                ins = e.matmul(p[:], w[:, k, oc * 128:(oc + 1) * 128], hT[:, k, blk], start=(k == 0), stop=(k == 7))
                            return ins
                        S.add("pe", f, reads=[wr, hT.r(b)], writes=[pr])
                        resid(p, pr, l, 2, o, b)


        def rev(v):
            (ps_, pn), (st_, n) = v.ap
            return bass.AP(v.tensor, v.offset + (n - 1) * st_, [[ps_, pn], [-st_, n]])

        SEQS = [(0, 256), (256, 512), (512, 1536)]

        def conv4(eng, xp, yp, n, wcol, bcol):
            lo, hi = 1, n - 2
            S.add(eng, lambda e: e.tensor_scalar(out=yp[:, lo:hi], in0=xp[:, lo:hi], scalar1=vecT[:, wcol(1):wcol(1) + 1],
                                                 scalar2=vecT[:, bcol:bcol + 1], op0=ALU.mult, op1=ALU.add),
                  reads=[xp.r(), vecT.r()], writes=[yp.r()])
            for j, sh in ((0, -1), (2, 1), (3, 2)):
                S.add(eng, lambda e, j=j, sh=sh: e.scalar_tensor_tensor(out=yp[:, lo:hi], in0=xp[:, lo + sh:hi + sh], scalar=vecT[:, wcol(j):wcol(j) + 1],
                                                                        in1=yp[:, lo:hi], op0=ALU.mult, op1=ALU.add),
                      reads=[xp.r(), yp.r(), vecT.r()], writes=[yp.r()])

        def odd_layer(l):
            norm_mod(l, 0)
            yl = A.alloc("yl", [128, 4, NTOK], BF16)
            gg = A.alloc("gg", [128, 4, NTOK], BF16)
            xc = A.alloc("xc", [128, 4, NTOK], F32)
            bdw = A.alloc("bdw", [128, 16, 128], BF16)
            fin = A.alloc("fin", [128, 16], F32)
            clT = A.alloc("clT", [128, 8], F32)
            dma("pool", bdw[:], d_lrubd.rearrange("c p n -> p c n"), writes=[bdw.r()])
            S.add("act", lambda e: e.activation(out=clT[:], in_=vecT[:, R_LAM:R_LAM + 8], func=AF.Exp, scale=-1.0), reads=[vecT.r()], writes=[clT.r()])
            S.add("act", lambda e: e.activation(out=clT[:], in_=clT[:], func=AF.Ln, bias=ONE_AP[:, 0:1], scale=1.0), reads=[clT.r(), ONE_AP.r()], writes=[clT.r()])
            S.add("dve", lambda e: e.tensor_scalar(out=clT[:], in0=clT[:], scalar1=-8.0, scalar2=0.0, op0=ALU.mult, op1=ALU.add), reads=[clT.r()], writes=[clT.r()])
            NP = 1544
            POFF = [1, 259, 517]
            xpad = [A.alloc("xpad%d" % i, [128, NP], F32) for i in range(2)]
            ypad = [A.alloc("ypad%d" % i, [128, NP], F32) for i in range(2)]
            for xp in xpad:
                S.add("pool", lambda e, xp=xp: e.memset(xp[:], 0.0), writes=[xp.r()])
            rotA = Rot([0, 1, 2, 3])
            wg, wgr = wget()
            for c in range(4):
                for b in range(3):
                    blk = slice(b * TB, (b + 1) * TB)
                    p, pr = rotA.next()

                    def f(e, p=p, c=c, blk=blk):
                        for k in range(8):
                            ins = e.matmul(p[:], wg[:, k, c * 128:(c + 1) * 128], hT[:, k, blk], start=(k == 0), stop=(k == 7))
                        return ins
                    S.add("pe", f, reads=[wgr, hT.r(b)], writes=[pr])
                    S.add("act", lambda e, p=p, c=c, blk=blk: e.activation(out=gg[:, c, blk], in_=p[:], func=AF.Gelu_apprx_tanh), reads=[pr], writes=[gg.r(c)])
            wx, wxr = wget()
            for c in range(4):
                xp = xpad[c % 2]
                yp = ypad[c % 2]
                for b in range(3):
                    blk = slice(b * TB, (b + 1) * TB)
                    p, pr = rotA.next()

                    def f(e, p=p, c=c, blk=blk):
                        for k in range(8):
                            ins = e.matmul(p[:], wx[:, k, c * 128:(c + 1) * 128], hT[:, k, blk], start=(k == 0), stop=(k == 7))
                        return ins
                    S.add("pe", f, reads=[wxr, hT.r(b)], writes=[pr])
                    if b == 0:
                        ov = xp[:, 1:1 + 2 * 258].rearrange("p (a b) -> p a b", b=258)[:, :, 0:256]
                        S.add("act", lambda e, p=p, ov=ov: e.activation(out=ov, in_=p[:].rearrange("p (a b) -> p a b", a=2), func=AF.Copy), reads=[pr], writes=[xp.r()])
                    else:
                        o0 = 517 + (b - 1) * 512
                        S.add("act", lambda e, p=p, xp=xp, o0=o0: e.activation(out=xp[:, o0:o0 + 512], in_=p[:], func=AF.Copy), reads=[pr], writes=[xp.r()])
                conv4("dve", xp, yp, NP, lambda j, c=c: R_CLW + j * 4 + c, R_CLB + c)
                for si, (s0, s1) in enumerate(SEQS):
                    S.add("act", lambda e, yp=yp, c=c, s0=s0, s1=s1, si=si: e.activation(out=xc[:, c, s0:s1], in_=yp[:, POFF[si]:POFF[si] + s1 - s0], func=AF.Copy),
                          reads=[yp.r()], writes=[xc.r(c)])
            A.free(*xpad, *ypad)
            xcb = A.alloc("xcb", [128, NTOK], BF16)
            ra_ = [A.alloc("lra%d" % i, [128, NTOK], F32) for i in range(2)]
            ig_ = [A.alloc("lig%d" % i, [128, NTOK], F32) for i in range(2)]
            tq_ = [A.alloc("ltq0", [128, NTOK], F32)] * 2
            hd = [A.alloc("lh0", [128, NTOK], F32), ig_[1]]
            rotL = Rot([4, 5, 6, 7])
            xcbs = [xcb, xcb]

            def lruA(u):
                c, d = u // 2, u % 2
                ra, ig = ra_[d], ig_[d]
                xb = xcbs[c % 2]
                if d == 0:
                    S.add("act", lambda e: e.activation(out=xb[:], in_=xc[:, c, :], func=AF.Copy), reads=[xc.r(c)], writes=[xb.r()])
                for b in range(3):
                    blk = slice(b * TB, (b + 1) * TB)
                    for gate, dstT, brow in ((0, ra, R_BA), (1, ig, R_BX)):
                        p, pr = rotL.next()
                        mm(p[:], bdw[:, (gate * 2 + d) * 4 + c, :], xb[:, blk], True, True, [bdw.r(), xb.r()], [pr])
                        col = brow + d * 4 + c
                        S.add("act", lambda e, p=p, dstT=dstT, blk=blk, col=col: e.activation(out=dstT[:, blk], in_=p[:], func=AF.Sigmoid,
                                                                                               bias=vecT[:, col:col + 1], scale=1.0),
                              reads=[pr, vecT.r()], writes=[dstT.r()])

            def lruB(u):
                c, d = u // 2, u % 2
                ra, ig, tq = ra_[d], ig_[d], tq_[d]
                S.add("act", lambda e: e.activation(out=ra[:], in_=ra[:], func=AF.Exp, scale=clT[:, d * 4 + c:d * 4 + c + 1]),
                      reads=[ra.r(), clT.r()], writes=[ra.r()])
                S.add("act", lambda e: e.activation(out=tq[:], in_=ra[:], func=AF.Square), reads=[ra.r()], writes=[tq.r()])
                S.add("dve", lambda e: e.tensor_scalar(out=tq[:], in0=tq[:], scalar1=-1.0, scalar2=1.0, op0=ALU.mult, op1=ALU.add), reads=[tq.r()], writes=[tq.r()])
                S.add("act", lambda e: e.activation(out=tq[:], in_=tq[:], func=AF.Sqrt), reads=[tq.r()], writes=[tq.r()])
                S.add("dve", lambda e: e.tensor_tensor(out=ig[:], in0=ig[:], in1=xc[:, c, :], op=ALU.mult), reads=[ig.r(), xc.r(c)], writes=[ig.r()])
                S.add("dve", lambda e: e.tensor_tensor(out=tq[:], in0=tq[:], in1=ig[:], op=ALU.mult), reads=[tq.r(), ig.r()], writes=[tq.r()])
                h_ = hd[d]
                for si, (s0, s1) in enumerate(SEQS):
                    init = 0.0 if si < 2 else vecT[:, R_LRU0 + d * 4 + c:R_LRU0 + d * 4 + c + 1]
                    if d == 0:
                        S.add("dve", lambda e, s0=s0, s1=s1, init=init: e.tensor_tensor_scan(
                            out=h_[:, s0:s1], data0=ra[:, s0:s1], data1=tq[:, s0:s1], initial=init, op0=ALU.mult, op1=ALU.add),
                            reads=[ra.r(), tq.r(), vecT.r()], writes=[h_.r()])
                    else:
                        S.add("dve", lambda e, s0=s0, s1=s1, init=init: e.tensor_tensor_scan(
                            out=rev(h_[:, s0:s1]), data0=rev(ra[:, s0:s1]), data1=rev(tq[:, s0:s1]), initial=init, op0=ALU.mult, op1=ALU.add),
                            reads=[ra.r(), tq.r(), vecT.r()], writes=[h_.r()])
                    if si < 2:
                        pos = s1 - 1 if d == 0 else s0
                        col = (si * 2 + d) * 4 + c
                        S.add("dve", lambda e, pos=pos, col=col: e.tensor_copy(out=fin[:, col:col + 1], in_=h_[:, pos:pos + 1]),
                              reads=[h_.r()], writes=[fin.r()])
                if d == 1:
                    S.add("dve", lambda e: e.tensor_tensor(out=hd[0][:], in0=hd[0][:], in1=hd[1][:], op=ALU.add), reads=[hd[0].r(), hd[1].r()], writes=[hd[0].r()])
                    S.add("dve", lambda e: e.tensor_tensor(out=yl[:, c, :], in0=hd[0][:], in1=gg[:, c, :], op=ALU.mult),
                          reads=[hd[0].r(), gg.r(c)], writes=[yl.r()])

            lruA(0)
            for u in range(8):
                if u + 1 < 8:
                    lruA(u + 1)
                lruB(u)
            p, pr = PS(4)
            S.add("pe", lambda e: e.transpose(p[0:16, 0:128], fin[:, 0:16], ident), reads=[fin.r(), cf32.r()], writes=[pr])
            fino = A.alloc("fino", [16, 128], F32)
            S.add("dve", lambda e: e.tensor_copy(out=fino[:], in_=p[0:16, 0:128]), reads=[pr], writes=[fino.r()])
            dma("sp", o_lru, fino[:], reads=[fino.r()])
            A.free(gg, xc, bdw, clT, xcb, *ra_, *ig_, tq_[0], hd[0], fin, fino)
            if STAGE < 13:
                for _ in range(12):
                    wget()

            aneg = A.alloc("aneg", [128, 32], F32)
            S.add("act", lambda e: e.activation(out=aneg[:], in_=bv[:, B_ALOG:B_ALOG + 32], func=AF.Exp), reads=[bv.r()], writes=[aneg.r()])
            S.add("dve", lambda e: e.tensor_scalar(out=aneg[:], in0=aneg[:], scalar1=-1.0, scalar2=0.0, op0=ALU.mult, op1=ALU.add), reads=[aneg.r()], writes=[aneg.r()])
            nmk = [cb16[:, 2, :], cb16[:, 3, :]]
            tri = [cb16[:, 7, :], cb16[:, 8, :]]
            def ssd_group(grp):
                tiles = list(range(0, 4)) if grp == 0 else list(range(4, 12))
                nt = len(tiles)
                t0 = tiles[0]
                ntk = nt * 128
                blocks = [0] if grp == 0 else [1, 2]
                seqs = [[0, 1], [2, 3]] if grp == 0 else [list(range(8))]
                if grp == 0:
                    NPg = 520
                    poff = [1, 259]
                    slen = 256
                else:
                    NPg = 1032
                    poff = [1]
                    slen = 1024
                xs_tok = A.alloc("xs_tok", [128, nt, 1024], BF16)
                z_tok = A.alloc("z_tok", [128, nt, 1024], BF16)
                B_tok = A.alloc("B_tok", [128, nt, 128], BF16)
                BT = A.alloc("BT", [128, ntk], BF16)
                CT = A.alloc("CT", [128, ntk], BF16)
                xpad = [A.alloc("sxp%d" % i, [128, NPg], F32) for i in range(3)]
                ypad = [A.alloc("syp%d" % i, [128, NPg], F32) for i in range(3)]
                xsf = [A.alloc("xsf%d" % i, [128, ntk], F32) for i in range(3)]
                for xp in xpad:
                    S.add("pool", lambda e, xp=xp: e.memset(xp[:], 0.0), writes=[xp.r()])
                rotA = Rot([0, 1, 2, 3])
                rotT = Rot([4, 5, 6, 7])
                wc1 = {}

                def c1A(ch):
                    wt_i, cc = ch // 4, ch % 4
                    if wt_i not in wc1:
                        wc1[wt_i] = wget()
                    w, wr = wc1[wt_i]
                    xp = xpad[ch % 3]
                    for b in blocks:
                        blk = slice(b * TB, (b + 1) * TB)
                        p, pr = rotA.next()

                        def f(e, p=p, blk=blk):
                            for k in range(8):
                                ins = e.matmul(p[:], w[:, k, cc * 128:(cc + 1) * 128], hT[:, k, blk], start=(k == 0), stop=(k == 7))
                            return ins
                        S.add("pe", f, reads=[wr, hT.r(b)], writes=[pr])
                        if grp == 0:
                            ov = xp[:, 1:1 + 2 * 258].rearrange("p (a b) -> p a b", b=258)[:, :, 0:256]
                            S.add("act", lambda e, p=p, ov=ov: e.activation(out=ov, in_=p[:].rearrange("p (a b) -> p a b", a=2), func=AF.Copy), reads=[pr], writes=[xp.r()])
                        else:
                            o0 = 1 + (b - 1) * 512
                            S.add("act", lambda e, p=p, o0=o0: e.activation(out=xp[:, o0:o0 + 512], in_=p[:], func=AF.Copy), reads=[pr], writes=[xp.r()])

                def c1B(ch):
                    xp = xpad[ch % 3]
                    yp = ypad[ch % 3]
                    xf = xsf[ch % 3]
                    conv4("dve", xp, yp, NPg, lambda j: R_CSW + j * 10 + ch, R_CSB + ch)
                    for si, po in enumerate(poff):
                        if ch <= 8:
                            dst, dreg = xf[:, si * slen:(si + 1) * slen], xf.r()
                        else:
                            dst, dreg = CT[:, si * slen:(si + 1) * slen], CT.r()
                        S.add("act", lambda e, po=po, dst=dst: e.activation(out=dst, in_=yp[:, po:po + slen], func=AF.Silu), reads=[yp.r()], writes=[dreg])
                    if ch == 8:
                        S.add("act", lambda e: e.activation(out=BT[:], in_=xf[:], func=AF.Copy), reads=[xf.r()], writes=[BT.r()])

                def c1C(ch):
                    xf = xsf[ch % 3]
                    if ch <= 8:
                        for t4 in range(0, nt, 4):
                            p, pr = rotT.next()

                            def ftr(e, p=p, t4=t4):
                                for q in range(4):
                                    ins = e.transpose(p[:, q * 128:(q + 1) * 128], xf[:, (t4 + q) * 128:(t4 + q + 1) * 128], ident)
                                return ins
                            S.add("pe", ftr, reads=[xf.r(), cf32.r()], writes=[pr])
                            pv = p[:].rearrange("p (q n) -> p q n", q=4)
                            if ch < 8:
                                S.add("dve", lambda e, pv=pv, t4=t4: e.tensor_copy(out=xs_tok[:, t4:t4 + 4, ch * 128:(ch + 1) * 128], in_=pv),
                                      reads=[pr], writes=[xs_tok.r()])
                            else:
                                S.add("dve", lambda e, pv=pv, t4=t4: e.tensor_copy(out=B_tok[:, t4:t4 + 4, :], in_=pv), reads=[pr], writes=[B_tok.r()])

                for step in range(12):
                    if step < 10:
                        c1A(step)
                    if 0 <= step - 1 < 10:
                        c1B(step - 1)
                    if 0 <= step - 2 < 10:
                        c1C(step - 2)
                A.free(*xpad, *ypad, *xsf)
                if SUB == 1:
                    return
                sm = lambda name, w=32: A.alloc(name, [128, nt, w], F32)
                dtr = sm("dtr")
                dta = sm("dta")
                aal = sm("aal")
                stats = sm("stats", 64)
                nacs = sm("nacs")
                eacs = sm("eacs")
                wdec = sm("wdec")
                cdec = sm("cdec")
                cdsel = A.alloc("cdsel", [128, nt, 2, 8], F32)
                wz0, wz0r = wget()
                wz1, wz1r = wget()
                for ti in range(nt):
                    tg = t0 + ti
                    b = tg // 4
                    for hf, (wz, wzr) in enumerate(((wz0, wz0r), (wz1, wz1r))):
                        p, pr = rotA.next()

                        def f(e, p=p, wz=wz, tg=tg):
                            for k in range(8):
                                ins = e.matmul(p[:], hT[:, k, tg * 128:(tg + 1) * 128], wz[:, k, :], start=(k == 0), stop=(k == 7))
                            return ins
                        S.add("pe", f, reads=[wzr, hT.r(b)], writes=[pr])
                        S.add("act", lambda e, p=p, ti=ti, hf=hf: e.activation(out=z_tok[:, ti, hf * 512:(hf + 1) * 512], in_=p[:], func=AF.Silu),
                              reads=[pr], writes=[z_tok.r()])
                wd, wdr = wget()
                for ti in range(nt):
                    tg = t0 + ti
                    b = tg // 4
                    p, pr = rotA.next()

                    def f(e, p=p, tg=tg):
                        for k in range(8):
                            ins = e.matmul(p[:, 0:32], hT[:, k, tg * 128:(tg + 1) * 128], wd[:, k, :], start=(k == 0), stop=(k == 7))
                        return ins
                    S.add("pe", f, reads=[wdr, hT.r(b)], writes=[pr])
                    S.add("dve", lambda e, p=p, ti=ti: e.tensor_tensor(out=dtr[:, ti, :], in0=p[:, 0:32], in1=bv[:, B_DTB:B_DTB + 32], op=ALU.add),
                          reads=[pr, bv.r()], writes=[dtr.r()])
                S.add("act", lambda e: e.activation(out=dtr[:], in_=dtr[:], func=AF.Exp), reads=[dtr.r()], writes=[dtr.r()])
                S.add("act", lambda e: e.activation(out=dta[:], in_=dtr[:], func=AF.Ln, bias=ONE_AP[:, 0:1], scale=1.0), reads=[dtr.r(), ONE_AP.r()], writes=[dta.r()])
                S.add("dve", lambda e: e.tensor_tensor(out=aal[:], in0=dta[:], in1=aneg[:].unsqueeze(1).to_broadcast([128, nt, 32]), op=ALU.mult),
                      reads=[dta.r(), aneg.r()], writes=[aal.r()])
                ahi = A.alloc("ahi", [128, nt, 32], BF16)
                alo = A.alloc("alo", [128, nt, 32], BF16)
                S.add("act", lambda e: e.activation(out=ahi[:], in_=aal[:], func=AF.Copy), reads=[aal.r()], writes=[ahi.r()])
                S.add("dve", lambda e: e.tensor_tensor(out=alo[:], in0=aal[:], in1=ahi[:], op=ALU.subtract), reads=[aal.r(), ahi.r()], writes=[alo.r()])
                for ti in range(nt):
                    p, pr = rotT.next()
                    mm(p[:, 0:16], utri, aal[:, ti, 0:16], True, True, [cf32.r(), aal.r()], [pr])
                    mm(p[:, 16:32], ltri, aal[:, ti, 16:32], True, True, [cf32.r(), aal.r()], [pr])
                    mm(p[:, 32:64], onesf, aal[:, ti, :], True, True, [cf32.r(), aal.r()], [pr])
                    S.add("dve", lambda e, p=p, ti=ti: e.tensor_copy(out=stats[:, ti, :], in_=p[:, 0:64]), reads=[pr], writes=[stats.r()])
                S.add("act", lambda e: e.activation(out=nacs[:], in_=dta[:], func=AF.Ln), reads=[dta.r()], writes=[nacs.r()])
                S.add("dve", lambda e: e.tensor_tensor(out=nacs[:], in0=nacs[:], in1=stats[:, :, 0:32], op=ALU.subtract),
                      reads=[stats.r(), nacs.r()], writes=[nacs.r()])
                S.add("act", lambda e: e.activation(out=eacs[:], in_=stats[:, :, 0:32], func=AF.Exp), reads=[stats.r()], writes=[eacs.r()])
                S.add("dve", lambda e: e.tensor_tensor(out=wdec[:], in0=stats[:, :, 32:64], in1=stats[:, :, 0:32], op=ALU.subtract), reads=[stats.r()], writes=[wdec.r()])
                S.add("act", lambda e: e.activation(out=wdec[:], in_=wdec[:], func=AF.Exp), reads=[wdec.r()], writes=[wdec.r()])
                S.add("dve", lambda e: e.tensor_tensor(out=wdec[:], in0=wdec[:], in1=dta[:], op=ALU.mult), reads=[wdec.r(), dta.r()], writes=[wdec.r()])
                S.add("act", lambda e: e.activation(out=cdec[:], in_=stats[:, :, 32:64], func=AF.Exp), reads=[stats.r()], writes=[cdec.r()])
                for g in range(2):
                    src = cdec[g * 64:(g + 1) * 64, :, :].rearrange("p t (d g h) -> p t d g h", d=2, g=2)[:, :, :, g, :]
                    S.add("dve", lambda e, g=g, src=src: e.tensor_copy(out=cdsel[g * 64:(g + 1) * 64, :, :, :], in_=src), reads=[cdec.r()], writes=[cdsel.r()])
                A.free(dtr, cdec, stats, aal)
                if SUB == 2:
                    return
                Sst = [A.alloc("Sst%d" % d, [128, 512], F32) for d in range(2)]
                hinb = A.alloc("hinb", [128, nt, 512], BF16)
                hinf = A.alloc("hinf", [128, 512], BF16)
                xw = [A.alloc("xw0", [128, 1024], BF16)] * 2
                stmp = A.alloc("stmp", [128, 512], F32)

                class _Stg:
                    def __getitem__(self, k):
                        return stmp[:].rearrange("p (a n) -> p a n", a=8)[k]

                    def r(self):
                        return stmp.r()
                stg = _Stg()
                rotP = Rot([5, 6])
                nxw = [0]

                def init_state(d):
                    St = Sst[d]
                    if grp == 0:
                        S.add("pool", lambda e: e.memset(St[:], 0.0), writes=[St.r()])
                    else:
                        dma("sp", stg[:], d_sssd[d].rearrange("(a p) n -> p a n", p=128), writes=[stg.r()])
                        for g in range(2):
                            p, pr = rotP.next()

                            def f(e, p=p, g=g):
                                for a4 in range(4):
                                    ins = e.transpose(p[0:64, a4 * 128:(a4 + 1) * 128], stg[:, g * 4 + a4, :], ident)
                                return ins
                            S.add("pe", f, reads=[stg.r(), cf32.r()], writes=[pr])
                            S.add("dve", lambda e, p=p, g=g: e.tensor_copy(out=St[g * 64:(g + 1) * 64, :], in_=p[0:64, :]), reads=[pr], writes=[St.r()])

                def out_state(d, sidx):
                    St = Sst[d]
                    for g in range(2):
                        p, pr = rotP.next()

                        def f(e, p=p, g=g):
                            for s4 in range(4):
                                ins = e.transpose(p[:, s4 * 64:(s4 + 1) * 64], St[g * 64:(g + 1) * 64, s4 * 128:(s4 + 1) * 128], ident[g * 64:(g + 1) * 64, g * 64:(g + 1) * 64])
                            return ins
                        S.add("pe", f, reads=[St.r(), cf32.r()], writes=[pr])
                        S.add("dve", lambda e, p=p, g=g: e.tensor_copy(out=stg[:, g * 4:(g + 1) * 4, :], in_=p[:, 0:256].rearrange("p (a n) -> p a n", a=4)), reads=[pr], writes=[stg.r()])
                    dma("sp", o_ssd[sidx, d].rearrange("(a p) n -> p a n", p=128), stg[:], reads=[stg.r()])

                def state_update(d, ti):
                    St = Sst[d]
                    x_ = xw[nxw[0] % 2]
                    nxw[0] += 1
                    S.add("dve", lambda e, x_=x_, ti=ti, d=d: e.tensor_tensor(
                        out=x_[:].rearrange("p (h q) -> p h q", h=16), in0=xs_tok[:, ti, :].rearrange("p (h q) -> p h q", h=16),
                        in1=wdec[:, ti, d * 16:(d + 1) * 16].unsqueeze(2).to_broadcast([128, 16, 64]), op=ALU.mult),
                        reads=[xs_tok.r(), wdec.r()], writes=[x_.r()])
                    p, pr = rotP.next()
                    for g in range(2):
                        mm(p[g * 64:(g + 1) * 64, :], B_tok[:, ti, g * 64:(g + 1) * 64], x_[:, g * 512:(g + 1) * 512], True, True, [B_tok.r(), x_.r()], [pr])
                    S.add("dve", lambda e, ti=ti, d=d: e.tensor_tensor(
                        out=stmp[:].rearrange("p (h q) -> p h q", h=8), in0=St[:].rearrange("p (h q) -> p h q", h=8),
                        in1=cdsel[:, ti, d, :].unsqueeze(2).to_broadcast([128, 8, 64]), op=ALU.mult),
                        reads=[St.r(), cdsel.r()], writes=[stmp.r()])
                    S.add("dve", lambda e, p=p: e.tensor_tensor(out=St[:], in0=p[:], in1=stmp[:], op=ALU.add), reads=[pr, stmp.r()], writes=[St.r()])

                for sidx, sq_ in enumerate(seqs):
                    init_state(1)
                    for ti in reversed(sq_):
                        S.add("act", lambda e, ti=ti: e.activation(out=hinb[:, ti, :], in_=Sst[1][:], func=AF.Copy), reads=[Sst[1].r()], writes=[hinb.r()])
                        if SUB != 5:
                            state_update(1, ti)
                    if grp == 0 and SUB != 4:
                        out_state(1, sidx)
                if SUB in (3, 4, 5):
                    return
                cbs = [A.alloc("cbs%d" % i, [128, 2, 128], F32) for i in range(2)]
                decb = [A.alloc("decb%d" % i, [128, 2, 128], F32) for i in range(3)]
                MTb = [A.alloc("MTb%d" % i, [128, 2, 128], BF16) for i in range(3)]
                yaccs = [A.alloc("yacc%d" % i, [128, 1024], F32) for i in range(2)]
                ytmp = A.alloc("ytmp", [128, 1024], F32)
                ssq = A.alloc("ssq", [128, 2], F32)
                rotD = Rot([1, 2, 3, 4])
                nm_ = 0
                tiles_seq = []
                for sidx, sq_ in enumerate(seqs):
                    for i_, ti in enumerate(sq_):
                        tiles_seq.append((sidx, ti, i_ == 0, i_ == len(sq_) - 1))

                def make_tail(sidx, ti, first, last, yacc):
                    tg = t0 + ti
                    tcol = slice(ti * 128, (ti + 1) * 128)
                    th = []
                    if first:
                        th.append(lambda: init_state(0))
                    th.append(lambda: S.add("act", lambda e: e.activation(out=hinf[:], in_=Sst[0][:], func=AF.Copy), reads=[Sst[0].r()], writes=[hinf.r()]))
                    for d in range(2):
                        for g in range(2):
                            def yo(d=d, g=g):
                                hin = hinf[:] if d == 0 else hinb[:, ti, :]
                                hreg = hinf.r() if d == 0 else hinb.r()
                                p, pr = rotP.next()
                                mm(p[:], CT[g * 64:(g + 1) * 64, tcol], hin[g * 64:(g + 1) * 64, :], True, True, [CT.r(), hreg], [pr])
                                ea = eacs[:, ti, d * 16 + g * 8:d * 16 + g * 8 + 8].unsqueeze(2).to_broadcast([128, 8, 64])
                                pv = p[:].rearrange("p (h q) -> p h q", h=8)
                                tsl = ytmp[:, g * 512:(g + 1) * 512].rearrange("p (h q) -> p h q", h=8)
                                S.add("dve", lambda e: e.tensor_tensor(out=tsl, in0=pv, in1=ea, op=ALU.mult), reads=[pr, eacs.r()], writes=[ytmp.r()])
                                S.add("dve", lambda e: e.tensor_tensor(out=yacc[:, g * 512:(g + 1) * 512], in0=yacc[:, g * 512:(g + 1) * 512],
                                                                       in1=ytmp[:, g * 512:(g + 1) * 512], op=ALU.add),
                                      reads=[yacc.r(), ytmp.r()], writes=[yacc.r()])
                            th.append(yo)
                    th.append(lambda: S.add("dve", lambda e: e.tensor_tensor(
                        out=ytmp[:].rearrange("p (h q) -> p h q", h=16), in0=xs_tok[:, ti, :].rearrange("p (h q) -> p h q", h=16),
                        in1=bv[:, B_D:B_D + 16].unsqueeze(2).to_broadcast([128, 16, 64]), op=ALU.mult),
                        reads=[xs_tok.r(), bv.r(), yacc.r()], writes=[ytmp.r()]))
                    th.append(lambda: S.add("dve", lambda e: e.tensor_tensor(out=yacc[:], in0=yacc[:], in1=ytmp[:], op=ALU.add), reads=[yacc.r(), ytmp.r()], writes=[yacc.r()]))
                    th.append(lambda: S.add("dve", lambda e: e.tensor_tensor(out=yacc[:], in0=yacc[:], in1=z_tok[:, ti, :], op=ALU.mult), reads=[yacc.r(), z_tok.r()], writes=[yacc.r()]))
                    th.append(lambda: S.add("act", lambda e: e.activation(out=ytmp[:], in_=yacc[:], func=AF.Square, accum_out=ssq[:, 0:1]), reads=[yacc.r()], writes=[ytmp.r(), ssq.r()]))
                    th.append(lambda: S.add("act", lambda e: e.activation(out=ssq[:, 0:1], in_=ssq[:, 0:1], func=AF.Ln, bias=EPS_AP[:, 0:1], scale=1.0 / 1024), reads=[ssq.r(), EPS_AP.r()], writes=[ssq.r()]))
                    th.append(lambda: S.add("act", lambda e: e.activation(out=ssq[:, 0:1], in_=ssq[:, 0:1], func=AF.Exp, scale=-0.5), reads=[ssq.r()], writes=[ssq.r()]))
                    th.append(lambda: S.add("dve", lambda e: e.scalar_tensor_tensor(out=ytmp[:], in0=yacc[:], scalar=ssq[:, 0:1], in1=bv[:, B_NORM:B_NORM + 1024], op0=ALU.mult, op1=ALU.mult),
                                            reads=[yacc.r(), ssq.r(), bv.r()], writes=[ytmp.r()]))
                    for hf in range(2):
                        def trp(hf=hf):
                            p, pr = PS(7) if hf == 0 else rotP.next()

                            def ftr(e):
                                for c in range(4):
                                    ins = e.transpose(p[:, c * 128:(c + 1) * 128], ytmp[:, (hf * 4 + c) * 128:(hf * 4 + c + 1) * 128], ident)
                                return ins
                            S.add("pe", ftr, reads=[ytmp.r(), cf32.r()], writes=[pr])
                            outv = hT[:, hf * 4:hf * 4 + 4, tg * 128:(tg + 1) * 128]
                            S.add("act", lambda e: e.activation(out=outv, in_=p[:].rearrange("p (c n) -> p c n", c=4), func=AF.Copy),
                                  reads=[pr], writes=[hT.r(tg // 4)])
                        th.append(trp)
                    th.append(lambda: state_update(0, ti))
                    if last and grp == 0:
                        th.append(lambda: out_state(0, sidx))
                    return th

                pending = []
                for tix, (sidx, ti, first, last) in enumerate(tiles_seq):
                    yacc = yaccs[tix % 2]
                    tcol = slice(ti * 128, (ti + 1) * 128)
                    cb_ = cbs[ti % 2]
                    for g in range(2):
                        p, pr = PS(7) if g == 0 else rotP.next()
                        mm(p[:, 0:128], BT[g * 64:(g + 1) * 64, tcol], CT[g * 64:(g + 1) * 64, tcol], True, True, [BT.r(), CT.r()], [pr])
                        S.add("act", lambda e, p=p, cb_=cb_, g=g: e.activation(out=cb_[:, g, :], in_=p[:, 0:128], func=AF.Copy), reads=[pr], writes=[cb_.r()])
                    ydp, ydr = PS(0)
                    munits = [(hp, d) for hp in range(8) for d in range(2)]
                    mts = {}

                    def emit_pdc(u, ti=ti, cb_=cb_, mts=mts, munits=munits):
                        nonlocal nm_
                        hp, d = munits[u]
                        g = hp // 4
                        pd, pdr = rotD.next()

                        def f(e, pd=pd, ti=ti, hp=hp, d=d):
                            for i_ in range(2):
                                hc = d * 16 + 2 * hp + i_
                                o_ = pd[:, i_ * 128:(i_ + 1) * 128]
                                e.matmul(o_, ahi[:, ti, hc:hc + 1].to_broadcast([128, 128]), tri[d], start=True, stop=False)
                                e.matmul(o_, alo[:, ti, hc:hc + 1].to_broadcast([128, 128]), tri[d], start=False, stop=False)
                                ins = e.matmul(o_, nmk[d], identb, start=False, stop=True)
                            return ins
                        S.add("pe", f, reads=[ahi.r(), alo.r(), cb16.r()], writes=[pdr])
                        dc = decb[nm_ % 3]
                        mt = MTb[nm_ % 3]
                        nm_ += 1
                        for i_ in range(2):
                            hc = d * 16 + 2 * hp + i_
                            S.add("act", lambda e, pd=pd, dc=dc, ti=ti, hc=hc, i_=i_: e.activation(out=dc[:, i_, :], in_=pd[:, i_ * 128:(i_ + 1) * 128], func=AF.Exp,
                                                                                                     bias=nacs[:, ti, hc:hc + 1], scale=1.0),
                                  reads=[pdr, nacs.r()], writes=[dc.r()])
                        S.add("pool", lambda e, dc=dc, mt=mt, g=g, cb_=cb_: e.tensor_tensor(
                            out=mt[:], in0=dc[:], in1=cb_[:, g, :].unsqueeze(1).to_broadcast([128, 2, 128]), op=ALU.mult),
                            reads=[dc.r(), cb_.r()], writes=[mt.r()])
                        mts[u] = mt

                    LAm = 3
                    for u in range(LAm):
                        emit_pdc(u)
                    for u, (hp, d) in enumerate(munits):
                        mt = mts.pop(u)

                        def fy(e, mt=mt, hp=hp, d=d, u=u, ti=ti):
                            for i_ in range(2):
                                h = 2 * hp + i_
                                ins = e.matmul(ydp[:, (h % 8) * 64:(h % 8 + 1) * 64], mt[:, i_, :], xs_tok[:, ti, h * 64:(h + 1) * 64],
                                               start=(u % 8 == 0 and i_ == 0), stop=(d == 1), skip_group_check=True)
                            return ins
                        S.add("pe", fy, reads=[mt.r(), xs_tok.r()], writes=[ydr])
                        if u + LAm < len(munits):
                            emit_pdc(u + LAm)
                        for _ in range(2):
                            if pending:
                                pending.pop(0)()
                        if u % 8 == 7:
                            g = u // 8
                            S.add("act", lambda e, g=g, yacc=yacc: e.activation(out=yacc[:, g * 512:(g + 1) * 512], in_=ydp[:], func=AF.Copy), reads=[ydr], writes=[yacc.r()])
                        if u == 15:
                            while pending:
                                pending.pop(0)()
                    pending = pending + make_tail(sidx, ti, first, last, yacc)
                while pending:
                    pending.pop(0)()
                A.free(xs_tok, z_tok, B_tok, BT, CT, dta, ahi, alo, nacs, eacs, wdec, cdsel, *Sst, hinb, hinf, xw[0], stmp,
                       *cbs, *decb, *MTb, *yaccs, ytmp, ssq)
            for grp in range((2 if SUB == 0 else 1) if STAGE >= 13 else 0):
                ssd_group(grp if SUB < 10 else 1)
            A.free(aneg)
            while wstate["got"] < wstate["woo"]:
                wget()
            rotW = Rot([0, 1, 2, 3])
            for t in range(4):
                w, wr = wget()
                for oc in range(2):
                    o = t * 2 + oc
                    for b in range(3):
                        blk = slice(b * TB, (b + 1) * TB)
                        p, pr = rotW.next()

                        def f(e, p=p, w=w, oc=oc, blk=blk):
                            for k in range(12):
                                rhs = yl[:, k, blk] if k < 4 else hT[:, k - 4, blk]
                                ins = e.matmul(p[:], w[:, k, oc * 128:(oc + 1) * 128], rhs, start=(k == 0), stop=(k == 11))
                            return ins
                        S.add("pe", f, reads=[wr, hT.r(b), yl.r()], writes=[pr])
                        resid(p, pr, l, 2, o, b)
            A.free(yl)

        if STAGE >= 1:
            for t in range(4):
                ada_tile(0, t, 0)
            ada_finish(0, 0)
        if STAGE >= 2:
            even_layer(0)
        if STAGE >= 10:
            ffn(0)
        if NLAYERS > 1:
            odd_layer(1)
            if STAGE >= 20:
                ffn(1)

        yst = [A.alloc("yst%d" % i, [128, D], F32) for i in range(2)]
        rot = Rot([0, 1, 2, 3, 4, 5, 6, 7])
        for t in range(12):
            ys = yst[t % 2]
            for hf in range(2):
                p, pr = rot.next()

                def tr(e, p=p, t=t, hf=hf):
                    for c in range(4):
                        ins = e.transpose(p[:, c * 128:(c + 1) * 128], xT[:, hf * 4 + c, t * 128:(t + 1) * 128], ident)
                    return ins
                S.add("pe", tr, reads=[xT.r(t), cf32.r()], writes=[pr])
                if hf == 0:
                    S.add("act", lambda e, ys=ys, p=p: e.activation(out=ys[:, 0:512], in_=p[:], func=AF.Copy), reads=[pr], writes=[ys.r()])
                else:
                    S.add("dve", lambda e, ys=ys, p=p: e.tensor_copy(out=ys[:, 512:1024], in_=p[:]), reads=[pr], writes=[ys.r()])
            dma("sp", o_y[t * 128:(t + 1) * 128, :], ys[:], reads=[ys.r()])
        assert STAGE < 99 or wstate["got"] == len(wplan), (wstate, len(wplan))
        S.emit(st)
    return nc


_CACHE = {}


def _prep_shared(inp):
    f = np.float32
    sh = {}
    sh["w_ada"] = np.ascontiguousarray(inp["w_ada"], f)
    wie = np.asarray(inp["w_in_even"][0], f)
    fcols = wie[:, 0:256]
    q = wie[:, 256:1024].reshape(D, 12, 64)[:, PERM, :].reshape(D, 768)
    k = wie[:, 1024:1280]
    v = wie[:, 1280:1536]
    sh["w_in_even"] = np.ascontiguousarray(np.concatenate([q, k, fcols, v], 1))
    woe = np.asarray(inp["w_out_even"][0], f)
    att = woe[256:].reshape(12, 64, D)[PERM].reshape(768, D)
    sh["w_out_even"] = np.ascontiguousarray(np.concatenate([woe[:256], att], 0))
    sh["w_in_odd"] = np.ascontiguousarray(inp["w_in_odd"][0], f)
    sh["w_out_odd"] = np.ascontiguousarray(inp["w_out_odd"][0], f)
    sh["ffn_w1"] = np.ascontiguousarray(inp["ffn_w1"], f)
    sh["ffn_w3"] = np.ascontiguousarray(inp["ffn_w3"], f)
    sh["ffn_w2"] = np.ascontiguousarray(inp["ffn_w2"], f)
    bd = np.zeros((2, 2, 4, 128, 128), f)
    for gi, wname in enumerate(("lru_wa", "lru_wx")):
        wsrc = np.asarray(inp[wname][0], f)
        for d in range(2):
            for blk in range(8):
                c, hlf = blk // 2, blk % 2
                bd[gi, d, c, hlf * 64:(hlf + 1) * 64, hlf * 64:(hlf + 1) * 64] = wsrc[d, blk]
    sh["lru_bd"] = bd.reshape(16, 128, 128)
    bvec = np.zeros((1, NBV), f)
    bvec[0, B_NORM:B_NORM + 1024] = inp["ssd_norm"][0]
    bvec[0, B_D:B_D + 16] = inp["ssd_d"][0]
    bvec[0, B_DTB:B_DTB + 32] = np.asarray(inp["ssd_dt_bias"][0]).reshape(32)
    bvec[0, B_ALOG:B_ALOG + 32] = np.asarray(inp["ssd_a_log"][0]).reshape(32)
    sh["bvec"] = bvec
    sh.update(_consts())
    return sh


def _vecs(inp, i):
    f = np.float32
    v = np.zeros((256, 128), f)
    v[R_BADA:R_BADA + 96] = np.asarray(inp["b_ada"], f).reshape(96, 128)
    v[R_CCTX:R_CCTX + 8] = np.asarray(inp["c_ctx"], f).reshape(8, 128)
    v[R_CI:R_CI + 8] = np.asarray(inp["c"][i], f).reshape(8, 128)
    v[R_NMIX:R_NMIX + 16] = np.asarray(inp["norm_mix"], f).reshape(16, 128)
    v[R_NFFN:R_NFFN + 16] = np.asarray(inp["norm_ffn"], f).reshape(16, 128)
    v[R_CLW:R_CLW + 16] = np.asarray(inp["conv_lru_w"][0], f).reshape(16, 128)
    v[R_CLB:R_CLB + 4] = np.asarray(inp["conv_lru_b"][0], f).reshape(4, 128)
    v[R_BA:R_BA + 8] = np.asarray(inp["lru_ba"][0], f).reshape(8, 128)
    v[R_BX:R_BX + 8] = np.asarray(inp["lru_bx"][0], f).reshape(8, 128)
    v[R_LAM:R_LAM + 8] = np.asarray(inp["lru_lambda"][0], f).reshape(8, 128)
    v[R_CSW:R_CSW + 40] = np.asarray(inp["conv_ssd_w"][0], f).reshape(40, 128)
    v[R_CSB:R_CSB + 10] = np.asarray(inp["conv_ssd_b"][0], f).reshape(10, 128)
    v[R_QN] = np.tile(np.asarray(inp["q_norm"][0], f), 2)
    v[R_KN] = np.tile(np.asarray(inp["k_norm"][0], f), 2)
    v[R_LRU0:R_LRU0 + 8] = np.asarray(inp["state_lru"][i, 0], f).reshape(8, 128)
    return v


def kernel(**inp):
    inp = {k: np.asarray(v) for k, v in inp.items()}
    if "nc" not in _CACHE:
        _CACHE["nc"] = build_program()
    nc = _CACHE["nc"]
    sh = _prep_shared(inp)
    in_maps = []
    for i in range(NCORES):
        m = dict(sh)
        m["x"] = np.ascontiguousarray(np.concatenate(
            [inp["x_prompt"][2 * i], inp["x_prompt"][2 * i + 1], inp["x_sample"][i]], 0), np.float32)
        m["vecs"] = _vecs(inp, i)
        m["cache_k"] = np.ascontiguousarray(inp["cache_k"][i, 0].reshape(512, 256), np.float32)
        m["cache_v"] = np.ascontiguousarray(inp["cache_v"][i, 0].reshape(512, 256), np.float32)
        m["state_ssd"] = np.ascontiguousarray(inp["state_ssd"][i, 0].reshape(2, 1024, 64), np.float32)
        in_maps.append(m)
    res = run_bass_kernel_spmd(nc, in_maps[:NRUN], core_ids=list(range(NRUN)))
    R = res.results
    y_prompt = np.zeros((16, 256, D), np.float32)
    y_sample = np.zeros((8, 1024, D), np.float32)
    new_k = np.zeros((16, 1, 256, 4, 64), np.float32)
    new_v = np.zeros((16, 1, 256, 4, 64), np.float32)
    new_lru = np.zeros((16, 1, 2, 512), np.float32)
    new_ssd = np.zeros((16, 1, 2, 16, 64, 64), np.float32)
    for i in range(NRUN):
        y = np.asarray(R[i]["y"])
        y_prompt[2 * i] = y[0:256]
        y_prompt[2 * i + 1] = y[256:512]
        y_sample[i] = y[512:]
        nk = np.asarray(R[i]["newk"]).reshape(2, 256, 4, 64)
        nv = np.asarray(R[i]["newv"]).reshape(2, 256, 4, 64)
        new_k[2 * i:2 * i + 2, 0] = nk
        new_v[2 * i:2 * i + 2, 0] = nv
        new_lru[2 * i:2 * i + 2, 0] = np.asarray(R[i]["newlru"]).reshape(2, 2, 512)
        new_ssd[2 * i:2 * i + 2, 0] = np.asarray(R[i]["newssd"]).reshape(2, 2, 16, 64, 64)
    return (y_prompt, y_sample, new_k, new_v, new_lru, new_ssd)
```

```python
import numpy as np
from contextlib import ExitStack
import concourse.bass as bass
import concourse.mybir as mybir
from concourse.bass_utils import run_bass_kernel_spmd

F32 = mybir.dt.float32
BF16 = mybir.dt.bfloat16
AF = mybir.ActivationFunctionType
ALU = mybir.AluOpType

ENGS = ("pe", "act", "dve", "pool", "sp")
NDMASEM = 6
NCORES = 8
NRUN = 8
EPS = 1e-6
NLAYERS = 2
STAGE = 99
SUB = 0


class Reg:
    __slots__ = ("name", "lw", "rd", "excl")

    def __init__(self, name, inherit=(), excl=False):
        self.name = name
        self.lw = None
        self.rd = list(inherit)
        self.excl = excl


class Op:
    __slots__ = ("eng", "fn", "deps", "needed", "dma", "sem", "val", "gidx", "thr")

    def __init__(self, eng, fn, dma):
        self.eng = eng
        self.fn = fn
        self.dma = dma
        self.deps = []
        self.needed = False
        self.sem = None
        self.val = 0
        self.thr = None


class Sched:
    def __init__(self, nc):
        self.nc = nc
        self.ops = {e: [] for e in ENGS}
        self.all = []
        self.ndma = {e: 0 for e in ENGS}
        self.dmaops = {e: [] for e in ENGS}

    def add(self, eng, fn, reads=(), writes=(), dma=False):
        op = Op(eng, fn, dma)
        deps = {}
        for r in reads:
            w = r.lw
            if w is not None and (w.dma or dma or w.eng != eng or eng != "pe"):
                deps[id(w)] = w
            if r.excl:
                for q in r.rd:
                    if q.eng != eng:
                        deps[id(q)] = q
        for t in writes:
            w = t.lw
            if w is not None and (w.dma or dma or w.eng != eng or eng != "pe"):
                deps[id(w)] = w
            for q in t.rd:
                if q.dma or dma or q.eng != eng or eng != "pe":
                    deps[id(q)] = q
        op.deps = list(deps.values())
        for r in reads:
            if not dma:
                r.rd = [q for q in r.rd if q.dma or q.eng != eng]
            r.rd.append(op)
        for t in writes:
            t.lw = op
            t.rd = []
        if dma:
            i = self.ndma[eng]
            self.ndma[eng] += 1
            if i >= NDMASEM:
                op.thr = self.dmaops[eng][i - NDMASEM]
            self.dmaops[eng].append(op)
        op.gidx = len(self.all)
        self.all.append(op)
        self.ops[eng].append(op)
        return op

    def emit(self, stack):
        nc = self.nc
        for op in self.all:
            for d in op.deps:
                d.needed = True
        esem = {e: stack.enter_context(nc.semaphore("s_" + e)) for e in ENGS if e != "sp"}
        dsem = {e: [stack.enter_context(nc.semaphore("d_%s%d" % (e, i))) for i in range(NDMASEM)]
                for e in ENGS if self.ndma[e] > 0}
        for e in ENGS:
            cnt = 0
            i = 0
            for op in self.ops[e]:
                if op.dma:
                    op.sem = dsem[e][i % NDMASEM]
                    op.val = 16 * (i // NDMASEM + 1)
                    i += 1
                elif op.needed:
                    cnt += 1
                    op.sem = esem[e]
                    op.val = cnt
        block = stack.enter_context(nc.Block())
        engh = {"pe": block.tensor, "act": block.scalar, "dve": block.vector,
                "pool": block.gpsimd, "sp": block.sync}
        for e in ENGS:
            ops = self.ops[e]
            if not ops:
                continue

            def body(eng, ops=ops, e=e):
                waited = {}
                for op in ops:
                    ds = list(op.deps)
                    if op.thr is not None:
                        ds.append(op.thr)
                    for d in ds:
                        k = id(d.sem)
                        if waited.get(k, 0) >= d.val:
                            continue
                        waited[k] = d.val
                        eng.wait_ge(d.sem, d.val)
                    ins = op.fn(eng)
                    if op.dma:
                        ins.then_inc(op.sem, 16)
                    elif op.needed:
                        ins.then_inc(op.sem, 1)
                for d in self.dmaops[e][-NDMASEM:]:
                    if waited.get(id(d.sem), 0) < d.val:
                        waited[id(d.sem)] = d.val
                        eng.wait_ge(d.sem, d.val)

            engh[e](body)


def _prune(ops):
    best = {}
    out = []
    for o in ops:
        if o.dma:
            out.append(o)
        else:
            b = best.get(o.eng)
            if b is None or o.gidx > b.gidx:
                best[o.eng] = o
    return out + list(best.values())


class TV:
    def __init__(self, ap, name, off, nw, inherit):
        self.ap = ap
        self.name = name
        self.off = off
        self.nw = nw
        self.inherit = inherit
        self.regs = {}

    def __getitem__(self, k):
        return self.ap[k]

    def r(self, key=0):
        g = self.regs.get(key)
        if g is None:
            g = Reg("%s.%s" % (self.name, key), self.inherit)
            self.regs[key] = g
        return g

    def rs(self, keys):
        return [self.r(k) for k in keys]


class Arena:
    def __init__(self, nc, stack, nwords):
        self.t = stack.enter_context(nc.sbuf_tensor("arena", [128, nwords], F32))
        self.n = nwords
        self.live = []
        self.dead = []
        self.peak = 0

    def alloc(self, name, shape, dt):
        n = int(np.prod(shape[1:]))
        nw = n if dt == F32 else (n + 1) // 2
        nw = (nw + 7) // 8 * 8
        off = 0
        for tv in sorted(self.live, key=lambda t: t.off):
            if tv.off - off >= nw:
                break
            off = max(off, tv.off + tv.nw)
        assert off + nw <= self.n, "SBUF arena overflow allocating %s (%d words at %d)" % (name, nw, off)
        self.peak = max(self.peak, off + nw)
        pend = []
        keep = []
        for (o, w, ops) in self.dead:
            if o < off + nw and off < o + w:
                pend += ops
                if not (off <= o and o + w <= off + nw):
                    keep.append((o, w, ops))
            else:
                keep.append((o, w, ops))
        self.dead = keep
        v = self.t[0:shape[0], off:off + nw]
        if dt != F32:
            v = v.bitcast(dt)
        v = v[:, 0:n]
        if len(shape) == 3:
            v = v.rearrange("p (a b) -> p a b", a=shape[1])
        elif len(shape) == 4:
            v = v.rearrange("p (a b c) -> p a b c", a=shape[1], b=shape[2])
        tv = TV(v, name, off, nw, _prune(pend))
        self.live.append(tv)
        return tv

    def free(self, *tvs):
        for tv in tvs:
            ops = list(tv.inherit)
            for g in tv.regs.values():
                if g.lw is not None:
                    ops.append(g.lw)
                ops += g.rd
            self.dead.append((tv.off, tv.nw, _prune(ops)))
            self.live.remove(tv)


D = 1024
NTOK = 1536
TB = 512
PERM = [0, 3, 1, 4, 2, 5, 6, 9, 7, 10, 8, 11]
DFF = 2816
R_BADA, R_CCTX, R_CI, R_NMIX, R_NFFN = 0, 96, 104, 112, 128
R_CLW, R_CLB, R_BA, R_BX, R_LAM, R_CSW, R_CSB, R_QN, R_KN, R_LRU0 = 144, 160, 164, 172, 180, 188, 228, 238, 239, 240
B_NORM, B_D, B_DTB, B_ALOG, NBV = 0, 1024, 1040, 1072, 1104


def _consts():
    c = {}
    ident = np.eye(128, dtype=np.float32)
    onesm = np.full((128, 128), 1.0 / 1024, np.float32)
    o64 = np.zeros((128, 128), np.float32)
    o64[:64, :64] = 1.0 / 64
    o64[64:, 64:] = 1.0 / 64
    R = np.zeros((128, 128), np.float32)
    for h in range(2):
        for ax in range(2):
            for i in range(16):
                p1 = h * 64 + ax * 32 + i
                p2 = p1 + 16
                R[p1, p2] = -1.0
                R[p2, p1] = 1.0
    j = np.arange(128)
    utri = (j[:, None] <= j[None, :]).astype(np.float32)
    ltri = (j[:, None] >= j[None, :]).astype(np.float32)
    nmf = np.where(j[:, None] < j[None, :], -32768.0, 0.0).astype(np.float32)
    nmb = np.where(j[:, None] > j[None, :], -32768.0, 0.0).astype(np.float32)
    onesf = np.ones((128, 128), np.float32)
    c["cf32"] = np.stack([ident, R.T.copy(), utri, ltri, onesf])
    a64 = 2 * np.pi * np.outer(np.arange(64), np.arange(64)) / 64
    c64 = np.zeros((2, 128, 128), np.float32)
    for g in range(2):
        c64[0, g * 64:(g + 1) * 64, g * 64:(g + 1) * 64] = np.cos(a64) / 8
        c64[1, g * 64:(g + 1) * 64, g * 64:(g + 1) * 64] = -np.sin(a64) / 8
    c["cb16"] = np.stack([onesm, o64, nmf, nmb, ident, c64[0], c64[1], utri, ltri])
    for S in (256, 1024):
        a = 2 * np.pi * (np.outer(np.arange(S), np.arange(S)) % S) / S
        c["dft%d" % S] = np.stack([np.cos(a), np.sin(a)]).astype(np.float32) / np.sqrt(S)
    s = np.arange(1024)
    row = (s // 64).astype(np.float32)
    col = (s % 64).astype(np.float32)
    freqs = (10000.0 ** (-np.arange(16, dtype=np.float32) / 16)).astype(np.float32)
    ang = np.zeros((64, 1024), np.float32)
    for d in range(64):
        ax = d // 32
        i = d % 16
        ang[d] = (row if ax == 0 else col) * freqs[i]
    ang = np.concatenate([ang, ang], 0)
    c["rope"] = np.stack([np.cos(ang), np.sin(ang)]).astype(np.float32)
    return c


def build_program():
    nc = bass.Bass("TRN2", target_bir_lowering=False)
    S = Sched(nc)

    def din(name, shape):
        return nc.dram_tensor(name, list(shape), F32, kind="ExternalInput").ap()

    def dout(name, shape):
        return nc.dram_tensor(name, list(shape), F32, kind="ExternalOutput").ap()

    d_x = din("x", [NTOK, D])
    d_vecs = din("vecs", [256, 128])
    d_bvec = din("bvec", [1, NBV])
    d_wada = din("w_ada", [2, D, 6 * D])
    d_wie = din("w_in_even", [D, 1536])
    d_woe = din("w_out_even", [D, D])
    d_wio = din("w_in_odd", [D, 3360])
    d_woo = din("w_out_odd", [1536, D])
    d_w1 = din("ffn_w1", [2, D, DFF])
    d_w3 = din("ffn_w3", [2, D, DFF])
    d_w2 = din("ffn_w2", [2, DFF, D])
    d_lrubd = din("lru_bd", [16, 128, 128])
    d_ck = din("cache_k", [512, 256])
    d_cv = din("cache_v", [512, 256])
    d_sssd = din("state_ssd", [2, 1024, 64])
    d_cf32 = din("cf32", [5, 128, 128])
    d_cb16 = din("cb16", [9, 128, 128])
    d_dft256 = din("dft256", [2, 256, 256])
    d_dft1024 = din("dft1024", [2, 1024, 1024])
    d_rope = din("rope", [2, 128, 1024])
    o_y = dout("y", [NTOK, D])
    o_k = dout("newk", [512, 256])
    o_v = dout("newv", [512, 256])
    o_lru = dout("newlru", [16, 128])
    o_ssd = dout("newssd", [2, 2, 1024, 64])

    with ExitStack() as st:
        A = Arena(nc, st, 53100)
        psb = []
        for i in range(8):
            t = st.enter_context(nc.psum_tensor("ps%d" % i, [128, 512], F32))
            psb.append((t, Reg("ps%d" % i, excl=True)))

        def PS(i):
            return psb[i]

        class Rot:
            def __init__(self, banks):
                self.b = list(banks)
                self.i = 0

            def next(self):
                r = psb[self.b[self.i % len(self.b)]]
                self.i += 1
                return r

        def dma(q, out, in_, reads=(), writes=()):
            return S.add(q, lambda e: e.dma_start(out=out, in_=in_), reads=reads, writes=writes, dma=True)

        def mm(out, lhsT, rhs, start, stop, reads, writes):
            return S.add("pe", lambda e: e.matmul(out, lhsT, rhs, start=start, stop=stop), reads=reads, writes=writes)

        NSLOT = 4
        slots = [A.alloc("wslot%d" % i, [128, 4096], BF16) for i in range(NSLOT)]
        wplan = []
        wstate = {"issued": 0, "got": 0}

        def wplan_add(ap, a, b):
            wplan.append((ap, a, b))

        def kp(ap2d):
            return ap2d.rearrange("(k p) n -> p k n", p=128)

        def wview(i):
            ap, a, b = wplan[i]
            sl = slots[i % NSLOT]
            return sl[:, 0:a * b].rearrange("p (a b) -> p a b", a=a), sl.r()

        def wget():
            i = wstate["got"]
            wstate["got"] += 1
            while wstate["issued"] < min(len(wplan), i + NSLOT - 1):
                j = wstate["issued"]
                v, rg = wview(j)
                dma("pool", v, wplan[j][0], writes=[rg])
                wstate["issued"] += 1
            return wview(i)

        def plan_ada(l):
            for t in range(12):
                wplan_add(kp(d_wada[l])[:, :, t * 512:(t + 1) * 512], 8, 512)

        def plan_ada_tiles(l, ts):
            for t in ts:
                wplan_add(kp(d_wada[l])[:, :, t * 512:(t + 1) * 512], 8, 512)

        def plan_ffn(l):
            for jt in range(6):
                n = 512 if jt < 5 else 256
                wplan_add(kp(d_w1[l])[:, :, jt * 512:jt * 512 + n], 8, n)
                wplan_add(kp(d_w3[l])[:, :, jt * 512:jt * 512 + n], 8, n)
            for o in range(8):
                wplan_add(kp(d_w2[l])[:, :, o * 128:(o + 1) * 128], 22, 128)

        plan_ada_tiles(0, range(0, 4))
        for t in range(3):
            wplan_add(kp(d_wie)[:, :, t * 512:(t + 1) * 512], 8, 512)
        plan_ada_tiles(0, range(4, 12))
        if NLAYERS > 1:
            plan_ada_tiles(1, range(12))
        for sb in range(2):
            for m in range(2):
                wplan_add(kp(d_dft1024[m])[:, :, sb * 512:(sb + 1) * 512], 8, 512)
        for t in range(2):
            wplan_add(kp(d_woe)[:, :, t * 512:(t + 1) * 512], 8, 512)
        plan_ffn(0)
        if NLAYERS > 1:
            wio = kp(d_wio)
            for t in range(2):
                wplan_add(wio[:, :, t * 512:(t + 1) * 512], 8, 512)
            for grp in range(2):
                wplan_add(wio[:, :, 2048:2560], 8, 512)
                wplan_add(wio[:, :, 2560:3072], 8, 512)
                wplan_add(wio[:, :, 3072:3328], 8, 256)
                wplan_add(wio[:, :, 1024:1536], 8, 512)
                wplan_add(wio[:, :, 1536:2048], 8, 512)
                wplan_add(wio[:, :, 3328:3360], 8, 32)
            wstate["woo"] = len(wplan)
            for t in range(4):
                wplan_add(kp(d_woo)[:, :, t * 256:(t + 1) * 256], 12, 256)
            plan_ffn(1)

        xT = A.alloc("xT", [128, 8, NTOK], F32)
        hT = A.alloc("hT", [128, 8, NTOK], BF16)
        cf32 = A.alloc("cf32", [128, 5, 128], F32)
        cb16 = A.alloc("cb16", [128, 9, 128], BF16)
        vecT = A.alloc("vecT", [128, 256], F32)
        bv = A.alloc("bv", [128, NBV], F32)
        mod = A.alloc("mod", [128, 2, 48, 2], F32)
        A1 = A.alloc("A1", [128, 2, 2, 8, 2], F32) if False else A.alloc("A1", [128, 64], F32)
        ident = cf32[:, 0, :]
        RT = cf32[:, 1, :]
        utri = cf32[:, 2, :]
        ltri = cf32[:, 3, :]
        onesf = cf32[:, 4, :]
        onesm = cb16[:, 0, :]
        o64 = cb16[:, 1, :]
        identb = cb16[:, 4, :]

        def A1v(l, kind, c, j):
            o = ((l * 2 + kind) * 8 + c) * 2 + j
            return A1[:, o:o + 1]

        def modv(l, q, c, j):
            return mod[:, l, q * 8 + c, j:j + 1]

        def xr(b):
            return xT.rs(range(4 * b, 4 * b + 4))

        dma("sp", cf32[:], d_cf32.rearrange("c p n -> p c n"), writes=[cf32.r()])
        dma("pool", cb16[:], d_cb16.rearrange("c p n -> p c n"), writes=[cb16.r()])
        vraw = A.alloc("vraw", [128, 2, 128], F32)
        dma("sp", vraw[:], d_vecs.rearrange("(a p) n -> p a n", p=128), writes=[vraw.r()])
        dma("sp", bv[:], d_bvec.partition_broadcast(128), writes=[bv.r()])
        for a in range(2):
            p, pr = PS(a)
            S.add("pe", lambda e, p=p, a=a: e.transpose(p[:, 0:128], vraw[:, a, :], ident),
                  reads=[vraw.r(), cf32.r()], writes=[pr])
            S.add("dve", lambda e, p=p, a=a: e.tensor_copy(out=vecT[:, a * 128:(a + 1) * 128], in_=p[:, 0:128]),
                  reads=[pr], writes=[vecT.r()])
        A.free(vraw)

        xst = [A.alloc("xst%d" % i, [128, D], F32) for i in range(2)]
        rot = Rot([2, 3, 4, 5, 6, 7])
        for t in range(12):
            xs_ = xst[t % 2]
            dma("sp", xs_[:], d_x[t * 128:(t + 1) * 128, :], writes=[xs_.r()])
            for hf in range(2):
                p, pr = rot.next()

                def tr(e, p=p, xs_=xs_, hf=hf):
                    for c in range(4):
                        ins = e.transpose(p[:, c * 128:(c + 1) * 128], xs_[:, (hf * 4 + c) * 128:(hf * 4 + c + 1) * 128], ident)
                    return ins
                S.add("pe", tr, reads=[xs_.r(), cf32.r()], writes=[pr])
                eng = "act" if hf == 0 else "dve"
                outv = xT[:, hf * 4:hf * 4 + 4, t * 128:(t + 1) * 128]
                inv = p[:].rearrange("p (c n) -> p c n", c=4)
                if eng == "act":
                    S.add("act", lambda e, o=outv, i=inv: e.activation(out=o, in_=i, func=AF.Copy), reads=[pr], writes=[xT.r(t)])
                else:
                    S.add("dve", lambda e, o=outv, i=inv: e.tensor_copy(out=o, in_=i), reads=[pr], writes=[xT.r(t)])
        A.free(*xst)

        scT = A.alloc("scT", [128, 8, 2], BF16)
        S.add("act", lambda e: e.activation(out=scT[:].rearrange("p k j -> p j k"),
                                            in_=vecT[:, R_CCTX:R_CCTX + 16].rearrange("p (j k) -> p j k", j=2), func=AF.Silu),
              reads=[vecT.r()], writes=[scT.r()])

        def ada_tile(l, t, bank):
            p, pr = PS(bank) if isinstance(bank, int) else bank
            wt, wr = wget()

            def f(e):
                for oc4 in range(4):
                    for k in range(8):
                        ins = e.matmul(p[:, oc4 * 2:oc4 * 2 + 2], wt[:, k, oc4 * 128:(oc4 + 1) * 128], scT[:, k, :],
                                       start=(k == 0), stop=(k == 7))
                return ins
            S.add("pe", f, reads=[wr, scT.r()], writes=[pr])
            r0 = R_BADA + l * 48 + t * 4
            S.add("dve", lambda e: e.tensor_tensor(out=mod[:, l, t * 4:(t + 1) * 4, :], in0=p[:, 0:8].rearrange("p (c j) -> p c j", j=2),
                                                   in1=vecT[:, r0:r0 + 4].unsqueeze(2).to_broadcast([128, 4, 2]),
                                                   op=ALU.add), reads=[pr, vecT.r()], writes=[mod.r()])

        def ada_finish(l, kind):
            q, rbase = ((1, R_NMIX), (4, R_NFFN))[kind]
            o = (l * 2 + kind) * 16
            S.add("dve", lambda e: e.scalar_tensor_tensor(
                out=A1[:, o:o + 16].rearrange("p (c j) -> p c j", j=2), in0=mod[:, l, q * 8:(q + 1) * 8, :], scalar=1.0,
                in1=vecT[:, rbase + l * 8:rbase + (l + 1) * 8].unsqueeze(2).to_broadcast([128, 8, 2]),
                op0=ALU.add, op1=ALU.mult), reads=[mod.r(), vecT.r()], writes=[A1.r()])

        def norm_mod(l, kind):
            qs = 0 if kind == 0 else 3
            sq = [A.alloc("nsq%d" % i, [128, TB], BF16) for i in range(4)]
            rs = [A.alloc("nrs%d" % i, [128, TB], F32) for i in range(3)]
            tt = [A.alloc("ntt%d" % i, [128, TB], F32) for i in range(3)]
            rot = Rot([6, 7])
            cnt = [0, 0]
            pbank = {}

            def stA(b):
                blk = slice(b * TB, (b + 1) * TB)
                p, pr = rot.next()
                pbank[b] = (p, pr)
                for c in range(8):
                    s_ = sq[cnt[0] % 4]
                    cnt[0] += 1
                    if c % 2 == 0:
                        S.add("act", lambda e, s_=s_, c=c: e.activation(out=s_[:], in_=xT[:, c, blk], func=AF.Square), reads=xr(b), writes=[s_.r()])
                    else:
                        S.add("dve", lambda e, s_=s_, c=c: e.tensor_tensor(out=s_[:], in0=xT[:, c, blk], in1=xT[:, c, blk], op=ALU.mult), reads=xr(b), writes=[s_.r()])
                    mm(p[:], onesm, s_[:], c == 0, c == 7, [s_.r(), cb16.r()], [pr])

            def stB(b):
                p, pr = pbank[b]
                r_ = rs[b]
                S.add("act", lambda e: e.activation(out=r_[:], in_=p[:], func=AF.Ln, bias=EPS_AP[:, 0:1], scale=1.0), reads=[pr, EPS_AP.r()], writes=[r_.r()])
                S.add("act", lambda e: e.activation(out=r_[:], in_=r_[:], func=AF.Exp, scale=-0.5), reads=[r_.r()], writes=[r_.r()])

            def stC(b):
                j = 0 if b == 0 else 1
                blk = slice(b * TB, (b + 1) * TB)
                r_ = rs[b]
                for c in range(8):
                    t_ = tt[cnt[1] % 3]
                    cnt[1] += 1
                    S.add("dve", lambda e, t_=t_, c=c: e.tensor_tensor(out=t_[:], in0=xT[:, c, blk], in1=r_[:], op=ALU.mult),
                          reads=xr(b) + [r_.r()], writes=[t_.r()])
                    S.add("act", lambda e, t_=t_, c=c: e.activation(out=hT[:, c, blk], in_=t_[:], func=AF.Identity,
                                                                    bias=modv(l, qs, c, j), scale=A1v(l, kind, c, j)),
                          reads=[t_.r(), mod.r(), A1.r()], writes=[hT.r(b)])

            stA(0)
            stA(1)
            stB(0)
            stC(0)
            stA(2)
            stB(1)
            stC(1)
            stB(2)
            stC(2)
            A.free(*sq, *rs, *tt)

        EPS_AP = A.alloc("eps", [128, 2], F32)
        S.add("pool", lambda e: e.memset(EPS_AP[:], EPS), writes=[EPS_AP.r()])
        ONE_AP = A.alloc("one", [128, 2], F32)
        S.add("pool", lambda e: e.memset(ONE_AP[:], 1.0), writes=[ONE_AP.r()])

        def resid(p, pr, l, q, o, b):
            j = 0 if b == 0 else 1
            blk = slice(b * TB, (b + 1) * TB)
            S.add("dve", lambda e: e.scalar_tensor_tensor(out=xT[:, o, blk], in0=p[:], scalar=modv(l, q, o, j), in1=xT[:, o, blk],
                                                          op0=ALU.mult, op1=ALU.add),
                  reads=[pr, mod.r()] + xr(b), writes=xr(b))

        def ffn(l):
            norm_mod(l, 1)
            actT = A.alloc("actT", [128, 22, NTOK], BF16)
            sl = [A.alloc("fsl%d" % i, [128, TB], BF16) for i in range(3)]
            rot1 = Rot([0, 1, 2, 3])
            n = 0
            for jt in range(6):
                w1, w1r = wget()
                w3, w3r = wget()
                ncs = 4 if jt < 5 else 2
                for cc in range(ncs):
                    jj = jt * 4 + cc
                    for b in range(3):
                        blk = slice(b * TB, (b + 1) * TB)
                        p1, p1r = rot1.next()
                        p3, p3r = rot1.next()

                        def f(e, w=w1, p=p1, cc=cc, blk=blk):
                            for k in range(8):
                                ins = e.matmul(p[:], w[:, k, cc * 128:(cc + 1) * 128], hT[:, k, blk], start=(k == 0), stop=(k == 7))
                            return ins
                        S.add("pe", f, reads=[w1r, hT.r(b)], writes=[p1r])

                        def f3(e, w=w3, p=p3, cc=cc, blk=blk):
                            for k in range(8):
                                ins = e.matmul(p[:], w[:, k, cc * 128:(cc + 1) * 128], hT[:, k, blk], start=(k == 0), stop=(k == 7))
                            return ins
                        S.add("pe", f3, reads=[w3r, hT.r(b)], writes=[p3r])
                        s_ = sl[n % 3]
                        n += 1
                        S.add("act", lambda e, s_=s_, p=p1: e.activation(out=s_[:], in_=p[:], func=AF.Silu), reads=[p1r], writes=[s_.r()])
                        S.add("dve", lambda e, s_=s_, p=p3, jj=jj, blk=blk: e.tensor_tensor(out=actT[:, jj, blk], in0=p[:], in1=s_[:], op=ALU.mult),
                              reads=[p3r, s_.r()], writes=[actT.r(b)])
            A.free(*sl)
            rot2 = Rot([4, 5, 6, 7])
            for o in range(8):
                w2, w2r = wget()
                for b in range(3):
                    blk = slice(b * TB, (b + 1) * TB)
                    p, pr = rot2.next()

                    def f(e, w=w2, p=p, blk=blk):
                        for jj in range(22):
                            ins = e.matmul(p[:], w[:, jj, :], actT[:, jj, blk], start=(jj == 0), stop=(jj == 21))
                        return ins
                    S.add("pe", f, reads=[w2r, actT.r(b)], writes=[pr])
                    resid(p, pr, l, 5, o, b)
            A.free(actT)

        def even_layer(l):
            norm_mod(l, 0)
            qT = A.alloc("qT", [128, 6, NTOK], BF16)
            kT = A.alloc("kT", [128, 2, 2048], BF16)
            vaug = A.alloc("vaug", [128, 16, 4, 128], BF16)
            ftok = A.alloc("ftok", [128, 12, 256], BF16)
            rope = A.alloc("rope", [128, 2, 1024], F32)
            dft256 = A.alloc("dft256", [128, 2, 2, 256], BF16)
            knew = A.alloc("knew", [128, 4, 256], F32)
            vnew = A.alloc("vnew", [128, 4, 256], F32)
            dma("sp", rope[:], d_rope.rearrange("c p n -> p c n"), writes=[rope.r()])
            for m in range(2):
                dma("pool", dft256[:, m, :, :], d_dft256[m].rearrange("(t p) n -> p t n", p=128), writes=[dft256.r()])
            S.add("pool", lambda e: e.memset(vaug[:], 1.0), writes=vaug.rs(range(16)))
            cvv = d_cv.rearrange("(t p) (j d) -> p t j d", p=128, j=4)
            for par in range(2):
                for tt_ in range(4):
                    dma("pool", vaug[:, 12 + tt_, par::2, par * 64:par * 64 + 64], cvv[:, tt_, par::2, :], writes=[vaug.r(12 + tt_)])
            ckt = A.alloc("ckt", [128, 4, 256], F32)
            dma("sp", ckt[:], d_ck.rearrange("(t p) f -> p t f", p=128), writes=[ckt.r()])
            rotc = Rot([4, 5])
            for kc in range(2):
                p, pr = rotc.next()

                def f(e, p=p, kc=kc):
                    for tt_ in range(4):
                        ins = e.transpose(p[:, tt_ * 128:(tt_ + 1) * 128], ckt[:, tt_, kc * 128:(kc + 1) * 128], ident)
                    return ins
                S.add("pe", f, reads=[ckt.r(), cf32.r()], writes=[pr])
                S.add("dve", lambda e, p=p, kc=kc: e.tensor_copy(out=kT[:, kc, 1536:2048], in_=p[:]), reads=[pr], writes=[kT.r(3)])
            A.free(ckt)

            if STAGE < 3:
                return
            sqn = [A.alloc("sqn%d" % i, [128, TB], BF16) for i in range(3)]
            rsn = [A.alloc("rsn%d" % i, [128, TB], F32) for i in range(3)]
            qn = [A.alloc("qn%d" % i, [128, TB], F32) for i in range(3)]
            t1 = [A.alloc("rt1%d" % i, [128, TB], F32) for i in range(2)]
            t2 = [A.alloc("rt2%d" % i, [128, TB], F32) for i in range(2)]
            rotA = Rot([0, 1, 6])
            rotB = Rot([2, 3])
            rotC = Rot([4, 5, 7])
            qunits = [(wt_i, cc, b) for wt_i in range(2) for cc in range(4) for b in range(3)]
            ust = {}
            wcur = {}

            def stageA(u):
                wt_i, cc, b = qunits[u]
                if (wt_i,) not in wcur:
                    wcur[(wt_i,)] = wget()
                w, wr = wcur[(wt_i,)]
                blk = slice(b * TB, (b + 1) * TB)
                pa, par_ = rotA.next()

                def f(e):
                    for k in range(8):
                        ins = e.matmul(pa[:], w[:, k, cc * 128:(cc + 1) * 128], hT[:, k, blk], start=(k == 0), stop=(k == 7))
                    return ins
                S.add("pe", f, reads=[wr, hT.r(b)], writes=[par_])
                s_ = sqn[u % 3]
                S.add("act", lambda e: e.activation(out=s_[:], in_=pa[:], func=AF.Square), reads=[par_], writes=[s_.r()])
                ust[u] = (pa, par_, s_)

            def stageB(u):
                wt_i, cc, b = qunits[u]
                qc = wt_i * 4 + cc
                isk = qc >= 6
                gain = vecT[:, R_KN:R_KN + 1] if isk else vecT[:, R_QN:R_QN + 1]
                pa, par_, s_ = ust[u]
                r_ = rsn[u % 3]
                q_ = qn[u % 3]
                pb, pbr = rotB.next()
                mm(pb[:], o64, s_[:], True, True, [s_.r(), cb16.r()], [pbr])
                S.add("act", lambda e: e.activation(out=r_[:], in_=pb[:], func=AF.Ln, bias=EPS_AP[:, 0:1], scale=1.0),
                      reads=[pbr, EPS_AP.r()], writes=[r_.r()])
                S.add("act", lambda e: e.activation(out=r_[:], in_=r_[:], func=AF.Exp, scale=-0.5), reads=[r_.r()], writes=[r_.r()])
                if b == 0 and not isk:
                    S.add("dve", lambda e: e.scalar_tensor_tensor(out=qT[:, qc, 0:TB], in0=pa[:], scalar=gain, in1=r_[:], op0=ALU.mult, op1=ALU.mult),
                          reads=[par_, r_.r(), vecT.r()], writes=[qT.r(0)])
                else:
                    S.add("dve", lambda e: e.scalar_tensor_tensor(out=q_[:], in0=pa[:], scalar=gain, in1=r_[:], op0=ALU.mult, op1=ALU.mult),
                          reads=[par_, r_.r(), vecT.r()], writes=[q_.r()])
                ust[u] = (q_,)

            def stageC(u):
                wt_i, cc, b = qunits[u]
                qc = wt_i * 4 + cc
                isk = qc >= 6
                (q_,) = ust.pop(u)
                blk = slice(b * TB, (b + 1) * TB)
                if isk:
                    dst, dreg = kT[:, qc - 6, b * TB:(b + 1) * TB], kT.r(b)
                else:
                    dst, dreg = qT[:, qc, blk], qT.r(b)
                if b == 0:
                    if isk:
                        S.add("act", lambda e: e.activation(out=dst, in_=q_[:], func=AF.Copy), reads=[q_.r()], writes=[dreg])
                        pc, pcr = rotC.next()

                        def ftr(e):
                            for tt_ in range(4):
                                ins = e.transpose(pc[:, tt_ * 128:(tt_ + 1) * 128], q_[:, tt_ * 128:(tt_ + 1) * 128], ident)
                            return ins
                        S.add("pe", ftr, reads=[q_.r(), cf32.r()], writes=[pcr])
                        kc = qc - 6
                        S.add("act", lambda e: e.activation(out=knew[:, :, kc * 128:(kc + 1) * 128], in_=pc[:].rearrange("p (t f) -> p t f", t=4), func=AF.Copy),
                              reads=[pcr], writes=[knew.r()])
                else:
                    a_ = t1[u % 2]
                    b_ = t2[u % 2]
                    pc, pcr = rotC.next()
                    mm(pc[:], RT, q_[:], True, True, [q_.r(), cf32.r()], [pcr])
                    rsl = slice((b - 1) * TB, b * TB)
                    S.add("dve", lambda e: e.tensor_tensor(out=a_[:], in0=q_[:], in1=rope[:, 0, rsl], op=ALU.mult), reads=[q_.r(), rope.r()], writes=[a_.r()])
                    S.add("dve", lambda e: e.tensor_tensor(out=b_[:], in0=pc[:], in1=rope[:, 1, rsl], op=ALU.mult), reads=[pcr, rope.r()], writes=[b_.r()])
                    S.add("dve", lambda e: e.tensor_tensor(out=dst, in0=a_[:], in1=b_[:], op=ALU.add), reads=[a_.r(), b_.r()], writes=[dreg])

            NU = len(qunits)
            for step in range(NU + 2):
                if step < NU:
                    stageA(step)
                if 0 <= step - 1 < NU:
                    stageB(step - 1)
                if 0 <= step - 2 < NU:
                    stageC(step - 2)
            dma("sp", o_k.rearrange("(t p) f -> p t f", p=128), knew[:], reads=[knew.r()])
            A.free(*sqn, *rsn, *qn, *t1, *t2)

            if STAGE < 4:
                return
            w, wr = wget()
            rotA = Rot([0, 1, 2, 3])
            for t in range(12):
                b = t // 4
                p, pr = rotA.next()

                def f(e, p=p, t=t, w=w):
                    for k in range(8):
                        ins = e.matmul(p[:], hT[:, k, t * 128:(t + 1) * 128], w[:, k, :], start=(k == 0), stop=(k == 7))
                    return ins
                S.add("pe", f, reads=[wr, hT.r(b)], writes=[pr])
                if not (SUB & 1):
                    S.add("act", lambda e, p=p, t=t: e.activation(out=ftok[:, t, :], in_=p[:, 0:256], func=AF.Copy), reads=[pr], writes=[ftok.r(t)])
                pv = p[:, 256:512].rearrange("p (j d) -> p j d", j=4)
                for par in range(2 if not (SUB & 2) else 0):
                    S.add("dve", lambda e, pv=pv, t=t, par=par: e.tensor_copy(out=vaug[:, t, par::2, par * 64:par * 64 + 64], in_=pv[:, par::2, :]),
                          reads=[pr], writes=[vaug.r(t)])
                if t < 4 and not (SUB & 4):
                    S.add("act", lambda e, p=p, t=t: e.activation(out=vnew[:, t, :], in_=p[:, 256:512], func=AF.Copy), reads=[pr], writes=[vnew.r()])
            dma("sp", o_v.rearrange("(t p) f -> p t f", p=128), vnew[:], reads=[vnew.r()])
            for t in range(4, 12):
                ada_tile(l, t, 4 + t % 4)
            ada_finish(l, 1)

            if STAGE < 5:
                return
            pT = [A.alloc("pT%d" % i, [128, TB], BF16) for i in range(6)]
            rtmp = [A.alloc("rtmp%d" % i, [128, TB], F32) for i in range(2)]
            evb = [A.alloc("evb%d" % i, [128, TB], F32) for i in range(2)]
            rotS = Rot([0, 1, 2, 3, 4, 5])
            rotO = Rot([6, 7])
            units = []
            for sidx in range(2):
                units.append((sidx * 256, 256, 0, [(sidx * 256 + kt * 128, 2 * sidx + kt) for kt in range(2)]))
            skeys = [(512 + kt * 128, 4 + kt) for kt in range(8)] + [(1536 + kt * 128, 12 + kt) for kt in range(4)]
            for qb in range(2):
                units.append((512 + qb * 512, 512, 1 + qb, skeys))
            n = 0
            nh = 0
            ada1_next = [0]
            LAG = 2
            for (q0, N, b, keys) in units:
                nk = len(keys)
                groups = [(qc, ki) for qc in range(6) for ki in range(nk)]
                pts = {}

                def emit_scores(gi, q0=q0, N=N, b=b, keys=keys, groups=groups, pts=pts):
                    nonlocal n
                    qc, ki = groups[gi]
                    kc = qc // 3
                    k0, vt = keys[ki]
                    kreg = kT.r(0 if k0 < 512 else (1 if k0 < 1024 else (2 if k0 < 1536 else 3)))
                    banks = []
                    for hh in range(2):
                        lo, hi = hh * 64, hh * 64 + 64
                        sp_, spr = rotS.next()
                        mm(sp_[:, 0:N], kT[lo:hi, kc, k0:k0 + 128], qT[lo:hi, qc, q0:q0 + N], True, True, [kreg, qT.r(b)], [spr])
                        banks.append((sp_, spr))
                    lst = []
                    for (sp_, spr) in banks:
                        pt = pT[n % 6]
                        n += 1
                        S.add("act", lambda e, pt=pt, sp_=sp_, N=N: e.activation(out=pt[:, 0:N], in_=sp_[:, 0:N], func=AF.Exp, scale=0.125),
                              reads=[spr], writes=[pt.r()])
                        lst.append(pt)
                    pts[gi] = lst

                for gi in range(min(LAG, len(groups))):
                    emit_scores(gi)
                accs = None
                for gi, (qc, ki) in enumerate(groups):
                    if ki == 0:
                        accs = [rotO.next(), rotO.next()]
                    k0, vt = keys[ki]
                    lst = pts.pop(gi)
                    for hh in range(2):
                        j = (qc // 3) * 2 + hh
                        acc, accr = accs[hh]
                        mm(acc[:, 0:N], vaug[:, vt, j, :], lst[hh][:, 0:N], ki == 0, ki == nk - 1, [vaug.r(vt), lst[hh].r()], [accr])
                    if gi + LAG < len(groups):
                        emit_scores(gi + LAG)
                    if NLAYERS > 1 and N == 512 and gi % 12 == 6 and ada1_next[0] < 12:
                        ada_tile(1, ada1_next[0], rotS.next())
                        ada1_next[0] += 1
                    if ki == nk - 1:
                        for hh in range(2):
                            acc, accr = accs[hh]
                            lo, hi = hh * 64, hh * 64 + 64
                            olo, ohi = (1 - hh) * 64, (1 - hh) * 64 + 64
                            rt = rtmp[nh % 2]
                            ev = evb[nh % 2]
                            nh += 1
                            S.add("dve", lambda e, ev=ev, acc=acc, N=N: e.tensor_copy(out=ev[:, 0:N], in_=acc[:, 0:N]), reads=[accr], writes=[ev.r()])
                            S.add("dve", lambda e, rt=rt, ev=ev, N=N, lo=lo, hi=hi, olo=olo, ohi=ohi: e.reciprocal(out=rt[lo:hi, 0:N], in_=ev[olo:ohi, 0:N]),
                                  reads=[ev.r()], writes=[rt.r()])
                            S.add("dve", lambda e, rt=rt, ev=ev, N=N, lo=lo, hi=hi, qc=qc, q0=q0: e.tensor_tensor(
                                out=hT[lo:hi, 2 + qc, q0:q0 + N], in0=ev[lo:hi, 0:N], in1=rt[lo:hi, 0:N], op=ALU.mult),
                                reads=[ev.r(), rt.r()], writes=[hT.r(b)])
            if NLAYERS > 1:
                while ada1_next[0] < 12:
                    ada_tile(1, ada1_next[0], rotS.next())
                    ada1_next[0] += 1
                ada_finish(1, 0)
                ada_finish(1, 1)
            A.free(*pT, *rtmp, *evb, qT, kT, vaug, rope, knew, vnew)

            if STAGE < 6:
                return
            uv = [A.alloc("uv%d" % i, [128, TB], BF16) for i in range(4)]
            rotU = Rot([0, 1, 2, 3])
            rotF = Rot([4, 5, 6, 7])
            c64c = cb16[:, 5, :]
            c64s = cb16[:, 6, :]
            n = 0

            def stageB(u, v, N, cch, q0, b):
                p, pr = rotF.next()
                mm(p[:, 0:N], c64c, u[:, 0:N], True, False, [cb16.r(), u.r()], [pr])
                mm(p[:, 0:N], c64s, v[:, 0:N], False, True, [cb16.r(), v.r()], [pr])
                S.add("act", lambda e: e.activation(out=hT[:, cch, q0:q0 + N], in_=p[:, 0:N], func=AF.Copy), reads=[pr], writes=[hT.r(b)])

            for sidx in range(2):
                for cch in range(2):
                    pair = []
                    for m in range(2):
                        p, pr = rotU.next()

                        def f(e, p=p, m=m, sidx=sidx, cch=cch):
                            for st_ in range(2):
                                ins = e.matmul(p[:, 0:256], ftok[:, 2 * sidx + st_, cch * 128:(cch + 1) * 128], dft256[:, m, st_, :],
                                               start=(st_ == 0), stop=(st_ == 1))
                            return ins
                        S.add("pe", f, reads=[ftok.r(2 * sidx), ftok.r(2 * sidx + 1), dft256.r()], writes=[pr])
                        u = uv[n % 4]
                        n += 1
                        S.add("dve", lambda e, u=u, p=p: e.tensor_copy(out=u[:, 0:256], in_=p[:, 0:256]), reads=[pr], writes=[u.r()])
                        pair.append(u)
                    stageB(pair[0], pair[1], 256, cch, sidx * 256, 0)
            for sb in range(2):
                wc, wcr = wget()
                wsn, wsr = wget()
                for cch in range(2):
                    pair = []
                    for m, (wm, wmr) in enumerate(((wc, wcr), (wsn, wsr))):
                        p, pr = rotU.next()

                        def f(e, p=p, wm=wm, cch=cch):
                            for st_ in range(8):
                                ins = e.matmul(p[:], ftok[:, 4 + st_, cch * 128:(cch + 1) * 128], wm[:, st_, :], start=(st_ == 0), stop=(st_ == 7))
                            return ins
                        S.add("pe", f, reads=ftok.rs(range(4, 12)) + [wmr], writes=[pr])
                        u = uv[n % 4]
                        n += 1
                        S.add("dve", lambda e, u=u, p=p: e.tensor_copy(out=u[:], in_=p[:]), reads=[pr], writes=[u.r()])
                        pair.append(u)
                    stageB(pair[0], pair[1], 512, cch, 512 + sb * 512, 1 + sb)
            A.free(*uv, ftok, dft256)

            if STAGE < 7:
                return
            rotW = Rot([0, 1, 2, 3])
            for t in range(2):
                w, wr = wget()
                for oc in range(4):
                    o = t * 4 + oc
                    for b in range(3):
                        blk = slice(b * TB, (b + 1) * TB)
                        p, pr = rotW.next()

                        def f(e, p=p, w=w, oc=oc, blk=blk):
                            for k in range(8):
                                ins = e.matmul(p[:], w[:, k, oc * 128:(oc + 1) * 128], hT[:, k, blk], start=(k == 0), stop=(k == 7))
                            return ins
                        S.add("pe", f, reads=[wr, hT.r(b)], writes=[pr])
                        resid(p, pr, l, 2, o, b)


        def rev(v):
            (ps_, pn), (st_, n) = v.ap
            return bass.AP(v.tensor, v.offset + (n - 1) * st_, [[ps_, pn], [-st_, n]])

        SEQS = [(0, 256), (256, 512), (512, 1536)]

        def conv4(eng, xp, yp, n, wcol, bcol):
            lo, hi = 1, n - 2
            S.add(eng, lambda e: e.tensor_scalar(out=yp[:, lo:hi], in0=xp[:, lo:hi], scalar1=vecT[:, wcol(1):wcol(1) + 1],
                                                 scalar2=vecT[:, bcol:bcol + 1], op0=ALU.mult, op1=ALU.add),
                  reads=[xp.r(), vecT.r()], writes=[yp.r()])
            for j, sh in ((0, -1), (2, 1), (3, 2)):
                S.add(eng, lambda e, j=j, sh=sh: e.scalar_tensor_tensor(out=yp[:, lo:hi], in0=xp[:, lo + sh:hi + sh], scalar=vecT[:, wcol(j):wcol(j) + 1],
                                                                        in1=yp[:, lo:hi], op0=ALU.mult, op1=ALU.add),
                      reads=[xp.r(), yp.r(), vecT.r()], writes=[yp.r()])

        def odd_layer(l):
            norm_mod(l, 0)
            yl = A.alloc("yl", [128, 4, NTOK], BF16)
            gg = A.alloc("gg", [128, 4, NTOK], BF16)
            xc = A.alloc("xc", [128, 4, NTOK], F32)
            bdw = A.alloc("bdw", [128, 16, 128], BF16)
            fin = A.alloc("fin", [128, 16], F32)
            clT = A.alloc("clT", [128, 8], F32)
            dma("pool", bdw[:], d_lrubd.rearrange("c p n -> p c n"), writes=[bdw.r()])
            S.add("act", lambda e: e.activation(out=clT[:], in_=vecT[:, R_LAM:R_LAM + 8], func=AF.Exp, scale=-1.0), reads=[vecT.r()], writes=[clT.r()])
            S.add("act", lambda e: e.activation(out=clT[:], in_=clT[:], func=AF.Ln, bias=ONE_AP[:, 0:1], scale=1.0), reads=[clT.r(), ONE_AP.r()], writes=[clT.r()])
            S.add("dve", lambda e: e.tensor_scalar(out=clT[:], in0=clT[:], scalar1=-8.0, scalar2=0.0, op0=ALU.mult, op1=ALU.add), reads=[clT.r()], writes=[clT.r()])
            NP = 1544
            POFF = [1, 259, 517]
            xpad = [A.alloc("xpad%d" % i, [128, NP], F32) for i in range(2)]
            ypad = [A.alloc("ypad%d" % i, [128, NP], F32) for i in range(2)]
            for xp in xpad:
                S.add("pool", lambda e, xp=xp: e.memset(xp[:], 0.0), writes=[xp.r()])
            rotA = Rot([0, 1, 2, 3])
            wg, wgr = wget()
            for c in range(4):
                for b in range(3):
                    blk = slice(b * TB, (b + 1) * TB)
                    p, pr = rotA.next()

                    def f(e, p=p, c=c, blk=blk):
                        for k in range(8):
                            ins = e.matmul(p[:], wg[:, k, c * 128:(c + 1) * 128], hT[:, k, blk], start=(k == 0), stop=(k == 7))
                        return ins
                    S.add("pe", f, reads=[wgr, hT.r(b)], writes=[pr])
                    S.add("act", lambda e, p=p, c=c, blk=blk: e.activation(out=gg[:, c, blk], in_=p[:], func=AF.Gelu_apprx_tanh), reads=[pr], writes=[gg.r(c)])
            wx, wxr = wget()
            for c in range(4):
                xp = xpad[c % 2]
                yp = ypad[c % 2]
                for b in range(3):
                    blk = slice(b * TB, (b + 1) * TB)
                    p, pr = rotA.next()

                    def f(e, p=p, c=c, blk=blk):
                        for k in range(8):
                            ins = e.matmul(p[:], wx[:, k, c * 128:(c + 1) * 128], hT[:, k, blk], start=(k == 0), stop=(k == 7))
                        return ins
                    S.add("pe", f, reads=[wxr, hT.r(b)], writes=[pr])
                    if b == 0:
                        ov = xp[:, 1:1 + 2 * 258].rearrange("p (a b) -> p a b", b=258)[:, :, 0:256]
                        S.add("act", lambda e, p=p, ov=ov: e.activation(out=ov, in_=p[:].rearrange("p (a b) -> p a b", a=2), func=AF.Copy), reads=[pr], writes=[xp.r()])
                    else:
                        o0 = 517 + (b - 1) * 512
                        S.add("act", lambda e, p=p, xp=xp, o0=o0: e.activation(out=xp[:, o0:o0 + 512], in_=p[:], func=AF.Copy), reads=[pr], writes=[xp.r()])
                conv4("dve", xp, yp, NP, lambda j, c=c: R_CLW + j * 4 + c, R_CLB + c)
                for si, (s0, s1) in enumerate(SEQS):
                    S.add("act", lambda e, yp=yp, c=c, s0=s0, s1=s1, si=si: e.activation(out=xc[:, c, s0:s1], in_=yp[:, POFF[si]:POFF[si] + s1 - s0], func=AF.Copy),
                          reads=[yp.r()], writes=[xc.r(c)])
            A.free(*xpad, *ypad)
            xcb = A.alloc("xcb", [128, NTOK], BF16)
            ra_ = [A.alloc("lra%d" % i, [128, NTOK], F32) for i in range(2)]
            ig_ = [A.alloc("lig%d" % i, [128, NTOK], F32) for i in range(2)]
            tq_ = [A.alloc("ltq0", [128, NTOK], F32)] * 2
            hd = [A.alloc("lh0", [128, NTOK], F32), ig_[1]]
            rotL = Rot([4, 5, 6, 7])
            xcbs = [xcb, xcb]

            def lruA(u):
                c, d = u // 2, u % 2
                ra, ig = ra_[d], ig_[d]
                xb = xcbs[c % 2]
                if d == 0:
                    S.add("act", lambda e: e.activation(out=xb[:], in_=xc[:, c, :], func=AF.Copy), reads=[xc.r(c)], writes=[xb.r()])
                for b in range(3):
                    blk = slice(b * TB, (b + 1) * TB)
                    for gate, dstT, brow in ((0, ra, R_BA), (1, ig, R_BX)):
                        p, pr = rotL.next()
                        mm(p[:], bdw[:, (gate * 2 + d) * 4 + c, :], xb[:, blk], True, True, [bdw.r(), xb.r()], [pr])
                        col = brow + d * 4 + c
                        S.add("act", lambda e, p=p, dstT=dstT, blk=blk, col=col: e.activation(out=dstT[:, blk], in_=p[:], func=AF.Sigmoid,
                                                                                               bias=vecT[:, col:col + 1], scale=1.0),
                              reads=[pr, vecT.r()], writes=[dstT.r()])

            def lruB(u):
                c, d = u // 2, u % 2
                ra, ig, tq = ra_[d], ig_[d], tq_[d]
                S.add("act", lambda e: e.activation(out=ra[:], in_=ra[:], func=AF.Exp, scale=clT[:, d * 4 + c:d * 4 + c + 1]),
                      reads=[ra.r(), clT.r()], writes=[ra.r()])
                S.add("act", lambda e: e.activation(out=tq[:], in_=ra[:], func=AF.Square), reads=[ra.r()], writes=[tq.r()])
                S.add("dve", lambda e: e.tensor_scalar(out=tq[:], in0=tq[:], scalar1=-1.0, scalar2=1.0, op0=ALU.mult, op1=ALU.add), reads=[tq.r()], writes=[tq.r()])
                S.add("act", lambda e: e.activation(out=tq[:], in_=tq[:], func=AF.Sqrt), reads=[tq.r()], writes=[tq.r()])
                S.add("dve", lambda e: e.tensor_tensor(out=ig[:], in0=ig[:], in1=xc[:, c, :], op=ALU.mult), reads=[ig.r(), xc.r(c)], writes=[ig.r()])
                S.add("dve", lambda e: e.tensor_tensor(out=tq[:], in0=tq[:], in1=ig[:], op=ALU.mult), reads=[tq.r(), ig.r()], writes=[tq.r()])
                h_ = hd[d]
                for si, (s0, s1) in enumerate(SEQS):
                    init = 0.0 if si < 2 else vecT[:, R_LRU0 + d * 4 + c:R_LRU0 + d * 4 + c + 1]
                    if d == 0:
                        S.add("dve", lambda e, s0=s0, s1=s1, init=init: e.tensor_tensor_scan(
                            out=h_[:, s0:s1], data0=ra[:, s0:s1], data1=tq[:, s0:s1], initial=init, op0=ALU.mult, op1=ALU.add),
                            reads=[ra.r(), tq.r(), vecT.r()], writes=[h_.r()])
                    else:
                        S.add("dve", lambda e, s0=s0, s1=s1, init=init: e.tensor_tensor_scan(
                            out=rev(h_[:, s0:s1]), data0=rev(ra[:, s0:s1]), data1=rev(tq[:, s0:s1]), initial=init, op0=ALU.mult, op1=ALU.add),
                            reads=[ra.r(), tq.r(), vecT.r()], writes=[h_.r()])
                    if si < 2:
                        pos = s1 - 1 if d == 0 else s0
                        col = (si * 2 + d) * 4 + c
                        S.add("dve", lambda e, pos=pos, col=col: e.tensor_copy(out=fin[:, col:col + 1], in_=h_[:, pos:pos + 1]),
                              reads=[h_.r()], writes=[fin.r()])
                if d == 1:
                    S.add("dve", lambda e: e.tensor_tensor(out=hd[0][:], in0=hd[0][:], in1=hd[1][:], op=ALU.add), reads=[hd[0].r(), hd[1].r()], writes=[hd[0].r()])
                    S.add("dve", lambda e: e.tensor_tensor(out=yl[:, c, :], in0=hd[0][:], in1=gg[:, c, :], op=ALU.mult),
                          reads=[hd[0].r(), gg.r(c)], writes=[yl.r()])

            lruA(0)
            for u in range(8):
                if u + 1 < 8:
                    lruA(u + 1)
                lruB(u)
            p, pr = PS(4)
            S.add("pe", lambda e: e.transpose(p[0:16, 0:128], fin[:, 0:16], ident), reads=[fin.r(), cf32.r()], writes=[pr])
            fino = A.alloc("fino", [16, 128], F32)
            S.add("dve", lambda e: e.tensor_copy(out=fino[:], in_=p[0:16, 0:128]), reads=[pr], writes=[fino.r()])
            dma("sp", o_lru, fino[:], reads=[fino.r()])
            A.free(gg, xc, bdw, clT, xcb, *ra_, *ig_, tq_[0], hd[0], fin, fino)
            if STAGE < 13:
                for _ in range(12):
                    wget()

            aneg = A.alloc("aneg", [128, 32], F32)
            S.add("act", lambda e: e.activation(out=aneg[:], in_=bv[:, B_ALOG:B_ALOG + 32], func=AF.Exp), reads=[bv.r()], writes=[aneg.r()])
            S.add("dve", lambda e: e.tensor_scalar(out=aneg[:], in0=aneg[:], scalar1=-1.0, scalar2=0.0, op0=ALU.mult, op1=ALU.add), reads=[aneg.r()], writes=[aneg.r()])
            nmk = [cb16[:, 2, :], cb16[:, 3, :]]
            tri = [cb16[:, 7, :], cb16[:, 8, :]]
            def ssd_group(grp):
                tiles = list(range(0, 4)) if grp == 0 else list(range(4, 12))
                nt = len(tiles)
                t0 = tiles[0]
                ntk = nt * 128
                blocks = [0] if grp == 0 else [1, 2]
                seqs = [[0, 1], [2, 3]] if grp == 0 else [list(range(8))]
                if grp == 0:
                    NPg = 520
                    poff = [1, 259]
                    slen = 256
                else:
                    NPg = 1032
                    poff = [1]
                    slen = 1024
                xs_tok = A.alloc("xs_tok", [128, nt, 1024], BF16)
                z_tok = A.alloc("z_tok", [128, nt, 1024], BF16)
                B_tok = A.alloc("B_tok", [128, nt, 128], BF16)
                BT = A.alloc("BT", [128, ntk], BF16)
                CT = A.alloc("CT", [128, ntk], BF16)
                xpad = [A.alloc("sxp%d" % i, [128, NPg], F32) for i in range(3)]
                ypad = [A.alloc("syp%d" % i, [128, NPg], F32) for i in range(3)]
                xsf = [A.alloc("xsf%d" % i, [128, ntk], F32) for i in range(3)]
                for xp in xpad:
                    S.add("pool", lambda e, xp=xp: e.memset(xp[:], 0.0), writes=[xp.r()])
                rotA = Rot([0, 1, 2, 3])
                rotT = Rot([4, 5, 6, 7])
                wc1 = {}

                def c1A(ch):
                    wt_i, cc = ch // 4, ch % 4
                    if wt_i not in wc1:
                        wc1[wt_i] = wget()
                    w, wr = wc1[wt_i]
                    xp = xpad[ch % 3]
                    for b in blocks:
                        blk = slice(b * TB, (b + 1) * TB)
                        p, pr = rotA.next()

                        def f(e, p=p, blk=blk):
                            for k in range(8):
                                ins = e.matmul(p[:], w[:, k, cc * 128:(cc + 1) * 128], hT[:, k, blk], start=(k == 0), stop=(k == 7))
                            return ins
                        S.add("pe", f, reads=[wr, hT.r(b)], writes=[pr])
                        if grp == 0:
                            ov = xp[:, 1:1 + 2 * 258].rearrange("p (a b) -> p a b", b=258)[:, :, 0:256]
                            S.add("act", lambda e, p=p, ov=ov: e.activation(out=ov, in_=p[:].rearrange("p (a b) -> p a b", a=2), func=AF.Copy), reads=[pr], writes=[xp.r()])
                        else:
                            o0 = 1 + (b - 1) * 512
                            S.add("act", lambda e, p=p, o0=o0: e.activation(out=xp[:, o0:o0 + 512], in_=p[:], func=AF.Copy), reads=[pr], writes=[xp.r()])

                def c1B(ch):
                    xp = xpad[ch % 3]
                    yp = ypad[ch % 3]
                    xf = xsf[ch % 3]
                    conv4("dve", xp, yp, NPg, lambda j: R_CSW + j * 10 + ch, R_CSB + ch)
                    for si, po in enumerate(poff):
                        if ch <= 8:
                            dst, dreg = xf[:, si * slen:(si + 1) * slen], xf.r()
                        else:
                            dst, dreg = CT[:, si * slen:(si + 1) * slen], CT.r()
                        S.add("act", lambda e, po=po, dst=dst: e.activation(out=dst, in_=yp[:, po:po + slen], func=AF.Silu), reads=[yp.r()], writes=[dreg])
                    if ch == 8:
                        S.add("act", lambda e: e.activation(out=BT[:], in_=xf[:], func=AF.Copy), reads=[xf.r()], writes=[BT.r()])

                def c1C(ch):
                    xf = xsf[ch % 3]
                    if ch <= 8:
                        for t4 in range(0, nt, 4):
                            p, pr = rotT.next()

                            def ftr(e, p=p, t4=t4):
                                for q in range(4):
                                    ins = e.transpose(p[:, q * 128:(q + 1) * 128], xf[:, (t4 + q) * 128:(t4 + q + 1) * 128], ident)
                                return ins
                            S.add("pe", ftr, reads=[xf.r(), cf32.r()], writes=[pr])
                            pv = p[:].rearrange("p (q n) -> p q n", q=4)
                            if ch < 8:
                                S.add("dve", lambda e, pv=pv, t4=t4: e.tensor_copy(out=xs_tok[:, t4:t4 + 4, ch * 128:(ch + 1) * 128], in_=pv),
                                      reads=[pr], writes=[xs_tok.r()])
                            else:
                                S.add("dve", lambda e, pv=pv, t4=t4: e.tensor_copy(out=B_tok[:, t4:t4 + 4, :], in_=pv), reads=[pr], writes=[B_tok.r()])

                for step in range(12):
                    if step < 10:
                        c1A(step)
                    if 0 <= step - 1 < 10:
                        c1B(step - 1)
                    if 0 <= step - 2 < 10:
                        c1C(step - 2)
                A.free(*xpad, *ypad, *xsf)
                if SUB == 1:
                    return
                sm = lambda name, w=32: A.alloc(name, [128, nt, w], F32)
                dtr = sm("dtr")
                dta = sm("dta")
                aal = sm("aal")
                stats = sm("stats", 64)
                nacs = sm("nacs")
                eacs = sm("eacs")
                wdec = sm("wdec")
                cdec = sm("cdec")
                cdsel = A.alloc("cdsel", [128, nt, 2, 8], F32)
                wz0, wz0r = wget()
                wz1, wz1r = wget()
                for ti in range(nt):
                    tg = t0 + ti
                    b = tg // 4
                    for hf, (wz, wzr) in enumerate(((wz0, wz0r), (wz1, wz1r))):
                        p, pr = rotA.next()

                        def f(e, p=p, wz=wz, tg=tg):
                            for k in range(8):
                                ins = e.matmul(p[:], hT[:, k, tg * 128:(tg + 1) * 128], wz[:, k, :], start=(k == 0), stop=(k == 7))
                            return ins
                        S.add("pe", f, reads=[wzr, hT.r(b)], writes=[pr])
                        S.add("act", lambda e, p=p, ti=ti, hf=hf: e.activation(out=z_tok[:, ti, hf * 512:(hf + 1) * 512], in_=p[:], func=AF.Silu),
                              reads=[pr], writes=[z_tok.r()])
                wd, wdr = wget()
                for ti in range(nt):
                    tg = t0 + ti
                    b = tg // 4
                    p, pr = rotA.next()

                    def f(e, p=p, tg=tg):
                        for k in range(8):
                            ins = e.matmul(p[:, 0:32], hT[:, k, tg * 128:(tg + 1) * 128], wd[:, k, :], start=(k == 0), stop=(k == 7))
                        return ins
                    S.add("pe", f, reads=[wdr, hT.r(b)], writes=[pr])
                    S.add("dve", lambda e, p=p, ti=ti: e.tensor_tensor(out=dtr[:, ti, :], in0=p[:, 0:32], in1=bv[:, B_DTB:B_DTB + 32], op=ALU.add),
                          reads=[pr, bv.r()], writes=[dtr.r()])
                S.add("act", lambda e: e.activation(out=dtr[:], in_=dtr[:], func=AF.Exp), reads=[dtr.r()], writes=[dtr.r()])
                S.add("act", lambda e: e.activation(out=dta[:], in_=dtr[:], func=AF.Ln, bias=ONE_AP[:, 0:1], scale=1.0), reads=[dtr.r(), ONE_AP.r()], writes=[dta.r()])
                S.add("dve", lambda e: e.tensor_tensor(out=aal[:], in0=dta[:], in1=aneg[:].unsqueeze(1).to_broadcast([128, nt, 32]), op=ALU.mult),
                      reads=[dta.r(), aneg.r()], writes=[aal.r()])
                ahi = A.alloc("ahi", [128, nt, 32], BF16)
                alo = A.alloc("alo", [128, nt, 32], BF16)
                S.add("act", lambda e: e.activation(out=ahi[:], in_=aal[:], func=AF.Copy), reads=[aal.r()], writes=[ahi.r()])
                S.add("dve", lambda e: e.tensor_tensor(out=alo[:], in0=aal[:], in1=ahi[:], op=ALU.subtract), reads=[aal.r(), ahi.r()], writes=[alo.r()])
                for ti in range(nt):
                    p, pr = rotT.next()
                    mm(p[:, 0:16], utri, aal[:, ti, 0:16], True, True, [cf32.r(), aal.r()], [pr])
                    mm(p[:, 16:32], ltri, aal[:, ti, 16:32], True, True, [cf32.r(), aal.r()], [pr])
                    mm(p[:, 32:64], onesf, aal[:, ti, :], True, True, [cf32.r(), aal.r()], [pr])
                    S.add("dve", lambda e, p=p, ti=ti: e.tensor_copy(out=stats[:, ti, :], in_=p[:, 0:64]), reads=[pr], writes=[stats.r()])
                S.add("act", lambda e: e.activation(out=nacs[:], in_=dta[:], func=AF.Ln), reads=[dta.r()], writes=[nacs.r()])
                S.add("dve", lambda e: e.tensor_tensor(out=nacs[:], in0=nacs[:], in1=stats[:, :, 0:32], op=ALU.subtract),
                      reads=[stats.r(), nacs.r()], writes=[nacs.r()])
                S.add("act", lambda e: e.activation(out=eacs[:], in_=stats[:, :, 0:32], func=AF.Exp), reads=[stats.r()], writes=[eacs.r()])
                S.add("dve", lambda e: e.tensor_tensor(out=wdec[:], in0=stats[:, :, 32:64], in1=stats[:, :, 0:32], op=ALU.subtract), reads=[stats.r()], writes=[wdec.r()])
                S.add("act", lambda e: e.activation(out=wdec[:], in_=wdec[:], func=AF.Exp), reads=[wdec.r()], writes=[wdec.r()])
                S.add("dve", lambda e: e.tensor_tensor(out=wdec[:], in0=wdec[:], in1=dta[:], op=ALU.mult), reads=[wdec.r(), dta.r()], writes=[wdec.r()])
                S.add("act", lambda e: e.activation(out=cdec[:], in_=stats[:, :, 32:64], func=AF.Exp), reads=[stats.r()], writes=[cdec.r()])
                for g in range(2):
                    src = cdec[g * 64:(g + 1) * 64, :, :].rearrange("p t (d g h) -> p t d g h", d=2, g=2)[:, :, :, g, :]
                    S.add("dve", lambda e, g=g, src=src: e.tensor_copy(out=cdsel[g * 64:(g + 1) * 64, :, :, :], in_=src), reads=[cdec.r()], writes=[cdsel.r()])
                A.free(dtr, cdec, stats, aal)
                if SUB == 2:
                    return
                Sst = [A.alloc("Sst%d" % d, [128, 512], F32) for d in range(2)]
                hinb = A.alloc("hinb", [128, nt, 512], BF16)
                hinf = A.alloc("hinf", [128, 512], BF16)
                xw = [A.alloc("xw0", [128, 1024], BF16)] * 2
                stmp = A.alloc("stmp", [128, 512], F32)

                class _Stg:
                    def __getitem__(self, k):
                        return stmp[:].rearrange("p (a n) -> p a n", a=8)[k]

                    def r(self):
                        return stmp.r()
                stg = _Stg()
                rotP = Rot([5, 6])
                nxw = [0]

                def init_state(d):
                    St = Sst[d]
                    if grp == 0:
                        S.add("pool", lambda e: e.memset(St[:], 0.0), writes=[St.r()])
                    else:
                        dma("sp", stg[:], d_sssd[d].rearrange("(a p) n -> p a n", p=128), writes=[stg.r()])
                        for g in range(2):
                            p, pr = rotP.next()

                            def f(e, p=p, g=g):
                                for a4 in range(4):
                                    ins = e.transpose(p[0:64, a4 * 128:(a4 + 1) * 128], stg[:, g * 4 + a4, :], ident)
                                return ins
                            S.add("pe", f, reads=[stg.r(), cf32.r()], writes=[pr])
                            S.add("dve", lambda e, p=p, g=g: e.tensor_copy(out=St[g * 64:(g + 1) * 64, :], in_=p[0:64, :]), reads=[pr], writes=[St.r()])

                def out_state(d, sidx):
                    St = Sst[d]
                    for g in range(2):
                        p, pr = rotP.next()

                        def f(e, p=p, g=g):
                            for s4 in range(4):
                                ins = e.transpose(p[:, s4 * 64:(s4 + 1) * 64], St[g * 64:(g + 1) * 64, s4 * 128:(s4 + 1) * 128], ident[g * 64:(g + 1) * 64, g * 64:(g + 1) * 64])
                            return ins
                        S.add("pe", f, reads=[St.r(), cf32.r()], writes=[pr])
                        S.add("dve", lambda e, p=p, g=g: e.tensor_copy(out=stg[:, g * 4:(g + 1) * 4, :], in_=p[:, 0:256].rearrange("p (a n) -> p a n", a=4)), reads=[pr], writes=[stg.r()])
                    dma("sp", o_ssd[sidx, d].rearrange("(a p) n -> p a n", p=128), stg[:], reads=[stg.r()])

                def state_update(d, ti):
                    St = Sst[d]
                    x_ = xw[nxw[0] % 2]
                    nxw[0] += 1
                    S.add("dve", lambda e, x_=x_, ti=ti, d=d: e.tensor_tensor(
                        out=x_[:].rearrange("p (h q) -> p h q", h=16), in0=xs_tok[:, ti, :].rearrange("p (h q) -> p h q", h=16),
                        in1=wdec[:, ti, d * 16:(d + 1) * 16].unsqueeze(2).to_broadcast([128, 16, 64]), op=ALU.mult),
                        reads=[xs_tok.r(), wdec.r()], writes=[x_.r()])
                    p, pr = rotP.next()
                    for g in range(2):
                        mm(p[g * 64:(g + 1) * 64, :], B_tok[:, ti, g * 64:(g + 1) * 64], x_[:, g * 512:(g + 1) * 512], True, True, [B_tok.r(), x_.r()], [pr])
                    S.add("dve", lambda e, ti=ti, d=d: e.tensor_tensor(
                        out=stmp[:].rearrange("p (h q) -> p h q", h=8), in0=St[:].rearrange("p (h q) -> p h q", h=8),
                        in1=cdsel[:, ti, d, :].unsqueeze(2).to_broadcast([128, 8, 64]), op=ALU.mult),
                        reads=[St.r(), cdsel.r()], writes=[stmp.r()])
                    S.add("dve", lambda e, p=p: e.tensor_tensor(out=St[:], in0=p[:], in1=stmp[:], op=ALU.add), reads=[pr, stmp.r()], writes=[St.r()])

                for sidx, sq_ in enumerate(seqs):
                    init_state(1)
                    for ti in reversed(sq_):
                        S.add("act", lambda e, ti=ti: e.activation(out=hinb[:, ti, :], in_=Sst[1][:], func=AF.Copy), reads=[Sst[1].r()], writes=[hinb.r()])
                        if SUB != 5:
                            state_update(1, ti)
                    if grp == 0 and SUB != 4:
                        out_state(1, sidx)
                if SUB in (3, 4, 5):
                    return
                cbs = [A.alloc("cbs%d" % i, [128, 2, 128], F32) for i in range(2)]
                decb = [A.alloc("decb%d" % i, [128, 2, 128], F32) for i in range(3)]
                MTb = [A.alloc("MTb%d" % i, [128, 2, 128], BF16) for i in range(3)]
                yaccs = [A.alloc("yacc%d" % i, [128, 1024], F32) for i in range(2)]
                ytmp = A.alloc("ytmp", [128, 1024], F32)
                ssq = A.alloc("ssq", [128, 2], F32)
                rotD = Rot([1, 2, 3, 4])
                nm_ = 0
                tiles_seq = []
                for sidx, sq_ in enumerate(seqs):
                    for i_, ti in enumerate(sq_):
                        tiles_seq.append((sidx, ti, i_ == 0, i_ == len(sq_) - 1))

                def make_tail(sidx, ti, first, last, yacc):
                    tg = t0 + ti
                    tcol = slice(ti * 128, (ti + 1) * 128)
                    th = []
                    if first:
                        th.append(lambda: init_state(0))
                    th.append(lambda: S.add("act", lambda e: e.activation(out=hinf[:], in_=Sst[0][:], func=AF.Copy), reads=[Sst[0].r()], writes=[hinf.r()]))
                    for d in range(2):
                        for g in range(2):
                            def yo(d=d, g=g):
                                hin = hinf[:] if d == 0 else hinb[:, ti, :]
                                hreg = hinf.r() if d == 0 else hinb.r()
                                p, pr = rotP.next()
                                mm(p[:], CT[g * 64:(g + 1) * 64, tcol], hin[g * 64:(g + 1) * 64, :], True, True, [CT.r(), hreg], [pr])
                                ea = eacs[:, ti, d * 16 + g * 8:d * 16 + g * 8 + 8].unsqueeze(2).to_broadcast([128, 8, 64])
                                pv = p[:].rearrange("p (h q) -> p h q", h=8)
                                tsl = ytmp[:, g * 512:(g + 1) * 512].rearrange("p (h q) -> p h q", h=8)
                                S.add("dve", lambda e: e.tensor_tensor(out=tsl, in0=pv, in1=ea, op=ALU.mult), reads=[pr, eacs.r()], writes=[ytmp.r()])
                                S.add("dve", lambda e: e.tensor_tensor(out=yacc[:, g * 512:(g + 1) * 512], in0=yacc[:, g * 512:(g + 1) * 512],
                                                                       in1=ytmp[:, g * 512:(g + 1) * 512], op=ALU.add),
                                      reads=[yacc.r(), ytmp.r()], writes=[yacc.r()])
                            th.append(yo)
                    th.append(lambda: S.add("dve", lambda e: e.tensor_tensor(
                        out=ytmp[:].rearrange("p (h q) -> p h q", h=16), in0=xs_tok[:, ti, :].rearrange("p (h q) -> p h q", h=16),
                        in1=bv[:, B_D:B_D + 16].unsqueeze(2).to_broadcast([128, 16, 64]), op=ALU.mult),
                        reads=[xs_tok.r(), bv.r(), yacc.r()], writes=[ytmp.r()]))
                    th.append(lambda: S.add("dve", lambda e: e.tensor_tensor(out=yacc[:], in0=yacc[:], in1=ytmp[:], op=ALU.add), reads=[yacc.r(), ytmp.r()], writes=[yacc.r()]))
                    th.append(lambda: S.add("dve", lambda e: e.tensor_tensor(out=yacc[:], in0=yacc[:], in1=z_tok[:, ti, :], op=ALU.mult), reads=[yacc.r(), z_tok.r()], writes=[yacc.r()]))
                    th.append(lambda: S.add("act", lambda e: e.activation(out=ytmp[:], in_=yacc[:], func=AF.Square, accum_out=ssq[:, 0:1]), reads=[yacc.r()], writes=[ytmp.r(), ssq.r()]))
                    th.append(lambda: S.add("act", lambda e: e.activation(out=ssq[:, 0:1], in_=ssq[:, 0:1], func=AF.Ln, bias=EPS_AP[:, 0:1], scale=1.0 / 1024), reads=[ssq.r(), EPS_AP.r()], writes=[ssq.r()]))
                    th.append(lambda: S.add("act", lambda e: e.activation(out=ssq[:, 0:1], in_=ssq[:, 0:1], func=AF.Exp, scale=-0.5), reads=[ssq.r()], writes=[ssq.r()]))
                    th.append(lambda: S.add("dve", lambda e: e.scalar_tensor_tensor(out=ytmp[:], in0=yacc[:], scalar=ssq[:, 0:1], in1=bv[:, B_NORM:B_NORM + 1024], op0=ALU.mult, op1=ALU.mult),
                                            reads=[yacc.r(), ssq.r(), bv.r()], writes=[ytmp.r()]))
                    for hf in range(2):
                        def trp(hf=hf):
                            p, pr = PS(7) if hf == 0 else rotP.next()

                            def ftr(e):
                                for c in range(4):
                                    ins = e.transpose(p[:, c * 128:(c + 1) * 128], ytmp[:, (hf * 4 + c) * 128:(hf * 4 + c + 1) * 128], ident)
                                return ins
                            S.add("pe", ftr, reads=[ytmp.r(), cf32.r()], writes=[pr])
                            outv = hT[:, hf * 4:hf * 4 + 4, tg * 128:(tg + 1) * 128]
                            S.add("act", lambda e: e.activation(out=outv, in_=p[:].rearrange("p (c n) -> p c n", c=4), func=AF.Copy),
                                  reads=[pr], writes=[hT.r(tg // 4)])
                        th.append(trp)
                    th.append(lambda: state_update(0, ti))
                    if last and grp == 0:
                        th.append(lambda: out_state(0, sidx))
                    return th

                pending = []
                for tix, (sidx, ti, first, last) in enumerate(tiles_seq):
                    yacc = yaccs[tix % 2]
                    tcol = slice(ti * 128, (ti + 1) * 128)
                    cb_ = cbs[ti % 2]
                    for g in range(2):
                        p, pr = PS(7) if g == 0 else rotP.next()
                        mm(p[:, 0:128], BT[g * 64:(g + 1) * 64, tcol], CT[g * 64:(g + 1) * 64, tcol], True, True, [BT.r(), CT.r()], [pr])
                        S.add("act", lambda e, p=p, cb_=cb_, g=g: e.activation(out=cb_[:, g, :], in_=p[:, 0:128], func=AF.Copy), reads=[pr], writes=[cb_.r()])
                    ydp, ydr = PS(0)
                    munits = [(hp, d) for hp in range(8) for d in range(2)]
                    mts = {}

                    def emit_pdc(u, ti=ti, cb_=cb_, mts=mts, munits=munits):
                        nonlocal nm_
                        hp, d = munits[u]
                        g = hp // 4
                        pd, pdr = rotD.next()

                        def f(e, pd=pd, ti=ti, hp=hp, d=d):
                            for i_ in range(2):
                                hc = d * 16 + 2 * hp + i_
                                o_ = pd[:, i_ * 128:(i_ + 1) * 128]
                                e.matmul(o_, ahi[:, ti, hc:hc + 1].to_broadcast([128, 128]), tri[d], start=True, stop=False)
                                e.matmul(o_, alo[:, ti, hc:hc + 1].to_broadcast([128, 128]), tri[d], start=False, stop=False)
                                ins = e.matmul(o_, nmk[d], identb, start=False, stop=True)
                            return ins
                        S.add("pe", f, reads=[ahi.r(), alo.r(), cb16.r()], writes=[pdr])
                        dc = decb[nm_ % 3]
                        mt = MTb[nm_ % 3]
                        nm_ += 1
                        for i_ in range(2):
                            hc = d * 16 + 2 * hp + i_
                            S.add("act", lambda e, pd=pd, dc=dc, ti=ti, hc=hc, i_=i_: e.activation(out=dc[:, i_, :], in_=pd[:, i_ * 128:(i_ + 1) * 128], func=AF.Exp,
                                                                                                     bias=nacs[:, ti, hc:hc + 1], scale=1.0),
                                  reads=[pdr, nacs.r()], writes=[dc.r()])
                        S.add("pool", lambda e, dc=dc, mt=mt, g=g, cb_=cb_: e.tensor_tensor(
                            out=mt[:], in0=dc[:], in1=cb_[:, g, :].unsqueeze(1).to_broadcast([128, 2, 128]), op=ALU.mult),
                            reads=[dc.r(), cb_.r()], writes=[mt.r()])
                        mts[u] = mt

                    LAm = 3
                    for u in range(LAm):
                        emit_pdc(u)
                    for u, (hp, d) in enumerate(munits):
                        mt = mts.pop(u)

                        def fy(e, mt=mt, hp=hp, d=d, u=u, ti=ti):
                            for i_ in range(2):
                                h = 2 * hp + i_
                                ins = e.matmul(ydp[:, (h % 8) * 64:(h % 8 + 1) * 64], mt[:, i_, :], xs_tok[:, ti, h * 64:(h + 1) * 64],
                                               start=(u % 8 == 0 and i_ == 0), stop=(d == 1), skip_group_check=True)
                            return ins
                        S.add("pe", fy, reads=[mt.r(), xs_tok.r()], writes=[ydr])
                        if u + LAm < len(munits):
                            emit_pdc(u + LAm)
                        if pending:
                            pending.pop(0)()
                        if u % 8 == 7:
                            g = u // 8
                            S.add("act", lambda e, g=g, yacc=yacc: e.activation(out=yacc[:, g * 512:(g + 1) * 512], in_=ydp[:], func=AF.Copy), reads=[ydr], writes=[yacc.r()])
                        if u == 15:
                            while pending:
                                pending.pop(0)()
                    pending = pending + make_tail(sidx, ti, first, last, yacc)
                while pending:
                    pending.pop(0)()
                A.free(xs_tok, z_tok, B_tok, BT, CT, dta, ahi, alo, nacs, eacs, wdec, cdsel, *Sst, hinb, hinf, xw[0], stmp,
                       *cbs, *decb, *MTb, *yaccs, ytmp, ssq)
            for grp in range((2 if SUB == 0 else 1) if STAGE >= 13 else 0):
                ssd_group(grp if SUB < 10 else 1)
            A.free(aneg)
            while wstate["got"] < wstate["woo"]:
                wget()
            rotW = Rot([0, 1, 2, 3])
            for t in range(4):
                w, wr = wget()
                for oc in range(2):
                    o = t * 2 + oc
                    for b in range(3):
                        blk = slice(b * TB, (b + 1) * TB)
                        p, pr = rotW.next()

                        def f(e, p=p, w=w, oc=oc, blk=blk):
                            for k in range(12):
                                rhs = yl[:, k, blk] if k < 4 else hT[:, k - 4, blk]
                                ins = e.matmul(p[:], w[:, k, oc * 128:(oc + 1) * 128], rhs, start=(k == 0), stop=(k == 11))
                            return ins
                        S.add("pe", f, reads=[wr, hT.r(b), yl.r()], writes=[pr])
                        resid(p, pr, l, 2, o, b)
            A.free(yl)

        if STAGE >= 1:
            for t in range(4):
                ada_tile(0, t, 0)
            ada_finish(0, 0)
        if STAGE >= 2:
            even_layer(0)
        if STAGE >= 10:
            ffn(0)
        if NLAYERS > 1:
            odd_layer(1)
            if STAGE >= 20:
                ffn(1)

        yst = [A.alloc("yst%d" % i, [128, D], F32) for i in range(2)]
        rot = Rot([0, 1, 2, 3, 4, 5, 6, 7])
        for t in range(12):
            ys = yst[t % 2]
            for hf in range(2):
                p, pr = rot.next()

                def tr(e, p=p, t=t, hf=hf):
                    for c in range(4):
                        ins = e.transpose(p[:, c * 128:(c + 1) * 128], xT[:, hf * 4 + c, t * 128:(t + 1) * 128], ident)
                    return ins
                S.add("pe", tr, reads=[xT.r(t), cf32.r()], writes=[pr])
                if hf == 0:
                    S.add("act", lambda e, ys=ys, p=p: e.activation(out=ys[:, 0:512], in_=p[:], func=AF.Copy), reads=[pr], writes=[ys.r()])
                else:
                    S.add("dve", lambda e, ys=ys, p=p: e.tensor_copy(out=ys[:, 512:1024], in_=p[:]), reads=[pr], writes=[ys.r()])
            dma("sp", o_y[t * 128:(t + 1) * 128, :], ys[:], reads=[ys.r()])
        assert STAGE < 99 or wstate["got"] == len(wplan), (wstate, len(wplan))
        S.emit(st)
    return nc


_CACHE = {}


def _prep_shared(inp):
    f = np.float32
    sh = {}
    sh["w_ada"] = np.ascontiguousarray(inp["w_ada"], f)
    wie = np.asarray(inp["w_in_even"][0], f)
    fcols = wie[:, 0:256]
    q = wie[:, 256:1024].reshape(D, 12, 64)[:, PERM, :].reshape(D, 768)
    k = wie[:, 1024:1280]
    v = wie[:, 1280:1536]
    sh["w_in_even"] = np.ascontiguousarray(np.concatenate([q, k, fcols, v], 1))
    woe = np.asarray(inp["w_out_even"][0], f)
    att = woe[256:].reshape(12, 64, D)[PERM].reshape(768, D)
    sh["w_out_even"] = np.ascontiguousarray(np.concatenate([woe[:256], att], 0))
    sh["w_in_odd"] = np.ascontiguousarray(inp["w_in_odd"][0], f)
    sh["w_out_odd"] = np.ascontiguousarray(inp["w_out_odd"][0], f)
    sh["ffn_w1"] = np.ascontiguousarray(inp["ffn_w1"], f)
    sh["ffn_w3"] = np.ascontiguousarray(inp["ffn_w3"], f)
    sh["ffn_w2"] = np.ascontiguousarray(inp["ffn_w2"], f)
    bd = np.zeros((2, 2, 4, 128, 128), f)
    for gi, wname in enumerate(("lru_wa", "lru_wx")):
        wsrc = np.asarray(inp[wname][0], f)
        for d in range(2):
            for blk in range(8):
                c, hlf = blk // 2, blk % 2
                bd[gi, d, c, hlf * 64:(hlf + 1) * 64, hlf * 64:(hlf + 1) * 64] = wsrc[d, blk]
    sh["lru_bd"] = bd.reshape(16, 128, 128)
    bvec = np.zeros((1, NBV), f)
    bvec[0, B_NORM:B_NORM + 1024] = inp["ssd_norm"][0]
    bvec[0, B_D:B_D + 16] = inp["ssd_d"][0]
    bvec[0, B_DTB:B_DTB + 32] = np.asarray(inp["ssd_dt_bias"][0]).reshape(32)
    bvec[0, B_ALOG:B_ALOG + 32] = np.asarray(inp["ssd_a_log"][0]).reshape(32)
    sh["bvec"] = bvec
    sh.update(_consts())
    return sh


def _vecs(inp, i):
    f = np.float32
    v = np.zeros((256, 128), f)
    v[R_BADA:R_BADA + 96] = np.asarray(inp["b_ada"], f).reshape(96, 128)
    v[R_CCTX:R_CCTX + 8] = np.asarray(inp["c_ctx"], f).reshape(8, 128)
    v[R_CI:R_CI + 8] = np.asarray(inp["c"][i], f).reshape(8, 128)
    v[R_NMIX:R_NMIX + 16] = np.asarray(inp["norm_mix"], f).reshape(16, 128)
    v[R_NFFN:R_NFFN + 16] = np.asarray(inp["norm_ffn"], f).reshape(16, 128)
    v[R_CLW:R_CLW + 16] = np.asarray(inp["conv_lru_w"][0], f).reshape(16, 128)
    v[R_CLB:R_CLB + 4] = np.asarray(inp["conv_lru_b"][0], f).reshape(4, 128)
    v[R_BA:R_BA + 8] = np.asarray(inp["lru_ba"][0], f).reshape(8, 128)
    v[R_BX:R_BX + 8] = np.asarray(inp["lru_bx"][0], f).reshape(8, 128)
    v[R_LAM:R_LAM + 8] = np.asarray(inp["lru_lambda"][0], f).reshape(8, 128)
    v[R_CSW:R_CSW + 40] = np.asarray(inp["conv_ssd_w"][0], f).reshape(40, 128)
    v[R_CSB:R_CSB + 10] = np.asarray(inp["conv_ssd_b"][0], f).reshape(10, 128)
    v[R_QN] = np.tile(np.asarray(inp["q_norm"][0], f), 2)
    v[R_KN] = np.tile(np.asarray(inp["k_norm"][0], f), 2)
    v[R_LRU0:R_LRU0 + 8] = np.asarray(inp["state_lru"][i, 0], f).reshape(8, 128)
    return v


def kernel(**inp):
    inp = {k: np.asarray(v) for k, v in inp.items()}
    if "nc" not in _CACHE:
        _CACHE["nc"] = build_program()
    nc = _CACHE["nc"]
    sh = _prep_shared(inp)
    in_maps = []
    for i in range(NCORES):
        m = dict(sh)
        m["x"] = np.ascontiguousarray(np.concatenate(
            [inp["x_prompt"][2 * i], inp["x_prompt"][2 * i + 1], inp["x_sample"][i]], 0), np.float32)
        m["vecs"] = _vecs(inp, i)
        m["cache_k"] = np.ascontiguousarray(inp["cache_k"][i, 0].reshape(512, 256), np.float32)
        m["cache_v"] = np.ascontiguousarray(inp["cache_v"][i, 0].reshape(512, 256), np.float32)
        m["state_ssd"] = np.ascontiguousarray(inp["state_ssd"][i, 0].reshape(2, 1024, 64), np.float32)
        in_maps.append(m)
    res = run_bass_kernel_spmd(nc, in_maps[:NRUN], core_ids=list(range(NRUN)))
    R = res.results
    y_prompt = np.zeros((16, 256, D), np.float32)
    y_sample = np.zeros((8, 1024, D), np.float32)
    new_k = np.zeros((16, 1, 256, 4, 64), np.float32)
    new_v = np.zeros((16, 1, 256, 4, 64), np.float32)
    new_lru = np.zeros((16, 1, 2, 512), np.float32)
    new_ssd = np.zeros((16, 1, 2, 16, 64, 64), np.float32)
    for i in range(NRUN):
        y = np.asarray(R[i]["y"])
        y_prompt[2 * i] = y[0:256]
        y_prompt[2 * i + 1] = y[256:512]
        y_sample[i] = y[512:]
        nk = np.asarray(R[i]["newk"]).reshape(2, 256, 4, 64)
        nv = np.asarray(R[i]["newv"]).reshape(2, 256, 4, 64)
        new_k[2 * i:2 * i + 2, 0] = nk
        new_v[2 * i:2 * i + 2, 0] = nv
        new_lru[2 * i:2 * i + 2, 0] = np.asarray(R[i]["newlru"]).reshape(2, 2, 512)
        new_ssd[2 * i:2 * i + 2, 0] = np.asarray(R[i]["newssd"]).reshape(2, 2, 16, 64, 64)
    return (y_prompt, y_sample, new_k, new_v, new_lru, new_ssd)
```

```python
import numpy as np
from contextlib import ExitStack
import concourse.bass as bass
import concourse.mybir as mybir
from concourse.bass_utils import run_bass_kernel_spmd

F32 = mybir.dt.float32
BF16 = mybir.dt.bfloat16
AF = mybir.ActivationFunctionType
ALU = mybir.AluOpType

ENGS = ("pe", "act", "dve", "pool", "sp")
NDMASEM = 6
NCORES = 8
NRUN = 8
EPS = 1e-6
NLAYERS = 2
STAGE = 99
SUB = 0


class Reg:
    __slots__ = ("name", "lw", "rd", "excl")

    def __init__(self, name, inherit=(), excl=False):
        self.name = name
        self.lw = None
        self.rd = list(inherit)
        self.excl = excl


class Op:
    __slots__ = ("eng", "fn", "deps", "needed", "dma", "sem", "val", "gidx", "thr")

    def __init__(self, eng, fn, dma):
        self.eng = eng
        self.fn = fn
        self.dma = dma
        self.deps = []
        self.needed = False
        self.sem = None
        self.val = 0
        self.thr = None


class Sched:
    def __init__(self, nc):
        self.nc = nc
        self.ops = {e: [] for e in ENGS}
        self.all = []
        self.ndma = {e: 0 for e in ENGS}
        self.dmaops = {e: [] for e in ENGS}

    def add(self, eng, fn, reads=(), writes=(), dma=False):
        op = Op(eng, fn, dma)
        deps = {}
        for r in reads:
            w = r.lw
            if w is not None and (w.dma or dma or w.eng != eng or eng != "pe"):
                deps[id(w)] = w
            if r.excl:
                for q in r.rd:
                    if q.eng != eng:
                        deps[id(q)] = q
        for t in writes:
            w = t.lw
            if w is not None and (w.dma or dma or w.eng != eng or eng != "pe"):
                deps[id(w)] = w
            for q in t.rd:
                if q.dma or dma or q.eng != eng or eng != "pe":
                    deps[id(q)] = q
        op.deps = list(deps.values())
        for r in reads:
            if not dma:
                r.rd = [q for q in r.rd if q.dma or q.eng != eng]
            r.rd.append(op)
        for t in writes:
            t.lw = op
            t.rd = []
        if dma:
            i = self.ndma[eng]
            self.ndma[eng] += 1
            if i >= NDMASEM:
                op.thr = self.dmaops[eng][i - NDMASEM]
            self.dmaops[eng].append(op)
        op.gidx = len(self.all)
        self.all.append(op)
        self.ops[eng].append(op)
        return op

    def emit(self, stack):
        nc = self.nc
        for op in self.all:
            for d in op.deps:
                d.needed = True
        esem = {e: stack.enter_context(nc.semaphore("s_" + e)) for e in ENGS if e != "sp"}
        dsem = {e: [stack.enter_context(nc.semaphore("d_%s%d" % (e, i))) for i in range(NDMASEM)]
                for e in ENGS if self.ndma[e] > 0}
        for e in ENGS:
            cnt = 0
            i = 0
            for op in self.ops[e]:
                if op.dma:
                    op.sem = dsem[e][i % NDMASEM]
                    op.val = 16 * (i // NDMASEM + 1)
                    i += 1
                elif op.needed:
                    cnt += 1
                    op.sem = esem[e]
                    op.val = cnt
        block = stack.enter_context(nc.Block())
        engh = {"pe": block.tensor, "act": block.scalar, "dve": block.vector,
                "pool": block.gpsimd, "sp": block.sync}
        for e in ENGS:
            ops = self.ops[e]
            if not ops:
                continue

            def body(eng, ops=ops, e=e):
                waited = {}
                for op in ops:
                    ds = list(op.deps)
                    if op.thr is not None:
                        ds.append(op.thr)
                    for d in ds:
                        k = id(d.sem)
                        if waited.get(k, 0) >= d.val:
                            continue
                        waited[k] = d.val
                        eng.wait_ge(d.sem, d.val)
                    ins = op.fn(eng)
                    if op.dma:
                        ins.then_inc(op.sem, 16)
                    elif op.needed:
                        ins.then_inc(op.sem, 1)
                for d in self.dmaops[e][-NDMASEM:]:
                    if waited.get(id(d.sem), 0) < d.val:
                        waited[id(d.sem)] = d.val
                        eng.wait_ge(d.sem, d.val)

            engh[e](body)


def _prune(ops):
    best = {}
    out = []
    for o in ops:
        if o.dma:
            out.append(o)
        else:
            b = best.get(o.eng)
            if b is None or o.gidx > b.gidx:
                best[o.eng] = o
    return out + list(best.values())


class TV:
    def __init__(self, ap, name, off, nw, inherit):
        self.ap = ap
        self.name = name
        self.off = off
        self.nw = nw
        self.inherit = inherit
        self.regs = {}

    def __getitem__(self, k):
        return self.ap[k]

    def r(self, key=0):
        g = self.regs.get(key)
        if g is None:
            g = Reg("%s.%s" % (self.name, key), self.inherit)
            self.regs[key] = g
        return g

    def rs(self, keys):
        return [self.r(k) for k in keys]


class Arena:
    def __init__(self, nc, stack, nwords):
        self.t = stack.enter_context(nc.sbuf_tensor("arena", [128, nwords], F32))
        self.n = nwords
        self.live = []
        self.dead = []
        self.peak = 0

    def alloc(self, name, shape, dt):
        n = int(np.prod(shape[1:]))
        nw = n if dt == F32 else (n + 1) // 2
        nw = (nw + 7) // 8 * 8
        off = 0
        for tv in sorted(self.live, key=lambda t: t.off):
            if tv.off - off >= nw:
                break
            off = max(off, tv.off + tv.nw)
        assert off + nw <= self.n, "SBUF arena overflow allocating %s (%d words at %d)" % (name, nw, off)
        self.peak = max(self.peak, off + nw)
        pend = []
        keep = []
        for (o, w, ops) in self.dead:
            if o < off + nw and off < o + w:
                pend += ops
                if not (off <= o and o + w <= off + nw):
                    keep.append((o, w, ops))
            else:
                keep.append((o, w, ops))
        self.dead = keep
        v = self.t[0:shape[0], off:off + nw]
        if dt != F32:
            v = v.bitcast(dt)
        v = v[:, 0:n]
        if len(shape) == 3:
            v = v.rearrange("p (a b) -> p a b", a=shape[1])
        elif len(shape) == 4:
            v = v.rearrange("p (a b c) -> p a b c", a=shape[1], b=shape[2])
        tv = TV(v, name, off, nw, _prune(pend))
        self.live.append(tv)
        return tv

    def free(self, *tvs):
        for tv in tvs:
            ops = list(tv.inherit)
            for g in tv.regs.values():
                if g.lw is not None:
                    ops.append(g.lw)
                ops += g.rd
            self.dead.append((tv.off, tv.nw, _prune(ops)))
            self.live.remove(tv)


D = 1024
NTOK = 1536
TB = 512
PERM = [0, 3, 1, 4, 2, 5, 6, 9, 7, 10, 8, 11]
DFF = 2816
R_BADA, R_CCTX, R_CI, R_NMIX, R_NFFN = 0, 96, 104, 112, 128
R_CLW, R_CLB, R_BA, R_BX, R_LAM, R_CSW, R_CSB, R_QN, R_KN, R_LRU0 = 144, 160, 164, 172, 180, 188, 228, 238, 239, 240
B_NORM, B_D, B_DTB, B_ALOG, NBV = 0, 1024, 1040, 1072, 1104


def _consts():
    c = {}
    ident = np.eye(128, dtype=np.float32)
    onesm = np.full((128, 128), 1.0 / 1024, np.float32)
    o64 = np.zeros((128, 128), np.float32)
    o64[:64, :64] = 1.0 / 64
    o64[64:, 64:] = 1.0 / 64
    R = np.zeros((128, 128), np.float32)
    for h in range(2):
        for ax in range(2):
            for i in range(16):
                p1 = h * 64 + ax * 32 + i
                p2 = p1 + 16
                R[p1, p2] = -1.0
                R[p2, p1] = 1.0
    j = np.arange(128)
    utri = (j[:, None] <= j[None, :]).astype(np.float32)
    ltri = (j[:, None] >= j[None, :]).astype(np.float32)
    nmf = np.where(j[:, None] < j[None, :], -32768.0, 0.0).astype(np.float32)
    nmb = np.where(j[:, None] > j[None, :], -32768.0, 0.0).astype(np.float32)
    onesf = np.ones((128, 128), np.float32)
    c["cf32"] = np.stack([ident, R.T.copy(), utri, ltri, onesf])
    a64 = 2 * np.pi * np.outer(np.arange(64), np.arange(64)) / 64
    c64 = np.zeros((2, 128, 128), np.float32)
    for g in range(2):
        c64[0, g * 64:(g + 1) * 64, g * 64:(g + 1) * 64] = np.cos(a64) / 8
        c64[1, g * 64:(g + 1) * 64, g * 64:(g + 1) * 64] = -np.sin(a64) / 8
    c["cb16"] = np.stack([onesm, o64, nmf, nmb, ident, c64[0], c64[1], utri, ltri])
    for S in (256, 1024):
        a = 2 * np.pi * (np.outer(np.arange(S), np.arange(S)) % S) / S
        c["dft%d" % S] = np.stack([np.cos(a), np.sin(a)]).astype(np.float32) / np.sqrt(S)
    s = np.arange(1024)
    row = (s // 64).astype(np.float32)
    col = (s % 64).astype(np.float32)
    freqs = (10000.0 ** (-np.arange(16, dtype=np.float32) / 16)).astype(np.float32)
    ang = np.zeros((64, 1024), np.float32)
    for d in range(64):
        ax = d // 32
        i = d % 16
        ang[d] = (row if ax == 0 else col) * freqs[i]
    ang = np.concatenate([ang, ang], 0)
    c["rope"] = np.stack([np.cos(ang), np.sin(ang)]).astype(np.float32)
    return c


def build_program():
    nc = bass.Bass("TRN2", target_bir_lowering=False)
    S = Sched(nc)

    def din(name, shape):
        return nc.dram_tensor(name, list(shape), F32, kind="ExternalInput").ap()

    def dout(name, shape):
        return nc.dram_tensor(name, list(shape), F32, kind="ExternalOutput").ap()

    d_x = din("x", [NTOK, D])
    d_vecs = din("vecs", [256, 128])
    d_bvec = din("bvec", [1, NBV])
    d_wada = din("w_ada", [2, D, 6 * D])
    d_wie = din("w_in_even", [D, 1536])
    d_woe = din("w_out_even", [D, D])
    d_wio = din("w_in_odd", [D, 3360])
    d_woo = din("w_out_odd", [1536, D])
    d_w1 = din("ffn_w1", [2, D, DFF])
    d_w3 = din("ffn_w3", [2, D, DFF])
    d_w2 = din("ffn_w2", [2, DFF, D])
    d_lrubd = din("lru_bd", [16, 128, 128])
    d_ck = din("cache_k", [512, 256])
    d_cv = din("cache_v", [512, 256])
    d_sssd = din("state_ssd", [2, 1024, 64])
    d_cf32 = din("cf32", [5, 128, 128])
    d_cb16 = din("cb16", [9, 128, 128])
    d_dft256 = din("dft256", [2, 256, 256])
    d_dft1024 = din("dft1024", [2, 1024, 1024])
    d_rope = din("rope", [2, 128, 1024])
    o_y = dout("y", [NTOK, D])
    o_k = dout("newk", [512, 256])
    o_v = dout("newv", [512, 256])
    o_lru = dout("newlru", [16, 128])
    o_ssd = dout("newssd", [2, 2, 1024, 64])

    with ExitStack() as st:
        A = Arena(nc, st, 53100)
        psb = []
        for i in range(8):
            t = st.enter_context(nc.psum_tensor("ps%d" % i, [128, 512], F32))
            psb.append((t, Reg("ps%d" % i, excl=True)))

        def PS(i):
            return psb[i]

        class Rot:
            def __init__(self, banks):
                self.b = list(banks)
                self.i = 0

            def next(self):
                r = psb[self.b[self.i % len(self.b)]]
                self.i += 1
                return r

        def dma(q, out, in_, reads=(), writes=()):
            return S.add(q, lambda e: e.dma_start(out=out, in_=in_), reads=reads, writes=writes, dma=True)

        def mm(out, lhsT, rhs, start, stop, reads, writes):
            return S.add("pe", lambda e: e.matmul(out, lhsT, rhs, start=start, stop=stop), reads=reads, writes=writes)

        NSLOT = 4
        slots = [A.alloc("wslot%d" % i, [128, 4096], BF16) for i in range(NSLOT)]
        wplan = []
        wstate = {"issued": 0, "got": 0}

        def wplan_add(ap, a, b):
            wplan.append((ap, a, b))

        def kp(ap2d):
            return ap2d.rearrange("(k p) n -> p k n", p=128)

        def wview(i):
            ap, a, b = wplan[i]
            sl = slots[i % NSLOT]
            return sl[:, 0:a * b].rearrange("p (a b) -> p a b", a=a), sl.r()

        def wget():
            i = wstate["got"]
            wstate["got"] += 1
            while wstate["issued"] < min(len(wplan), i + NSLOT - 1):
                j = wstate["issued"]
                v, rg = wview(j)
                dma("pool", v, wplan[j][0], writes=[rg])
                wstate["issued"] += 1
            return wview(i)

        def plan_ada(l):
            for t in range(12):
                wplan_add(kp(d_wada[l])[:, :, t * 512:(t + 1) * 512], 8, 512)

        def plan_ada_tiles(l, ts):
            for t in ts:
                wplan_add(kp(d_wada[l])[:, :, t * 512:(t + 1) * 512], 8, 512)

        def plan_ffn(l):
            for jt in range(6):
                n = 512 if jt < 5 else 256
                wplan_add(kp(d_w1[l])[:, :, jt * 512:jt * 512 + n], 8, n)
                wplan_add(kp(d_w3[l])[:, :, jt * 512:jt * 512 + n], 8, n)
            for o in range(8):
                wplan_add(kp(d_w2[l])[:, :, o * 128:(o + 1) * 128], 22, 128)

        plan_ada_tiles(0, range(0, 4))
        for t in range(3):
            wplan_add(kp(d_wie)[:, :, t * 512:(t + 1) * 512], 8, 512)
        plan_ada_tiles(0, range(4, 12))
        if NLAYERS > 1:
            plan_ada_tiles(1, range(12))
        for sb in range(2):
            for m in range(2):
                wplan_add(kp(d_dft1024[m])[:, :, sb * 512:(sb + 1) * 512], 8, 512)
        for t in range(2):
            wplan_add(kp(d_woe)[:, :, t * 512:(t + 1) * 512], 8, 512)
        plan_ffn(0)
        if NLAYERS > 1:
            wio = kp(d_wio)
            for t in range(2):
                wplan_add(wio[:, :, t * 512:(t + 1) * 512], 8, 512)
            for grp in range(2):
                wplan_add(wio[:, :, 2048:2560], 8, 512)
                wplan_add(wio[:, :, 2560:3072], 8, 512)
                wplan_add(wio[:, :, 3072:3328], 8, 256)
                wplan_add(wio[:, :, 1024:1536], 8, 512)
                wplan_add(wio[:, :, 1536:2048], 8, 512)
                wplan_add(wio[:, :, 3328:3360], 8, 32)
            wstate["woo"] = len(wplan)
            for t in range(4):
                wplan_add(kp(d_woo)[:, :, t * 256:(t + 1) * 256], 12, 256)
            plan_ffn(1)

        xT = A.alloc("xT", [128, 8, NTOK], F32)
        hT = A.alloc("hT", [128, 8, NTOK], BF16)
        cf32 = A.alloc("cf32", [128, 5, 128], F32)
        cb16 = A.alloc("cb16", [128, 9, 128], BF16)
        vecT = A.alloc("vecT", [128, 256], F32)
        bv = A.alloc("bv", [128, NBV], F32)
        mod = A.alloc("mod", [128, 2, 48, 2], F32)
        A1 = A.alloc("A1", [128, 2, 2, 8, 2], F32) if False else A.alloc("A1", [128, 64], F32)
        ident = cf32[:, 0, :]
        RT = cf32[:, 1, :]
        utri = cf32[:, 2, :]
        ltri = cf32[:, 3, :]
        onesf = cf32[:, 4, :]
        onesm = cb16[:, 0, :]
        o64 = cb16[:, 1, :]
        identb = cb16[:, 4, :]

        def A1v(l, kind, c, j):
            o = ((l * 2 + kind) * 8 + c) * 2 + j
            return A1[:, o:o + 1]

        def modv(l, q, c, j):
            return mod[:, l, q * 8 + c, j:j + 1]

        def xr(b):
            return xT.rs(range(4 * b, 4 * b + 4))

        dma("sp", cf32[:], d_cf32.rearrange("c p n -> p c n"), writes=[cf32.r()])
        dma("pool", cb16[:], d_cb16.rearrange("c p n -> p c n"), writes=[cb16.r()])
        vraw = A.alloc("vraw", [128, 2, 128], F32)
        dma("sp", vraw[:], d_vecs.rearrange("(a p) n -> p a n", p=128), writes=[vraw.r()])
        dma("sp", bv[:], d_bvec.partition_broadcast(128), writes=[bv.r()])
        for a in range(2):
            p, pr = PS(a)
            S.add("pe", lambda e, p=p, a=a: e.transpose(p[:, 0:128], vraw[:, a, :], ident),
                  reads=[vraw.r(), cf32.r()], writes=[pr])
            S.add("dve", lambda e, p=p, a=a: e.tensor_copy(out=vecT[:, a * 128:(a + 1) * 128], in_=p[:, 0:128]),
                  reads=[pr], writes=[vecT.r()])
        A.free(vraw)

        xst = [A.alloc("xst%d" % i, [128, D], F32) for i in range(2)]
        rot = Rot([2, 3, 4, 5, 6, 7])
        for t in range(12):
            xs_ = xst[t % 2]
            dma("sp", xs_[:], d_x[t * 128:(t + 1) * 128, :], writes=[xs_.r()])
            for hf in range(2):
                p, pr = rot.next()

                def tr(e, p=p, xs_=xs_, hf=hf):
                    for c in range(4):
                        ins = e.transpose(p[:, c * 128:(c + 1) * 128], xs_[:, (hf * 4 + c) * 128:(hf * 4 + c + 1) * 128], ident)
                    return ins
                S.add("pe", tr, reads=[xs_.r(), cf32.r()], writes=[pr])
                eng = "act" if hf == 0 else "dve"
                outv = xT[:, hf * 4:hf * 4 + 4, t * 128:(t + 1) * 128]
                inv = p[:].rearrange("p (c n) -> p c n", c=4)
                if eng == "act":
                    S.add("act", lambda e, o=outv, i=inv: e.activation(out=o, in_=i, func=AF.Copy), reads=[pr], writes=[xT.r(t)])
                else:
                    S.add("dve", lambda e, o=outv, i=inv: e.tensor_copy(out=o, in_=i), reads=[pr], writes=[xT.r(t)])
        A.free(*xst)

        scT = A.alloc("scT", [128, 8, 2], BF16)
        S.add("act", lambda e: e.activation(out=scT[:].rearrange("p k j -> p j k"),
                                            in_=vecT[:, R_CCTX:R_CCTX + 16].rearrange("p (j k) -> p j k", j=2), func=AF.Silu),
              reads=[vecT.r()], writes=[scT.r()])

        def ada_tile(l, t, bank):
            p, pr = PS(bank) if isinstance(bank, int) else bank
            wt, wr = wget()

            def f(e):
                for oc4 in range(4):
                    for k in range(8):
                        ins = e.matmul(p[:, oc4 * 2:oc4 * 2 + 2], wt[:, k, oc4 * 128:(oc4 + 1) * 128], scT[:, k, :],
                                       start=(k == 0), stop=(k == 7))
                return ins
            S.add("pe", f, reads=[wr, scT.r()], writes=[pr])
            r0 = R_BADA + l * 48 + t * 4
            S.add("dve", lambda e: e.tensor_tensor(out=mod[:, l, t * 4:(t + 1) * 4, :], in0=p[:, 0:8].rearrange("p (c j) -> p c j", j=2),
                                                   in1=vecT[:, r0:r0 + 4].unsqueeze(2).to_broadcast([128, 4, 2]),
                                                   op=ALU.add), reads=[pr, vecT.r()], writes=[mod.r()])

        def ada_finish(l, kind):
            q, rbase = ((1, R_NMIX), (4, R_NFFN))[kind]
            o = (l * 2 + kind) * 16
            S.add("dve", lambda e: e.scalar_tensor_tensor(
                out=A1[:, o:o + 16].rearrange("p (c j) -> p c j", j=2), in0=mod[:, l, q * 8:(q + 1) * 8, :], scalar=1.0,
                in1=vecT[:, rbase + l * 8:rbase + (l + 1) * 8].unsqueeze(2).to_broadcast([128, 8, 2]),
                op0=ALU.add, op1=ALU.mult), reads=[mod.r(), vecT.r()], writes=[A1.r()])

        def norm_mod(l, kind):
            qs = 0 if kind == 0 else 3
            sq = [A.alloc("nsq%d" % i, [128, TB], BF16) for i in range(4)]
            rs = [A.alloc("nrs%d" % i, [128, TB], F32) for i in range(3)]
            tt = [A.alloc("ntt%d" % i, [128, TB], F32) for i in range(3)]
            rot = Rot([6, 7])
            cnt = [0, 0]
            pbank = {}

            def stA(b):
                blk = slice(b * TB, (b + 1) * TB)
                p, pr = rot.next()
                pbank[b] = (p, pr)
                for c in range(8):
                    s_ = sq[cnt[0] % 4]
                    cnt[0] += 1
                    if c % 2 == 0:
                        S.add("act", lambda e, s_=s_, c=c: e.activation(out=s_[:], in_=xT[:, c, blk], func=AF.Square), reads=xr(b), writes=[s_.r()])
                    else:
                        S.add("dve", lambda e, s_=s_, c=c: e.tensor_tensor(out=s_[:], in0=xT[:, c, blk], in1=xT[:, c, blk], op=ALU.mult), reads=xr(b), writes=[s_.r()])
                    mm(p[:], onesm, s_[:], c == 0, c == 7, [s_.r(), cb16.r()], [pr])

            def stB(b):
                p, pr = pbank[b]
                r_ = rs[b]
                S.add("act", lambda e: e.activation(out=r_[:], in_=p[:], func=AF.Ln, bias=EPS_AP[:, 0:1], scale=1.0), reads=[pr, EPS_AP.r()], writes=[r_.r()])
                S.add("act", lambda e: e.activation(out=r_[:], in_=r_[:], func=AF.Exp, scale=-0.5), reads=[r_.r()], writes=[r_.r()])

            def stC(b):
                j = 0 if b == 0 else 1
                blk = slice(b * TB, (b + 1) * TB)
                r_ = rs[b]
                for c in range(8):
                    t_ = tt[cnt[1] % 3]
                    cnt[1] += 1
                    S.add("dve", lambda e, t_=t_, c=c: e.tensor_tensor(out=t_[:], in0=xT[:, c, blk], in1=r_[:], op=ALU.mult),
                          reads=xr(b) + [r_.r()], writes=[t_.r()])
                    S.add("act", lambda e, t_=t_, c=c: e.activation(out=hT[:, c, blk], in_=t_[:], func=AF.Identity,
                                                                    bias=modv(l, qs, c, j), scale=A1v(l, kind, c, j)),
                          reads=[t_.r(), mod.r(), A1.r()], writes=[hT.r(b)])

            stA(0)
            stA(1)
            stB(0)
            stC(0)
            stA(2)
            stB(1)
            stC(1)
            stB(2)
            stC(2)
            A.free(*sq, *rs, *tt)

        EPS_AP = A.alloc("eps", [128, 2], F32)
        S.add("pool", lambda e: e.memset(EPS_AP[:], EPS), writes=[EPS_AP.r()])
        ONE_AP = A.alloc("one", [128, 2], F32)
        S.add("pool", lambda e: e.memset(ONE_AP[:], 1.0), writes=[ONE_AP.r()])

        def resid(p, pr, l, q, o, b):
            j = 0 if b == 0 else 1
            blk = slice(b * TB, (b + 1) * TB)
            S.add("dve", lambda e: e.scalar_tensor_tensor(out=xT[:, o, blk], in0=p[:], scalar=modv(l, q, o, j), in1=xT[:, o, blk],
                                                          op0=ALU.mult, op1=ALU.add),
                  reads=[pr, mod.r()] + xr(b), writes=xr(b))

        def ffn(l):
            norm_mod(l, 1)
            actT = A.alloc("actT", [128, 22, NTOK], BF16)
            sl = [A.alloc("fsl%d" % i, [128, TB], BF16) for i in range(3)]
            rot1 = Rot([0, 1, 2, 3])
            n = 0
            for jt in range(6):
                w1, w1r = wget()
                w3, w3r = wget()
                ncs = 4 if jt < 5 else 2
                for cc in range(ncs):
                    jj = jt * 4 + cc
                    for b in range(3):
                        blk = slice(b * TB, (b + 1) * TB)
                        p1, p1r = rot1.next()
                        p3, p3r = rot1.next()

                        def f(e, w=w1, p=p1, cc=cc, blk=blk):
                            for k in range(8):
                                ins = e.matmul(p[:], w[:, k, cc * 128:(cc + 1) * 128], hT[:, k, blk], start=(k == 0), stop=(k == 7))
                            return ins
                        S.add("pe", f, reads=[w1r, hT.r(b)], writes=[p1r])

                        def f3(e, w=w3, p=p3, cc=cc, blk=blk):
                            for k in range(8):
                                ins = e.matmul(p[:], w[:, k, cc * 128:(cc + 1) * 128], hT[:, k, blk], start=(k == 0), stop=(k == 7))
                            return ins
                        S.add("pe", f3, reads=[w3r, hT.r(b)], writes=[p3r])
                        s_ = sl[n % 3]
                        n += 1
                        S.add("act", lambda e, s_=s_, p=p1: e.activation(out=s_[:], in_=p[:], func=AF.Silu), reads=[p1r], writes=[s_.r()])
                        S.add("dve", lambda e, s_=s_, p=p3, jj=jj, blk=blk: e.tensor_tensor(out=actT[:, jj, blk], in0=p[:], in1=s_[:], op=ALU.mult),
                              reads=[p3r, s_.r()], writes=[actT.r(b)])
            A.free(*sl)
            rot2 = Rot([4, 5, 6, 7])
            for o in range(8):
                w2, w2r = wget()
                for b in range(3):
                    blk = slice(b * TB, (b + 1) * TB)
                    p, pr = rot2.next()

                    def f(e, w=w2, p=p, blk=blk):
                        for jj in range(22):
                            ins = e.matmul(p[:], w[:, jj, :], actT[:, jj, blk], start=(jj == 0), stop=(jj == 21))
                        return ins
                    S.add("pe", f, reads=[w2r, actT.r(b)], writes=[pr])
                    resid(p, pr, l, 5, o, b)
            A.free(actT)

        def even_layer(l):
            norm_mod(l, 0)
            qT = A.alloc("qT", [128, 6, NTOK], BF16)
            kT = A.alloc("kT", [128, 2, 2048], BF16)
            vaug = A.alloc("vaug", [128, 16, 4, 128], BF16)
            ftok = A.alloc("ftok", [128, 12, 256], BF16)
            rope = A.alloc("rope", [128, 2, 1024], F32)
            dft256 = A.alloc("dft256", [128, 2, 2, 256], BF16)
            knew = A.alloc("knew", [128, 4, 256], F32)
            vnew = A.alloc("vnew", [128, 4, 256], F32)
            dma("sp", rope[:], d_rope.rearrange("c p n -> p c n"), writes=[rope.r()])
            for m in range(2):
                dma("pool", dft256[:, m, :, :], d_dft256[m].rearrange("(t p) n -> p t n", p=128), writes=[dft256.r()])
            S.add("pool", lambda e: e.memset(vaug[:], 1.0), writes=vaug.rs(range(16)))
            cvv = d_cv.rearrange("(t p) (j d) -> p t j d", p=128, j=4)
            for par in range(2):
                for tt_ in range(4):
                    dma("pool", vaug[:, 12 + tt_, par::2, par * 64:par * 64 + 64], cvv[:, tt_, par::2, :], writes=[vaug.r(12 + tt_)])
            ckt = A.alloc("ckt", [128, 4, 256], F32)
            dma("sp", ckt[:], d_ck.rearrange("(t p) f -> p t f", p=128), writes=[ckt.r()])
            rotc = Rot([4, 5])
            for kc in range(2):
                p, pr = rotc.next()

                def f(e, p=p, kc=kc):
                    for tt_ in range(4):
                        ins = e.transpose(p[:, tt_ * 128:(tt_ + 1) * 128], ckt[:, tt_, kc * 128:(kc + 1) * 128], ident)
                    return ins
                S.add("pe", f, reads=[ckt.r(), cf32.r()], writes=[pr])
                S.add("dve", lambda e, p=p, kc=kc: e.tensor_copy(out=kT[:, kc, 1536:2048], in_=p[:]), reads=[pr], writes=[kT.r(3)])
            A.free(ckt)

            if STAGE < 3:
                return
            sqn = [A.alloc("sqn%d" % i, [128, TB], BF16) for i in range(3)]
            rsn = [A.alloc("rsn%d" % i, [128, TB], F32) for i in range(3)]
            qn = [A.alloc("qn%d" % i, [128, TB], F32) for i in range(3)]
            t1 = [A.alloc("rt1%d" % i, [128, TB], F32) for i in range(2)]
            t2 = [A.alloc("rt2%d" % i, [128, TB], F32) for i in range(2)]
            rotA = Rot([0, 1, 6])
            rotB = Rot([2, 3])
            rotC = Rot([4, 5, 7])
            qunits = [(wt_i, cc, b) for wt_i in range(2) for cc in range(4) for b in range(3)]
            ust = {}
            wcur = {}

            def stageA(u):
                wt_i, cc, b = qunits[u]
                if (wt_i,) not in wcur:
                    wcur[(wt_i,)] = wget()
                w, wr = wcur[(wt_i,)]
                blk = slice(b * TB, (b + 1) * TB)
                pa, par_ = rotA.next()

                def f(e):
                    for k in range(8):
                        ins = e.matmul(pa[:], w[:, k, cc * 128:(cc + 1) * 128], hT[:, k, blk], start=(k == 0), stop=(k == 7))
                    return ins
                S.add("pe", f, reads=[wr, hT.r(b)], writes=[par_])
                s_ = sqn[u % 3]
                S.add("act", lambda e: e.activation(out=s_[:], in_=pa[:], func=AF.Square), reads=[par_], writes=[s_.r()])
                ust[u] = (pa, par_, s_)

            def stageB(u):
                wt_i, cc, b = qunits[u]
                qc = wt_i * 4 + cc
                isk = qc >= 6
                gain = vecT[:, R_KN:R_KN + 1] if isk else vecT[:, R_QN:R_QN + 1]
                pa, par_, s_ = ust[u]
                r_ = rsn[u % 3]
                q_ = qn[u % 3]
                pb, pbr = rotB.next()
                mm(pb[:], o64, s_[:], True, True, [s_.r(), cb16.r()], [pbr])
                S.add("act", lambda e: e.activation(out=r_[:], in_=pb[:], func=AF.Ln, bias=EPS_AP[:, 0:1], scale=1.0),
                      reads=[pbr, EPS_AP.r()], writes=[r_.r()])
                S.add("act", lambda e: e.activation(out=r_[:], in_=r_[:], func=AF.Exp, scale=-0.5), reads=[r_.r()], writes=[r_.r()])
                if b == 0 and not isk:
                    S.add("dve", lambda e: e.scalar_tensor_tensor(out=qT[:, qc, 0:TB], in0=pa[:], scalar=gain, in1=r_[:], op0=ALU.mult, op1=ALU.mult),
                          reads=[par_, r_.r(), vecT.r()], writes=[qT.r(0)])
                else:
                    S.add("dve", lambda e: e.scalar_tensor_tensor(out=q_[:], in0=pa[:], scalar=gain, in1=r_[:], op0=ALU.mult, op1=ALU.mult),
                          reads=[par_, r_.r(), vecT.r()], writes=[q_.r()])
                ust[u] = (q_,)

            def stageC(u):
                wt_i, cc, b = qunits[u]
                qc = wt_i * 4 + cc
                isk = qc >= 6
                (q_,) = ust.pop(u)
                blk = slice(b * TB, (b + 1) * TB)
                if isk:
                    dst, dreg = kT[:, qc - 6, b * TB:(b + 1) * TB], kT.r(b)
                else:
                    dst, dreg = qT[:, qc, blk], qT.r(b)
                if b == 0:
                    if isk:
                        S.add("act", lambda e: e.activation(out=dst, in_=q_[:], func=AF.Copy), reads=[q_.r()], writes=[dreg])
                        pc, pcr = rotC.next()

                        def ftr(e):
                            for tt_ in range(4):
                                ins = e.transpose(pc[:, tt_ * 128:(tt_ + 1) * 128], q_[:, tt_ * 128:(tt_ + 1) * 128], ident)
                            return ins
                        S.add("pe", ftr, reads=[q_.r(), cf32.r()], writes=[pcr])
                        kc = qc - 6
                        S.add("act", lambda e: e.activation(out=knew[:, :, kc * 128:(kc + 1) * 128], in_=pc[:].rearrange("p (t f) -> p t f", t=4), func=AF.Copy),
                              reads=[pcr], writes=[knew.r()])
                else:
                    a_ = t1[u % 2]
                    b_ = t2[u % 2]
                    pc, pcr = rotC.next()
                    mm(pc[:], RT, q_[:], True, True, [q_.r(), cf32.r()], [pcr])
                    rsl = slice((b - 1) * TB, b * TB)
                    S.add("dve", lambda e: e.tensor_tensor(out=a_[:], in0=q_[:], in1=rope[:, 0, rsl], op=ALU.mult), reads=[q_.r(), rope.r()], writes=[a_.r()])
                    S.add("dve", lambda e: e.tensor_tensor(out=b_[:], in0=pc[:], in1=rope[:, 1, rsl], op=ALU.mult), reads=[pcr, rope.r()], writes=[b_.r()])
                    S.add("dve", lambda e: e.tensor_tensor(out=dst, in0=a_[:], in1=b_[:], op=ALU.add), reads=[a_.r(), b_.r()], writes=[dreg])

            NU = len(qunits)
            for step in range(NU + 2):
                if step < NU:
                    stageA(step)
                if 0 <= step - 1 < NU:
                    stageB(step - 1)
                if 0 <= step - 2 < NU:
                    stageC(step - 2)
            dma("sp", o_k.rearrange("(t p) f -> p t f", p=128), knew[:], reads=[knew.r()])
            A.free(*sqn, *rsn, *qn, *t1, *t2)

            if STAGE < 4:
                return
            w, wr = wget()
            rotA = Rot([0, 1, 2, 3])
            for t in range(12):
                b = t // 4
                p, pr = rotA.next()

                def f(e, p=p, t=t, w=w):
                    for k in range(8):
                        ins = e.matmul(p[:], hT[:, k, t * 128:(t + 1) * 128], w[:, k, :], start=(k == 0), stop=(k == 7))
                    return ins
                S.add("pe", f, reads=[wr, hT.r(b)], writes=[pr])
                if not (SUB & 1):
                    S.add("act", lambda e, p=p, t=t: e.activation(out=ftok[:, t, :], in_=p[:, 0:256], func=AF.Copy), reads=[pr], writes=[ftok.r(t)])
                pv = p[:, 256:512].rearrange("p (j d) -> p j d", j=4)
                for par in range(2 if not (SUB & 2) else 0):
                    S.add("dve", lambda e, pv=pv, t=t, par=par: e.tensor_copy(out=vaug[:, t, par::2, par * 64:par * 64 + 64], in_=pv[:, par::2, :]),
                          reads=[pr], writes=[vaug.r(t)])
                if t < 4 and not (SUB & 4):
                    S.add("act", lambda e, p=p, t=t: e.activation(out=vnew[:, t, :], in_=p[:, 256:512], func=AF.Copy), reads=[pr], writes=[vnew.r()])
            dma("sp", o_v.rearrange("(t p) f -> p t f", p=128), vnew[:], reads=[vnew.r()])
            for t in range(4, 12):
                ada_tile(l, t, 4 + t % 4)
            ada_finish(l, 1)

            if STAGE < 5:
                return
            pT = [A.alloc("pT%d" % i, [128, TB], BF16) for i in range(6)]
            rtmp = [A.alloc("rtmp%d" % i, [128, TB], F32) for i in range(2)]
            evb = [A.alloc("evb%d" % i, [128, TB], F32) for i in range(2)]
            rotS = Rot([0, 1, 2, 3, 4, 5])
            rotO = Rot([6, 7])
            units = []
            for sidx in range(2):
                units.append((sidx * 256, 256, 0, [(sidx * 256 + kt * 128, 2 * sidx + kt) for kt in range(2)]))
            skeys = [(512 + kt * 128, 4 + kt) for kt in range(8)] + [(1536 + kt * 128, 12 + kt) for kt in range(4)]
            for qb in range(2):
                units.append((512 + qb * 512, 512, 1 + qb, skeys))
            n = 0
            nh = 0
            ada1_next = [0]
            LAG = 2
            for (q0, N, b, keys) in units:
                nk = len(keys)
                groups = [(qc, ki) for qc in range(6) for ki in range(nk)]
                pts = {}

                def emit_scores(gi, q0=q0, N=N, b=b, keys=keys, groups=groups, pts=pts):
                    nonlocal n
                    qc, ki = groups[gi]
                    kc = qc // 3
                    k0, vt = keys[ki]
                    kreg = kT.r(0 if k0 < 512 else (1 if k0 < 1024 else (2 if k0 < 1536 else 3)))
                    banks = []
                    for hh in range(2):
                        lo, hi = hh * 64, hh * 64 + 64
                        sp_, spr = rotS.next()
                        mm(sp_[:, 0:N], kT[lo:hi, kc, k0:k0 + 128], qT[lo:hi, qc, q0:q0 + N], True, True, [kreg, qT.r(b)], [spr])
                        banks.append((sp_, spr))
                    lst = []
                    for (sp_, spr) in banks:
                        pt = pT[n % 6]
                        n += 1
                        S.add("act", lambda e, pt=pt, sp_=sp_, N=N: e.activation(out=pt[:, 0:N], in_=sp_[:, 0:N], func=AF.Exp, scale=0.125),
                              reads=[spr], writes=[pt.r()])
                        lst.append(pt)
                    pts[gi] = lst

                for gi in range(min(LAG, len(groups))):
                    emit_scores(gi)
                accs = None
                for gi, (qc, ki) in enumerate(groups):
                    if ki == 0:
                        accs = [rotO.next(), rotO.next()]
                    k0, vt = keys[ki]
                    lst = pts.pop(gi)
                    for hh in range(2):
                        j = (qc // 3) * 2 + hh
                        acc, accr = accs[hh]
                        mm(acc[:, 0:N], vaug[:, vt, j, :], lst[hh][:, 0:N], ki == 0, ki == nk - 1, [vaug.r(vt), lst[hh].r()], [accr])
                    if gi + LAG < len(groups):
                        emit_scores(gi + LAG)
                    if NLAYERS > 1 and N == 512 and gi % 12 == 6 and ada1_next[0] < 12:
                        ada_tile(1, ada1_next[0], rotS.next())
                        ada1_next[0] += 1
                    if ki == nk - 1:
                        for hh in range(2):
                            acc, accr = accs[hh]
                            lo, hi = hh * 64, hh * 64 + 64
                            olo, ohi = (1 - hh) * 64, (1 - hh) * 64 + 64
                            rt = rtmp[nh % 2]
                            ev = evb[nh % 2]
                            nh += 1
                            S.add("dve", lambda e, ev=ev, acc=acc, N=N: e.tensor_copy(out=ev[:, 0:N], in_=acc[:, 0:N]), reads=[accr], writes=[ev.r()])
                            S.add("dve", lambda e, rt=rt, ev=ev, N=N, lo=lo, hi=hi, olo=olo, ohi=ohi: e.reciprocal(out=rt[lo:hi, 0:N], in_=ev[olo:ohi, 0:N]),
                                  reads=[ev.r()], writes=[rt.r()])
                            S.add("dve", lambda e, rt=rt, ev=ev, N=N, lo=lo, hi=hi, qc=qc, q0=q0: e.tensor_tensor(
                                out=hT[lo:hi, 2 + qc, q0:q0 + N], in0=ev[lo:hi, 0:N], in1=rt[lo:hi, 0:N], op=ALU.mult),
                                reads=[ev.r(), rt.r()], writes=[hT.r(b)])
            if NLAYERS > 1:
                while ada1_next[0] < 12:
                    ada_tile(1, ada1_next[0], rotS.next())
                    ada1_next[0] += 1
                ada_finish(1, 0)
                ada_finish(1, 1)
            A.free(*pT, *rtmp, *evb, qT, kT, vaug, rope, knew, vnew)

            if STAGE < 6:
                return
            uv = [A.alloc("uv%d" % i, [128, TB], BF16) for i in range(4)]
            rotU = Rot([0, 1, 2, 3])
            rotF = Rot([4, 5, 6, 7])
            c64c = cb16[:, 5, :]
            c64s = cb16[:, 6, :]
            n = 0

            def stageB(u, v, N, cch, q0, b):
                p, pr = rotF.next()
                mm(p[:, 0:N], c64c, u[:, 0:N], True, False, [cb16.r(), u.r()], [pr])
                mm(p[:, 0:N], c64s, v[:, 0:N], False, True, [cb16.r(), v.r()], [pr])
                S.add("act", lambda e: e.activation(out=hT[:, cch, q0:q0 + N], in_=p[:, 0:N], func=AF.Copy), reads=[pr], writes=[hT.r(b)])

            for sidx in range(2):
                for cch in range(2):
                    pair = []
                    for m in range(2):
                        p, pr = rotU.next()

                        def f(e, p=p, m=m, sidx=sidx, cch=cch):
                            for st_ in range(2):
                                ins = e.matmul(p[:, 0:256], ftok[:, 2 * sidx + st_, cch * 128:(cch + 1) * 128], dft256[:, m, st_, :],
                                               start=(st_ == 0), stop=(st_ == 1))
                            return ins
                        S.add("pe", f, reads=[ftok.r(2 * sidx), ftok.r(2 * sidx + 1), dft256.r()], writes=[pr])
                        u = uv[n % 4]
                        n += 1
                        S.add("dve", lambda e, u=u, p=p: e.tensor_copy(out=u[:, 0:256], in_=p[:, 0:256]), reads=[pr], writes=[u.r()])
                        pair.append(u)
                    stageB(pair[0], pair[1], 256, cch, sidx * 256, 0)
            for sb in range(2):
                wc, wcr = wget()
                wsn, wsr = wget()
                for cch in range(2):
                    pair = []
                    for m, (wm, wmr) in enumerate(((wc, wcr), (wsn, wsr))):
                        p, pr = rotU.next()

                        def f(e, p=p, wm=wm, cch=cch):
                            for st_ in range(8):
                                ins = e.matmul(p[:], ftok[:, 4 + st_, cch * 128:(cch + 1) * 128], wm[:, st_, :], start=(st_ == 0), stop=(st_ == 7))
                            return ins
                        S.add("pe", f, reads=ftok.rs(range(4, 12)) + [wmr], writes=[pr])
                        u = uv[n % 4]
                        n += 1
                        S.add("dve", lambda e, u=u, p=p: e.tensor_copy(out=u[:], in_=p[:]), reads=[pr], writes=[u.r()])
                        pair.append(u)
                    stageB(pair[0], pair[1], 512, cch, 512 + sb * 512, 1 + sb)
            A.free(*uv, ftok, dft256)

            if STAGE < 7:
                return
            rotW = Rot([0, 1, 2, 3])
            for t in range(2):
                w, wr = wget()
                for oc in range(4):
                    o = t * 4 + oc
                    for b in range(3):
                        blk = slice(b * TB, (b + 1) * TB)
                        p, pr = rotW.next()

                        def f(e, p=p, w=w, oc=oc, blk=blk):
                            for k in range(8):
                                ins = e.matmul(p[:], w[:, k, oc * 128:(oc + 1) * 128], hT[:, k, blk], start=(k == 0), stop=(k == 7))
                            return ins
                        S.add("pe", f, reads=[wr, hT.r(b)], writes=[pr])
                        resid(p, pr, l, 2, o, b)


        def rev(v):
            (ps_, pn), (st_, n) = v.ap
            return bass.AP(v.tensor, v.offset + (n - 1) * st_, [[ps_, pn], [-st_, n]])

        SEQS = [(0, 256), (256, 512), (512, 1536)]

        def conv4(eng, xp, yp, n, wcol, bcol):
            lo, hi = 1, n - 2
            S.add(eng, lambda e: e.tensor_scalar(out=yp[:, lo:hi], in0=xp[:, lo:hi], scalar1=vecT[:, wcol(1):wcol(1) + 1],
                                                 scalar2=vecT[:, bcol:bcol + 1], op0=ALU.mult, op1=ALU.add),
                  reads=[xp.r(), vecT.r()], writes=[yp.r()])
            for j, sh in ((0, -1), (2, 1), (3, 2)):
                S.add(eng, lambda e, j=j, sh=sh: e.scalar_tensor_tensor(out=yp[:, lo:hi], in0=xp[:, lo + sh:hi + sh], scalar=vecT[:, wcol(j):wcol(j) + 1],
                                                                        in1=yp[:, lo:hi], op0=ALU.mult, op1=ALU.add),
                      reads=[xp.r(), yp.r(), vecT.r()], writes=[yp.r()])

        def odd_layer(l):
            norm_mod(l, 0)
            yl = A.alloc("yl", [128, 4, NTOK], BF16)
            gg = A.alloc("gg", [128, 4, NTOK], BF16)
            xc = A.alloc("xc", [128, 4, NTOK], F32)
            bdw = A.alloc("bdw", [128, 16, 128], BF16)
            fin = A.alloc("fin", [128, 16], F32)
            clT = A.alloc("clT", [128, 8], F32)
            dma("pool", bdw[:], d_lrubd.rearrange("c p n -> p c n"), writes=[bdw.r()])
            S.add("act", lambda e: e.activation(out=clT[:], in_=vecT[:, R_LAM:R_LAM + 8], func=AF.Exp, scale=-1.0), reads=[vecT.r()], writes=[clT.r()])
            S.add("act", lambda e: e.activation(out=clT[:], in_=clT[:], func=AF.Ln, bias=ONE_AP[:, 0:1], scale=1.0), reads=[clT.r(), ONE_AP.r()], writes=[clT.r()])
            S.add("dve", lambda e: e.tensor_scalar(out=clT[:], in0=clT[:], scalar1=-8.0, scalar2=0.0, op0=ALU.mult, op1=ALU.add), reads=[clT.r()], writes=[clT.r()])
            NP = 1544
            POFF = [1, 259, 517]
            xpad = [A.alloc("xpad%d" % i, [128, NP], F32) for i in range(2)]
            ypad = [A.alloc("ypad%d" % i, [128, NP], F32) for i in range(2)]
            for xp in xpad:
                S.add("pool", lambda e, xp=xp: e.memset(xp[:], 0.0), writes=[xp.r()])
            rotA = Rot([0, 1, 2, 3])
            wg, wgr = wget()
            for c in range(4):
                for b in range(3):
                    blk = slice(b * TB, (b + 1) * TB)
                    p, pr = rotA.next()

                    def f(e, p=p, c=c, blk=blk):
                        for k in range(8):
                            ins = e.matmul(p[:], wg[:, k, c * 128:(c + 1) * 128], hT[:, k, blk], start=(k == 0), stop=(k == 7))
                        return ins
                    S.add("pe", f, reads=[wgr, hT.r(b)], writes=[pr])
                    S.add("act", lambda e, p=p, c=c, blk=blk: e.activation(out=gg[:, c, blk], in_=p[:], func=AF.Gelu_apprx_tanh), reads=[pr], writes=[gg.r(c)])
            wx, wxr = wget()
            for c in range(4):
                xp = xpad[c % 2]
                yp = ypad[c % 2]
                for b in range(3):
                    blk = slice(b * TB, (b + 1) * TB)
                    p, pr = rotA.next()

                    def f(e, p=p, c=c, blk=blk):
                        for k in range(8):
                            ins = e.matmul(p[:], wx[:, k, c * 128:(c + 1) * 128], hT[:, k, blk], start=(k == 0), stop=(k == 7))
                        return ins
                    S.add("pe", f, reads=[wxr, hT.r(b)], writes=[pr])
                    if b == 0:
                        ov = xp[:, 1:1 + 2 * 258].rearrange("p (a b) -> p a b", b=258)[:, :, 0:256]
                        S.add("act", lambda e, p=p, ov=ov: e.activation(out=ov, in_=p[:].rearrange("p (a b) -> p a b", a=2), func=AF.Copy), reads=[pr], writes=[xp.r()])
                    else:
                        o0 = 517 + (b - 1) * 512
                        S.add("act", lambda e, p=p, xp=xp, o0=o0: e.activation(out=xp[:, o0:o0 + 512], in_=p[:], func=AF.Copy), reads=[pr], writes=[xp.r()])
                conv4("dve", xp, yp, NP, lambda j, c=c: R_CLW + j * 4 + c, R_CLB + c)
                for si, (s0, s1) in enumerate(SEQS):
                    S.add("act", lambda e, yp=yp, c=c, s0=s0, s1=s1, si=si: e.activation(out=xc[:, c, s0:s1], in_=yp[:, POFF[si]:POFF[si] + s1 - s0], func=AF.Copy),
                          reads=[yp.r()], writes=[xc.r(c)])
            A.free(*xpad, *ypad)
            xcb = A.alloc("xcb", [128, NTOK], BF16)
            ra_ = [A.alloc("lra%d" % i, [128, NTOK], F32) for i in range(2)]
            ig_ = [A.alloc("lig%d" % i, [128, NTOK], F32) for i in range(2)]
            tq_ = [A.alloc("ltq0", [128, NTOK], F32)] * 2
            hd = [A.alloc("lh0", [128, NTOK], F32), ig_[1]]
            rotL = Rot([4, 5, 6, 7])
            xcbs = [xcb, xcb]

            def lruA(u):
                c, d = u // 2, u % 2
                ra, ig = ra_[d], ig_[d]
                xb = xcbs[c % 2]
                if d == 0:
                    S.add("act", lambda e: e.activation(out=xb[:], in_=xc[:, c, :], func=AF.Copy), reads=[xc.r(c)], writes=[xb.r()])
                for b in range(3):
                    blk = slice(b * TB, (b + 1) * TB)
                    for gate, dstT, brow in ((0, ra, R_BA), (1, ig, R_BX)):
                        p, pr = rotL.next()
                        mm(p[:], bdw[:, (gate * 2 + d) * 4 + c, :], xb[:, blk], True, True, [bdw.r(), xb.r()], [pr])
                        col = brow + d * 4 + c
                        S.add("act", lambda e, p=p, dstT=dstT, blk=blk, col=col: e.activation(out=dstT[:, blk], in_=p[:], func=AF.Sigmoid,
                                                                                               bias=vecT[:, col:col + 1], scale=1.0),
                              reads=[pr, vecT.r()], writes=[dstT.r()])

            def lruB(u):
                c, d = u // 2, u % 2
                ra, ig, tq = ra_[d], ig_[d], tq_[d]
                S.add("act", lambda e: e.activation(out=ra[:], in_=ra[:], func=AF.Exp, scale=clT[:, d * 4 + c:d * 4 + c + 1]),
                      reads=[ra.r(), clT.r()], writes=[ra.r()])
                S.add("act", lambda e: e.activation(out=tq[:], in_=ra[:], func=AF.Square), reads=[ra.r()], writes=[tq.r()])
                S.add("dve", lambda e: e.tensor_scalar(out=tq[:], in0=tq[:], scalar1=-1.0, scalar2=1.0, op0=ALU.mult, op1=ALU.add), reads=[tq.r()], writes=[tq.r()])
                S.add("act", lambda e: e.activation(out=tq[:], in_=tq[:], func=AF.Sqrt), reads=[tq.r()], writes=[tq.r()])
                S.add("dve", lambda e: e.tensor_tensor(out=ig[:], in0=ig[:], in1=xc[:, c, :], op=ALU.mult), reads=[ig.r(), xc.r(c)], writes=[ig.r()])
                S.add("dve", lambda e: e.tensor_tensor(out=tq[:], in0=tq[:], in1=ig[:], op=ALU.mult), reads=[tq.r(), ig.r()], writes=[tq.r()])
                h_ = hd[d]
                for si, (s0, s1) in enumerate(SEQS):
                    init = 0.0 if si < 2 else vecT[:, R_LRU0 + d * 4 + c:R_LRU0 + d * 4 + c + 1]
                    if d == 0:
                        S.add("dve", lambda e, s0=s0, s1=s1, init=init: e.tensor_tensor_scan(
                            out=h_[:, s0:s1], data0=ra[:, s0:s1], data1=tq[:, s0:s1], initial=init, op0=ALU.mult, op1=ALU.add),
                            reads=[ra.r(), tq.r(), vecT.r()], writes=[h_.r()])
                    else:
                        S.add("dve", lambda e, s0=s0, s1=s1, init=init: e.tensor_tensor_scan(
                            out=rev(h_[:, s0:s1]), data0=rev(ra[:, s0:s1]), data1=rev(tq[:, s0:s1]), initial=init, op0=ALU.mult, op1=ALU.add),
                            reads=[ra.r(), tq.r(), vecT.r()], writes=[h_.r()])
                    if si < 2:
                        pos = s1 - 1 if d == 0 else s0
                        col = (si * 2 + d) * 4 + c
                        S.add("dve", lambda e, pos=pos, col=col: e.tensor_copy(out=fin[:, col:col + 1], in_=h_[:, pos:pos + 1]),
                              reads=[h_.r()], writes=[fin.r()])
                if d == 1:
                    S.add("dve", lambda e: e.tensor_tensor(out=hd[0][:], in0=hd[0][:], in1=hd[1][:], op=ALU.add), reads=[hd[0].r(), hd[1].r()], writes=[hd[0].r()])
                    S.add("dve", lambda e: e.tensor_tensor(out=yl[:, c, :], in0=hd[0][:], in1=gg[:, c, :], op=ALU.mult),
                          reads=[hd[0].r(), gg.r(c)], writes=[yl.r()])

            lruA(0)
            for u in range(8):
                if u + 1 < 8:
                    lruA(u + 1)
                lruB(u)
            p, pr = PS(4)
            S.add("pe", lambda e: e.transpose(p[0:16, 0:128], fin[:, 0:16], ident), reads=[fin.r(), cf32.r()], writes=[pr])
            fino = A.alloc("fino", [16, 128], F32)
            S.add("dve", lambda e: e.tensor_copy(out=fino[:], in_=p[0:16, 0:128]), reads=[pr], writes=[fino.r()])
            dma("sp", o_lru, fino[:], reads=[fino.r()])
            A.free(gg, xc, bdw, clT, xcb, *ra_, *ig_, tq_[0], hd[0], fin, fino)
            if STAGE < 13:
                for _ in range(12):
                    wget()

            aneg = A.alloc("aneg", [128, 32], F32)
            S.add("act", lambda e: e.activation(out=aneg[:], in_=bv[:, B_ALOG:B_ALOG + 32], func=AF.Exp), reads=[bv.r()], writes=[aneg.r()])
            S.add("dve", lambda e: e.tensor_scalar(out=aneg[:], in0=aneg[:], scalar1=-1.0, scalar2=0.0, op0=ALU.mult, op1=ALU.add), reads=[aneg.r()], writes=[aneg.r()])
            nmk = [cb16[:, 2, :], cb16[:, 3, :]]
            tri = [cb16[:, 7, :], cb16[:, 8, :]]
            def ssd_group(grp):
                tiles = list(range(0, 4)) if grp == 0 else list(range(4, 12))
                nt = len(tiles)
                t0 = tiles[0]
                ntk = nt * 128
                blocks = [0] if grp == 0 else [1, 2]
                seqs = [[0, 1], [2, 3]] if grp == 0 else [list(range(8))]
                if grp == 0:
                    NPg = 520
                    poff = [1, 259]
                    slen = 256
                else:
                    NPg = 1032
                    poff = [1]
                    slen = 1024
                xs_tok = A.alloc("xs_tok", [128, nt, 1024], BF16)
                z_tok = A.alloc("z_tok", [128, nt, 1024], BF16)
                B_tok = A.alloc("B_tok", [128, nt, 128], BF16)
                BT = A.alloc("BT", [128, ntk], BF16)
                CT = A.alloc("CT", [128, ntk], BF16)
                xpad = [A.alloc("sxp%d" % i, [128, NPg], F32) for i in range(3)]
                ypad = [A.alloc("syp%d" % i, [128, NPg], F32) for i in range(3)]
                xsf = [A.alloc("xsf%d" % i, [128, ntk], F32) for i in range(3)]
                for xp in xpad:
                    S.add("pool", lambda e, xp=xp: e.memset(xp[:], 0.0), writes=[xp.r()])
                rotA = Rot([0, 1, 2, 3])
                rotT = Rot([4, 5, 6, 7])
                wc1 = {}

                def c1A(ch):
                    wt_i, cc = ch // 4, ch % 4
                    if wt_i not in wc1:
                        wc1[wt_i] = wget()
                    w, wr = wc1[wt_i]
                    xp = xpad[ch % 3]
                    for b in blocks:
                        blk = slice(b * TB, (b + 1) * TB)
                        p, pr = rotA.next()

                        def f(e, p=p, blk=blk):
                            for k in range(8):
                                ins = e.matmul(p[:], w[:, k, cc * 128:(cc + 1) * 128], hT[:, k, blk], start=(k == 0), stop=(k == 7))
                            return ins
                        S.add("pe", f, reads=[wr, hT.r(b)], writes=[pr])
                        if grp == 0:
                            ov = xp[:, 1:1 + 2 * 258].rearrange("p (a b) -> p a b", b=258)[:, :, 0:256]
                            S.add("act", lambda e, p=p, ov=ov: e.activation(out=ov, in_=p[:].rearrange("p (a b) -> p a b", a=2), func=AF.Copy), reads=[pr], writes=[xp.r()])
                        else:
                            o0 = 1 + (b - 1) * 512
                            S.add("act", lambda e, p=p, o0=o0: e.activation(out=xp[:, o0:o0 + 512], in_=p[:], func=AF.Copy), reads=[pr], writes=[xp.r()])

                def c1B(ch):
                    xp = xpad[ch % 3]
                    yp = ypad[ch % 3]
                    xf = xsf[ch % 3]
                    conv4("dve", xp, yp, NPg, lambda j: R_CSW + j * 10 + ch, R_CSB + ch)
                    for si, po in enumerate(poff):
                        if ch <= 8:
                            dst, dreg = xf[:, si * slen:(si + 1) * slen], xf.r()
                        else:
                            dst, dreg = CT[:, si * slen:(si + 1) * slen], CT.r()
                        S.add("act", lambda e, po=po, dst=dst: e.activation(out=dst, in_=yp[:, po:po + slen], func=AF.Silu), reads=[yp.r()], writes=[dreg])
                    if ch == 8:
                        S.add("act", lambda e: e.activation(out=BT[:], in_=xf[:], func=AF.Copy), reads=[xf.r()], writes=[BT.r()])

                def c1C(ch):
                    xf = xsf[ch % 3]
                    if ch <= 8:
                        for t4 in range(0, nt, 4):
                            p, pr = rotT.next()

                            def ftr(e, p=p, t4=t4):
                                for q in range(4):
                                    ins = e.transpose(p[:, q * 128:(q + 1) * 128], xf[:, (t4 + q) * 128:(t4 + q + 1) * 128], ident)
                                return ins
                            S.add("pe", ftr, reads=[xf.r(), cf32.r()], writes=[pr])
                            pv = p[:].rearrange("p (q n) -> p q n", q=4)
                            if ch < 8:
                                S.add("dve", lambda e, pv=pv, t4=t4: e.tensor_copy(out=xs_tok[:, t4:t4 + 4, ch * 128:(ch + 1) * 128], in_=pv),
                                      reads=[pr], writes=[xs_tok.r()])
                            else:
                                S.add("dve", lambda e, pv=pv, t4=t4: e.tensor_copy(out=B_tok[:, t4:t4 + 4, :], in_=pv), reads=[pr], writes=[B_tok.r()])

                for step in range(12):
                    if step < 10:
                        c1A(step)
                    if 0 <= step - 1 < 10:
                        c1B(step - 1)
                    if 0 <= step - 2 < 10:
                        c1C(step - 2)
                A.free(*xpad, *ypad, *xsf)
                if SUB == 1:
                    return
                sm = lambda name, w=32: A.alloc(name, [128, nt, w], F32)
                dtr = sm("dtr")
                dta = sm("dta")
                aal = sm("aal")
                stats = sm("stats", 64)
                nacs = sm("nacs")
                eacs = sm("eacs")
                wdec = sm("wdec")
                cdec = sm("cdec")
                cdsel = A.alloc("cdsel", [128, nt, 2, 8], F32)
                wz0, wz0r = wget()
                wz1, wz1r = wget()
                for ti in range(nt):
                    tg = t0 + ti
                    b = tg // 4
                    for hf, (wz, wzr) in enumerate(((wz0, wz0r), (wz1, wz1r))):
                        p, pr = rotA.next()

                        def f(e, p=p, wz=wz, tg=tg):
                            for k in range(8):
                                ins = e.matmul(p[:], hT[:, k, tg * 128:(tg + 1) * 128], wz[:, k, :], start=(k == 0), stop=(k == 7))
                            return ins
                        S.add("pe", f, reads=[wzr, hT.r(b)], writes=[pr])
                        S.add("act", lambda e, p=p, ti=ti, hf=hf: e.activation(out=z_tok[:, ti, hf * 512:(hf + 1) * 512], in_=p[:], func=AF.Silu),
                              reads=[pr], writes=[z_tok.r()])
                wd, wdr = wget()
                for ti in range(nt):
                    tg = t0 + ti
                    b = tg // 4
                    p, pr = rotA.next()

                    def f(e, p=p, tg=tg):
                        for k in range(8):
                            ins = e.matmul(p[:, 0:32], hT[:, k, tg * 128:(tg + 1) * 128], wd[:, k, :], start=(k == 0), stop=(k == 7))
                        return ins
                    S.add("pe", f, reads=[wdr, hT.r(b)], writes=[pr])
                    S.add("dve", lambda e, p=p, ti=ti: e.tensor_tensor(out=dtr[:, ti, :], in0=p[:, 0:32], in1=bv[:, B_DTB:B_DTB + 32], op=ALU.add),
                          reads=[pr, bv.r()], writes=[dtr.r()])
                S.add("act", lambda e: e.activation(out=dtr[:], in_=dtr[:], func=AF.Exp), reads=[dtr.r()], writes=[dtr.r()])
                S.add("act", lambda e: e.activation(out=dta[:], in_=dtr[:], func=AF.Ln, bias=ONE_AP[:, 0:1], scale=1.0), reads=[dtr.r(), ONE_AP.r()], writes=[dta.r()])
                S.add("dve", lambda e: e.tensor_tensor(out=aal[:], in0=dta[:], in1=aneg[:].unsqueeze(1).to_broadcast([128, nt, 32]), op=ALU.mult),
                      reads=[dta.r(), aneg.r()], writes=[aal.r()])
                ahi = A.alloc("ahi", [128, nt, 32], BF16)
                alo = A.alloc("alo", [128, nt, 32], BF16)
                S.add("act", lambda e: e.activation(out=ahi[:], in_=aal[:], func=AF.Copy), reads=[aal.r()], writes=[ahi.r()])
                S.add("dve", lambda e: e.tensor_tensor(out=alo[:], in0=aal[:], in1=ahi[:], op=ALU.subtract), reads=[aal.r(), ahi.r()], writes=[alo.r()])
                for ti in range(nt):
                    p, pr = rotT.next()
                    mm(p[:, 0:16], utri, aal[:, ti, 0:16], True, True, [cf32.r(), aal.r()], [pr])
                    mm(p[:, 16:32], ltri, aal[:, ti, 16:32], True, True, [cf32.r(), aal.r()], [pr])
                    mm(p[:, 32:64], onesf, aal[:, ti, :], True, True, [cf32.r(), aal.r()], [pr])
                    S.add("dve", lambda e, p=p, ti=ti: e.tensor_copy(out=stats[:, ti, :], in_=p[:, 0:64]), reads=[pr], writes=[stats.r()])
                S.add("act", lambda e: e.activation(out=nacs[:], in_=dta[:], func=AF.Ln), reads=[dta.r()], writes=[nacs.r()])
                S.add("dve", lambda e: e.tensor_tensor(out=nacs[:], in0=nacs[:], in1=stats[:, :, 0:32], op=ALU.subtract),
                      reads=[stats.r(), nacs.r()], writes=[nacs.r()])
                S.add("act", lambda e: e.activation(out=eacs[:], in_=stats[:, :, 0:32], func=AF.Exp), reads=[stats.r()], writes=[eacs.r()])
                S.add("dve", lambda e: e.tensor_tensor(out=wdec[:], in0=stats[:, :, 32:64], in1=stats[:, :, 0:32], op=ALU.subtract), reads=[stats.r()], writes=[wdec.r()])
                S.add("act", lambda e: e.activation(out=wdec[:], in_=wdec[:], func=AF.Exp), reads=[wdec.r()], writes=[wdec.r()])
                S.add("dve", lambda e: e.tensor_tensor(out=wdec[:], in0=wdec[:], in1=dta[:], op=ALU.mult), reads=[wdec.r(), dta.r()], writes=[wdec.r()])
                S.add("act", lambda e: e.activation(out=cdec[:], in_=stats[:, :, 32:64], func=AF.Exp), reads=[stats.r()], writes=[cdec.r()])
                for g in range(2):
                    src = cdec[g * 64:(g + 1) * 64, :, :].rearrange("p t (d g h) -> p t d g h", d=2, g=2)[:, :, :, g, :]
                    S.add("dve", lambda e, g=g, src=src: e.tensor_copy(out=cdsel[g * 64:(g + 1) * 64, :, :, :], in_=src), reads=[cdec.r()], writes=[cdsel.r()])
                A.free(dtr, cdec, stats, aal)
                if SUB == 2:
                    return
                Sst = [A.alloc("Sst%d" % d, [128, 512], F32) for d in range(2)]
                hinb = A.alloc("hinb", [128, nt, 512], BF16)
                hinf = A.alloc("hinf", [128, 512], BF16)
                xw = [A.alloc("xw0", [128, 1024], BF16)] * 2
                stmp = A.alloc("stmp", [128, 512], F32)

                class _Stg:
                    def __getitem__(self, k):
                        return stmp[:].rearrange("p (a n) -> p a n", a=8)[k]

                    def r(self):
                        return stmp.r()
                stg = _Stg()
                rotP = Rot([5, 6])
                nxw = [0]

                def init_state(d):
                    St = Sst[d]
                    if grp == 0:
                        S.add("pool", lambda e: e.memset(St[:], 0.0), writes=[St.r()])
                    else:
                        dma("sp", stg[:], d_sssd[d].rearrange("(a p) n -> p a n", p=128), writes=[stg.r()])
                        for g in range(2):
                            p, pr = rotP.next()

                            def f(e, p=p, g=g):
                                for a4 in range(4):
                                    ins = e.transpose(p[0:64, a4 * 128:(a4 + 1) * 128], stg[:, g * 4 + a4, :], ident)
                                return ins
                            S.add("pe", f, reads=[stg.r(), cf32.r()], writes=[pr])
                            S.add("dve", lambda e, p=p, g=g: e.tensor_copy(out=St[g * 64:(g + 1) * 64, :], in_=p[0:64, :]), reads=[pr], writes=[St.r()])

                def out_state(d, sidx):
                    St = Sst[d]
                    for g in range(2):
                        p, pr = rotP.next()

                        def f(e, p=p, g=g):
                            for s4 in range(4):
                                ins = e.transpose(p[:, s4 * 64:(s4 + 1) * 64], St[g * 64:(g + 1) * 64, s4 * 128:(s4 + 1) * 128], ident[g * 64:(g + 1) * 64, g * 64:(g + 1) * 64])
                            return ins
                        S.add("pe", f, reads=[St.r(), cf32.r()], writes=[pr])
                        S.add("dve", lambda e, p=p, g=g: e.tensor_copy(out=stg[:, g * 4:(g + 1) * 4, :], in_=p[:, 0:256].rearrange("p (a n) -> p a n", a=4)), reads=[pr], writes=[stg.r()])
                    dma("sp", o_ssd[sidx, d].rearrange("(a p) n -> p a n", p=128), stg[:], reads=[stg.r()])

                def state_update(d, ti):
                    St = Sst[d]
                    x_ = xw[nxw[0] % 2]
                    nxw[0] += 1
                    S.add("dve", lambda e, x_=x_, ti=ti, d=d: e.tensor_tensor(
                        out=x_[:].rearrange("p (h q) -> p h q", h=16), in0=xs_tok[:, ti, :].rearrange("p (h q) -> p h q", h=16),
                        in1=wdec[:, ti, d * 16:(d + 1) * 16].unsqueeze(2).to_broadcast([128, 16, 64]), op=ALU.mult),
                        reads=[xs_tok.r(), wdec.r()], writes=[x_.r()])
                    p, pr = rotP.next()
                    for g in range(2):
                        mm(p[g * 64:(g + 1) * 64, :], B_tok[:, ti, g * 64:(g + 1) * 64], x_[:, g * 512:(g + 1) * 512], True, True, [B_tok.r(), x_.r()], [pr])
                    S.add("dve", lambda e, ti=ti, d=d: e.tensor_tensor(
                        out=stmp[:].rearrange("p (h q) -> p h q", h=8), in0=St[:].rearrange("p (h q) -> p h q", h=8),
                        in1=cdsel[:, ti, d, :].unsqueeze(2).to_broadcast([128, 8, 64]), op=ALU.mult),
                        reads=[St.r(), cdsel.r()], writes=[stmp.r()])
                    S.add("dve", lambda e, p=p: e.tensor_tensor(out=St[:], in0=p[:], in1=stmp[:], op=ALU.add), reads=[pr, stmp.r()], writes=[St.r()])

                for sidx, sq_ in enumerate(seqs):
                    init_state(1)
                    for ti in reversed(sq_):
                        S.add("act", lambda e, ti=ti: e.activation(out=hinb[:, ti, :], in_=Sst[1][:], func=AF.Copy), reads=[Sst[1].r()], writes=[hinb.r()])
                        if SUB != 5:
                            state_update(1, ti)
                    if grp == 0 and SUB != 4:
                        out_state(1, sidx)
                if SUB in (3, 4, 5):
                    return
                cbs = [A.alloc("cbs%d" % i, [128, 2, 128], F32) for i in range(2)]
                decb = [A.alloc("decb%d" % i, [128, 2, 128], F32) for i in range(3)]
                MTb = [A.alloc("MTb%d" % i, [128, 2, 128], BF16) for i in range(3)]
                yaccs = [A.alloc("yacc%d" % i, [128, 1024], F32) for i in range(2)]
                ytmp = A.alloc("ytmp", [128, 1024], F32)
                ssq = A.alloc("ssq", [128, 2], F32)
                rotD = Rot([1, 2, 3, 4])
                nm_ = 0
                tiles_seq = []
                for sidx, sq_ in enumerate(seqs):
                    for i_, ti in enumerate(sq_):
                        tiles_seq.append((sidx, ti, i_ == 0, i_ == len(sq_) - 1))

                def make_tail(sidx, ti, first, last, yacc):
                    tg = t0 + ti
                    tcol = slice(ti * 128, (ti + 1) * 128)
                    th = []
                    if first:
                        th.append(lambda: init_state(0))
                    th.append(lambda: S.add("act", lambda e: e.activation(out=hinf[:], in_=Sst[0][:], func=AF.Copy), reads=[Sst[0].r()], writes=[hinf.r()]))
                    for d in range(2):
                        for g in range(2):
                            def yo(d=d, g=g):
                                hin = hinf[:] if d == 0 else hinb[:, ti, :]
                                hreg = hinf.r() if d == 0 else hinb.r()
                                p, pr = rotP.next()
                                mm(p[:], CT[g * 64:(g + 1) * 64, tcol], hin[g * 64:(g + 1) * 64, :], True, True, [CT.r(), hreg], [pr])
                                ea = eacs[:, ti, d * 16 + g * 8:d * 16 + g * 8 + 8].unsqueeze(2).to_broadcast([128, 8, 64])
                                pv = p[:].rearrange("p (h q) -> p h q", h=8)
                                tsl = ytmp[:, g * 512:(g + 1) * 512].rearrange("p (h q) -> p h q", h=8)
                                S.add("dve", lambda e: e.tensor_tensor(out=tsl, in0=pv, in1=ea, op=ALU.mult), reads=[pr, eacs.r()], writes=[ytmp.r()])
                                S.add("dve", lambda e: e.tensor_tensor(out=yacc[:, g * 512:(g + 1) * 512], in0=yacc[:, g * 512:(g + 1) * 512],
                                                                       in1=ytmp[:, g * 512:(g + 1) * 512], op=ALU.add),
                                      reads=[yacc.r(), ytmp.r()], writes=[yacc.r()])
                            th.append(yo)
                    th.append(lambda: S.add("dve", lambda e: e.tensor_tensor(
                        out=ytmp[:].rearrange("p (h q) -> p h q", h=16), in0=xs_tok[:, ti, :].rearrange("p (h q) -> p h q", h=16),
                        in1=bv[:, B_D:B_D + 16].unsqueeze(2).to_broadcast([128, 16, 64]), op=ALU.mult),
                        reads=[xs_tok.r(), bv.r(), yacc.r()], writes=[ytmp.r()]))
                    th.append(lambda: S.add("dve", lambda e: e.tensor_tensor(out=yacc[:], in0=yacc[:], in1=ytmp[:], op=ALU.add), reads=[yacc.r(), ytmp.r()], writes=[yacc.r()]))
                    th.append(lambda: S.add("dve", lambda e: e.tensor_tensor(out=yacc[:], in0=yacc[:], in1=z_tok[:, ti, :], op=ALU.mult), reads=[yacc.r(), z_tok.r()], writes=[yacc.r()]))
                    th.append(lambda: S.add("act", lambda e: e.activation(out=ytmp[:], in_=yacc[:], func=AF.Square, accum_out=ssq[:, 0:1]), reads=[yacc.r()], writes=[ytmp.r(), ssq.r()]))
                    th.append(lambda: S.add("act", lambda e: e.activation(out=ssq[:, 0:1], in_=ssq[:, 0:1], func=AF.Ln, bias=EPS_AP[:, 0:1], scale=1.0 / 1024), reads=[ssq.r(), EPS_AP.r()], writes=[ssq.r()]))
                    th.append(lambda: S.add("act", lambda e: e.activation(out=ssq[:, 0:1], in_=ssq[:, 0:1], func=AF.Exp, scale=-0.5), reads=[ssq.r()], writes=[ssq.r()]))
                    th.append(lambda: S.add("dve", lambda e: e.scalar_tensor_tensor(out=ytmp[:], in0=yacc[:], scalar=ssq[:, 0:1], in1=bv[:, B_NORM:B_NORM + 1024], op0=ALU.mult, op1=ALU.mult),
                                            reads=[yacc.r(), ssq.r(), bv.r()], writes=[ytmp.r()]))
                    for hf in range(2):
                        def trp(hf=hf):
                            p, pr = PS(7) if hf == 0 else rotP.next()

                            def ftr(e):
                                for c in range(4):
                                    ins = e.transpose(p[:, c * 128:(c + 1) * 128], ytmp[:, (hf * 4 + c) * 128:(hf * 4 + c + 1) * 128], ident)
                                return ins
                            S.add("pe", ftr, reads=[ytmp.r(), cf32.r()], writes=[pr])
                            outv = hT[:, hf * 4:hf * 4 + 4, tg * 128:(tg + 1) * 128]
                            S.add("act", lambda e: e.activation(out=outv, in_=p[:].rearrange("p (c n) -> p c n", c=4), func=AF.Copy),
                                  reads=[pr], writes=[hT.r(tg // 4)])
                        th.append(trp)
                    th.append(lambda: state_update(0, ti))
                    if last and grp == 0:
                        th.append(lambda: out_state(0, sidx))
                    return th

                pending = []
                for tix, (sidx, ti, first, last) in enumerate(tiles_seq):
                    yacc = yaccs[tix % 2]
                    tcol = slice(ti * 128, (ti + 1) * 128)
                    cb_ = cbs[ti % 2]
                    for g in range(2):
                        p, pr = PS(7) if g == 0 else rotP.next()
                        mm(p[:, 0:128], BT[g * 64:(g + 1) * 64, tcol], CT[g * 64:(g + 1) * 64, tcol], True, True, [BT.r(), CT.r()], [pr])
                        S.add("act", lambda e, p=p, cb_=cb_, g=g: e.activation(out=cb_[:, g, :], in_=p[:, 0:128], func=AF.Copy), reads=[pr], writes=[cb_.r()])
                    ydp, ydr = PS(0)
                    munits = [(hp, d) for hp in range(8) for d in range(2)]
                    mts = {}

                    def emit_pdc(u, ti=ti, cb_=cb_, mts=mts, munits=munits):
                        nonlocal nm_
                        hp, d = munits[u]
                        g = hp // 4
                        pd, pdr = rotD.next()

                        def f(e, pd=pd, ti=ti, hp=hp, d=d):
                            for i_ in range(2):
                                hc = d * 16 + 2 * hp + i_
                                o_ = pd[:, i_ * 128:(i_ + 1) * 128]
                                e.matmul(o_, ahi[:, ti, hc:hc + 1].to_broadcast([128, 128]), tri[d], start=True, stop=False)
                                e.matmul(o_, alo[:, ti, hc:hc + 1].to_broadcast([128, 128]), tri[d], start=False, stop=False)
                                ins = e.matmul(o_, nmk[d], identb, start=False, stop=True)
                            return ins
                        S.add("pe", f, reads=[ahi.r(), alo.r(), cb16.r()], writes=[pdr])
                        dc = decb[nm_ % 3]
                        mt = MTb[nm_ % 3]
                        nm_ += 1
                        for i_ in range(2):
                            hc = d * 16 + 2 * hp + i_
                            S.add("act", lambda e, pd=pd, dc=dc, ti=ti, hc=hc, i_=i_: e.activation(out=dc[:, i_, :], in_=pd[:, i_ * 128:(i_ + 1) * 128], func=AF.Exp,
                                                                                                     bias=nacs[:, ti, hc:hc + 1], scale=1.0),
                                  reads=[pdr, nacs.r()], writes=[dc.r()])
                        S.add("pool" if u % 2 == 1 else "dve", lambda e, dc=dc, mt=mt, g=g, cb_=cb_: e.tensor_tensor(
                            out=mt[:], in0=dc[:], in1=cb_[:, g, :].unsqueeze(1).to_broadcast([128, 2, 128]), op=ALU.mult),
                            reads=[dc.r(), cb_.r()], writes=[mt.r()])
                        mts[u] = mt

                    LAm = 3
                    for u in range(LAm):
                        emit_pdc(u)
                    for u, (hp, d) in enumerate(munits):
                        mt = mts.pop(u)

                        def fy(e, mt=mt, hp=hp, d=d, u=u, ti=ti):
                            for i_ in range(2):
                                h = 2 * hp + i_
                                ins = e.matmul(ydp[:, (h % 8) * 64:(h % 8 + 1) * 64], mt[:, i_, :], xs_tok[:, ti, h * 64:(h + 1) * 64],
                                               start=(u % 8 == 0 and i_ == 0), stop=(d == 1), skip_group_check=True)
                            return ins
                        S.add("pe", fy, reads=[mt.r(), xs_tok.r()], writes=[ydr])
                        if u + LAm < len(munits):
                            emit_pdc(u + LAm)
                        if pending:
                            pending.pop(0)()
                        if u % 8 == 7:
                            g = u // 8
                            S.add("act", lambda e, g=g, yacc=yacc: e.activation(out=yacc[:, g * 512:(g + 1) * 512], in_=ydp[:], func=AF.Copy), reads=[ydr], writes=[yacc.r()])
                        if u == 15:
                            while pending:
                                pending.pop(0)()
                    pending = pending + make_tail(sidx, ti, first, last, yacc)
                while pending:
                    pending.pop(0)()
                A.free(xs_tok, z_tok, B_tok, BT, CT, dta, ahi, alo, nacs, eacs, wdec, cdsel, *Sst, hinb, hinf, xw[0], stmp,
                       *cbs, *decb, *MTb, *yaccs, ytmp, ssq)
            for grp in range((2 if SUB == 0 else 1) if STAGE >= 13 else 0):
                ssd_group(grp if SUB < 10 else 1)
            A.free(aneg)
            while wstate["got"] < wstate["woo"]:
                wget()
            rotW = Rot([0, 1, 2, 3])
            for t in range(4):
                w, wr = wget()
                for oc in range(2):
                    o = t * 2 + oc
                    for b in range(3):
                        blk = slice(b * TB, (b + 1) * TB)
                        p, pr = rotW.next()

                        def f(e, p=p, w=w, oc=oc, blk=blk):
                            for k in range(12):
                                rhs = yl[:, k, blk] if k < 4 else hT[:, k - 4, blk]
                                ins = e.matmul(p[:], w[:, k, oc * 128:(oc + 1) * 128], rhs, start=(k == 0), stop=(k == 11))
                            return ins
                        S.add("pe", f, reads=[wr, hT.r(b), yl.r()], writes=[pr])
                        resid(p, pr, l, 2, o, b)
            A.free(yl)

        if STAGE >= 1:
            for t in range(4):
                ada_tile(0, t, 0)
            ada_finish(0, 0)
        if STAGE >= 2:
            even_layer(0)
        if STAGE >= 10:
            ffn(0)
        if NLAYERS > 1:
            odd_layer(1)
            if STAGE >= 20:
                ffn(1)

        yst = [A.alloc("yst%d" % i, [128, D], F32) for i in range(2)]
        rot = Rot([0, 1, 2, 3, 4, 5, 6, 7])
        for t in range(12):
            ys = yst[t % 2]
            for hf in range(2):
                p, pr = rot.next()

                def tr(e, p=p, t=t, hf=hf):
                    for c in range(4):
                        ins = e.transpose(p[:, c * 128:(c + 1) * 128], xT[:, hf * 4 + c, t * 128:(t + 1) * 128], ident)
                    return ins
                S.add("pe", tr, reads=[xT.r(t), cf32.r()], writes=[pr])
                if hf == 0:
                    S.add("act", lambda e, ys=ys, p=p: e.activation(out=ys[:, 0:512], in_=p[:], func=AF.Copy), reads=[pr], writes=[ys.r()])
                else:
                    S.add("dve", lambda e, ys=ys, p=p: e.tensor_copy(out=ys[:, 512:1024], in_=p[:]), reads=[pr], writes=[ys.r()])
            dma("sp", o_y[t * 128:(t + 1) * 128, :], ys[:], reads=[ys.r()])
        assert STAGE < 99 or wstate["got"] == len(wplan), (wstate, len(wplan))
        S.emit(st)
    return nc


_CACHE = {}


def _prep_shared(inp):
    f = np.float32
    sh = {}
    sh["w_ada"] = np.ascontiguousarray(inp["w_ada"], f)
    wie = np.asarray(inp["w_in_even"][0], f)
    fcols = wie[:, 0:256]
    q = wie[:, 256:1024].reshape(D, 12, 64)[:, PERM, :].reshape(D, 768)
    k = wie[:, 1024:1280]
    v = wie[:, 1280:1536]
    sh["w_in_even"] = np.ascontiguousarray(np.concatenate([q, k, fcols, v], 1))
    woe = np.asarray(inp["w_out_even"][0], f)
    att = woe[256:].reshape(12, 64, D)[PERM].reshape(768, D)
    sh["w_out_even"] = np.ascontiguousarray(np.concatenate([woe[:256], att], 0))
    sh["w_in_odd"] = np.ascontiguousarray(inp["w_in_odd"][0], f)
    sh["w_out_odd"] = np.ascontiguousarray(inp["w_out_odd"][0], f)
    sh["ffn_w1"] = np.ascontiguousarray(inp["ffn_w1"], f)
    sh["ffn_w3"] = np.ascontiguousarray(inp["ffn_w3"], f)
    sh["ffn_w2"] = np.ascontiguousarray(inp["ffn_w2"], f)
    bd = np.zeros((2, 2, 4, 128, 128), f)
    for gi, wname in enumerate(("lru_wa", "lru_wx")):
        wsrc = np.asarray(inp[wname][0], f)
        for d in range(2):
            for blk in range(8):
                c, hlf = blk // 2, blk % 2
                bd[gi, d, c, hlf * 64:(hlf + 1) * 64, hlf * 64:(hlf + 1) * 64] = wsrc[d, blk]
    sh["lru_bd"] = bd.reshape(16, 128, 128)
    bvec = np.zeros((1, NBV), f)
    bvec[0, B_NORM:B_NORM + 1024] = inp["ssd_norm"][0]
    bvec[0, B_D:B_D + 16] = inp["ssd_d"][0]
    bvec[0, B_DTB:B_DTB + 32] = np.asarray(inp["ssd_dt_bias"][0]).reshape(32)
    bvec[0, B_ALOG:B_ALOG + 32] = np.asarray(inp["ssd_a_log"][0]).reshape(32)
    sh["bvec"] = bvec
    sh.update(_consts())
    return sh


def _vecs(inp, i):
    f = np.float32
    v = np.zeros((256, 128), f)
    v[R_BADA:R_BADA + 96] = np.asarray(inp["b_ada"], f).reshape(96, 128)
    v[R_CCTX:R_CCTX + 8] = np.asarray(inp["c_ctx"], f).reshape(8, 128)
    v[R_CI:R_CI + 8] = np.asarray(inp["c"][i], f).reshape(8, 128)
    v[R_NMIX:R_NMIX + 16] = np.asarray(inp["norm_mix"], f).reshape(16, 128)
    v[R_NFFN:R_NFFN + 16] = np.asarray(inp["norm_ffn"], f).reshape(16, 128)
    v[R_CLW:R_CLW + 16] = np.asarray(inp["conv_lru_w"][0], f).reshape(16, 128)
    v[R_CLB:R_CLB + 4] = np.asarray(inp["conv_lru_b"][0], f).reshape(4, 128)
    v[R_BA:R_BA + 8] = np.asarray(inp["lru_ba"][0], f).reshape(8, 128)
    v[R_BX:R_BX + 8] = np.asarray(inp["lru_bx"][0], f).reshape(8, 128)
    v[R_LAM:R_LAM + 8] = np.asarray(inp["lru_lambda"][0], f).reshape(8, 128)
    v[R_CSW:R_CSW + 40] = np.asarray(inp["conv_ssd_w"][0], f).reshape(40, 128)
    v[R_CSB:R_CSB + 10] = np.asarray(inp["conv_ssd_b"][0], f).reshape(10, 128)
    v[R_QN] = np.tile(np.asarray(inp["q_norm"][0], f), 2)
    v[R_KN] = np.tile(np.asarray(inp["k_norm"][0], f), 2)
    v[R_LRU0:R_LRU0 + 8] = np.asarray(inp["state_lru"][i, 0], f).reshape(8, 128)
    return v


def kernel(**inp):
    inp = {k: np.asarray(v) for k, v in inp.items()}
    if "nc" not in _CACHE:
        _CACHE["nc"] = build_program()
    nc = _CACHE["nc"]
    sh = _prep_shared(inp)
    in_maps = []
    for i in range(NCORES):
        m = dict(sh)
        m["x"] = np.ascontiguousarray(np.concatenate(
            [inp["x_prompt"][2 * i], inp["x_prompt"][2 * i + 1], inp["x_sample"][i]], 0), np.float32)
        m["vecs"] = _vecs(inp, i)
        m["cache_k"] = np.ascontiguousarray(inp["cache_k"][i, 0].reshape(512, 256), np.float32)
        m["cache_v"] = np.ascontiguousarray(inp["cache_v"][i, 0].reshape(512, 256), np.float32)
        m["state_ssd"] = np.ascontiguousarray(inp["state_ssd"][i, 0].reshape(2, 1024, 64), np.float32)
        in_maps.append(m)
    res = run_bass_kernel_spmd(nc, in_maps[:NRUN], core_ids=list(range(NRUN)))
    R = res.results
    y_prompt = np.zeros((16, 256, D), np.float32)
    y_sample = np.zeros((8, 1024, D), np.float32)
    new_k = np.zeros((16, 1, 256, 4, 64), np.float32)
    new_v = np.zeros((16, 1, 256, 4, 64), np.float32)
    new_lru = np.zeros((16, 1, 2, 512), np.float32)
    new_ssd = np.zeros((16, 1, 2, 16, 64, 64), np.float32)
    for i in range(NRUN):
        y = np.asarray(R[i]["y"])
        y_prompt[2 * i] = y[0:256]
        y_prompt[2 * i + 1] = y[256:512]
        y_sample[i] = y[512:]
        nk = np.asarray(R[i]["newk"]).reshape(2, 256, 4, 64)
        nv = np.asarray(R[i]["newv"]).reshape(2, 256, 4, 64)
        new_k[2 * i:2 * i + 2, 0] = nk
        new_v[2 * i:2 * i + 2, 0] = nv
        new_lru[2 * i:2 * i + 2, 0] = np.asarray(R[i]["newlru"]).reshape(2, 2, 512)
        new_ssd[2 * i:2 * i + 2, 0] = np.asarray(R[i]["newssd"]).reshape(2, 2, 16, 64, 64)
    return (y_prompt, y_sample, new_k, new_v, new_lru, new_ssd)
```

```python
import numpy as np
from contextlib import ExitStack
import concourse.bass as bass
import concourse.mybir as mybir
from concourse.bass_utils import run_bass_kernel_spmd

F32 = mybir.dt.float32
BF16 = mybir.dt.bfloat16
AF = mybir.ActivationFunctionType
ALU = mybir.AluOpType

ENGS = ("pe", "act", "dve", "pool", "sp")
NDMASEM = 6
NCORES = 8
NRUN = 8
EPS = 1e-6
NLAYERS = 2
STAGE = 99
SUB = 0


class Reg:
    __slots__ = ("name", "lw", "rd", "excl")

    def __init__(self, name, inherit=(), excl=False):
        self.name = name
        self.lw = None
        self.rd = list(inherit)
        self.excl = excl


class Op:
    __slots__ = ("eng", "fn", "deps", "needed", "dma", "sem", "val", "gidx", "thr")

    def __init__(self, eng, fn, dma):
        self.eng = eng
        self.fn = fn
        self.dma = dma
        self.deps = []
        self.needed = False
        self.sem = None
        self.val = 0
        self.thr = None


class Sched:
    def __init__(self, nc):
        self.nc = nc
        self.ops = {e: [] for e in ENGS}
        self.all = []
        self.ndma = {e: 0 for e in ENGS}
        self.dmaops = {e: [] for e in ENGS}

    def add(self, eng, fn, reads=(), writes=(), dma=False):
        op = Op(eng, fn, dma)
        deps = {}
        for r in reads:
            w = r.lw
            if w is not None and (w.dma or dma or w.eng != eng or eng != "pe"):
                deps[id(w)] = w
            if r.excl:
                for q in r.rd:
                    if q.eng != eng:
                        deps[id(q)] = q
        for t in writes:
            w = t.lw
            if w is not None and (w.dma or dma or w.eng != eng or eng != "pe"):
                deps[id(w)] = w
            for q in t.rd:
                if q.dma or dma or q.eng != eng or eng != "pe":
                    deps[id(q)] = q
        op.deps = list(deps.values())
        for r in reads:
            if not dma:
                r.rd = [q for q in r.rd if q.dma or q.eng != eng]
            r.rd.append(op)
        for t in writes:
            t.lw = op
            t.rd = []
        if dma:
            i = self.ndma[eng]
            self.ndma[eng] += 1
            if i >= NDMASEM:
                op.thr = self.dmaops[eng][i - NDMASEM]
            self.dmaops[eng].append(op)
        op.gidx = len(self.all)
        self.all.append(op)
        self.ops[eng].append(op)
        return op

    def emit(self, stack):
        nc = self.nc
        for op in self.all:
            for d in op.deps:
                d.needed = True
        esem = {e: stack.enter_context(nc.semaphore("s_" + e)) for e in ENGS if e != "sp"}
        dsem = {e: [stack.enter_context(nc.semaphore("d_%s%d" % (e, i))) for i in range(NDMASEM)]
                for e in ENGS if self.ndma[e] > 0}
        for e in ENGS:
            cnt = 0
            i = 0
            for op in self.ops[e]:
                if op.dma:
                    op.sem = dsem[e][i % NDMASEM]
                    op.val = 16 * (i // NDMASEM + 1)
                    i += 1
                elif op.needed:
                    cnt += 1
                    op.sem = esem[e]
                    op.val = cnt
        block = stack.enter_context(nc.Block())
        engh = {"pe": block.tensor, "act": block.scalar, "dve": block.vector,
                "pool": block.gpsimd, "sp": block.sync}
        for e in ENGS:
            ops = self.ops[e]
            if not ops:
                continue

            def body(eng, ops=ops, e=e):
                waited = {}
                for op in ops:
                    ds = list(op.deps)
                    if op.thr is not None:
                        ds.append(op.thr)
                    for d in ds:
                        k = id(d.sem)
                        if waited.get(k, 0) >= d.val:
                            continue
                        waited[k] = d.val
                        eng.wait_ge(d.sem, d.val)
                    ins = op.fn(eng)
                    if op.dma:
                        ins.then_inc(op.sem, 16)
                    elif op.needed:
                        ins.then_inc(op.sem, 1)
                for d in self.dmaops[e][-NDMASEM:]:
                    if waited.get(id(d.sem), 0) < d.val:
                        waited[id(d.sem)] = d.val
                        eng.wait_ge(d.sem, d.val)

            engh[e](body)


def _prune(ops):
    best = {}
    out = []
    for o in ops:
        if o.dma:
            out.append(o)
        else:
            b = best.get(o.eng)
            if b is None or o.gidx > b.gidx:
                best[o.eng] = o
    return out + list(best.values())


class TV:
    def __init__(self, ap, name, off, nw, inherit):
        self.ap = ap
        self.name = name
        self.off = off
        self.nw = nw
        self.inherit = inherit
        self.regs = {}

    def __getitem__(self, k):
        return self.ap[k]

    def r(self, key=0):
        g = self.regs.get(key)
        if g is None:
            g = Reg("%s.%s" % (self.name, key), self.inherit)
            self.regs[key] = g
        return g

    def rs(self, keys):
        return [self.r(k) for k in keys]


class Arena:
    def __init__(self, nc, stack, nwords):
        self.t = stack.enter_context(nc.sbuf_tensor("arena", [128, nwords], F32))
        self.n = nwords
        self.live = []
        self.dead = []
        self.peak = 0

    def alloc(self, name, shape, dt):
        n = int(np.prod(shape[1:]))
        nw = n if dt == F32 else (n + 1) // 2
        nw = (nw + 7) // 8 * 8
        off = 0
        for tv in sorted(self.live, key=lambda t: t.off):
            if tv.off - off >= nw:
                break
            off = max(off, tv.off + tv.nw)
        assert off + nw <= self.n, "SBUF arena overflow allocating %s (%d words at %d)" % (name, nw, off)
        self.peak = max(self.peak, off + nw)
        pend = []
        keep = []
        for (o, w, ops) in self.dead:
            if o < off + nw and off < o + w:
                pend += ops
                if not (off <= o and o + w <= off + nw):
                    keep.append((o, w, ops))
            else:
                keep.append((o, w, ops))
        self.dead = keep
        v = self.t[0:shape[0], off:off + nw]
        if dt != F32:
            v = v.bitcast(dt)
        v = v[:, 0:n]
        if len(shape) == 3:
            v = v.rearrange("p (a b) -> p a b", a=shape[1])
        elif len(shape) == 4:
            v = v.rearrange("p (a b c) -> p a b c", a=shape[1], b=shape[2])
        tv = TV(v, name, off, nw, _prune(pend))
        self.live.append(tv)
        return tv

    def free(self, *tvs):
        for tv in tvs:
            ops = list(tv.inherit)
            for g in tv.regs.values():
                if g.lw is not None:
                    ops.append(g.lw)
                ops += g.rd
            self.dead.append((tv.off, tv.nw, _prune(ops)))
            self.live.remove(tv)


D = 1024
NTOK = 1536
TB = 512
PERM = [0, 3, 1, 4, 2, 5, 6, 9, 7, 10, 8, 11]
DFF = 2816
R_BADA, R_CCTX, R_CI, R_NMIX, R_NFFN = 0, 96, 104, 112, 128
R_CLW, R_CLB, R_BA, R_BX, R_LAM, R_CSW, R_CSB, R_QN, R_KN, R_LRU0 = 144, 160, 164, 172, 180, 188, 228, 238, 239, 240
B_NORM, B_D, B_DTB, B_ALOG, NBV = 0, 1024, 1040, 1072, 1104


def _consts():
    c = {}
    ident = np.eye(128, dtype=np.float32)
    onesm = np.full((128, 128), 1.0 / 1024, np.float32)
    o64 = np.zeros((128, 128), np.float32)
    o64[:64, :64] = 1.0 / 64
    o64[64:, 64:] = 1.0 / 64
    R = np.zeros((128, 128), np.float32)
    for h in range(2):
        for ax in range(2):
            for i in range(16):
                p1 = h * 64 + ax * 32 + i
                p2 = p1 + 16
                R[p1, p2] = -1.0
                R[p2, p1] = 1.0
    j = np.arange(128)
    utri = (j[:, None] <= j[None, :]).astype(np.float32)
    ltri = (j[:, None] >= j[None, :]).astype(np.float32)
    nmf = np.where(j[:, None] < j[None, :], -32768.0, 0.0).astype(np.float32)
    nmb = np.where(j[:, None] > j[None, :], -32768.0, 0.0).astype(np.float32)
    onesf = np.ones((128, 128), np.float32)
    c["cf32"] = np.stack([ident, R.T.copy(), utri, ltri, onesf])
    a64 = 2 * np.pi * np.outer(np.arange(64), np.arange(64)) / 64
    c64 = np.zeros((2, 128, 128), np.float32)
    for g in range(2):
        c64[0, g * 64:(g + 1) * 64, g * 64:(g + 1) * 64] = np.cos(a64) / 8
        c64[1, g * 64:(g + 1) * 64, g * 64:(g + 1) * 64] = -np.sin(a64) / 8
    c["cb16"] = np.stack([onesm, o64, nmf, nmb, ident, c64[0], c64[1], utri, ltri])
    for S in (256, 1024):
        a = 2 * np.pi * (np.outer(np.arange(S), np.arange(S)) % S) / S
        c["dft%d" % S] = np.stack([np.cos(a), np.sin(a)]).astype(np.float32) / np.sqrt(S)
    s = np.arange(1024)
    row = (s // 64).astype(np.float32)
    col = (s % 64).astype(np.float32)
    freqs = (10000.0 ** (-np.arange(16, dtype=np.float32) / 16)).astype(np.float32)
    ang = np.zeros((64, 1024), np.float32)
    for d in range(64):
        ax = d // 32
        i = d % 16
        ang[d] = (row if ax == 0 else col) * freqs[i]
    ang = np.concatenate([ang, ang], 0)
    c["rope"] = np.stack([np.cos(ang), np.sin(ang)]).astype(np.float32)
    return c


def build_program():
    nc = bass.Bass("TRN2", target_bir_lowering=False)
    S = Sched(nc)

    def din(name, shape):
        return nc.dram_tensor(name, list(shape), F32, kind="ExternalInput").ap()

    def dout(name, shape):
        return nc.dram_tensor(name, list(shape), F32, kind="ExternalOutput").ap()

    d_x = din("x", [NTOK, D])
    d_vecs = din("vecs", [256, 128])
    d_bvec = din("bvec", [1, NBV])
    d_wada = din("w_ada", [2, D, 6 * D])
    d_wie = din("w_in_even", [D, 1536])
    d_woe = din("w_out_even", [D, D])
    d_wio = din("w_in_odd", [D, 3360])
    d_woo = din("w_out_odd", [1536, D])
    d_w1 = din("ffn_w1", [2, D, DFF])
    d_w3 = din("ffn_w3", [2, D, DFF])
    d_w2 = din("ffn_w2", [2, DFF, D])
    d_lrubd = din("lru_bd", [16, 128, 128])
    d_ck = din("cache_k", [512, 256])
    d_cv = din("cache_v", [512, 256])
    d_sssd = din("state_ssd", [2, 1024, 64])
    d_cf32 = din("cf32", [5, 128, 128])
    d_cb16 = din("cb16", [9, 128, 128])
    d_dft256 = din("dft256", [2, 256, 256])
    d_dft1024 = din("dft1024", [2, 1024, 1024])
    d_rope = din("rope", [2, 128, 1024])
    o_y = dout("y", [NTOK, D])
    o_k = dout("newk", [512, 256])
    o_v = dout("newv", [512, 256])
    o_lru = dout("newlru", [16, 128])
    o_ssd = dout("newssd", [2, 2, 1024, 64])

    with ExitStack() as st:
        A = Arena(nc, st, 53100)
        psb = []
        for i in range(8):
            t = st.enter_context(nc.psum_tensor("ps%d" % i, [128, 512], F32))
            psb.append((t, Reg("ps%d" % i, excl=True)))

        def PS(i):
            return psb[i]

        class Rot:
            def __init__(self, banks):
                self.b = list(banks)
                self.i = 0

            def next(self):
                r = psb[self.b[self.i % len(self.b)]]
                self.i += 1
                return r

        def dma(q, out, in_, reads=(), writes=()):
            return S.add(q, lambda e: e.dma_start(out=out, in_=in_), reads=reads, writes=writes, dma=True)

        def mm(out, lhsT, rhs, start, stop, reads, writes):
            return S.add("pe", lambda e: e.matmul(out, lhsT, rhs, start=start, stop=stop), reads=reads, writes=writes)

        NSLOT = 4
        slots = [A.alloc("wslot%d" % i, [128, 4096], BF16) for i in range(NSLOT)]
        wplan = []
        wstate = {"issued": 0, "got": 0}

        def wplan_add(ap, a, b):
            wplan.append((ap, a, b))

        def kp(ap2d):
            return ap2d.rearrange("(k p) n -> p k n", p=128)

        def wview(i):
            ap, a, b = wplan[i]
            sl = slots[i % NSLOT]
            return sl[:, 0:a * b].rearrange("p (a b) -> p a b", a=a), sl.r()

        def wget():
            i = wstate["got"]
            wstate["got"] += 1
            while wstate["issued"] < min(len(wplan), i + NSLOT - 1):
                j = wstate["issued"]
                v, rg = wview(j)
                dma("pool", v, wplan[j][0], writes=[rg])
                wstate["issued"] += 1
            return wview(i)

        def plan_ada(l):
            for t in range(12):
                wplan_add(kp(d_wada[l])[:, :, t * 512:(t + 1) * 512], 8, 512)

        def plan_ada_tiles(l, ts):
            for t in ts:
                wplan_add(kp(d_wada[l])[:, :, t * 512:(t + 1) * 512], 8, 512)

        def plan_ffn(l):
            for jt in range(6):
                n = 512 if jt < 5 else 256
                wplan_add(kp(d_w1[l])[:, :, jt * 512:jt * 512 + n], 8, n)
                wplan_add(kp(d_w3[l])[:, :, jt * 512:jt * 512 + n], 8, n)
            for o in range(8):
                wplan_add(kp(d_w2[l])[:, :, o * 128:(o + 1) * 128], 22, 128)

        plan_ada_tiles(0, range(0, 4))
        for t in range(3):
            wplan_add(kp(d_wie)[:, :, t * 512:(t + 1) * 512], 8, 512)
        plan_ada_tiles(0, range(4, 12))
        if NLAYERS > 1:
            plan_ada_tiles(1, range(12))
        for sb in range(2):
            for m in range(2):
                wplan_add(kp(d_dft1024[m])[:, :, sb * 512:(sb + 1) * 512], 8, 512)
        for t in range(2):
            wplan_add(kp(d_woe)[:, :, t * 512:(t + 1) * 512], 8, 512)
        plan_ffn(0)
        if NLAYERS > 1:
            wio = kp(d_wio)
            for t in range(2):
                wplan_add(wio[:, :, t * 512:(t + 1) * 512], 8, 512)
            for grp in range(2):
                wplan_add(wio[:, :, 2048:2560], 8, 512)
                wplan_add(wio[:, :, 2560:3072], 8, 512)
                wplan_add(wio[:, :, 3072:3328], 8, 256)
                wplan_add(wio[:, :, 3328:3360], 8, 32)
                wplan_add(wio[:, :, 1024:1536], 8, 512)
                wplan_add(wio[:, :, 1536:2048], 8, 512)
            wstate["woo"] = len(wplan)
            for t in range(4):
                wplan_add(kp(d_woo)[:, :, t * 256:(t + 1) * 256], 12, 256)
            plan_ffn(1)

        xT = A.alloc("xT", [128, 8, NTOK], F32)
        hT = A.alloc("hT", [128, 8, NTOK], BF16)
        cf32 = A.alloc("cf32", [128, 5, 128], F32)
        cb16 = A.alloc("cb16", [128, 9, 128], BF16)
        vecT = A.alloc("vecT", [128, 256], F32)
        bv = A.alloc("bv", [128, NBV], F32)
        mod = A.alloc("mod", [128, 2, 48, 2], F32)
        A1 = A.alloc("A1", [128, 2, 2, 8, 2], F32) if False else A.alloc("A1", [128, 64], F32)
        ident = cf32[:, 0, :]
        RT = cf32[:, 1, :]
        utri = cf32[:, 2, :]
        ltri = cf32[:, 3, :]
        onesf = cf32[:, 4, :]
        onesm = cb16[:, 0, :]
        o64 = cb16[:, 1, :]
        identb = cb16[:, 4, :]

        def A1v(l, kind, c, j):
            o = ((l * 2 + kind) * 8 + c) * 2 + j
            return A1[:, o:o + 1]

        def modv(l, q, c, j):
            return mod[:, l, q * 8 + c, j:j + 1]

        def xr(b):
            return xT.rs(range(4 * b, 4 * b + 4))

        dma("sp", cf32[:], d_cf32.rearrange("c p n -> p c n"), writes=[cf32.r()])
        dma("pool", cb16[:], d_cb16.rearrange("c p n -> p c n"), writes=[cb16.r()])
        vraw = A.alloc("vraw", [128, 2, 128], F32)
        dma("sp", vraw[:], d_vecs.rearrange("(a p) n -> p a n", p=128), writes=[vraw.r()])
        dma("sp", bv[:], d_bvec.partition_broadcast(128), writes=[bv.r()])
        for a in range(2):
            p, pr = PS(a)
            S.add("pe", lambda e, p=p, a=a: e.transpose(p[:, 0:128], vraw[:, a, :], ident),
                  reads=[vraw.r(), cf32.r()], writes=[pr])
            S.add("dve", lambda e, p=p, a=a: e.tensor_copy(out=vecT[:, a * 128:(a + 1) * 128], in_=p[:, 0:128]),
                  reads=[pr], writes=[vecT.r()])
        A.free(vraw)

        xst = [A.alloc("xst%d" % i, [128, D], F32) for i in range(2)]
        rot = Rot([2, 3, 4, 5, 6, 7])
        for t in range(12):
            xs_ = xst[t % 2]
            dma("sp", xs_[:], d_x[t * 128:(t + 1) * 128, :], writes=[xs_.r()])
            for hf in range(2):
                p, pr = rot.next()

                def tr(e, p=p, xs_=xs_, hf=hf):
                    for c in range(4):
                        ins = e.transpose(p[:, c * 128:(c + 1) * 128], xs_[:, (hf * 4 + c) * 128:(hf * 4 + c + 1) * 128], ident)
                    return ins
                S.add("pe", tr, reads=[xs_.r(), cf32.r()], writes=[pr])
                eng = "act" if hf == 0 else "dve"
                outv = xT[:, hf * 4:hf * 4 + 4, t * 128:(t + 1) * 128]
                inv = p[:].rearrange("p (c n) -> p c n", c=4)
                if eng == "act":
                    S.add("act", lambda e, o=outv, i=inv: e.activation(out=o, in_=i, func=AF.Copy), reads=[pr], writes=[xT.r(t)])
                else:
                    S.add("dve", lambda e, o=outv, i=inv: e.tensor_copy(out=o, in_=i), reads=[pr], writes=[xT.r(t)])
        A.free(*xst)

        scT = A.alloc("scT", [128, 8, 2], BF16)
        S.add("act", lambda e: e.activation(out=scT[:].rearrange("p k j -> p j k"),
                                            in_=vecT[:, R_CCTX:R_CCTX + 16].rearrange("p (j k) -> p j k", j=2), func=AF.Silu),
              reads=[vecT.r()], writes=[scT.r()])

        def ada_tile(l, t, bank):
            p, pr = PS(bank) if isinstance(bank, int) else bank
            wt, wr = wget()

            def f(e):
                for oc4 in range(4):
                    for k in range(8):
                        ins = e.matmul(p[:, oc4 * 2:oc4 * 2 + 2], wt[:, k, oc4 * 128:(oc4 + 1) * 128], scT[:, k, :],
                                       start=(k == 0), stop=(k == 7))
                return ins
            S.add("pe", f, reads=[wr, scT.r()], writes=[pr])
            r0 = R_BADA + l * 48 + t * 4
            S.add("dve", lambda e: e.tensor_tensor(out=mod[:, l, t * 4:(t + 1) * 4, :], in0=p[:, 0:8].rearrange("p (c j) -> p c j", j=2),
                                                   in1=vecT[:, r0:r0 + 4].unsqueeze(2).to_broadcast([128, 4, 2]),
                                                   op=ALU.add), reads=[pr, vecT.r()], writes=[mod.r()])

        def ada_finish(l, kind):
            q, rbase = ((1, R_NMIX), (4, R_NFFN))[kind]
            o = (l * 2 + kind) * 16
            S.add("dve", lambda e: e.scalar_tensor_tensor(
                out=A1[:, o:o + 16].rearrange("p (c j) -> p c j", j=2), in0=mod[:, l, q * 8:(q + 1) * 8, :], scalar=1.0,
                in1=vecT[:, rbase + l * 8:rbase + (l + 1) * 8].unsqueeze(2).to_broadcast([128, 8, 2]),
                op0=ALU.add, op1=ALU.mult), reads=[mod.r(), vecT.r()], writes=[A1.r()])

        def norm_mod(l, kind):
            qs = 0 if kind == 0 else 3
            sq = [A.alloc("nsq%d" % i, [128, TB], BF16) for i in range(4)]
            rs = [A.alloc("nrs%d" % i, [128, TB], F32) for i in range(3)]
            tt = [A.alloc("ntt%d" % i, [128, TB], F32) for i in range(3)]
            rot = Rot([6, 7])
            cnt = [0, 0]
            pbank = {}

            def stA(b):
                blk = slice(b * TB, (b + 1) * TB)
                p, pr = rot.next()
                pbank[b] = (p, pr)
                for c in range(8):
                    s_ = sq[cnt[0] % 4]
                    cnt[0] += 1
                    if c % 2 == 0:
                        S.add("act", lambda e, s_=s_, c=c: e.activation(out=s_[:], in_=xT[:, c, blk], func=AF.Square), reads=xr(b), writes=[s_.r()])
                    else:
                        S.add("dve", lambda e, s_=s_, c=c: e.tensor_tensor(out=s_[:], in0=xT[:, c, blk], in1=xT[:, c, blk], op=ALU.mult), reads=xr(b), writes=[s_.r()])
                    mm(p[:], onesm, s_[:], c == 0, c == 7, [s_.r(), cb16.r()], [pr])

            def stB(b):
                p, pr = pbank[b]
                r_ = rs[b]
                S.add("act", lambda e: e.activation(out=r_[:], in_=p[:], func=AF.Ln, bias=EPS_AP[:, 0:1], scale=1.0), reads=[pr, EPS_AP.r()], writes=[r_.r()])
                S.add("act", lambda e: e.activation(out=r_[:], in_=r_[:], func=AF.Exp, scale=-0.5), reads=[r_.r()], writes=[r_.r()])

            def stC(b):
                j = 0 if b == 0 else 1
                blk = slice(b * TB, (b + 1) * TB)
                r_ = rs[b]
                for c in range(8):
                    t_ = tt[cnt[1] % 3]
                    cnt[1] += 1
                    S.add("dve", lambda e, t_=t_, c=c: e.tensor_tensor(out=t_[:], in0=xT[:, c, blk], in1=r_[:], op=ALU.mult),
                          reads=xr(b) + [r_.r()], writes=[t_.r()])
                    S.add("act", lambda e, t_=t_, c=c: e.activation(out=hT[:, c, blk], in_=t_[:], func=AF.Identity,
                                                                    bias=modv(l, qs, c, j), scale=A1v(l, kind, c, j)),
                          reads=[t_.r(), mod.r(), A1.r()], writes=[hT.r(b)])

            stA(0)
            stA(1)
            stB(0)
            stC(0)
            stA(2)
            stB(1)
            stC(1)
            stB(2)
            stC(2)
            A.free(*sq, *rs, *tt)

        EPS_AP = A.alloc("eps", [128, 2], F32)
        S.add("pool", lambda e: e.memset(EPS_AP[:], EPS), writes=[EPS_AP.r()])
        ONE_AP = A.alloc("one", [128, 2], F32)
        S.add("pool", lambda e: e.memset(ONE_AP[:], 1.0), writes=[ONE_AP.r()])

        def resid(p, pr, l, q, o, b):
            j = 0 if b == 0 else 1
            blk = slice(b * TB, (b + 1) * TB)
            S.add("dve", lambda e: e.scalar_tensor_tensor(out=xT[:, o, blk], in0=p[:], scalar=modv(l, q, o, j), in1=xT[:, o, blk],
                                                          op0=ALU.mult, op1=ALU.add),
                  reads=[pr, mod.r()] + xr(b), writes=xr(b))

        def ffn(l):
            norm_mod(l, 1)
            actT = A.alloc("actT", [128, 22, NTOK], BF16)
            sl = [A.alloc("fsl%d" % i, [128, TB], BF16) for i in range(3)]
            rot1 = Rot([0, 1, 2, 3])
            n = 0
            for jt in range(6):
                w1, w1r = wget()
                w3, w3r = wget()
                ncs = 4 if jt < 5 else 2
                for cc in range(ncs):
                    jj = jt * 4 + cc
                    for b in range(3):
                        blk = slice(b * TB, (b + 1) * TB)
                        p1, p1r = rot1.next()
                        p3, p3r = rot1.next()

                        def f(e, w=w1, p=p1, cc=cc, blk=blk):
                            for k in range(8):
                                ins = e.matmul(p[:], w[:, k, cc * 128:(cc + 1) * 128], hT[:, k, blk], start=(k == 0), stop=(k == 7))
                            return ins
                        S.add("pe", f, reads=[w1r, hT.r(b)], writes=[p1r])

                        def f3(e, w=w3, p=p3, cc=cc, blk=blk):
                            for k in range(8):
                                ins = e.matmul(p[:], w[:, k, cc * 128:(cc + 1) * 128], hT[:, k, blk], start=(k == 0), stop=(k == 7))
                            return ins
                        S.add("pe", f3, reads=[w3r, hT.r(b)], writes=[p3r])
                        s_ = sl[n % 3]
                        n += 1
                        S.add("act", lambda e, s_=s_, p=p1: e.activation(out=s_[:], in_=p[:], func=AF.Silu), reads=[p1r], writes=[s_.r()])
                        S.add("dve", lambda e, s_=s_, p=p3, jj=jj, blk=blk: e.tensor_tensor(out=actT[:, jj, blk], in0=p[:], in1=s_[:], op=ALU.mult),
                              reads=[p3r, s_.r()], writes=[actT.r(b)])
            A.free(*sl)
            rot2 = Rot([4, 5, 6, 7])
            for o in range(8):
                w2, w2r = wget()
                for b in range(3):
                    blk = slice(b * TB, (b + 1) * TB)
                    p, pr = rot2.next()

                    def f(e, w=w2, p=p, blk=blk):
                        for jj in range(22):
                            ins = e.matmul(p[:], w[:, jj, :], actT[:, jj, blk], start=(jj == 0), stop=(jj == 21))
                        return ins
                    S.add("pe", f, reads=[w2r, actT.r(b)], writes=[pr])
                    resid(p, pr, l, 5, o, b)
            A.free(actT)

        def even_layer(l):
            norm_mod(l, 0)
            qT = A.alloc("qT", [128, 6, NTOK], BF16)
            kT = A.alloc("kT", [128, 2, 2048], BF16)
            vaug = A.alloc("vaug", [128, 16, 4, 128], BF16)
            ftok = A.alloc("ftok", [128, 12, 256], BF16)
            rope = A.alloc("rope", [128, 2, 1024], F32)
            dft256 = A.alloc("dft256", [128, 2, 2, 256], BF16)
            knew = A.alloc("knew", [128, 4, 256], F32)
            vnew = A.alloc("vnew", [128, 4, 256], F32)
            dma("sp", rope[:], d_rope.rearrange("c p n -> p c n"), writes=[rope.r()])
            for m in range(2):
                dma("pool", dft256[:, m, :, :], d_dft256[m].rearrange("(t p) n -> p t n", p=128), writes=[dft256.r()])
            S.add("pool", lambda e: e.memset(vaug[:], 1.0), writes=vaug.rs(range(16)))
            cvv = d_cv.rearrange("(t p) (j d) -> p t j d", p=128, j=4)
            for par in range(2):
                for tt_ in range(4):
                    dma("pool", vaug[:, 12 + tt_, par::2, par * 64:par * 64 + 64], cvv[:, tt_, par::2, :], writes=[vaug.r(12 + tt_)])
            ckt = A.alloc("ckt", [128, 4, 256], F32)
            dma("sp", ckt[:], d_ck.rearrange("(t p) f -> p t f", p=128), writes=[ckt.r()])
            rotc = Rot([4, 5])
            for kc in range(2):
                p, pr = rotc.next()

                def f(e, p=p, kc=kc):
                    for tt_ in range(4):
                        ins = e.transpose(p[:, tt_ * 128:(tt_ + 1) * 128], ckt[:, tt_, kc * 128:(kc + 1) * 128], ident)
                    return ins
                S.add("pe", f, reads=[ckt.r(), cf32.r()], writes=[pr])
                S.add("dve", lambda e, p=p, kc=kc: e.tensor_copy(out=kT[:, kc, 1536:2048], in_=p[:]), reads=[pr], writes=[kT.r(3)])
            A.free(ckt)

            if STAGE < 3:
                return
            sqn = [A.alloc("sqn%d" % i, [128, TB], BF16) for i in range(3)]
            rsn = [A.alloc("rsn%d" % i, [128, TB], F32) for i in range(3)]
            qn = [A.alloc("qn%d" % i, [128, TB], F32) for i in range(3)]
            t1 = [A.alloc("rt1%d" % i, [128, TB], F32) for i in range(2)]
            t2 = [A.alloc("rt2%d" % i, [128, TB], F32) for i in range(2)]
            rotA = Rot([0, 1, 6])
            rotB = Rot([2, 3])
            rotC = Rot([4, 5, 7])
            qunits = [(wt_i, cc, b) for wt_i in range(2) for cc in range(4) for b in range(3)]
            ust = {}
            wcur = {}

            def stageA(u):
                wt_i, cc, b = qunits[u]
                if (wt_i,) not in wcur:
                    wcur[(wt_i,)] = wget()
                w, wr = wcur[(wt_i,)]
                blk = slice(b * TB, (b + 1) * TB)
                pa, par_ = rotA.next()

                def f(e):
                    for k in range(8):
                        ins = e.matmul(pa[:], w[:, k, cc * 128:(cc + 1) * 128], hT[:, k, blk], start=(k == 0), stop=(k == 7))
                    return ins
                S.add("pe", f, reads=[wr, hT.r(b)], writes=[par_])
                s_ = sqn[u % 3]
                S.add("act", lambda e: e.activation(out=s_[:], in_=pa[:], func=AF.Square), reads=[par_], writes=[s_.r()])
                ust[u] = (pa, par_, s_)

            def stageB(u):
                wt_i, cc, b = qunits[u]
                qc = wt_i * 4 + cc
                isk = qc >= 6
                gain = vecT[:, R_KN:R_KN + 1] if isk else vecT[:, R_QN:R_QN + 1]
                pa, par_, s_ = ust[u]
                r_ = rsn[u % 3]
                q_ = qn[u % 3]
                pb, pbr = rotB.next()
                mm(pb[:], o64, s_[:], True, True, [s_.r(), cb16.r()], [pbr])
                S.add("act", lambda e: e.activation(out=r_[:], in_=pb[:], func=AF.Ln, bias=EPS_AP[:, 0:1], scale=1.0),
                      reads=[pbr, EPS_AP.r()], writes=[r_.r()])
                S.add("act", lambda e: e.activation(out=r_[:], in_=r_[:], func=AF.Exp, scale=-0.5), reads=[r_.r()], writes=[r_.r()])
                if b == 0 and not isk:
                    S.add("dve", lambda e: e.scalar_tensor_tensor(out=qT[:, qc, 0:TB], in0=pa[:], scalar=gain, in1=r_[:], op0=ALU.mult, op1=ALU.mult),
                          reads=[par_, r_.r(), vecT.r()], writes=[qT.r(0)])
                else:
                    S.add("dve", lambda e: e.scalar_tensor_tensor(out=q_[:], in0=pa[:], scalar=gain, in1=r_[:], op0=ALU.mult, op1=ALU.mult),
                          reads=[par_, r_.r(), vecT.r()], writes=[q_.r()])
                ust[u] = (q_,)

            def stageC(u):
                wt_i, cc, b = qunits[u]
                qc = wt_i * 4 + cc
                isk = qc >= 6
                (q_,) = ust.pop(u)
                blk = slice(b * TB, (b + 1) * TB)
                if isk:
                    dst, dreg = kT[:, qc - 6, b * TB:(b + 1) * TB], kT.r(b)
                else:
                    dst, dreg = qT[:, qc, blk], qT.r(b)
                if b == 0:
                    if isk:
                        S.add("act", lambda e: e.activation(out=dst, in_=q_[:], func=AF.Copy), reads=[q_.r()], writes=[dreg])
                        pc, pcr = rotC.next()

                        def ftr(e):
                            for tt_ in range(4):
                                ins = e.transpose(pc[:, tt_ * 128:(tt_ + 1) * 128], q_[:, tt_ * 128:(tt_ + 1) * 128], ident)
                            return ins
                        S.add("pe", ftr, reads=[q_.r(), cf32.r()], writes=[pcr])
                        kc = qc - 6
                        S.add("act", lambda e: e.activation(out=knew[:, :, kc * 128:(kc + 1) * 128], in_=pc[:].rearrange("p (t f) -> p t f", t=4), func=AF.Copy),
                              reads=[pcr], writes=[knew.r()])
                else:
                    a_ = t1[u % 2]
                    b_ = t2[u % 2]
                    pc, pcr = rotC.next()
                    mm(pc[:], RT, q_[:], True, True, [q_.r(), cf32.r()], [pcr])
                    rsl = slice((b - 1) * TB, b * TB)
                    S.add("dve", lambda e: e.tensor_tensor(out=a_[:], in0=q_[:], in1=rope[:, 0, rsl], op=ALU.mult), reads=[q_.r(), rope.r()], writes=[a_.r()])
                    S.add("dve", lambda e: e.tensor_tensor(out=b_[:], in0=pc[:], in1=rope[:, 1, rsl], op=ALU.mult), reads=[pcr, rope.r()], writes=[b_.r()])
                    S.add("dve", lambda e: e.tensor_tensor(out=dst, in0=a_[:], in1=b_[:], op=ALU.add), reads=[a_.r(), b_.r()], writes=[dreg])

            NU = len(qunits)
            for step in range(NU + 2):
                if step < NU:
                    stageA(step)
                if 0 <= step - 1 < NU:
                    stageB(step - 1)
                if 0 <= step - 2 < NU:
                    stageC(step - 2)
            dma("sp", o_k.rearrange("(t p) f -> p t f", p=128), knew[:], reads=[knew.r()])
            A.free(*sqn, *rsn, *qn, *t1, *t2)

            if STAGE < 4:
                return
            w, wr = wget()
            rotA = Rot([0, 1, 2, 3])
            for t in range(12):
                b = t // 4
                p, pr = rotA.next()

                def f(e, p=p, t=t, w=w):
                    for k in range(8):
                        ins = e.matmul(p[:], hT[:, k, t * 128:(t + 1) * 128], w[:, k, :], start=(k == 0), stop=(k == 7))
                    return ins
                S.add("pe", f, reads=[wr, hT.r(b)], writes=[pr])
                if not (SUB & 1):
                    S.add("act", lambda e, p=p, t=t: e.activation(out=ftok[:, t, :], in_=p[:, 0:256], func=AF.Copy), reads=[pr], writes=[ftok.r(t)])
                pv = p[:, 256:512].rearrange("p (j d) -> p j d", j=4)
                for par in range(2 if not (SUB & 2) else 0):
                    S.add("dve", lambda e, pv=pv, t=t, par=par: e.tensor_copy(out=vaug[:, t, par::2, par * 64:par * 64 + 64], in_=pv[:, par::2, :]),
                          reads=[pr], writes=[vaug.r(t)])
                if t < 4 and not (SUB & 4):
                    S.add("act", lambda e, p=p, t=t: e.activation(out=vnew[:, t, :], in_=p[:, 256:512], func=AF.Copy), reads=[pr], writes=[vnew.r()])
            dma("sp", o_v.rearrange("(t p) f -> p t f", p=128), vnew[:], reads=[vnew.r()])
            for t in range(4, 12):
                ada_tile(l, t, 4 + t % 4)
            ada_finish(l, 1)

            if STAGE < 5:
                return
            pT = [A.alloc("pT%d" % i, [128, TB], BF16) for i in range(6)]
            rtmp = [A.alloc("rtmp%d" % i, [128, TB], F32) for i in range(2)]
            evb = [A.alloc("evb%d" % i, [128, TB], F32) for i in range(2)]
            rotS = Rot([0, 1, 2, 3, 4, 5])
            rotO = Rot([6, 7])
            units = []
            for sidx in range(2):
                units.append((sidx * 256, 256, 0, [(sidx * 256 + kt * 128, 2 * sidx + kt) for kt in range(2)]))
            skeys = [(512 + kt * 128, 4 + kt) for kt in range(8)] + [(1536 + kt * 128, 12 + kt) for kt in range(4)]
            for qb in range(2):
                units.append((512 + qb * 512, 512, 1 + qb, skeys))
            n = 0
            nh = 0
            ada1_next = [0]
            LAG = 2
            for (q0, N, b, keys) in units:
                nk = len(keys)
                groups = [(qc, ki) for qc in range(6) for ki in range(nk)]
                pts = {}

                def emit_scores(gi, q0=q0, N=N, b=b, keys=keys, groups=groups, pts=pts):
                    nonlocal n
                    qc, ki = groups[gi]
                    kc = qc // 3
                    k0, vt = keys[ki]
                    kreg = kT.r(0 if k0 < 512 else (1 if k0 < 1024 else (2 if k0 < 1536 else 3)))
                    banks = []
                    for hh in range(2):
                        lo, hi = hh * 64, hh * 64 + 64
                        sp_, spr = rotS.next()
                        mm(sp_[:, 0:N], kT[lo:hi, kc, k0:k0 + 128], qT[lo:hi, qc, q0:q0 + N], True, True, [kreg, qT.r(b)], [spr])
                        banks.append((sp_, spr))
                    lst = []
                    for (sp_, spr) in banks:
                        pt = pT[n % 6]
                        n += 1
                        S.add("act", lambda e, pt=pt, sp_=sp_, N=N: e.activation(out=pt[:, 0:N], in_=sp_[:, 0:N], func=AF.Exp, scale=0.125),
                              reads=[spr], writes=[pt.r()])
                        lst.append(pt)
                    pts[gi] = lst

                for gi in range(min(LAG, len(groups))):
                    emit_scores(gi)
                accs = None
                for gi, (qc, ki) in enumerate(groups):
                    if ki == 0:
                        accs = [rotO.next(), rotO.next()]
                    k0, vt = keys[ki]
                    lst = pts.pop(gi)
                    for hh in range(2):
                        j = (qc // 3) * 2 + hh
                        acc, accr = accs[hh]
                        mm(acc[:, 0:N], vaug[:, vt, j, :], lst[hh][:, 0:N], ki == 0, ki == nk - 1, [vaug.r(vt), lst[hh].r()], [accr])
                    if gi + LAG < len(groups):
                        emit_scores(gi + LAG)
                    if NLAYERS > 1 and N == 512 and gi % 12 == 6 and ada1_next[0] < 12:
                        ada_tile(1, ada1_next[0], rotS.next())
                        ada1_next[0] += 1
                    if ki == nk - 1:
                        for hh in range(2):
                            acc, accr = accs[hh]
                            lo, hi = hh * 64, hh * 64 + 64
                            olo, ohi = (1 - hh) * 64, (1 - hh) * 64 + 64
                            rt = rtmp[nh % 2]
                            ev = evb[nh % 2]
                            nh += 1
                            S.add("dve", lambda e, ev=ev, acc=acc, N=N: e.tensor_copy(out=ev[:, 0:N], in_=acc[:, 0:N]), reads=[accr], writes=[ev.r()])
                            S.add("dve", lambda e, rt=rt, ev=ev, N=N, lo=lo, hi=hi, olo=olo, ohi=ohi: e.reciprocal(out=rt[lo:hi, 0:N], in_=ev[olo:ohi, 0:N]),
                                  reads=[ev.r()], writes=[rt.r()])
                            S.add("dve", lambda e, rt=rt, ev=ev, N=N, lo=lo, hi=hi, qc=qc, q0=q0: e.tensor_tensor(
                                out=hT[lo:hi, 2 + qc, q0:q0 + N], in0=ev[lo:hi, 0:N], in1=rt[lo:hi, 0:N], op=ALU.mult),
                                reads=[ev.r(), rt.r()], writes=[hT.r(b)])
            if NLAYERS > 1:
                while ada1_next[0] < 12:
                    ada_tile(1, ada1_next[0], rotS.next())
                    ada1_next[0] += 1
                ada_finish(1, 0)
                ada_finish(1, 1)
            A.free(*pT, *rtmp, *evb, qT, kT, vaug, rope, knew, vnew)

            if STAGE < 6:
                return
            uv = [A.alloc("uv%d" % i, [128, TB], BF16) for i in range(4)]
            rotU = Rot([0, 1, 2, 3])
            rotF = Rot([4, 5, 6, 7])
            c64c = cb16[:, 5, :]
            c64s = cb16[:, 6, :]
            n = 0

            def stageB(u, v, N, cch, q0, b):
                p, pr = rotF.next()
                mm(p[:, 0:N], c64c, u[:, 0:N], True, False, [cb16.r(), u.r()], [pr])
                mm(p[:, 0:N], c64s, v[:, 0:N], False, True, [cb16.r(), v.r()], [pr])
                S.add("act", lambda e: e.activation(out=hT[:, cch, q0:q0 + N], in_=p[:, 0:N], func=AF.Copy), reads=[pr], writes=[hT.r(b)])

            for sidx in range(2):
                for cch in range(2):
                    pair = []
                    for m in range(2):
                        p, pr = rotU.next()

                        def f(e, p=p, m=m, sidx=sidx, cch=cch):
                            for st_ in range(2):
                                ins = e.matmul(p[:, 0:256], ftok[:, 2 * sidx + st_, cch * 128:(cch + 1) * 128], dft256[:, m, st_, :],
                                               start=(st_ == 0), stop=(st_ == 1))
                            return ins
                        S.add("pe", f, reads=[ftok.r(2 * sidx), ftok.r(2 * sidx + 1), dft256.r()], writes=[pr])
                        u = uv[n % 4]
                        n += 1
                        S.add("dve", lambda e, u=u, p=p: e.tensor_copy(out=u[:, 0:256], in_=p[:, 0:256]), reads=[pr], writes=[u.r()])
                        pair.append(u)
                    stageB(pair[0], pair[1], 256, cch, sidx * 256, 0)
            for sb in range(2):
                wc, wcr = wget()
                wsn, wsr = wget()
                for cch in range(2):
                    pair = []
                    for m, (wm, wmr) in enumerate(((wc, wcr), (wsn, wsr))):
                        p, pr = rotU.next()

                        def f(e, p=p, wm=wm, cch=cch):
                            for st_ in range(8):
                                ins = e.matmul(p[:], ftok[:, 4 + st_, cch * 128:(cch + 1) * 128], wm[:, st_, :], start=(st_ == 0), stop=(st_ == 7))
                            return ins
                        S.add("pe", f, reads=ftok.rs(range(4, 12)) + [wmr], writes=[pr])
                        u = uv[n % 4]
                        n += 1
                        S.add("dve", lambda e, u=u, p=p: e.tensor_copy(out=u[:], in_=p[:]), reads=[pr], writes=[u.r()])
                        pair.append(u)
                    stageB(pair[0], pair[1], 512, cch, 512 + sb * 512, 1 + sb)
            A.free(*uv, ftok, dft256)

            if STAGE < 7:
                return
            rotW = Rot([0, 1, 2, 3])
            for t in range(2):
                w, wr = wget()
                for oc in range(4):
                    o = t * 4 + oc
                    for b in range(3):
                        blk = slice(b * TB, (b + 1) * TB)
                        p, pr = rotW.next()

                        def f(e, p=p, w=w, oc=oc, blk=blk):
                            for k in range(8):
                                ins = e.matmul(p[:], w[:, k, oc * 128:(oc + 1) * 128], hT[:, k, blk], start=(k == 0), stop=(k == 7))
                            return ins
                        S.add("pe", f, reads=[wr, hT.r(b)], writes=[pr])
                        resid(p, pr, l, 2, o, b)


        def rev(v):
            (ps_, pn), (st_, n) = v.ap
            return bass.AP(v.tensor, v.offset + (n - 1) * st_, [[ps_, pn], [-st_, n]])

        SEQS = [(0, 256), (256, 512), (512, 1536)]

        def conv4(eng, xp, yp, n, wcol, bcol):
            lo, hi = 1, n - 2
            S.add(eng, lambda e: e.tensor_scalar(out=yp[:, lo:hi], in0=xp[:, lo:hi], scalar1=vecT[:, wcol(1):wcol(1) + 1],
                                                 scalar2=vecT[:, bcol:bcol + 1], op0=ALU.mult, op1=ALU.add),
                  reads=[xp.r(), vecT.r()], writes=[yp.r()])
            for j, sh in ((0, -1), (2, 1), (3, 2)):
                S.add(eng, lambda e, j=j, sh=sh: e.scalar_tensor_tensor(out=yp[:, lo:hi], in0=xp[:, lo + sh:hi + sh], scalar=vecT[:, wcol(j):wcol(j) + 1],
                                                                        in1=yp[:, lo:hi], op0=ALU.mult, op1=ALU.add),
                      reads=[xp.r(), yp.r(), vecT.r()], writes=[yp.r()])

        def odd_layer(l):
            norm_mod(l, 0)
            yl = A.alloc("yl", [128, 4, NTOK], BF16)
            gg = A.alloc("gg", [128, 4, NTOK], BF16)
            xc = A.alloc("xc", [128, 4, NTOK], F32)
            bdw = A.alloc("bdw", [128, 16, 128], BF16)
            fin = A.alloc("fin", [128, 16], F32)
            clT = A.alloc("clT", [128, 8], F32)
            dma("pool", bdw[:], d_lrubd.rearrange("c p n -> p c n"), writes=[bdw.r()])
            S.add("act", lambda e: e.activation(out=clT[:], in_=vecT[:, R_LAM:R_LAM + 8], func=AF.Exp, scale=-1.0), reads=[vecT.r()], writes=[clT.r()])
            S.add("act", lambda e: e.activation(out=clT[:], in_=clT[:], func=AF.Ln, bias=ONE_AP[:, 0:1], scale=1.0), reads=[clT.r(), ONE_AP.r()], writes=[clT.r()])
            S.add("dve", lambda e: e.tensor_scalar(out=clT[:], in0=clT[:], scalar1=-8.0, scalar2=0.0, op0=ALU.mult, op1=ALU.add), reads=[clT.r()], writes=[clT.r()])
            NP = 1544
            POFF = [1, 259, 517]
            xpad = [A.alloc("xpad%d" % i, [128, NP], F32) for i in range(2)]
            ypad = [A.alloc("ypad%d" % i, [128, NP], F32) for i in range(2)]
            for xp in xpad:
                S.add("pool", lambda e, xp=xp: e.memset(xp[:], 0.0), writes=[xp.r()])
            rotA = Rot([0, 1, 2, 3])
            wg, wgr = wget()
            for c in range(4):
                for b in range(3):
                    blk = slice(b * TB, (b + 1) * TB)
                    p, pr = rotA.next()

                    def f(e, p=p, c=c, blk=blk):
                        for k in range(8):
                            ins = e.matmul(p[:], wg[:, k, c * 128:(c + 1) * 128], hT[:, k, blk], start=(k == 0), stop=(k == 7))
                        return ins
                    S.add("pe", f, reads=[wgr, hT.r(b)], writes=[pr])
                    S.add("act", lambda e, p=p, c=c, blk=blk: e.activation(out=gg[:, c, blk], in_=p[:], func=AF.Gelu_apprx_tanh), reads=[pr], writes=[gg.r(c)])
            wx, wxr = wget()
            for c in range(4):
                xp = xpad[c % 2]
                yp = ypad[c % 2]
                for b in range(3):
                    blk = slice(b * TB, (b + 1) * TB)
                    p, pr = rotA.next()

                    def f(e, p=p, c=c, blk=blk):
                        for k in range(8):
                            ins = e.matmul(p[:], wx[:, k, c * 128:(c + 1) * 128], hT[:, k, blk], start=(k == 0), stop=(k == 7))
                        return ins
                    S.add("pe", f, reads=[wxr, hT.r(b)], writes=[pr])
                    if b == 0:
                        ov = xp[:, 1:1 + 2 * 258].rearrange("p (a b) -> p a b", b=258)[:, :, 0:256]
                        S.add("act", lambda e, p=p, ov=ov: e.activation(out=ov, in_=p[:].rearrange("p (a b) -> p a b", a=2), func=AF.Copy), reads=[pr], writes=[xp.r()])
                    else:
                        o0 = 517 + (b - 1) * 512
                        S.add("act", lambda e, p=p, xp=xp, o0=o0: e.activation(out=xp[:, o0:o0 + 512], in_=p[:], func=AF.Copy), reads=[pr], writes=[xp.r()])
                conv4("dve", xp, yp, NP, lambda j, c=c: R_CLW + j * 4 + c, R_CLB + c)
                for si, (s0, s1) in enumerate(SEQS):
                    S.add("act", lambda e, yp=yp, c=c, s0=s0, s1=s1, si=si: e.activation(out=xc[:, c, s0:s1], in_=yp[:, POFF[si]:POFF[si] + s1 - s0], func=AF.Copy),
                          reads=[yp.r()], writes=[xc.r(c)])
            A.free(*xpad, *ypad)
            xcb = A.alloc("xcb", [128, NTOK], BF16)
            ra_ = [A.alloc("lra%d" % i, [128, NTOK], F32) for i in range(2)]
            ig_ = [A.alloc("lig%d" % i, [128, NTOK], F32) for i in range(2)]
            tq_ = [A.alloc("ltq0", [128, NTOK], F32)] * 2
            hd = [A.alloc("lh0", [128, NTOK], F32), ig_[1]]
            rotL = Rot([4, 5, 6, 7])
            xcbs = [xcb, xcb]

            def lruA(u):
                c, d = u // 2, u % 2
                ra, ig = ra_[d], ig_[d]
                xb = xcbs[c % 2]
                if d == 0:
                    S.add("act", lambda e: e.activation(out=xb[:], in_=xc[:, c, :], func=AF.Copy), reads=[xc.r(c)], writes=[xb.r()])
                for b in range(3):
                    blk = slice(b * TB, (b + 1) * TB)
                    for gate, dstT, brow in ((0, ra, R_BA), (1, ig, R_BX)):
                        p, pr = rotL.next()
                        mm(p[:], bdw[:, (gate * 2 + d) * 4 + c, :], xb[:, blk], True, True, [bdw.r(), xb.r()], [pr])
                        col = brow + d * 4 + c
                        S.add("act", lambda e, p=p, dstT=dstT, blk=blk, col=col: e.activation(out=dstT[:, blk], in_=p[:], func=AF.Sigmoid,
                                                                                               bias=vecT[:, col:col + 1], scale=1.0),
                              reads=[pr, vecT.r()], writes=[dstT.r()])

            def lruB(u):
                c, d = u // 2, u % 2
                ra, ig, tq = ra_[d], ig_[d], tq_[d]
                S.add("act", lambda e: e.activation(out=ra[:], in_=ra[:], func=AF.Exp, scale=clT[:, d * 4 + c:d * 4 + c + 1]),
                      reads=[ra.r(), clT.r()], writes=[ra.r()])
                S.add("act", lambda e: e.activation(out=tq[:], in_=ra[:], func=AF.Square), reads=[ra.r()], writes=[tq.r()])
                S.add("dve", lambda e: e.tensor_scalar(out=tq[:], in0=tq[:], scalar1=-1.0, scalar2=1.0, op0=ALU.mult, op1=ALU.add), reads=[tq.r()], writes=[tq.r()])
                S.add("act", lambda e: e.activation(out=tq[:], in_=tq[:], func=AF.Sqrt), reads=[tq.r()], writes=[tq.r()])
                S.add("dve", lambda e: e.tensor_tensor(out=ig[:], in0=ig[:], in1=xc[:, c, :], op=ALU.mult), reads=[ig.r(), xc.r(c)], writes=[ig.r()])
                S.add("dve", lambda e: e.tensor_tensor(out=tq[:], in0=tq[:], in1=ig[:], op=ALU.mult), reads=[tq.r(), ig.r()], writes=[tq.r()])
                h_ = hd[d]
                for si, (s0, s1) in enumerate(SEQS):
                    init = 0.0 if si < 2 else vecT[:, R_LRU0 + d * 4 + c:R_LRU0 + d * 4 + c + 1]
                    if d == 0:
                        S.add("dve", lambda e, s0=s0, s1=s1, init=init: e.tensor_tensor_scan(
                            out=h_[:, s0:s1], data0=ra[:, s0:s1], data1=tq[:, s0:s1], initial=init, op0=ALU.mult, op1=ALU.add),
                            reads=[ra.r(), tq.r(), vecT.r()], writes=[h_.r()])
                    else:
                        S.add("dve", lambda e, s0=s0, s1=s1, init=init: e.tensor_tensor_scan(
                            out=rev(h_[:, s0:s1]), data0=rev(ra[:, s0:s1]), data1=rev(tq[:, s0:s1]), initial=init, op0=ALU.mult, op1=ALU.add),
                            reads=[ra.r(), tq.r(), vecT.r()], writes=[h_.r()])
                    if si < 2:
                        pos = s1 - 1 if d == 0 else s0
                        col = (si * 2 + d) * 4 + c
                        S.add("dve", lambda e, pos=pos, col=col: e.tensor_copy(out=fin[:, col:col + 1], in_=h_[:, pos:pos + 1]),
                              reads=[h_.r()], writes=[fin.r()])
                if d == 1:
                    S.add("dve", lambda e: e.tensor_tensor(out=hd[0][:], in0=hd[0][:], in1=hd[1][:], op=ALU.add), reads=[hd[0].r(), hd[1].r()], writes=[hd[0].r()])
                    S.add("dve", lambda e: e.tensor_tensor(out=yl[:, c, :], in0=hd[0][:], in1=gg[:, c, :], op=ALU.mult),
                          reads=[hd[0].r(), gg.r(c)], writes=[yl.r()])

            lruA(0)
            for u in range(8):
                if u + 1 < 8:
                    lruA(u + 1)
                lruB(u)
            p, pr = PS(4)
            S.add("pe", lambda e: e.transpose(p[0:16, 0:128], fin[:, 0:16], ident), reads=[fin.r(), cf32.r()], writes=[pr])
            fino = A.alloc("fino", [16, 128], F32)
            S.add("dve", lambda e: e.tensor_copy(out=fino[:], in_=p[0:16, 0:128]), reads=[pr], writes=[fino.r()])
            dma("sp", o_lru, fino[:], reads=[fino.r()])
            A.free(gg, xc, bdw, clT, xcb, *ra_, *ig_, tq_[0], hd[0], fin, fino)
            if STAGE < 13:
                for _ in range(12):
                    wget()

            aneg = A.alloc("aneg", [128, 32], F32)
            S.add("act", lambda e: e.activation(out=aneg[:], in_=bv[:, B_ALOG:B_ALOG + 32], func=AF.Exp), reads=[bv.r()], writes=[aneg.r()])
            S.add("dve", lambda e: e.tensor_scalar(out=aneg[:], in0=aneg[:], scalar1=-1.0, scalar2=0.0, op0=ALU.mult, op1=ALU.add), reads=[aneg.r()], writes=[aneg.r()])
            nmk = [cb16[:, 2, :], cb16[:, 3, :]]
            tri = [cb16[:, 7, :], cb16[:, 8, :]]
            def ssd_group(grp):
                tiles = list(range(0, 4)) if grp == 0 else list(range(4, 12))
                nt = len(tiles)
                t0 = tiles[0]
                ntk = nt * 128
                blocks = [0] if grp == 0 else [1, 2]
                seqs = [[0, 1], [2, 3]] if grp == 0 else [list(range(8))]
                if grp == 0:
                    NPg = 520
                    poff = [1, 259]
                    slen = 256
                else:
                    NPg = 1032
                    poff = [1]
                    slen = 1024
                xs_tok = A.alloc("xs_tok", [128, nt, 1024], BF16)
                z_tok = A.alloc("z_tok", [128, nt, 1024], BF16)
                B_tok = A.alloc("B_tok", [128, nt, 128], BF16)
                BT = A.alloc("BT", [128, ntk], BF16)
                CT = A.alloc("CT", [128, ntk], BF16)
                xpad = [A.alloc("sxp%d" % i, [128, NPg], F32) for i in range(3)]
                ypad = [A.alloc("syp%d" % i, [128, NPg], F32) for i in range(3)]
                xsf = [A.alloc("xsf%d" % i, [128, ntk], F32) for i in range(3)]
                for xp in xpad:
                    S.add("pool", lambda e, xp=xp: e.memset(xp[:], 0.0), writes=[xp.r()])
                rotA = Rot([0, 1, 2, 3])
                rotT = Rot([4, 5, 6, 7])
                wc1 = {}

                def c1A(ch):
                    wt_i, cc = ch // 4, ch % 4
                    if wt_i not in wc1:
                        wc1[wt_i] = wget()
                    w, wr = wc1[wt_i]
                    xp = xpad[ch % 3]
                    for b in blocks:
                        blk = slice(b * TB, (b + 1) * TB)
                        p, pr = rotA.next()

                        def f(e, p=p, blk=blk):
                            for k in range(8):
                                ins = e.matmul(p[:], w[:, k, cc * 128:(cc + 1) * 128], hT[:, k, blk], start=(k == 0), stop=(k == 7))
                            return ins
                        S.add("pe", f, reads=[wr, hT.r(b)], writes=[pr])
                        if grp == 0:
                            ov = xp[:, 1:1 + 2 * 258].rearrange("p (a b) -> p a b", b=258)[:, :, 0:256]
                            S.add("act", lambda e, p=p, ov=ov: e.activation(out=ov, in_=p[:].rearrange("p (a b) -> p a b", a=2), func=AF.Copy), reads=[pr], writes=[xp.r()])
                        else:
                            o0 = 1 + (b - 1) * 512
                            S.add("act", lambda e, p=p, o0=o0: e.activation(out=xp[:, o0:o0 + 512], in_=p[:], func=AF.Copy), reads=[pr], writes=[xp.r()])

                def c1B(ch):
                    xp = xpad[ch % 3]
                    yp = ypad[ch % 3]
                    xf = xsf[ch % 3]
                    conv4("dve", xp, yp, NPg, lambda j: R_CSW + j * 10 + ch, R_CSB + ch)
                    for si, po in enumerate(poff):
                        if ch <= 8:
                            dst, dreg = xf[:, si * slen:(si + 1) * slen], xf.r()
                        else:
                            dst, dreg = CT[:, si * slen:(si + 1) * slen], CT.r()
                        S.add("act", lambda e, po=po, dst=dst: e.activation(out=dst, in_=yp[:, po:po + slen], func=AF.Silu), reads=[yp.r()], writes=[dreg])
                    if ch == 8:
                        S.add("act", lambda e: e.activation(out=BT[:], in_=xf[:], func=AF.Copy), reads=[xf.r()], writes=[BT.r()])

                def c1C(ch):
                    xf = xsf[ch % 3]
                    if ch <= 8:
                        for t4 in range(0, nt, 4):
                            p, pr = rotT.next()

                            def ftr(e, p=p, t4=t4):
                                for q in range(4):
                                    ins = e.transpose(p[:, q * 128:(q + 1) * 128], xf[:, (t4 + q) * 128:(t4 + q + 1) * 128], ident)
                                return ins
                            S.add("pe", ftr, reads=[xf.r(), cf32.r()], writes=[pr])
                            pv = p[:].rearrange("p (q n) -> p q n", q=4)
                            if ch < 8:
                                S.add("dve", lambda e, pv=pv, t4=t4: e.tensor_copy(out=xs_tok[:, t4:t4 + 4, ch * 128:(ch + 1) * 128], in_=pv),
                                      reads=[pr], writes=[xs_tok.r()])
                            else:
                                S.add("dve", lambda e, pv=pv, t4=t4: e.tensor_copy(out=B_tok[:, t4:t4 + 4, :], in_=pv), reads=[pr], writes=[B_tok.r()])

                for step in range(12):
                    if step < 10:
                        c1A(step)
                    if 0 <= step - 1 < 10:
                        c1B(step - 1)
                    if 0 <= step - 2 < 10:
                        c1C(step - 2)
                A.free(*xpad, *ypad, *xsf)
                if SUB == 1:
                    return
                sm = lambda name, w=32: A.alloc(name, [128, nt, w], F32)
                dtr = sm("dtr")
                dta = sm("dta")
                aal = sm("aal")
                stats = sm("stats", 64)
                nacs = sm("nacs")
                eacs = sm("eacs")
                wdec = sm("wdec")
                cdec = sm("cdec")
                cdsel = A.alloc("cdsel", [128, nt, 2, 8], F32)
                wd, wdr = wget()
                for ti in range(nt):
                    tg = t0 + ti
                    b = tg // 4
                    p, pr = rotA.next()

                    def f(e, p=p, tg=tg):
                        for k in range(8):
                            ins = e.matmul(p[:, 0:32], hT[:, k, tg * 128:(tg + 1) * 128], wd[:, k, :], start=(k == 0), stop=(k == 7))
                        return ins
                    S.add("pe", f, reads=[wdr, hT.r(b)], writes=[pr])
                    S.add("dve", lambda e, p=p, ti=ti: e.tensor_tensor(out=dtr[:, ti, :], in0=p[:, 0:32], in1=bv[:, B_DTB:B_DTB + 32], op=ALU.add),
                          reads=[pr, bv.r()], writes=[dtr.r()])
                S.add("act", lambda e: e.activation(out=dtr[:], in_=dtr[:], func=AF.Exp), reads=[dtr.r()], writes=[dtr.r()])
                S.add("act", lambda e: e.activation(out=dta[:], in_=dtr[:], func=AF.Ln, bias=ONE_AP[:, 0:1], scale=1.0), reads=[dtr.r(), ONE_AP.r()], writes=[dta.r()])
                S.add("dve", lambda e: e.tensor_tensor(out=aal[:], in0=dta[:], in1=aneg[:].unsqueeze(1).to_broadcast([128, nt, 32]), op=ALU.mult),
                      reads=[dta.r(), aneg.r()], writes=[aal.r()])
                ahi = A.alloc("ahi", [128, nt, 32], BF16)
                alo = A.alloc("alo", [128, nt, 32], BF16)
                S.add("act", lambda e: e.activation(out=ahi[:], in_=aal[:], func=AF.Copy), reads=[aal.r()], writes=[ahi.r()])
                S.add("dve", lambda e: e.tensor_tensor(out=alo[:], in0=aal[:], in1=ahi[:], op=ALU.subtract), reads=[aal.r(), ahi.r()], writes=[alo.r()])
                for ti in range(nt):
                    p, pr = rotT.next()
                    mm(p[:, 0:16], utri, aal[:, ti, 0:16], True, True, [cf32.r(), aal.r()], [pr])
                    mm(p[:, 16:32], ltri, aal[:, ti, 16:32], True, True, [cf32.r(), aal.r()], [pr])
                    mm(p[:, 32:64], onesf, aal[:, ti, :], True, True, [cf32.r(), aal.r()], [pr])
                    S.add("dve", lambda e, p=p, ti=ti: e.tensor_copy(out=stats[:, ti, :], in_=p[:, 0:64]), reads=[pr], writes=[stats.r()])
                S.add("act", lambda e: e.activation(out=nacs[:], in_=dta[:], func=AF.Ln), reads=[dta.r()], writes=[nacs.r()])
                S.add("dve", lambda e: e.tensor_tensor(out=nacs[:], in0=nacs[:], in1=stats[:, :, 0:32], op=ALU.subtract),
                      reads=[stats.r(), nacs.r()], writes=[nacs.r()])
                S.add("act", lambda e: e.activation(out=eacs[:], in_=stats[:, :, 0:32], func=AF.Exp), reads=[stats.r()], writes=[eacs.r()])
                S.add("dve", lambda e: e.tensor_tensor(out=wdec[:], in0=stats[:, :, 32:64], in1=stats[:, :, 0:32], op=ALU.subtract), reads=[stats.r()], writes=[wdec.r()])
                S.add("act", lambda e: e.activation(out=wdec[:], in_=wdec[:], func=AF.Exp), reads=[wdec.r()], writes=[wdec.r()])
                S.add("dve", lambda e: e.tensor_tensor(out=wdec[:], in0=wdec[:], in1=dta[:], op=ALU.mult), reads=[wdec.r(), dta.r()], writes=[wdec.r()])
                S.add("act", lambda e: e.activation(out=cdec[:], in_=stats[:, :, 32:64], func=AF.Exp), reads=[stats.r()], writes=[cdec.r()])
                for g in range(2):
                    src = cdec[g * 64:(g + 1) * 64, :, :].rearrange("p t (d g h) -> p t d g h", d=2, g=2)[:, :, :, g, :]
                    S.add("dve", lambda e, g=g, src=src: e.tensor_copy(out=cdsel[g * 64:(g + 1) * 64, :, :, :], in_=src), reads=[cdec.r()], writes=[cdsel.r()])
                A.free(dtr, cdec, stats, aal)
                if SUB == 2:
                    return
                Sst = [A.alloc("Sst%d" % d, [128, 512], F32) for d in range(2)]
                hinb = A.alloc("hinb", [128, nt, 512], BF16)
                hinf = A.alloc("hinf", [128, 512], BF16)
                xw = [A.alloc("xw0", [128, 1024], BF16)] * 2
                stmp = A.alloc("stmp", [128, 512], F32)

                class _Stg:
                    def __getitem__(self, k):
                        return stmp[:].rearrange("p (a n) -> p a n", a=8)[k]

                    def r(self):
                        return stmp.r()
                stg = _Stg()
                rotP = Rot([5, 6])
                nxw = [0]

                def init_state(d):
                    St = Sst[d]
                    if grp == 0:
                        S.add("pool", lambda e: e.memset(St[:], 0.0), writes=[St.r()])
                    else:
                        dma("sp", stg[:], d_sssd[d].rearrange("(a p) n -> p a n", p=128), writes=[stg.r()])
                        for g in range(2):
                            p, pr = rotP.next()

                            def f(e, p=p, g=g):
                                for a4 in range(4):
                                    ins = e.transpose(p[0:64, a4 * 128:(a4 + 1) * 128], stg[:, g * 4 + a4, :], ident)
                                return ins
                            S.add("pe", f, reads=[stg.r(), cf32.r()], writes=[pr])
                            S.add("dve", lambda e, p=p, g=g: e.tensor_copy(out=St[g * 64:(g + 1) * 64, :], in_=p[0:64, :]), reads=[pr], writes=[St.r()])

                def out_state(d, sidx):
                    St = Sst[d]
                    for g in range(2):
                        p, pr = rotP.next()

                        def f(e, p=p, g=g):
                            for s4 in range(4):
                                ins = e.transpose(p[:, s4 * 64:(s4 + 1) * 64], St[g * 64:(g + 1) * 64, s4 * 128:(s4 + 1) * 128], ident[g * 64:(g + 1) * 64, g * 64:(g + 1) * 64])
                            return ins
                        S.add("pe", f, reads=[St.r(), cf32.r()], writes=[pr])
                        S.add("dve", lambda e, p=p, g=g: e.tensor_copy(out=stg[:, g * 4:(g + 1) * 4, :], in_=p[:, 0:256].rearrange("p (a n) -> p a n", a=4)), reads=[pr], writes=[stg.r()])
                    dma("sp", o_ssd[sidx, d].rearrange("(a p) n -> p a n", p=128), stg[:], reads=[stg.r()])

                def state_update(d, ti):
                    St = Sst[d]
                    x_ = xw[nxw[0] % 2]
                    nxw[0] += 1
                    S.add("dve", lambda e, x_=x_, ti=ti, d=d: e.tensor_tensor(
                        out=x_[:].rearrange("p (h q) -> p h q", h=16), in0=xs_tok[:, ti, :].rearrange("p (h q) -> p h q", h=16),
                        in1=wdec[:, ti, d * 16:(d + 1) * 16].unsqueeze(2).to_broadcast([128, 16, 64]), op=ALU.mult),
                        reads=[xs_tok.r(), wdec.r()], writes=[x_.r()])
                    p, pr = rotP.next()
                    for g in range(2):
                        mm(p[g * 64:(g + 1) * 64, :], B_tok[:, ti, g * 64:(g + 1) * 64], x_[:, g * 512:(g + 1) * 512], True, True, [B_tok.r(), x_.r()], [pr])
                    S.add("dve", lambda e, ti=ti, d=d: e.tensor_tensor(
                        out=stmp[:].rearrange("p (h q) -> p h q", h=8), in0=St[:].rearrange("p (h q) -> p h q", h=8),
                        in1=cdsel[:, ti, d, :].unsqueeze(2).to_broadcast([128, 8, 64]), op=ALU.mult),
                        reads=[St.r(), cdsel.r()], writes=[stmp.r()])
                    S.add("dve", lambda e, p=p: e.tensor_tensor(out=St[:], in0=p[:], in1=stmp[:], op=ALU.add), reads=[pr, stmp.r()], writes=[St.r()])

                wz0, wz0r = wget()
                wz1, wz1r = wget()

                def zproj(ti):
                    tg = t0 + ti
                    b = tg // 4
                    for hf, (wz, wzr) in enumerate(((wz0, wz0r), (wz1, wz1r))):
                        p, pr = rotA.next()

                        def f(e, p=p, wz=wz):
                            for k in range(8):
                                ins = e.matmul(p[:], hT[:, k, tg * 128:(tg + 1) * 128], wz[:, k, :], start=(k == 0), stop=(k == 7))
                            return ins
                        S.add("pe", f, reads=[wzr, hT.r(b)], writes=[pr])
                        S.add("act", lambda e, p=p, hf=hf: e.activation(out=z_tok[:, ti, hf * 512:(hf + 1) * 512], in_=p[:], func=AF.Silu),
                              reads=[pr], writes=[z_tok.r()])
                for sidx, sq_ in enumerate(seqs):
                    init_state(1)
                    for ti in reversed(sq_):
                        zproj(ti)
                        S.add("act", lambda e, ti=ti: e.activation(out=hinb[:, ti, :], in_=Sst[1][:], func=AF.Copy), reads=[Sst[1].r()], writes=[hinb.r()])
                        if SUB != 5:
                            state_update(1, ti)
                    if grp == 0 and SUB != 4:
                        out_state(1, sidx)
                if SUB in (3, 4, 5):
                    return
                cbs = [A.alloc("cbs%d" % i, [128, 2, 128], F32) for i in range(2)]
                decb = [A.alloc("decb%d" % i, [128, 2, 128], F32) for i in range(3)]
                MTb = [A.alloc("MTb%d" % i, [128, 2, 128], BF16) for i in range(3)]
                yaccs = [A.alloc("yacc%d" % i, [128, 1024], F32) for i in range(2)]
                ytmp = A.alloc("ytmp", [128, 1024], F32)
                ssq = A.alloc("ssq", [128, 2], F32)
                rotD = Rot([1, 2, 3, 4])
                nm_ = 0
                tiles_seq = []
                for sidx, sq_ in enumerate(seqs):
                    for i_, ti in enumerate(sq_):
                        tiles_seq.append((sidx, ti, i_ == 0, i_ == len(sq_) - 1))

                def make_tail(sidx, ti, first, last, yacc):
                    tg = t0 + ti
                    tcol = slice(ti * 128, (ti + 1) * 128)
                    th = []
                    if first:
                        th.append(lambda: init_state(0))
                    th.append(lambda: S.add("act", lambda e: e.activation(out=hinf[:], in_=Sst[0][:], func=AF.Copy), reads=[Sst[0].r()], writes=[hinf.r()]))
                    for d in range(2):
                        for g in range(2):
                            def yo(d=d, g=g):
                                hin = hinf[:] if d == 0 else hinb[:, ti, :]
                                hreg = hinf.r() if d == 0 else hinb.r()
                                p, pr = rotP.next()
                                mm(p[:], CT[g * 64:(g + 1) * 64, tcol], hin[g * 64:(g + 1) * 64, :], True, True, [CT.r(), hreg], [pr])
                                ea = eacs[:, ti, d * 16 + g * 8:d * 16 + g * 8 + 8].unsqueeze(2).to_broadcast([128, 8, 64])
                                pv = p[:].rearrange("p (h q) -> p h q", h=8)
                                tsl = ytmp[:, g * 512:(g + 1) * 512].rearrange("p (h q) -> p h q", h=8)
                                S.add("dve", lambda e: e.tensor_tensor(out=tsl, in0=pv, in1=ea, op=ALU.mult), reads=[pr, eacs.r()], writes=[ytmp.r()])
                                S.add("dve", lambda e: e.tensor_tensor(out=yacc[:, g * 512:(g + 1) * 512], in0=yacc[:, g * 512:(g + 1) * 512],
                                                                       in1=ytmp[:, g * 512:(g + 1) * 512], op=ALU.add),
                                      reads=[yacc.r(), ytmp.r()], writes=[yacc.r()])
                            th.append(yo)
                    th.append(lambda: S.add("dve", lambda e: e.tensor_tensor(
                        out=ytmp[:].rearrange("p (h q) -> p h q", h=16), in0=xs_tok[:, ti, :].rearrange("p (h q) -> p h q", h=16),
                        in1=bv[:, B_D:B_D + 16].unsqueeze(2).to_broadcast([128, 16, 64]), op=ALU.mult),
                        reads=[xs_tok.r(), bv.r(), yacc.r()], writes=[ytmp.r()]))
                    th.append(lambda: S.add("dve", lambda e: e.tensor_tensor(out=yacc[:], in0=yacc[:], in1=ytmp[:], op=ALU.add), reads=[yacc.r(), ytmp.r()], writes=[yacc.r()]))
                    th.append(lambda: S.add("dve", lambda e: e.tensor_tensor(out=yacc[:], in0=yacc[:], in1=z_tok[:, ti, :], op=ALU.mult), reads=[yacc.r(), z_tok.r()], writes=[yacc.r()]))
                    th.append(lambda: S.add("act", lambda e: e.activation(out=ytmp[:], in_=yacc[:], func=AF.Square, accum_out=ssq[:, 0:1]), reads=[yacc.r()], writes=[ytmp.r(), ssq.r()]))
                    th.append(lambda: S.add("act", lambda e: e.activation(out=ssq[:, 0:1], in_=ssq[:, 0:1], func=AF.Ln, bias=EPS_AP[:, 0:1], scale=1.0 / 1024), reads=[ssq.r(), EPS_AP.r()], writes=[ssq.r()]))
                    th.append(lambda: S.add("act", lambda e: e.activation(out=ssq[:, 0:1], in_=ssq[:, 0:1], func=AF.Exp, scale=-0.5), reads=[ssq.r()], writes=[ssq.r()]))
                    th.append(lambda: S.add("dve", lambda e: e.scalar_tensor_tensor(out=ytmp[:], in0=yacc[:], scalar=ssq[:, 0:1], in1=bv[:, B_NORM:B_NORM + 1024], op0=ALU.mult, op1=ALU.mult),
                                            reads=[yacc.r(), ssq.r(), bv.r()], writes=[ytmp.r()]))
                    for hf in range(2):
                        def trp(hf=hf):
                            p, pr = PS(7) if hf == 0 else rotP.next()

                            def ftr(e):
                                for c in range(4):
                                    ins = e.transpose(p[:, c * 128:(c + 1) * 128], ytmp[:, (hf * 4 + c) * 128:(hf * 4 + c + 1) * 128], ident)
                                return ins
                            S.add("pe", ftr, reads=[ytmp.r(), cf32.r()], writes=[pr])
                            outv = hT[:, hf * 4:hf * 4 + 4, tg * 128:(tg + 1) * 128]
                            S.add("act", lambda e: e.activation(out=outv, in_=p[:].rearrange("p (c n) -> p c n", c=4), func=AF.Copy),
                                  reads=[pr], writes=[hT.r(tg // 4)])
                        th.append(trp)
                    th.append(lambda: state_update(0, ti))
                    if last and grp == 0:
                        th.append(lambda: out_state(0, sidx))
                    return th

                pending = []
                for tix, (sidx, ti, first, last) in enumerate(tiles_seq):
                    yacc = yaccs[tix % 2]
                    tcol = slice(ti * 128, (ti + 1) * 128)
                    cb_ = cbs[ti % 2]
                    for g in range(2):
                        p, pr = PS(7) if g == 0 else rotP.next()
                        mm(p[:, 0:128], BT[g * 64:(g + 1) * 64, tcol], CT[g * 64:(g + 1) * 64, tcol], True, True, [BT.r(), CT.r()], [pr])
                        S.add("act", lambda e, p=p, cb_=cb_, g=g: e.activation(out=cb_[:, g, :], in_=p[:, 0:128], func=AF.Copy), reads=[pr], writes=[cb_.r()])
                    ydp, ydr = PS(0)
                    munits = [(hp, d) for hp in range(8) for d in range(2)]
                    mts = {}

                    def emit_pdc(u, ti=ti, cb_=cb_, mts=mts, munits=munits):
                        nonlocal nm_
                        hp, d = munits[u]
                        g = hp // 4
                        pd, pdr = rotD.next()

                        def f(e, pd=pd, ti=ti, hp=hp, d=d):
                            for i_ in range(2):
                                hc = d * 16 + 2 * hp + i_
                                o_ = pd[:, i_ * 128:(i_ + 1) * 128]
                                e.matmul(o_, ahi[:, ti, hc:hc + 1].to_broadcast([128, 128]), tri[d], start=True, stop=False)
                                e.matmul(o_, alo[:, ti, hc:hc + 1].to_broadcast([128, 128]), tri[d], start=False, stop=False)
                                ins = e.matmul(o_, nmk[d], identb, start=False, stop=True)
                            return ins
                        S.add("pe", f, reads=[ahi.r(), alo.r(), cb16.r()], writes=[pdr])
                        dc = decb[nm_ % 3]
                        mt = MTb[nm_ % 3]
                        nm_ += 1
                        for i_ in range(2):
                            hc = d * 16 + 2 * hp + i_
                            S.add("act", lambda e, pd=pd, dc=dc, ti=ti, hc=hc, i_=i_: e.activation(out=dc[:, i_, :], in_=pd[:, i_ * 128:(i_ + 1) * 128], func=AF.Exp,
                                                                                                     bias=nacs[:, ti, hc:hc + 1], scale=1.0),
                                  reads=[pdr, nacs.r()], writes=[dc.r()])
                        S.add("pool" if u % 2 == 1 else "dve", lambda e, dc=dc, mt=mt, g=g, cb_=cb_: e.tensor_tensor(
                            out=mt[:], in0=dc[:], in1=cb_[:, g, :].unsqueeze(1).to_broadcast([128, 2, 128]), op=ALU.mult),
                            reads=[dc.r(), cb_.r()], writes=[mt.r()])
                        mts[u] = mt

                    LAm = 3
                    for u in range(LAm):
                        emit_pdc(u)
                    for u, (hp, d) in enumerate(munits):
                        mt = mts.pop(u)

                        def fy(e, mt=mt, hp=hp, d=d, u=u, ti=ti):
                            for i_ in range(2):
                                h = 2 * hp + i_
                                ins = e.matmul(ydp[:, (h % 8) * 64:(h % 8 + 1) * 64], mt[:, i_, :], xs_tok[:, ti, h * 64:(h + 1) * 64],
                                               start=(u % 8 == 0 and i_ == 0), stop=(d == 1), skip_group_check=True)
                            return ins
                        S.add("pe", fy, reads=[mt.r(), xs_tok.r()], writes=[ydr])
                        if u + LAm < len(munits):
                            emit_pdc(u + LAm)
                        if pending:
                            pending.pop(0)()
                        if u % 8 == 7:
                            g = u // 8
                            S.add("act", lambda e, g=g, yacc=yacc: e.activation(out=yacc[:, g * 512:(g + 1) * 512], in_=ydp[:], func=AF.Copy), reads=[ydr], writes=[yacc.r()])
                        if u == 15:
                            while pending:
                                pending.pop(0)()
                    pending = pending + make_tail(sidx, ti, first, last, yacc)
                while pending:
                    pending.pop(0)()
                A.free(xs_tok, z_tok, B_tok, BT, CT, dta, ahi, alo, nacs, eacs, wdec, cdsel, *Sst, hinb, hinf, xw[0], stmp,
                       *cbs, *decb, *MTb, *yaccs, ytmp, ssq)
            for grp in range((2 if SUB == 0 else 1) if STAGE >= 13 else 0):
                ssd_group(grp if SUB < 10 else 1)
            A.free(aneg)
            while wstate["got"] < wstate["woo"]:
                wget()
            rotW = Rot([0, 1, 2, 3])
            for t in range(4):
                w, wr = wget()
                for oc in range(2):
                    o = t * 2 + oc
                    for b in range(3):
                        blk = slice(b * TB, (b + 1) * TB)
                        p, pr = rotW.next()

                        def f(e, p=p, w=w, oc=oc, blk=blk):
                            for k in range(12):
                                rhs = yl[:, k, blk] if k < 4 else hT[:, k - 4, blk]
                                ins = e.matmul(p[:], w[:, k, oc * 128:(oc + 1) * 128], rhs, start=(k == 0), stop=(k == 11))
                            return ins
                        S.add("pe", f, reads=[wr, hT.r(b), yl.r()], writes=[pr])
                        resid(p, pr, l, 2, o, b)
            A.free(yl)

        if STAGE >= 1:
            for t in range(4):
                ada_tile(0, t, 0)
            ada_finish(0, 0)
        if STAGE >= 2:
            even_layer(0)
        if STAGE >= 10:
            ffn(0)
        if NLAYERS > 1:
            odd_layer(1)
            if STAGE >= 20:
                ffn(1)

        yst = [A.alloc("yst%d" % i, [128, D], F32) for i in range(2)]
        rot = Rot([0, 1, 2, 3, 4, 5, 6, 7])
        for t in range(12):
            ys = yst[t % 2]
            for hf in range(2):
                p, pr = rot.next()

                def tr(e, p=p, t=t, hf=hf):
                    for c in range(4):
                        ins = e.transpose(p[:, c * 128:(c + 1) * 128], xT[:, hf * 4 + c, t * 128:(t + 1) * 128], ident)
                    return ins
                S.add("pe", tr, reads=[xT.r(t), cf32.r()], writes=[pr])
                if hf == 0:
                    S.add("act", lambda e, ys=ys, p=p: e.activation(out=ys[:, 0:512], in_=p[:], func=AF.Copy), reads=[pr], writes=[ys.r()])
                else:
                    S.add("dve", lambda e, ys=ys, p=p: e.tensor_copy(out=ys[:, 512:1024], in_=p[:]), reads=[pr], writes=[ys.r()])
            dma("sp", o_y[t * 128:(t + 1) * 128, :], ys[:], reads=[ys.r()])
        assert STAGE < 99 or wstate["got"] == len(wplan), (wstate, len(wplan))
        S.emit(st)
    return nc


_CACHE = {}


def _prep_shared(inp):
    f = np.float32
    sh = {}
    sh["w_ada"] = np.ascontiguousarray(inp["w_ada"], f)
    wie = np.asarray(inp["w_in_even"][0], f)
    fcols = wie[:, 0:256]
    q = wie[:, 256:1024].reshape(D, 12, 64)[:, PERM, :].reshape(D, 768)
    k = wie[:, 1024:1280]
    v = wie[:, 1280:1536]
    sh["w_in_even"] = np.ascontiguousarray(np.concatenate([q, k, fcols, v], 1))
    woe = np.asarray(inp["w_out_even"][0], f)
    att = woe[256:].reshape(12, 64, D)[PERM].reshape(768, D)
    sh["w_out_even"] = np.ascontiguousarray(np.concatenate([woe[:256], att], 0))
    sh["w_in_odd"] = np.ascontiguousarray(inp["w_in_odd"][0], f)
    sh["w_out_odd"] = np.ascontiguousarray(inp["w_out_odd"][0], f)
    sh["ffn_w1"] = np.ascontiguousarray(inp["ffn_w1"], f)
    sh["ffn_w3"] = np.ascontiguousarray(inp["ffn_w3"], f)
    sh["ffn_w2"] = np.ascontiguousarray(inp["ffn_w2"], f)
    bd = np.zeros((2, 2, 4, 128, 128), f)
    for gi, wname in enumerate(("lru_wa", "lru_wx")):
        wsrc = np.asarray(inp[wname][0], f)
        for d in range(2):
            for blk in range(8):
                c, hlf = blk // 2, blk % 2
                bd[gi, d, c, hlf * 64:(hlf + 1) * 64, hlf * 64:(hlf + 1) * 64] = wsrc[d, blk]
    sh["lru_bd"] = bd.reshape(16, 128, 128)
    bvec = np.zeros((1, NBV), f)
    bvec[0, B_NORM:B_NORM + 1024] = inp["ssd_norm"][0]
    bvec[0, B_D:B_D + 16] = inp["ssd_d"][0]
    bvec[0, B_DTB:B_DTB + 32] = np.asarray(inp["ssd_dt_bias"][0]).reshape(32)
    bvec[0, B_ALOG:B_ALOG + 32] = np.asarray(inp["ssd_a_log"][0]).reshape(32)
    sh["bvec"] = bvec
    sh.update(_consts())
    return sh


def _vecs(inp, i):
    f = np.float32
    v = np.zeros((256, 128), f)
    v[R_BADA:R_BADA + 96] = np.asarray(inp["b_ada"], f).reshape(96, 128)
    v[R_CCTX:R_CCTX + 8] = np.asarray(inp["c_ctx"], f).reshape(8, 128)
    v[R_CI:R_CI + 8] = np.asarray(inp["c"][i], f).reshape(8, 128)
    v[R_NMIX:R_NMIX + 16] = np.asarray(inp["norm_mix"], f).reshape(16, 128)
    v[R_NFFN:R_NFFN + 16] = np.asarray(inp["norm_ffn"], f).reshape(16, 128)
    v[R_CLW:R_CLW + 16] = np.asarray(inp["conv_lru_w"][0], f).reshape(16, 128)
    v[R_CLB:R_CLB + 4] = np.asarray(inp["conv_lru_b"][0], f).reshape(4, 128)
    v[R_BA:R_BA + 8] = np.asarray(inp["lru_ba"][0], f).reshape(8, 128)
    v[R_BX:R_BX + 8] = np.asarray(inp["lru_bx"][0], f).reshape(8, 128)
    v[R_LAM:R_LAM + 8] = np.asarray(inp["lru_lambda"][0], f).reshape(8, 128)
    v[R_CSW:R_CSW + 40] = np.asarray(inp["conv_ssd_w"][0], f).reshape(40, 128)
    v[R_CSB:R_CSB + 10] = np.asarray(inp["conv_ssd_b"][0], f).reshape(10, 128)
    v[R_QN] = np.tile(np.asarray(inp["q_norm"][0], f), 2)
    v[R_KN] = np.tile(np.asarray(inp["k_norm"][0], f), 2)
    v[R_LRU0:R_LRU0 + 8] = np.asarray(inp["state_lru"][i, 0], f).reshape(8, 128)
    return v


def kernel(**inp):
    inp = {k: np.asarray(v) for k, v in inp.items()}
    if "nc" not in _CACHE:
        _CACHE["nc"] = build_program()
    nc = _CACHE["nc"]
    sh = _prep_shared(inp)
    in_maps = []
    for i in range(NCORES):
        m = dict(sh)
        m["x"] = np.ascontiguousarray(np.concatenate(
            [inp["x_prompt"][2 * i], inp["x_prompt"][2 * i + 1], inp["x_sample"][i]], 0), np.float32)
        m["vecs"] = _vecs(inp, i)
        m["cache_k"] = np.ascontiguousarray(inp["cache_k"][i, 0].reshape(512, 256), np.float32)
        m["cache_v"] = np.ascontiguousarray(inp["cache_v"][i, 0].reshape(512, 256), np.float32)
        m["state_ssd"] = np.ascontiguousarray(inp["state_ssd"][i, 0].reshape(2, 1024, 64), np.float32)
        in_maps.append(m)
    res = run_bass_kernel_spmd(nc, in_maps[:NRUN], core_ids=list(range(NRUN)))
    R = res.results
    y_prompt = np.zeros((16, 256, D), np.float32)
    y_sample = np.zeros((8, 1024, D), np.float32)
    new_k = np.zeros((16, 1, 256, 4, 64), np.float32)
    new_v = np.zeros((16, 1, 256, 4, 64), np.float32)
    new_lru = np.zeros((16, 1, 2, 512), np.float32)
    new_ssd = np.zeros((16, 1, 2, 16, 64, 64), np.float32)
    for i in range(NRUN):
        y = np.asarray(R[i]["y"])
        y_prompt[2 * i] = y[0:256]
        y_prompt[2 * i + 1] = y[256:512]
        y_sample[i] = y[512:]
        nk = np.asarray(R[i]["newk"]).reshape(2, 256, 4, 64)
        nv = np.asarray(R[i]["newv"]).reshape(2, 256, 4, 64)
        new_k[2 * i:2 * i + 2, 0] = nk
        new_v[2 * i:2 * i + 2, 0] = nv
        new_lru[2 * i:2 * i + 2, 0] = np.asarray(R[i]["newlru"]).reshape(2, 2, 512)
        new_ssd[2 * i:2 * i + 2, 0] = np.asarray(R[i]["newssd"]).reshape(2, 2, 16, 64, 64)
    return (y_prompt, y_sample, new_k, new_v, new_lru, new_ssd)
```

```python
import numpy as np
from contextlib import ExitStack
import concourse.bass as bass
import concourse.mybir as mybir
from concourse.bass_utils import run_bass_kernel_spmd

F32 = mybir.dt.float32
BF16 = mybir.dt.bfloat16
AF = mybir.ActivationFunctionType
ALU = mybir.AluOpType

ENGS = ("pe", "act", "dve", "pool", "sp")
NDMASEM = 6
NCORES = 8
NRUN = 8
EPS = 1e-6
NLAYERS = 2
STAGE = 99
SUB = 0


class Reg:
    __slots__ = ("name", "lw", "rd", "excl")

    def __init__(self, name, inherit=(), excl=False):
        self.name = name
        self.lw = None
        self.rd = list(inherit)
        self.excl = excl


class Op:
    __slots__ = ("eng", "fn", "deps", "needed", "dma", "sem", "val", "gidx", "thr")

    def __init__(self, eng, fn, dma):
        self.eng = eng
        self.fn = fn
        self.dma = dma
        self.deps = []
        self.needed = False
        self.sem = None
        self.val = 0
        self.thr = None


class Sched:
    def __init__(self, nc):
        self.nc = nc
        self.ops = {e: [] for e in ENGS}
        self.all = []
        self.ndma = {e: 0 for e in ENGS}
        self.dmaops = {e: [] for e in ENGS}

    def add(self, eng, fn, reads=(), writes=(), dma=False):
        op = Op(eng, fn, dma)
        deps = {}
        for r in reads:
            w = r.lw
            if w is not None and (w.dma or dma or w.eng != eng or eng != "pe"):
                deps[id(w)] = w
            if r.excl:
                for q in r.rd:
                    if q.eng != eng:
                        deps[id(q)] = q
        for t in writes:
            w = t.lw
            if w is not None and (w.dma or dma or w.eng != eng or eng != "pe"):
                deps[id(w)] = w
            for q in t.rd:
                if q.dma or dma or q.eng != eng or eng != "pe":
                    deps[id(q)] = q
        op.deps = list(deps.values())
        for r in reads:
            if not dma:
                r.rd = [q for q in r.rd if q.dma or q.eng != eng]
            r.rd.append(op)
        for t in writes:
            t.lw = op
            t.rd = []
        if dma:
            i = self.ndma[eng]
            self.ndma[eng] += 1
            if i >= NDMASEM:
                op.thr = self.dmaops[eng][i - NDMASEM]
            self.dmaops[eng].append(op)
        op.gidx = len(self.all)
        self.all.append(op)
        self.ops[eng].append(op)
        return op

    def emit(self, stack):
        nc = self.nc
        for op in self.all:
            for d in op.deps:
                d.needed = True
        esem = {e: stack.enter_context(nc.semaphore("s_" + e)) for e in ENGS if e != "sp"}
        dsem = {e: [stack.enter_context(nc.semaphore("d_%s%d" % (e, i))) for i in range(NDMASEM)]
                for e in ENGS if self.ndma[e] > 0}
        for e in ENGS:
            cnt = 0
            i = 0
            for op in self.ops[e]:
                if op.dma:
                    op.sem = dsem[e][i % NDMASEM]
                    op.val = 16 * (i // NDMASEM + 1)
                    i += 1
                elif op.needed:
                    cnt += 1
                    op.sem = esem[e]
                    op.val = cnt
        block = stack.enter_context(nc.Block())
        engh = {"pe": block.tensor, "act": block.scalar, "dve": block.vector,
                "pool": block.gpsimd, "sp": block.sync}
        for e in ENGS:
            ops = self.ops[e]
            if not ops:
                continue

            def body(eng, ops=ops, e=e):
                waited = {}
                for op in ops:
                    ds = list(op.deps)
                    if op.thr is not None:
                        ds.append(op.thr)
                    for d in ds:
                        k = id(d.sem)
                        if waited.get(k, 0) >= d.val:
                            continue
                        waited[k] = d.val
                        eng.wait_ge(d.sem, d.val)
                    ins = op.fn(eng)
                    if op.dma:
                        ins.then_inc(op.sem, 16)
                    elif op.needed:
                        ins.then_inc(op.sem, 1)
                for d in self.dmaops[e][-NDMASEM:]:
                    if waited.get(id(d.sem), 0) < d.val:
                        waited[id(d.sem)] = d.val
                        eng.wait_ge(d.sem, d.val)

            engh[e](body)


def _prune(ops):
    best = {}
    out = []
    for o in ops:
        if o.dma:
            out.append(o)
        else:
            b = best.get(o.eng)
            if b is None or o.gidx > b.gidx:
                best[o.eng] = o
    return out + list(best.values())


class TV:
    def __init__(self, ap, name, off, nw, inherit):
        self.ap = ap
        self.name = name
        self.off = off
        self.nw = nw
        self.inherit = inherit
        self.regs = {}

    def __getitem__(self, k):
        return self.ap[k]

    def r(self, key=0):
        g = self.regs.get(key)
        if g is None:
            g = Reg("%s.%s" % (self.name, key), self.inherit)
            self.regs[key] = g
        return g

    def rs(self, keys):
        return [self.r(k) for k in keys]


class Arena:
    def __init__(self, nc, stack, nwords):
        self.t = stack.enter_context(nc.sbuf_tensor("arena", [128, nwords], F32))
        self.n = nwords
        self.live = []
        self.dead = []
        self.peak = 0

    def alloc(self, name, shape, dt):
        n = int(np.prod(shape[1:]))
        nw = n if dt == F32 else (n + 1) // 2
        nw = (nw + 7) // 8 * 8
        off = 0
        for tv in sorted(self.live, key=lambda t: t.off):
            if tv.off - off >= nw:
                break
            off = max(off, tv.off + tv.nw)
        assert off + nw <= self.n, "SBUF arena overflow allocating %s (%d words at %d)" % (name, nw, off)
        self.peak = max(self.peak, off + nw)
        pend = []
        keep = []
        for (o, w, ops) in self.dead:
            if o < off + nw and off < o + w:
                pend += ops
                if not (off <= o and o + w <= off + nw):
                    keep.append((o, w, ops))
            else:
                keep.append((o, w, ops))
        self.dead = keep
        v = self.t[0:shape[0], off:off + nw]
        if dt != F32:
            v = v.bitcast(dt)
        v = v[:, 0:n]
        if len(shape) == 3:
            v = v.rearrange("p (a b) -> p a b", a=shape[1])
        elif len(shape) == 4:
            v = v.rearrange("p (a b c) -> p a b c", a=shape[1], b=shape[2])
        tv = TV(v, name, off, nw, _prune(pend))
        self.live.append(tv)
        return tv

    def free(self, *tvs):
        for tv in tvs:
            ops = list(tv.inherit)
            for g in tv.regs.values():
                if g.lw is not None:
                    ops.append(g.lw)
                ops += g.rd
            self.dead.append((tv.off, tv.nw, _prune(ops)))
            self.live.remove(tv)


D = 1024
NTOK = 1536
TB = 512
PERM = [0, 3, 1, 4, 2, 5, 6, 9, 7, 10, 8, 11]
DFF = 2816
R_BADA, R_CCTX, R_CI, R_NMIX, R_NFFN = 0, 96, 104, 112, 128
R_CLW, R_CLB, R_BA, R_BX, R_LAM, R_CSW, R_CSB, R_QN, R_KN, R_LRU0 = 144, 160, 164, 172, 180, 188, 228, 238, 239, 240
B_NORM, B_D, B_DTB, B_ALOG, NBV = 0, 1024, 1040, 1072, 1104


def _consts():
    c = {}
    ident = np.eye(128, dtype=np.float32)
    onesm = np.full((128, 128), 1.0 / 1024, np.float32)
    o64 = np.zeros((128, 128), np.float32)
    o64[:64, :64] = 1.0 / 64
    o64[64:, 64:] = 1.0 / 64
    R = np.zeros((128, 128), np.float32)
    for h in range(2):
        for ax in range(2):
            for i in range(16):
                p1 = h * 64 + ax * 32 + i
                p2 = p1 + 16
                R[p1, p2] = -1.0
                R[p2, p1] = 1.0
    j = np.arange(128)
    utri = (j[:, None] <= j[None, :]).astype(np.float32)
    ltri = (j[:, None] >= j[None, :]).astype(np.float32)
    nmf = np.where(j[:, None] < j[None, :], -32768.0, 0.0).astype(np.float32)
    nmb = np.where(j[:, None] > j[None, :], -32768.0, 0.0).astype(np.float32)
    onesf = np.ones((128, 128), np.float32)
    c["cf32"] = np.stack([ident, R.T.copy(), utri, ltri, onesf])
    a64 = 2 * np.pi * np.outer(np.arange(64), np.arange(64)) / 64
    c64 = np.zeros((2, 128, 128), np.float32)
    for g in range(2):
        c64[0, g * 64:(g + 1) * 64, g * 64:(g + 1) * 64] = np.cos(a64) / 8
        c64[1, g * 64:(g + 1) * 64, g * 64:(g + 1) * 64] = -np.sin(a64) / 8
    c["cb16"] = np.stack([onesm, o64, nmf, nmb, ident, c64[0], c64[1], utri, ltri])
    for S in (256, 1024):
        a = 2 * np.pi * (np.outer(np.arange(S), np.arange(S)) % S) / S
        c["dft%d" % S] = np.stack([np.cos(a), np.sin(a)]).astype(np.float32) / np.sqrt(S)
    s = np.arange(1024)
    row = (s // 64).astype(np.float32)
    col = (s % 64).astype(np.float32)
    freqs = (10000.0 ** (-np.arange(16, dtype=np.float32) / 16)).astype(np.float32)
    ang = np.zeros((64, 1024), np.float32)
    for d in range(64):
        ax = d // 32
        i = d % 16
        ang[d] = (row if ax == 0 else col) * freqs[i]
    ang = np.concatenate([ang, ang], 0)
    c["rope"] = np.stack([np.cos(ang), np.sin(ang)]).astype(np.float32)
    return c


def build_program():
    nc = bass.Bass("TRN2", target_bir_lowering=False)
    S = Sched(nc)

    def din(name, shape):
        return nc.dram_tensor(name, list(shape), F32, kind="ExternalInput").ap()

    def dout(name, shape):
        return nc.dram_tensor(name, list(shape), F32, kind="ExternalOutput").ap()

    d_x = din("x", [NTOK, D])
    d_vecs = din("vecs", [256, 128])
    d_bvec = din("bvec", [1, NBV])
    d_wada = din("w_ada", [2, D, 6 * D])
    d_wie = din("w_in_even", [D, 1536])
    d_woe = din("w_out_even", [D, D])
    d_wio = din("w_in_odd", [D, 3360])
    d_woo = din("w_out_odd", [1536, D])
    d_w1 = din("ffn_w1", [2, D, DFF])
    d_w3 = din("ffn_w3", [2, D, DFF])
    d_w2 = din("ffn_w2", [2, DFF, D])
    d_lrubd = din("lru_bd", [16, 128, 128])
    d_ck = din("cache_k", [512, 256])
    d_cv = din("cache_v", [512, 256])
    d_sssd = din("state_ssd", [2, 1024, 64])
    d_cf32 = din("cf32", [5, 128, 128])
    d_cb16 = din("cb16", [9, 128, 128])
    d_dft256 = din("dft256", [2, 256, 256])
    d_dft1024 = din("dft1024", [2, 1024, 1024])
    d_rope = din("rope", [2, 128, 1024])
    o_y = dout("y", [NTOK, D])
    o_k = dout("newk", [512, 256])
    o_v = dout("newv", [512, 256])
    o_lru = dout("newlru", [16, 128])
    o_ssd = dout("newssd", [2, 2, 1024, 64])

    with ExitStack() as st:
        A = Arena(nc, st, 53100)
        psb = []
        for i in range(8):
            t = st.enter_context(nc.psum_tensor("ps%d" % i, [128, 512], F32))
            psb.append((t, Reg("ps%d" % i, excl=True)))

        def PS(i):
            return psb[i]

        class Rot:
            def __init__(self, banks):
                self.b = list(banks)
                self.i = 0

            def next(self):
                r = psb[self.b[self.i % len(self.b)]]
                self.i += 1
                return r

        def dma(q, out, in_, reads=(), writes=()):
            return S.add(q, lambda e: e.dma_start(out=out, in_=in_), reads=reads, writes=writes, dma=True)

        def mm(out, lhsT, rhs, start, stop, reads, writes):
            return S.add("pe", lambda e: e.matmul(out, lhsT, rhs, start=start, stop=stop), reads=reads, writes=writes)

        NSLOT = 4
        slots = [A.alloc("wslot%d" % i, [128, 4096], BF16) for i in range(NSLOT)]
        wplan = []
        wstate = {"issued": 0, "got": 0}

        def wplan_add(ap, a, b):
            wplan.append((ap, a, b))

        def kp(ap2d):
            return ap2d.rearrange("(k p) n -> p k n", p=128)

        def wview(i):
            ap, a, b = wplan[i]
            sl = slots[i % NSLOT]
            return sl[:, 0:a * b].rearrange("p (a b) -> p a b", a=a), sl.r()

        def wget():
            i = wstate["got"]
            wstate["got"] += 1
            while wstate["issued"] < min(len(wplan), i + NSLOT - 1):
                j = wstate["issued"]
                v, rg = wview(j)
                dma("pool", v, wplan[j][0], writes=[rg])
                wstate["issued"] += 1
            return wview(i)

        def plan_ada(l):
            for t in range(12):
                wplan_add(kp(d_wada[l])[:, :, t * 512:(t + 1) * 512], 8, 512)

        def plan_ada_tiles(l, ts):
            for t in ts:
                wplan_add(kp(d_wada[l])[:, :, t * 512:(t + 1) * 512], 8, 512)

        def plan_ffn(l):
            for jt in range(6):
                n = 512 if jt < 5 else 256
                wplan_add(kp(d_w1[l])[:, :, jt * 512:jt * 512 + n], 8, n)
                wplan_add(kp(d_w3[l])[:, :, jt * 512:jt * 512 + n], 8, n)
            for o in range(8):
                wplan_add(kp(d_w2[l])[:, :, o * 128:(o + 1) * 128], 22, 128)

        plan_ada_tiles(0, range(0, 4))
        for t in range(3):
            wplan_add(kp(d_wie)[:, :, t * 512:(t + 1) * 512], 8, 512)
        plan_ada_tiles(0, range(4, 12))
        if NLAYERS > 1:
            plan_ada_tiles(1, range(12))
        for sb in range(2):
            for m in range(2):
                wplan_add(kp(d_dft1024[m])[:, :, sb * 512:(sb + 1) * 512], 8, 512)
        for t in range(2):
            wplan_add(kp(d_woe)[:, :, t * 512:(t + 1) * 512], 8, 512)
        plan_ffn(0)
        if NLAYERS > 1:
            wio = kp(d_wio)
            for t in range(2):
                wplan_add(wio[:, :, t * 512:(t + 1) * 512], 8, 512)
            for grp in range(2):
                wplan_add(wio[:, :, 2048:2560], 8, 512)
                wplan_add(wio[:, :, 2560:3072], 8, 512)
                wplan_add(wio[:, :, 3072:3328], 8, 256)
                wplan_add(wio[:, :, 3328:3360], 8, 32)
                wplan_add(wio[:, :, 1024:1536], 8, 512)
                wplan_add(wio[:, :, 1536:2048], 8, 512)
            wstate["woo"] = len(wplan)
            for t in range(4):
                wplan_add(kp(d_woo)[:, :, t * 256:(t + 1) * 256], 12, 256)
            plan_ffn(1)

        xT = A.alloc("xT", [128, 8, NTOK], F32)
        hT = A.alloc("hT", [128, 8, NTOK], BF16)
        cf32 = A.alloc("cf32", [128, 5, 128], F32)
        cb16 = A.alloc("cb16", [128, 9, 128], BF16)
        vecT = A.alloc("vecT", [128, 256], F32)
        bv = A.alloc("bv", [128, NBV], F32)
        mod = A.alloc("mod", [128, 2, 48, 2], F32)
        A1 = A.alloc("A1", [128, 2, 2, 8, 2], F32) if False else A.alloc("A1", [128, 64], F32)
        ident = cf32[:, 0, :]
        RT = cf32[:, 1, :]
        utri = cf32[:, 2, :]
        ltri = cf32[:, 3, :]
        onesf = cf32[:, 4, :]
        onesm = cb16[:, 0, :]
        o64 = cb16[:, 1, :]
        identb = cb16[:, 4, :]

        def A1v(l, kind, c, j):
            o = ((l * 2 + kind) * 8 + c) * 2 + j
            return A1[:, o:o + 1]

        def modv(l, q, c, j):
            return mod[:, l, q * 8 + c, j:j + 1]

        def xr(b):
            return xT.rs(range(4 * b, 4 * b + 4))

        dma("sp", cf32[:], d_cf32.rearrange("c p n -> p c n"), writes=[cf32.r()])
        dma("pool", cb16[:], d_cb16.rearrange("c p n -> p c n"), writes=[cb16.r()])
        vraw = A.alloc("vraw", [128, 2, 128], F32)
        dma("sp", vraw[:], d_vecs.rearrange("(a p) n -> p a n", p=128), writes=[vraw.r()])
        dma("sp", bv[:], d_bvec.partition_broadcast(128), writes=[bv.r()])
        for a in range(2):
            p, pr = PS(a)
            S.add("pe", lambda e, p=p, a=a: e.transpose(p[:, 0:128], vraw[:, a, :], ident),
                  reads=[vraw.r(), cf32.r()], writes=[pr])
            S.add("dve", lambda e, p=p, a=a: e.tensor_copy(out=vecT[:, a * 128:(a + 1) * 128], in_=p[:, 0:128]),
                  reads=[pr], writes=[vecT.r()])
        A.free(vraw)

        xst = [A.alloc("xst%d" % i, [128, D], F32) for i in range(4)]
        rot = Rot([2, 3, 4, 5, 6, 7])
        for t in range(12):
            xs_ = xst[t % 4]
            dma("sp", xs_[:], d_x[t * 128:(t + 1) * 128, :], writes=[xs_.r()])
            for hf in range(2):
                p, pr = rot.next()

                def tr(e, p=p, xs_=xs_, hf=hf):
                    for c in range(4):
                        ins = e.transpose(p[:, c * 128:(c + 1) * 128], xs_[:, (hf * 4 + c) * 128:(hf * 4 + c + 1) * 128], ident)
                    return ins
                S.add("pe", tr, reads=[xs_.r(), cf32.r()], writes=[pr])
                eng = "act" if hf == 0 else "dve"
                outv = xT[:, hf * 4:hf * 4 + 4, t * 128:(t + 1) * 128]
                inv = p[:].rearrange("p (c n) -> p c n", c=4)
                if eng == "act":
                    S.add("act", lambda e, o=outv, i=inv: e.activation(out=o, in_=i, func=AF.Copy), reads=[pr], writes=[xT.r(t)])
                else:
                    S.add("dve", lambda e, o=outv, i=inv: e.tensor_copy(out=o, in_=i), reads=[pr], writes=[xT.r(t)])
        A.free(*xst)

        scT = A.alloc("scT", [128, 8, 2], BF16)
        S.add("act", lambda e: e.activation(out=scT[:].rearrange("p k j -> p j k"),
                                            in_=vecT[:, R_CCTX:R_CCTX + 16].rearrange("p (j k) -> p j k", j=2), func=AF.Silu),
              reads=[vecT.r()], writes=[scT.r()])

        def ada_tile(l, t, bank):
            p, pr = PS(bank) if isinstance(bank, int) else bank
            wt, wr = wget()

            def f(e):
                for oc4 in range(4):
                    for k in range(8):
                        ins = e.matmul(p[:, oc4 * 2:oc4 * 2 + 2], wt[:, k, oc4 * 128:(oc4 + 1) * 128], scT[:, k, :],
                                       start=(k == 0), stop=(k == 7))
                return ins
            S.add("pe", f, reads=[wr, scT.r()], writes=[pr])
            r0 = R_BADA + l * 48 + t * 4
            S.add("dve", lambda e: e.tensor_tensor(out=mod[:, l, t * 4:(t + 1) * 4, :], in0=p[:, 0:8].rearrange("p (c j) -> p c j", j=2),
                                                   in1=vecT[:, r0:r0 + 4].unsqueeze(2).to_broadcast([128, 4, 2]),
                                                   op=ALU.add), reads=[pr, vecT.r()], writes=[mod.r()])

        def ada_finish(l, kind):
            q, rbase = ((1, R_NMIX), (4, R_NFFN))[kind]
            o = (l * 2 + kind) * 16
            S.add("dve", lambda e: e.scalar_tensor_tensor(
                out=A1[:, o:o + 16].rearrange("p (c j) -> p c j", j=2), in0=mod[:, l, q * 8:(q + 1) * 8, :], scalar=1.0,
                in1=vecT[:, rbase + l * 8:rbase + (l + 1) * 8].unsqueeze(2).to_broadcast([128, 8, 2]),
                op0=ALU.add, op1=ALU.mult), reads=[mod.r(), vecT.r()], writes=[A1.r()])

        def norm_mod(l, kind):
            qs = 0 if kind == 0 else 3
            sq = [A.alloc("nsq%d" % i, [128, TB], BF16) for i in range(4)]
            rs = [A.alloc("nrs%d" % i, [128, TB], F32) for i in range(3)]
            tt = [A.alloc("ntt%d" % i, [128, TB], F32) for i in range(3)]
            rot = Rot([6, 7])
            cnt = [0, 0]
            pbank = {}

            def stA(b):
                blk = slice(b * TB, (b + 1) * TB)
                p, pr = rot.next()
                pbank[b] = (p, pr)
                for c in range(8):
                    s_ = sq[cnt[0] % 4]
                    cnt[0] += 1
                    if c % 2 == 0:
                        S.add("act", lambda e, s_=s_, c=c: e.activation(out=s_[:], in_=xT[:, c, blk], func=AF.Square), reads=xr(b), writes=[s_.r()])
                    else:
                        S.add("dve", lambda e, s_=s_, c=c: e.tensor_tensor(out=s_[:], in0=xT[:, c, blk], in1=xT[:, c, blk], op=ALU.mult), reads=xr(b), writes=[s_.r()])
                    mm(p[:], onesm, s_[:], c == 0, c == 7, [s_.r(), cb16.r()], [pr])

            def stB(b):
                p, pr = pbank[b]
                r_ = rs[b]
                S.add("act", lambda e: e.activation(out=r_[:], in_=p[:], func=AF.Ln, bias=EPS_AP[:, 0:1], scale=1.0), reads=[pr, EPS_AP.r()], writes=[r_.r()])
                S.add("act", lambda e: e.activation(out=r_[:], in_=r_[:], func=AF.Exp, scale=-0.5), reads=[r_.r()], writes=[r_.r()])

            def stC(b):
                j = 0 if b == 0 else 1
                blk = slice(b * TB, (b + 1) * TB)
                r_ = rs[b]
                for c in range(8):
                    t_ = tt[cnt[1] % 3]
                    cnt[1] += 1
                    S.add("dve", lambda e, t_=t_, c=c: e.tensor_tensor(out=t_[:], in0=xT[:, c, blk], in1=r_[:], op=ALU.mult),
                          reads=xr(b) + [r_.r()], writes=[t_.r()])
                    S.add("act", lambda e, t_=t_, c=c: e.activation(out=hT[:, c, blk], in_=t_[:], func=AF.Identity,
                                                                    bias=modv(l, qs, c, j), scale=A1v(l, kind, c, j)),
                          reads=[t_.r(), mod.r(), A1.r()], writes=[hT.r(b)])

            stA(0)
            stA(1)
            stB(0)
            stC(0)
            stA(2)
            stB(1)
            stC(1)
            stB(2)
            stC(2)
            A.free(*sq, *rs, *tt)

        EPS_AP = A.alloc("eps", [128, 2], F32)
        S.add("pool", lambda e: e.memset(EPS_AP[:], EPS), writes=[EPS_AP.r()])
        ONE_AP = A.alloc("one", [128, 2], F32)
        S.add("pool", lambda e: e.memset(ONE_AP[:], 1.0), writes=[ONE_AP.r()])

        def resid(p, pr, l, q, o, b):
            j = 0 if b == 0 else 1
            blk = slice(b * TB, (b + 1) * TB)
            S.add("dve", lambda e: e.scalar_tensor_tensor(out=xT[:, o, blk], in0=p[:], scalar=modv(l, q, o, j), in1=xT[:, o, blk],
                                                          op0=ALU.mult, op1=ALU.add),
                  reads=[pr, mod.r()] + xr(b), writes=xr(b))

        def ffn(l):
            norm_mod(l, 1)
            actT = A.alloc("actT", [128, 22, NTOK], BF16)
            sl = [A.alloc("fsl%d" % i, [128, TB], BF16) for i in range(3)]
            rot1 = Rot([0, 1, 2, 3])
            n = 0
            for jt in range(6):
                w1, w1r = wget()
                w3, w3r = wget()
                ncs = 4 if jt < 5 else 2
                for cc in range(ncs):
                    jj = jt * 4 + cc
                    for b in range(3):
                        blk = slice(b * TB, (b + 1) * TB)
                        p1, p1r = rot1.next()
                        p3, p3r = rot1.next()

                        def f(e, w=w1, p=p1, cc=cc, blk=blk):
                            for k in range(8):
                                ins = e.matmul(p[:], w[:, k, cc * 128:(cc + 1) * 128], hT[:, k, blk], start=(k == 0), stop=(k == 7))
                            return ins
                        S.add("pe", f, reads=[w1r, hT.r(b)], writes=[p1r])

                        def f3(e, w=w3, p=p3, cc=cc, blk=blk):
                            for k in range(8):
                                ins = e.matmul(p[:], w[:, k, cc * 128:(cc + 1) * 128], hT[:, k, blk], start=(k == 0), stop=(k == 7))
                            return ins
                        S.add("pe", f3, reads=[w3r, hT.r(b)], writes=[p3r])
                        s_ = sl[n % 3]
                        n += 1
                        S.add("act", lambda e, s_=s_, p=p1: e.activation(out=s_[:], in_=p[:], func=AF.Silu), reads=[p1r], writes=[s_.r()])
                        S.add("dve", lambda e, s_=s_, p=p3, jj=jj, blk=blk: e.tensor_tensor(out=actT[:, jj, blk], in0=p[:], in1=s_[:], op=ALU.mult),
                              reads=[p3r, s_.r()], writes=[actT.r(b)])
            A.free(*sl)
            rot2 = Rot([4, 5, 6, 7])
            for o in range(8):
                w2, w2r = wget()
                for b in range(3):
                    blk = slice(b * TB, (b + 1) * TB)
                    p, pr = rot2.next()

                    def f(e, w=w2, p=p, blk=blk):
                        for jj in range(22):
                            ins = e.matmul(p[:], w[:, jj, :], actT[:, jj, blk], start=(jj == 0), stop=(jj == 21))
                        return ins
                    S.add("pe", f, reads=[w2r, actT.r(b)], writes=[pr])
                    resid(p, pr, l, 5, o, b)
            A.free(actT)

        def even_layer(l):
            norm_mod(l, 0)
            qT = A.alloc("qT", [128, 6, NTOK], BF16)
            kT = A.alloc("kT", [128, 2, 2048], BF16)
            vaug = A.alloc("vaug", [128, 16, 4, 128], BF16)
            ftok = A.alloc("ftok", [128, 12, 256], BF16)
            rope = A.alloc("rope", [128, 2, 1024], F32)
            dft256 = A.alloc("dft256", [128, 2, 2, 256], BF16)
            knew = A.alloc("knew", [128, 4, 256], F32)
            vnew = A.alloc("vnew", [128, 4, 256], F32)
            dma("sp", rope[:], d_rope.rearrange("c p n -> p c n"), writes=[rope.r()])
            for m in range(2):
                dma("pool", dft256[:, m, :, :], d_dft256[m].rearrange("(t p) n -> p t n", p=128), writes=[dft256.r()])
            S.add("pool", lambda e: e.memset(vaug[:], 1.0), writes=vaug.rs(range(16)))
            cvv = d_cv.rearrange("(t p) (j d) -> p t j d", p=128, j=4)
            for par in range(2):
                for tt_ in range(4):
                    dma("pool", vaug[:, 12 + tt_, par::2, par * 64:par * 64 + 64], cvv[:, tt_, par::2, :], writes=[vaug.r(12 + tt_)])
            ckt = A.alloc("ckt", [128, 4, 256], F32)
            dma("sp", ckt[:], d_ck.rearrange("(t p) f -> p t f", p=128), writes=[ckt.r()])
            rotc = Rot([4, 5])
            for kc in range(2):
                p, pr = rotc.next()

                def f(e, p=p, kc=kc):
                    for tt_ in range(4):
                        ins = e.transpose(p[:, tt_ * 128:(tt_ + 1) * 128], ckt[:, tt_, kc * 128:(kc + 1) * 128], ident)
                    return ins
                S.add("pe", f, reads=[ckt.r(), cf32.r()], writes=[pr])
                S.add("dve", lambda e, p=p, kc=kc: e.tensor_copy(out=kT[:, kc, 1536:2048], in_=p[:]), reads=[pr], writes=[kT.r(3)])
            A.free(ckt)

            if STAGE < 3:
                return
            sqn = [A.alloc("sqn%d" % i, [128, TB], BF16) for i in range(3)]
            rsn = [A.alloc("rsn%d" % i, [128, TB], F32) for i in range(3)]
            qn = [A.alloc("qn%d" % i, [128, TB], F32) for i in range(3)]
            t1 = [A.alloc("rt1%d" % i, [128, TB], F32) for i in range(2)]
            t2 = [A.alloc("rt2%d" % i, [128, TB], F32) for i in range(2)]
            rotA = Rot([0, 1, 6])
            rotB = Rot([2, 3])
            rotC = Rot([4, 5, 7])
            qunits = [(wt_i, cc, b) for wt_i in range(2) for cc in range(4) for b in range(3)]
            ust = {}
            wcur = {}

            def stageA(u):
                wt_i, cc, b = qunits[u]
                if (wt_i,) not in wcur:
                    wcur[(wt_i,)] = wget()
                w, wr = wcur[(wt_i,)]
                blk = slice(b * TB, (b + 1) * TB)
                pa, par_ = rotA.next()

                def f(e):
                    for k in range(8):
                        ins = e.matmul(pa[:], w[:, k, cc * 128:(cc + 1) * 128], hT[:, k, blk], start=(k == 0), stop=(k == 7))
                    return ins
                S.add("pe", f, reads=[wr, hT.r(b)], writes=[par_])
                s_ = sqn[u % 3]
                S.add("act", lambda e: e.activation(out=s_[:], in_=pa[:], func=AF.Square), reads=[par_], writes=[s_.r()])
                ust[u] = (pa, par_, s_)

            def stageB(u):
                wt_i, cc, b = qunits[u]
                qc = wt_i * 4 + cc
                isk = qc >= 6
                gain = vecT[:, R_KN:R_KN + 1] if isk else vecT[:, R_QN:R_QN + 1]
                pa, par_, s_ = ust[u]
                r_ = rsn[u % 3]
                q_ = qn[u % 3]
                pb, pbr = rotB.next()
                mm(pb[:], o64, s_[:], True, True, [s_.r(), cb16.r()], [pbr])
                S.add("act", lambda e: e.activation(out=r_[:], in_=pb[:], func=AF.Ln, bias=EPS_AP[:, 0:1], scale=1.0),
                      reads=[pbr, EPS_AP.r()], writes=[r_.r()])
                S.add("act", lambda e: e.activation(out=r_[:], in_=r_[:], func=AF.Exp, scale=-0.5), reads=[r_.r()], writes=[r_.r()])
                if b == 0 and not isk:
                    S.add("dve", lambda e: e.scalar_tensor_tensor(out=qT[:, qc, 0:TB], in0=pa[:], scalar=gain, in1=r_[:], op0=ALU.mult, op1=ALU.mult),
                          reads=[par_, r_.r(), vecT.r()], writes=[qT.r(0)])
                else:
                    S.add("dve", lambda e: e.scalar_tensor_tensor(out=q_[:], in0=pa[:], scalar=gain, in1=r_[:], op0=ALU.mult, op1=ALU.mult),
                          reads=[par_, r_.r(), vecT.r()], writes=[q_.r()])
                ust[u] = (q_,)

            def stageC(u):
                wt_i, cc, b = qunits[u]
                qc = wt_i * 4 + cc
                isk = qc >= 6
                (q_,) = ust.pop(u)
                blk = slice(b * TB, (b + 1) * TB)
                if isk:
                    dst, dreg = kT[:, qc - 6, b * TB:(b + 1) * TB], kT.r(b)
                else:
                    dst, dreg = qT[:, qc, blk], qT.r(b)
                if b == 0:
                    if isk:
                        S.add("act", lambda e: e.activation(out=dst, in_=q_[:], func=AF.Copy), reads=[q_.r()], writes=[dreg])
                        pc, pcr = rotC.next()

                        def ftr(e):
                            for tt_ in range(4):
                                ins = e.transpose(pc[:, tt_ * 128:(tt_ + 1) * 128], q_[:, tt_ * 128:(tt_ + 1) * 128], ident)
                            return ins
                        S.add("pe", ftr, reads=[q_.r(), cf32.r()], writes=[pcr])
                        kc = qc - 6
                        S.add("act", lambda e: e.activation(out=knew[:, :, kc * 128:(kc + 1) * 128], in_=pc[:].rearrange("p (t f) -> p t f", t=4), func=AF.Copy),
                              reads=[pcr], writes=[knew.r()])
                else:
                    a_ = t1[u % 2]
                    b_ = t2[u % 2]
                    pc, pcr = rotC.next()
                    mm(pc[:], RT, q_[:], True, True, [q_.r(), cf32.r()], [pcr])
                    rsl = slice((b - 1) * TB, b * TB)
                    S.add("dve", lambda e: e.tensor_tensor(out=a_[:], in0=q_[:], in1=rope[:, 0, rsl], op=ALU.mult), reads=[q_.r(), rope.r()], writes=[a_.r()])
                    S.add("dve", lambda e: e.tensor_tensor(out=b_[:], in0=pc[:], in1=rope[:, 1, rsl], op=ALU.mult), reads=[pcr, rope.r()], writes=[b_.r()])
                    S.add("dve", lambda e: e.tensor_tensor(out=dst, in0=a_[:], in1=b_[:], op=ALU.add), reads=[a_.r(), b_.r()], writes=[dreg])

            NU = len(qunits)
            for step in range(NU + 2):
                if step < NU:
                    stageA(step)
                if 0 <= step - 1 < NU:
                    stageB(step - 1)
                if 0 <= step - 2 < NU:
                    stageC(step - 2)
            dma("sp", o_k.rearrange("(t p) f -> p t f", p=128), knew[:], reads=[knew.r()])
            A.free(*sqn, *rsn, *qn, *t1, *t2)

            if STAGE < 4:
                return
            w, wr = wget()
            rotA = Rot([0, 1, 2, 3])
            for t in range(12):
                b = t // 4
                p, pr = rotA.next()

                def f(e, p=p, t=t, w=w):
                    for k in range(8):
                        ins = e.matmul(p[:], hT[:, k, t * 128:(t + 1) * 128], w[:, k, :], start=(k == 0), stop=(k == 7))
                    return ins
                S.add("pe", f, reads=[wr, hT.r(b)], writes=[pr])
                if not (SUB & 1):
                    S.add("act", lambda e, p=p, t=t: e.activation(out=ftok[:, t, :], in_=p[:, 0:256], func=AF.Copy), reads=[pr], writes=[ftok.r(t)])
                pv = p[:, 256:512].rearrange("p (j d) -> p j d", j=4)
                for par in range(2 if not (SUB & 2) else 0):
                    S.add("dve", lambda e, pv=pv, t=t, par=par: e.tensor_copy(out=vaug[:, t, par::2, par * 64:par * 64 + 64], in_=pv[:, par::2, :]),
                          reads=[pr], writes=[vaug.r(t)])
                if t < 4 and not (SUB & 4):
                    S.add("act", lambda e, p=p, t=t: e.activation(out=vnew[:, t, :], in_=p[:, 256:512], func=AF.Copy), reads=[pr], writes=[vnew.r()])
            dma("sp", o_v.rearrange("(t p) f -> p t f", p=128), vnew[:], reads=[vnew.r()])
            for t in range(4, 12):
                ada_tile(l, t, 4 + t % 4)
            ada_finish(l, 1)

            if STAGE < 5:
                return
            pT = [A.alloc("pT%d" % i, [128, TB], BF16) for i in range(6)]
            rtmp = [A.alloc("rtmp%d" % i, [128, TB], F32) for i in range(2)]
            evb = [A.alloc("evb%d" % i, [128, TB], F32) for i in range(2)]
            rotS = Rot([0, 1, 2, 3, 4, 5])
            rotO = Rot([6, 7])
            units = []
            for sidx in range(2):
                units.append((sidx * 256, 256, 0, [(sidx * 256 + kt * 128, 2 * sidx + kt) for kt in range(2)]))
            skeys = [(512 + kt * 128, 4 + kt) for kt in range(8)] + [(1536 + kt * 128, 12 + kt) for kt in range(4)]
            for qb in range(2):
                units.append((512 + qb * 512, 512, 1 + qb, skeys))
            n = 0
            nh = 0
            ada1_next = [0]
            LAG = 2
            for (q0, N, b, keys) in units:
                nk = len(keys)
                groups = [(qc, ki) for qc in range(6) for ki in range(nk)]
                pts = {}

                def emit_scores(gi, q0=q0, N=N, b=b, keys=keys, groups=groups, pts=pts):
                    nonlocal n
                    qc, ki = groups[gi]
                    kc = qc // 3
                    k0, vt = keys[ki]
                    kreg = kT.r(0 if k0 < 512 else (1 if k0 < 1024 else (2 if k0 < 1536 else 3)))
                    banks = []
                    for hh in range(2):
                        lo, hi = hh * 64, hh * 64 + 64
                        sp_, spr = rotS.next()
                        mm(sp_[:, 0:N], kT[lo:hi, kc, k0:k0 + 128], qT[lo:hi, qc, q0:q0 + N], True, True, [kreg, qT.r(b)], [spr])
                        banks.append((sp_, spr))
                    lst = []
                    for (sp_, spr) in banks:
                        pt = pT[n % 6]
                        n += 1
                        S.add("act", lambda e, pt=pt, sp_=sp_, N=N: e.activation(out=pt[:, 0:N], in_=sp_[:, 0:N], func=AF.Exp, scale=0.125),
                              reads=[spr], writes=[pt.r()])
                        lst.append(pt)
                    pts[gi] = lst

                for gi in range(min(LAG, len(groups))):
                    emit_scores(gi)
                accs = None
                for gi, (qc, ki) in enumerate(groups):
                    if ki == 0:
                        accs = [rotO.next(), rotO.next()]
                    k0, vt = keys[ki]
                    lst = pts.pop(gi)
                    for hh in range(2):
                        j = (qc // 3) * 2 + hh
                        acc, accr = accs[hh]
                        mm(acc[:, 0:N], vaug[:, vt, j, :], lst[hh][:, 0:N], ki == 0, ki == nk - 1, [vaug.r(vt), lst[hh].r()], [accr])
                    if gi + LAG < len(groups):
                        emit_scores(gi + LAG)
                    if NLAYERS > 1 and N == 512 and gi % 12 == 6 and ada1_next[0] < 12:
                        ada_tile(1, ada1_next[0], rotS.next())
                        ada1_next[0] += 1
                    if ki == nk - 1:
                        for hh in range(2):
                            acc, accr = accs[hh]
                            lo, hi = hh * 64, hh * 64 + 64
                            olo, ohi = (1 - hh) * 64, (1 - hh) * 64 + 64
                            rt = rtmp[nh % 2]
                            ev = evb[nh % 2]
                            nh += 1
                            S.add("dve", lambda e, ev=ev, acc=acc, N=N: e.tensor_copy(out=ev[:, 0:N], in_=acc[:, 0:N]), reads=[accr], writes=[ev.r()])
                            S.add("dve", lambda e, rt=rt, ev=ev, N=N, lo=lo, hi=hi, olo=olo, ohi=ohi: e.reciprocal(out=rt[lo:hi, 0:N], in_=ev[olo:ohi, 0:N]),
                                  reads=[ev.r()], writes=[rt.r()])
                            S.add("dve", lambda e, rt=rt, ev=ev, N=N, lo=lo, hi=hi, qc=qc, q0=q0: e.tensor_tensor(
                                out=hT[lo:hi, 2 + qc, q0:q0 + N], in0=ev[lo:hi, 0:N], in1=rt[lo:hi, 0:N], op=ALU.mult),
                                reads=[ev.r(), rt.r()], writes=[hT.r(b)])
            if NLAYERS > 1:
                while ada1_next[0] < 12:
                    ada_tile(1, ada1_next[0], rotS.next())
                    ada1_next[0] += 1
                ada_finish(1, 0)
                ada_finish(1, 1)
            A.free(*pT, *rtmp, *evb, qT, kT, vaug, rope, knew, vnew)

            if STAGE < 6:
                return
            uv = [A.alloc("uv%d" % i, [128, TB], BF16) for i in range(4)]
            rotU = Rot([0, 1, 2, 3])
            rotF = Rot([4, 5, 6, 7])
            c64c = cb16[:, 5, :]
            c64s = cb16[:, 6, :]
            n = 0

            def stageB(u, v, N, cch, q0, b):
                p, pr = rotF.next()
                mm(p[:, 0:N], c64c, u[:, 0:N], True, False, [cb16.r(), u.r()], [pr])
                mm(p[:, 0:N], c64s, v[:, 0:N], False, True, [cb16.r(), v.r()], [pr])
                S.add("act", lambda e: e.activation(out=hT[:, cch, q0:q0 + N], in_=p[:, 0:N], func=AF.Copy), reads=[pr], writes=[hT.r(b)])

            for sidx in range(2):
                for cch in range(2):
                    pair = []
                    for m in range(2):
                        p, pr = rotU.next()

                        def f(e, p=p, m=m, sidx=sidx, cch=cch):
                            for st_ in range(2):
                                ins = e.matmul(p[:, 0:256], ftok[:, 2 * sidx + st_, cch * 128:(cch + 1) * 128], dft256[:, m, st_, :],
                                               start=(st_ == 0), stop=(st_ == 1))
                            return ins
                        S.add("pe", f, reads=[ftok.r(2 * sidx), ftok.r(2 * sidx + 1), dft256.r()], writes=[pr])
                        u = uv[n % 4]
                        n += 1
                        S.add("dve", lambda e, u=u, p=p: e.tensor_copy(out=u[:, 0:256], in_=p[:, 0:256]), reads=[pr], writes=[u.r()])
                        pair.append(u)
                    stageB(pair[0], pair[1], 256, cch, sidx * 256, 0)
            for sb in range(2):
                wc, wcr = wget()
                wsn, wsr = wget()
                for cch in range(2):
                    pair = []
                    for m, (wm, wmr) in enumerate(((wc, wcr), (wsn, wsr))):
                        p, pr = rotU.next()

                        def f(e, p=p, wm=wm, cch=cch):
                            for st_ in range(8):
                                ins = e.matmul(p[:], ftok[:, 4 + st_, cch * 128:(cch + 1) * 128], wm[:, st_, :], start=(st_ == 0), stop=(st_ == 7))
                            return ins
                        S.add("pe", f, reads=ftok.rs(range(4, 12)) + [wmr], writes=[pr])
                        u = uv[n % 4]
                        n += 1
                        S.add("dve", lambda e, u=u, p=p: e.tensor_copy(out=u[:], in_=p[:]), reads=[pr], writes=[u.r()])
                        pair.append(u)
                    stageB(pair[0], pair[1], 512, cch, 512 + sb * 512, 1 + sb)
            A.free(*uv, ftok, dft256)

            if STAGE < 7:
                return
            rotW = Rot([0, 1, 2, 3])
            for t in range(2):
                w, wr = wget()
                for oc in range(4):
                    o = t * 4 + oc
                    for b in range(3):
                        blk = slice(b * TB, (b + 1) * TB)
                        p, pr = rotW.next()

                        def f(e, p=p, w=w, oc=oc, blk=blk):
                            for k in range(8):
                                ins = e.matmul(p[:], w[:, k, oc * 128:(oc + 1) * 128], hT[:, k, blk], start=(k == 0), stop=(k == 7))
                            return ins
                        S.add("pe", f, reads=[wr, hT.r(b)], writes=[pr])
                        resid(p, pr, l, 2, o, b)


        def rev(v):
            (ps_, pn), (st_, n) = v.ap
            return bass.AP(v.tensor, v.offset + (n - 1) * st_, [[ps_, pn], [-st_, n]])

        SEQS = [(0, 256), (256, 512), (512, 1536)]

        def conv4(eng, xp, yp, n, wcol, bcol):
            lo, hi = 1, n - 2
            S.add(eng, lambda e: e.tensor_scalar(out=yp[:, lo:hi], in0=xp[:, lo:hi], scalar1=vecT[:, wcol(1):wcol(1) + 1],
                                                 scalar2=vecT[:, bcol:bcol + 1], op0=ALU.mult, op1=ALU.add),
                  reads=[xp.r(), vecT.r()], writes=[yp.r()])
            for j, sh in ((0, -1), (2, 1), (3, 2)):
                S.add(eng, lambda e, j=j, sh=sh: e.scalar_tensor_tensor(out=yp[:, lo:hi], in0=xp[:, lo + sh:hi + sh], scalar=vecT[:, wcol(j):wcol(j) + 1],
                                                                        in1=yp[:, lo:hi], op0=ALU.mult, op1=ALU.add),
                      reads=[xp.r(), yp.r(), vecT.r()], writes=[yp.r()])

        def odd_layer(l):
            norm_mod(l, 0)
            yl = A.alloc("yl", [128, 4, NTOK], BF16)
            gg = A.alloc("gg", [128, 4, NTOK], BF16)
            xc = A.alloc("xc", [128, 4, NTOK], F32)
            bdw = A.alloc("bdw", [128, 16, 128], BF16)
            fin = A.alloc("fin", [128, 16], F32)
            clT = A.alloc("clT", [128, 8], F32)
            dma("pool", bdw[:], d_lrubd.rearrange("c p n -> p c n"), writes=[bdw.r()])
            S.add("act", lambda e: e.activation(out=clT[:], in_=vecT[:, R_LAM:R_LAM + 8], func=AF.Exp, scale=-1.0), reads=[vecT.r()], writes=[clT.r()])
            S.add("act", lambda e: e.activation(out=clT[:], in_=clT[:], func=AF.Ln, bias=ONE_AP[:, 0:1], scale=1.0), reads=[clT.r(), ONE_AP.r()], writes=[clT.r()])
            S.add("dve", lambda e: e.tensor_scalar(out=clT[:], in0=clT[:], scalar1=-8.0, scalar2=0.0, op0=ALU.mult, op1=ALU.add), reads=[clT.r()], writes=[clT.r()])
            NP = 1544
            POFF = [1, 259, 517]
            xpad = [A.alloc("xpad%d" % i, [128, NP], F32) for i in range(2)]
            ypad = [A.alloc("ypad%d" % i, [128, NP], F32) for i in range(2)]
            for xp in xpad:
                S.add("pool", lambda e, xp=xp: e.memset(xp[:], 0.0), writes=[xp.r()])
            rotA = Rot([0, 1, 2, 3])
            wg, wgr = wget()
            for c in range(4):
                for b in range(3):
                    blk = slice(b * TB, (b + 1) * TB)
                    p, pr = rotA.next()

                    def f(e, p=p, c=c, blk=blk):
                        for k in range(8):
                            ins = e.matmul(p[:], wg[:, k, c * 128:(c + 1) * 128], hT[:, k, blk], start=(k == 0), stop=(k == 7))
                        return ins
                    S.add("pe", f, reads=[wgr, hT.r(b)], writes=[pr])
                    S.add("act", lambda e, p=p, c=c, blk=blk: e.activation(out=gg[:, c, blk], in_=p[:], func=AF.Gelu_apprx_tanh), reads=[pr], writes=[gg.r(c)])
            wx, wxr = wget()
            for c in range(4):
                xp = xpad[c % 2]
                yp = ypad[c % 2]
                for b in range(3):
                    blk = slice(b * TB, (b + 1) * TB)
                    p, pr = rotA.next()

                    def f(e, p=p, c=c, blk=blk):
                        for k in range(8):
                            ins = e.matmul(p[:], wx[:, k, c * 128:(c + 1) * 128], hT[:, k, blk], start=(k == 0), stop=(k == 7))
                        return ins
                    S.add("pe", f, reads=[wxr, hT.r(b)], writes=[pr])
                    if b == 0:
                        ov = xp[:, 1:1 + 2 * 258].rearrange("p (a b) -> p a b", b=258)[:, :, 0:256]
                        S.add("act", lambda e, p=p, ov=ov: e.activation(out=ov, in_=p[:].rearrange("p (a b) -> p a b", a=2), func=AF.Copy), reads=[pr], writes=[xp.r()])
                    else:
                        o0 = 517 + (b - 1) * 512
                        S.add("act", lambda e, p=p, xp=xp, o0=o0: e.activation(out=xp[:, o0:o0 + 512], in_=p[:], func=AF.Copy), reads=[pr], writes=[xp.r()])
                conv4("dve", xp, yp, NP, lambda j, c=c: R_CLW + j * 4 + c, R_CLB + c)
                for si, (s0, s1) in enumerate(SEQS):
                    S.add("act", lambda e, yp=yp, c=c, s0=s0, s1=s1, si=si: e.activation(out=xc[:, c, s0:s1], in_=yp[:, POFF[si]:POFF[si] + s1 - s0], func=AF.Copy),
                          reads=[yp.r()], writes=[xc.r(c)])
            A.free(*xpad, *ypad)
            xcb = A.alloc("xcb", [128, NTOK], BF16)
            ra_ = [A.alloc("lra%d" % i, [128, NTOK], F32) for i in range(2)]
            ig_ = [A.alloc("lig%d" % i, [128, NTOK], F32) for i in range(2)]
            tq_ = [A.alloc("ltq0", [128, NTOK], F32)] * 2
            hd = [A.alloc("lh0", [128, NTOK], F32), ig_[1]]
            rotL = Rot([4, 5, 6, 7])
            xcbs = [xcb, xcb]

            def lruA(u):
                c, d = u // 2, u % 2
                ra, ig = ra_[d], ig_[d]
                xb = xcbs[c % 2]
                if d == 0:
                    S.add("act", lambda e: e.activation(out=xb[:], in_=xc[:, c, :], func=AF.Copy), reads=[xc.r(c)], writes=[xb.r()])
                for b in range(3):
                    blk = slice(b * TB, (b + 1) * TB)
                    for gate, dstT, brow in ((0, ra, R_BA), (1, ig, R_BX)):
                        p, pr = rotL.next()
                        mm(p[:], bdw[:, (gate * 2 + d) * 4 + c, :], xb[:, blk], True, True, [bdw.r(), xb.r()], [pr])
                        col = brow + d * 4 + c
                        S.add("act", lambda e, p=p, dstT=dstT, blk=blk, col=col: e.activation(out=dstT[:, blk], in_=p[:], func=AF.Sigmoid,
                                                                                               bias=vecT[:, col:col + 1], scale=1.0),
                              reads=[pr, vecT.r()], writes=[dstT.r()])

            def lruB(u):
                c, d = u // 2, u % 2
                ra, ig, tq = ra_[d], ig_[d], tq_[d]
                S.add("act", lambda e: e.activation(out=ra[:], in_=ra[:], func=AF.Exp, scale=clT[:, d * 4 + c:d * 4 + c + 1]),
                      reads=[ra.r(), clT.r()], writes=[ra.r()])
                S.add("dve", lambda e: e.scalar_tensor_tensor(out=tq[:], in0=ra[:], scalar=-1.0, in1=ra[:], op0=ALU.mult, op1=ALU.mult), reads=[ra.r()], writes=[tq.r()])
                S.add("act", lambda e: e.activation(out=tq[:], in_=tq[:], func=AF.Sqrt, bias=ONE_AP[:, 0:1], scale=1.0), reads=[tq.r(), ONE_AP.r()], writes=[tq.r()])
                S.add("dve", lambda e: e.tensor_tensor(out=ig[:], in0=ig[:], in1=xc[:, c, :], op=ALU.mult), reads=[ig.r(), xc.r(c)], writes=[ig.r()])
                S.add("dve", lambda e: e.tensor_tensor(out=tq[:], in0=tq[:], in1=ig[:], op=ALU.mult), reads=[tq.r(), ig.r()], writes=[tq.r()])
                h_ = hd[d]
                for si, (s0, s1) in enumerate(SEQS):
                    init = 0.0 if si < 2 else vecT[:, R_LRU0 + d * 4 + c:R_LRU0 + d * 4 + c + 1]
                    if d == 0:
                        S.add("dve", lambda e, s0=s0, s1=s1, init=init: e.tensor_tensor_scan(
                            out=h_[:, s0:s1], data0=ra[:, s0:s1], data1=tq[:, s0:s1], initial=init, op0=ALU.mult, op1=ALU.add),
                            reads=[ra.r(), tq.r(), vecT.r()], writes=[h_.r()])
                    else:
                        S.add("dve", lambda e, s0=s0, s1=s1, init=init: e.tensor_tensor_scan(
                            out=rev(h_[:, s0:s1]), data0=rev(ra[:, s0:s1]), data1=rev(tq[:, s0:s1]), initial=init, op0=ALU.mult, op1=ALU.add),
                            reads=[ra.r(), tq.r(), vecT.r()], writes=[h_.r()])
                    if si < 2:
                        pos = s1 - 1 if d == 0 else s0
                        col = (si * 2 + d) * 4 + c
                        S.add("dve", lambda e, pos=pos, col=col: e.tensor_copy(out=fin[:, col:col + 1], in_=h_[:, pos:pos + 1]),
                              reads=[h_.r()], writes=[fin.r()])
                if d == 1:
                    S.add("dve", lambda e: e.tensor_tensor(out=hd[0][:], in0=hd[0][:], in1=hd[1][:], op=ALU.add), reads=[hd[0].r(), hd[1].r()], writes=[hd[0].r()])
                    S.add("dve", lambda e: e.tensor_tensor(out=yl[:, c, :], in0=hd[0][:], in1=gg[:, c, :], op=ALU.mult),
                          reads=[hd[0].r(), gg.r(c)], writes=[yl.r()])

            lruA(0)
            for u in range(8):
                if u + 1 < 8:
                    lruA(u + 1)
                lruB(u)
            p, pr = PS(4)
            S.add("pe", lambda e: e.transpose(p[0:16, 0:128], fin[:, 0:16], ident), reads=[fin.r(), cf32.r()], writes=[pr])
            fino = A.alloc("fino", [16, 128], F32)
            S.add("dve", lambda e: e.tensor_copy(out=fino[:], in_=p[0:16, 0:128]), reads=[pr], writes=[fino.r()])
            dma("sp", o_lru, fino[:], reads=[fino.r()])
            A.free(gg, xc, bdw, clT, xcb, *ra_, *ig_, tq_[0], hd[0], fin, fino)
            if STAGE < 13:
                for _ in range(12):
                    wget()

            aneg = A.alloc("aneg", [128, 32], F32)
            S.add("act", lambda e: e.activation(out=aneg[:], in_=bv[:, B_ALOG:B_ALOG + 32], func=AF.Exp), reads=[bv.r()], writes=[aneg.r()])
            S.add("dve", lambda e: e.tensor_scalar(out=aneg[:], in0=aneg[:], scalar1=-1.0, scalar2=0.0, op0=ALU.mult, op1=ALU.add), reads=[aneg.r()], writes=[aneg.r()])
            nmk = [cb16[:, 2, :], cb16[:, 3, :]]
            tri = [cb16[:, 7, :], cb16[:, 8, :]]
            def ssd_group(grp):
                tiles = list(range(0, 4)) if grp == 0 else list(range(4, 12))
                nt = len(tiles)
                t0 = tiles[0]
                ntk = nt * 128
                blocks = [0] if grp == 0 else [1, 2]
                seqs = [[0, 1], [2, 3]] if grp == 0 else [list(range(8))]
                if grp == 0:
                    NPg = 520
                    poff = [1, 259]
                    slen = 256
                else:
                    NPg = 1032
                    poff = [1]
                    slen = 1024
                xs_tok = A.alloc("xs_tok", [128, nt, 1024], BF16)
                z_tok = A.alloc("z_tok", [128, nt, 1024], BF16)
                B_tok = A.alloc("B_tok", [128, nt, 128], BF16)
                BT = A.alloc("BT", [128, ntk], BF16)
                CT = A.alloc("CT", [128, ntk], BF16)
                xpad = [A.alloc("sxp%d" % i, [128, NPg], F32) for i in range(3)]
                ypad = [A.alloc("syp%d" % i, [128, NPg], F32) for i in range(3)]
                xsf = [A.alloc("xsf%d" % i, [128, ntk], F32) for i in range(3)]
                for xp in xpad:
                    S.add("pool", lambda e, xp=xp: e.memset(xp[:], 0.0), writes=[xp.r()])
                rotA = Rot([0, 1, 2, 3])
                rotT = Rot([4, 5, 6, 7])
                wc1 = {}

                def c1A(ch):
                    wt_i, cc = ch // 4, ch % 4
                    if wt_i not in wc1:
                        wc1[wt_i] = wget()
                    w, wr = wc1[wt_i]
                    xp = xpad[ch % 3]
                    for b in blocks:
                        blk = slice(b * TB, (b + 1) * TB)
                        p, pr = rotA.next()

                        def f(e, p=p, blk=blk):
                            for k in range(8):
                                ins = e.matmul(p[:], w[:, k, cc * 128:(cc + 1) * 128], hT[:, k, blk], start=(k == 0), stop=(k == 7))
                            return ins
                        S.add("pe", f, reads=[wr, hT.r(b)], writes=[pr])
                        if grp == 0:
                            ov = xp[:, 1:1 + 2 * 258].rearrange("p (a b) -> p a b", b=258)[:, :, 0:256]
                            S.add("act", lambda e, p=p, ov=ov: e.activation(out=ov, in_=p[:].rearrange("p (a b) -> p a b", a=2), func=AF.Copy), reads=[pr], writes=[xp.r()])
                        else:
                            o0 = 1 + (b - 1) * 512
                            S.add("act", lambda e, p=p, o0=o0: e.activation(out=xp[:, o0:o0 + 512], in_=p[:], func=AF.Copy), reads=[pr], writes=[xp.r()])

                def c1B(ch):
                    xp = xpad[ch % 3]
                    yp = ypad[ch % 3]
                    xf = xsf[ch % 3]
                    conv4("dve", xp, yp, NPg, lambda j: R_CSW + j * 10 + ch, R_CSB + ch)
                    for si, po in enumerate(poff):
                        if ch <= 8:
                            dst, dreg = xf[:, si * slen:(si + 1) * slen], xf.r()
                        else:
                            dst, dreg = CT[:, si * slen:(si + 1) * slen], CT.r()
                        S.add("act", lambda e, po=po, dst=dst: e.activation(out=dst, in_=yp[:, po:po + slen], func=AF.Silu), reads=[yp.r()], writes=[dreg])
                    if ch == 8:
                        S.add("act", lambda e: e.activation(out=BT[:], in_=xf[:], func=AF.Copy), reads=[xf.r()], writes=[BT.r()])

                def c1C(ch):
                    xf = xsf[ch % 3]
                    if ch <= 8:
                        for t4 in range(0, nt, 4):
                            p, pr = rotT.next()

                            def ftr(e, p=p, t4=t4):
                                for q in range(4):
                                    ins = e.transpose(p[:, q * 128:(q + 1) * 128], xf[:, (t4 + q) * 128:(t4 + q + 1) * 128], ident)
                                return ins
                            S.add("pe", ftr, reads=[xf.r(), cf32.r()], writes=[pr])
                            pv = p[:].rearrange("p (q n) -> p q n", q=4)
                            if ch < 8:
                                S.add("dve", lambda e, pv=pv, t4=t4: e.tensor_copy(out=xs_tok[:, t4:t4 + 4, ch * 128:(ch + 1) * 128], in_=pv),
                                      reads=[pr], writes=[xs_tok.r()])
                            else:
                                S.add("dve", lambda e, pv=pv, t4=t4: e.tensor_copy(out=B_tok[:, t4:t4 + 4, :], in_=pv), reads=[pr], writes=[B_tok.r()])

                for step in range(12):
                    if step < 10:
                        c1A(step)
                    if 0 <= step - 1 < 10:
                        c1B(step - 1)
                    if 0 <= step - 2 < 10:
                        c1C(step - 2)
                A.free(*xpad, *ypad, *xsf)
                if SUB == 1:
                    return
                sm = lambda name, w=32: A.alloc(name, [128, nt, w], F32)
                dtr = sm("dtr")
                dta = sm("dta")
                aal = sm("aal")
                stats = sm("stats", 64)
                nacs = sm("nacs")
                eacs = sm("eacs")
                wdec = sm("wdec")
                cdec = sm("cdec")
                cdsel = A.alloc("cdsel", [128, nt, 2, 8], F32)
                wd, wdr = wget()
                for ti in range(nt):
                    tg = t0 + ti
                    b = tg // 4
                    p, pr = rotA.next()

                    def f(e, p=p, tg=tg):
                        for k in range(8):
                            ins = e.matmul(p[:, 0:32], hT[:, k, tg * 128:(tg + 1) * 128], wd[:, k, :], start=(k == 0), stop=(k == 7))
                        return ins
                    S.add("pe", f, reads=[wdr, hT.r(b)], writes=[pr])
                    S.add("dve", lambda e, p=p, ti=ti: e.tensor_tensor(out=dtr[:, ti, :], in0=p[:, 0:32], in1=bv[:, B_DTB:B_DTB + 32], op=ALU.add),
                          reads=[pr, bv.r()], writes=[dtr.r()])
                S.add("act", lambda e: e.activation(out=dtr[:], in_=dtr[:], func=AF.Exp), reads=[dtr.r()], writes=[dtr.r()])
                S.add("act", lambda e: e.activation(out=dta[:], in_=dtr[:], func=AF.Ln, bias=ONE_AP[:, 0:1], scale=1.0), reads=[dtr.r(), ONE_AP.r()], writes=[dta.r()])
                S.add("dve", lambda e: e.tensor_tensor(out=aal[:], in0=dta[:], in1=aneg[:].unsqueeze(1).to_broadcast([128, nt, 32]), op=ALU.mult),
                      reads=[dta.r(), aneg.r()], writes=[aal.r()])
                ahi = A.alloc("ahi", [128, nt, 32], BF16)
                alo = A.alloc("alo", [128, nt, 32], BF16)
                S.add("act", lambda e: e.activation(out=ahi[:], in_=aal[:], func=AF.Copy), reads=[aal.r()], writes=[ahi.r()])
                S.add("dve", lambda e: e.tensor_tensor(out=alo[:], in0=aal[:], in1=ahi[:], op=ALU.subtract), reads=[aal.r(), ahi.r()], writes=[alo.r()])
                for ti in range(nt):
                    p, pr = rotT.next()
                    mm(p[:, 0:16], utri, aal[:, ti, 0:16], True, True, [cf32.r(), aal.r()], [pr])
                    mm(p[:, 16:32], ltri, aal[:, ti, 16:32], True, True, [cf32.r(), aal.r()], [pr])
                    mm(p[:, 32:64], onesf, aal[:, ti, :], True, True, [cf32.r(), aal.r()], [pr])
                    S.add("dve", lambda e, p=p, ti=ti: e.tensor_copy(out=stats[:, ti, :], in_=p[:, 0:64]), reads=[pr], writes=[stats.r()])
                S.add("act", lambda e: e.activation(out=nacs[:], in_=dta[:], func=AF.Ln), reads=[dta.r()], writes=[nacs.r()])
                S.add("dve", lambda e: e.tensor_tensor(out=nacs[:], in0=nacs[:], in1=stats[:, :, 0:32], op=ALU.subtract),
                      reads=[stats.r(), nacs.r()], writes=[nacs.r()])
                S.add("act", lambda e: e.activation(out=eacs[:], in_=stats[:, :, 0:32], func=AF.Exp), reads=[stats.r()], writes=[eacs.r()])
                S.add("dve", lambda e: e.tensor_tensor(out=wdec[:], in0=stats[:, :, 32:64], in1=stats[:, :, 0:32], op=ALU.subtract), reads=[stats.r()], writes=[wdec.r()])
                S.add("act", lambda e: e.activation(out=wdec[:], in_=wdec[:], func=AF.Exp), reads=[wdec.r()], writes=[wdec.r()])
                S.add("dve", lambda e: e.tensor_tensor(out=wdec[:], in0=wdec[:], in1=dta[:], op=ALU.mult), reads=[wdec.r(), dta.r()], writes=[wdec.r()])
                S.add("act", lambda e: e.activation(out=cdec[:], in_=stats[:, :, 32:64], func=AF.Exp), reads=[stats.r()], writes=[cdec.r()])
                for g in range(2):
                    src = cdec[g * 64:(g + 1) * 64, :, :].rearrange("p t (d g h) -> p t d g h", d=2, g=2)[:, :, :, g, :]
                    S.add("dve", lambda e, g=g, src=src: e.tensor_copy(out=cdsel[g * 64:(g + 1) * 64, :, :, :], in_=src), reads=[cdec.r()], writes=[cdsel.r()])
                A.free(dtr, cdec, stats, aal)
                if SUB == 2:
                    return
                Sst = [A.alloc("Sst%d" % d, [128, 512], F32) for d in range(2)]
                hinb = A.alloc("hinb", [128, nt, 512], BF16)
                hinf = A.alloc("hinf", [128, 512], BF16)
                xw = [A.alloc("xw0", [128, 1024], BF16)] * 2
                stmp = A.alloc("stmp", [128, 512], F32)

                class _Stg:
                    def __getitem__(self, k):
                        return stmp[:].rearrange("p (a n) -> p a n", a=8)[k]

                    def r(self):
                        return stmp.r()
                stg = _Stg()
                rotP = Rot([5, 6])
                nxw = [0]

                def init_state(d):
                    St = Sst[d]
                    if grp == 0:
                        S.add("pool", lambda e: e.memset(St[:], 0.0), writes=[St.r()])
                    else:
                        dma("sp", stg[:], d_sssd[d].rearrange("(a p) n -> p a n", p=128), writes=[stg.r()])
                        for g in range(2):
                            p, pr = rotP.next()

                            def f(e, p=p, g=g):
                                for a4 in range(4):
                                    ins = e.transpose(p[0:64, a4 * 128:(a4 + 1) * 128], stg[:, g * 4 + a4, :], ident)
                                return ins
                            S.add("pe", f, reads=[stg.r(), cf32.r()], writes=[pr])
                            S.add("dve", lambda e, p=p, g=g: e.tensor_copy(out=St[g * 64:(g + 1) * 64, :], in_=p[0:64, :]), reads=[pr], writes=[St.r()])

                def out_state(d, sidx):
                    St = Sst[d]
                    for g in range(2):
                        p, pr = rotP.next()

                        def f(e, p=p, g=g):
                            for s4 in range(4):
                                ins = e.transpose(p[:, s4 * 64:(s4 + 1) * 64], St[g * 64:(g + 1) * 64, s4 * 128:(s4 + 1) * 128], ident[g * 64:(g + 1) * 64, g * 64:(g + 1) * 64])
                            return ins
                        S.add("pe", f, reads=[St.r(), cf32.r()], writes=[pr])
                        S.add("dve", lambda e, p=p, g=g: e.tensor_copy(out=stg[:, g * 4:(g + 1) * 4, :], in_=p[:, 0:256].rearrange("p (a n) -> p a n", a=4)), reads=[pr], writes=[stg.r()])
                    dma("sp", o_ssd[sidx, d].rearrange("(a p) n -> p a n", p=128), stg[:], reads=[stg.r()])

                def state_update(d, ti):
                    St = Sst[d]
                    x_ = xw[nxw[0] % 2]
                    nxw[0] += 1
                    S.add("dve", lambda e, x_=x_, ti=ti, d=d: e.tensor_tensor(
                        out=x_[:].rearrange("p (h q) -> p h q", h=16), in0=xs_tok[:, ti, :].rearrange("p (h q) -> p h q", h=16),
                        in1=wdec[:, ti, d * 16:(d + 1) * 16].unsqueeze(2).to_broadcast([128, 16, 64]), op=ALU.mult),
                        reads=[xs_tok.r(), wdec.r()], writes=[x_.r()])
                    p, pr = rotP.next()
                    for g in range(2):
                        mm(p[g * 64:(g + 1) * 64, :], B_tok[:, ti, g * 64:(g + 1) * 64], x_[:, g * 512:(g + 1) * 512], True, True, [B_tok.r(), x_.r()], [pr])
                    S.add("dve", lambda e, ti=ti, d=d: e.tensor_tensor(
                        out=stmp[:].rearrange("p (h q) -> p h q", h=8), in0=St[:].rearrange("p (h q) -> p h q", h=8),
                        in1=cdsel[:, ti, d, :].unsqueeze(2).to_broadcast([128, 8, 64]), op=ALU.mult),
                        reads=[St.r(), cdsel.r()], writes=[stmp.r()])
                    S.add("dve", lambda e, p=p: e.tensor_tensor(out=St[:], in0=p[:], in1=stmp[:], op=ALU.add), reads=[pr, stmp.r()], writes=[St.r()])

                wz0, wz0r = wget()
                wz1, wz1r = wget()

                def zproj(ti):
                    tg = t0 + ti
                    b = tg // 4
                    for hf, (wz, wzr) in enumerate(((wz0, wz0r), (wz1, wz1r))):
                        p, pr = rotA.next()

                        def f(e, p=p, wz=wz):
                            for k in range(8):
                                ins = e.matmul(p[:], hT[:, k, tg * 128:(tg + 1) * 128], wz[:, k, :], start=(k == 0), stop=(k == 7))
                            return ins
                        S.add("pe", f, reads=[wzr, hT.r(b)], writes=[pr])
                        S.add("act", lambda e, p=p, hf=hf: e.activation(out=z_tok[:, ti, hf * 512:(hf + 1) * 512], in_=p[:], func=AF.Silu),
                              reads=[pr], writes=[z_tok.r()])
                for sidx, sq_ in enumerate(seqs):
                    init_state(1)
                    for ti in reversed(sq_):
                        zproj(ti)
                        S.add("act", lambda e, ti=ti: e.activation(out=hinb[:, ti, :], in_=Sst[1][:], func=AF.Copy), reads=[Sst[1].r()], writes=[hinb.r()])
                        if SUB != 5:
                            state_update(1, ti)
                    if grp == 0 and SUB != 4:
                        out_state(1, sidx)
                if SUB in (3, 4, 5):
                    return
                cbs = [A.alloc("cbs%d" % i, [128, 2, 128], F32) for i in range(2)]
                decb = [A.alloc("decb%d" % i, [128, 2, 128], F32) for i in range(3)]
                MTb = [A.alloc("MTb%d" % i, [128, 2, 128], BF16) for i in range(3)]
                yaccs = [A.alloc("yacc%d" % i, [128, 1024], F32) for i in range(2)]
                ytmp = A.alloc("ytmp", [128, 1024], F32)
                ssq = A.alloc("ssq", [128, 2], F32)
                rotD = Rot([1, 2, 3, 4])
                nm_ = 0
                tiles_seq = []
                for sidx, sq_ in enumerate(seqs):
                    for i_, ti in enumerate(sq_):
                        tiles_seq.append((sidx, ti, i_ == 0, i_ == len(sq_) - 1))

                def make_tail(sidx, ti, first, last, yacc):
                    tg = t0 + ti
                    tcol = slice(ti * 128, (ti + 1) * 128)
                    th = []
                    if first:
                        th.append(lambda: init_state(0))
                    th.append(lambda: S.add("act", lambda e: e.activation(out=hinf[:], in_=Sst[0][:], func=AF.Copy), reads=[Sst[0].r()], writes=[hinf.r()]))
                    for d in range(2):
                        for g in range(2):
                            def yo(d=d, g=g):
                                hin = hinf[:] if d == 0 else hinb[:, ti, :]
                                hreg = hinf.r() if d == 0 else hinb.r()
                                p, pr = rotP.next()
                                mm(p[:], CT[g * 64:(g + 1) * 64, tcol], hin[g * 64:(g + 1) * 64, :], True, True, [CT.r(), hreg], [pr])
                                ea = eacs[:, ti, d * 16 + g * 8:d * 16 + g * 8 + 8].unsqueeze(2).to_broadcast([128, 8, 64])
                                pv = p[:].rearrange("p (h q) -> p h q", h=8)
                                tsl = ytmp[:, g * 512:(g + 1) * 512].rearrange("p (h q) -> p h q", h=8)
                                S.add("dve", lambda e: e.tensor_tensor(out=tsl, in0=pv, in1=ea, op=ALU.mult), reads=[pr, eacs.r()], writes=[ytmp.r()])
                                S.add("dve", lambda e: e.tensor_tensor(out=yacc[:, g * 512:(g + 1) * 512], in0=yacc[:, g * 512:(g + 1) * 512],
                                                                       in1=ytmp[:, g * 512:(g + 1) * 512], op=ALU.add),
                                      reads=[yacc.r(), ytmp.r()], writes=[yacc.r()])
                            th.append(yo)
                    th.append(lambda: S.add("dve", lambda e: e.tensor_tensor(
                        out=ytmp[:].rearrange("p (h q) -> p h q", h=16), in0=xs_tok[:, ti, :].rearrange("p (h q) -> p h q", h=16),
                        in1=bv[:, B_D:B_D + 16].unsqueeze(2).to_broadcast([128, 16, 64]), op=ALU.mult),
                        reads=[xs_tok.r(), bv.r(), yacc.r()], writes=[ytmp.r()]))
                    th.append(lambda: S.add("dve", lambda e: e.tensor_tensor(out=yacc[:], in0=yacc[:], in1=ytmp[:], op=ALU.add), reads=[yacc.r(), ytmp.r()], writes=[yacc.r()]))
                    th.append(lambda: S.add("dve", lambda e: e.tensor_tensor(out=yacc[:], in0=yacc[:], in1=z_tok[:, ti, :], op=ALU.mult), reads=[yacc.r(), z_tok.r()], writes=[yacc.r()]))
                    th.append(lambda: S.add("act", lambda e: e.activation(out=ytmp[:], in_=yacc[:], func=AF.Square, accum_out=ssq[:, 0:1]), reads=[yacc.r()], writes=[ytmp.r(), ssq.r()]))
                    th.append(lambda: S.add("act", lambda e: e.activation(out=ssq[:, 0:1], in_=ssq[:, 0:1], func=AF.Ln, bias=EPS_AP[:, 0:1], scale=1.0 / 1024), reads=[ssq.r(), EPS_AP.r()], writes=[ssq.r()]))
                    th.append(lambda: S.add("act", lambda e: e.activation(out=ssq[:, 0:1], in_=ssq[:, 0:1], func=AF.Exp, scale=-0.5), reads=[ssq.r()], writes=[ssq.r()]))
                    th.append(lambda: S.add("dve", lambda e: e.scalar_tensor_tensor(out=ytmp[:], in0=yacc[:], scalar=ssq[:, 0:1], in1=bv[:, B_NORM:B_NORM + 1024], op0=ALU.mult, op1=ALU.mult),
                                            reads=[yacc.r(), ssq.r(), bv.r()], writes=[ytmp.r()]))
                    for hf in range(2):
                        def trp(hf=hf):
                            p, pr = PS(7) if hf == 0 else rotP.next()

                            def ftr(e):
                                for c in range(4):
                                    ins = e.transpose(p[:, c * 128:(c + 1) * 128], ytmp[:, (hf * 4 + c) * 128:(hf * 4 + c + 1) * 128], ident)
                                return ins
                            S.add("pe", ftr, reads=[ytmp.r(), cf32.r()], writes=[pr])
                            outv = hT[:, hf * 4:hf * 4 + 4, tg * 128:(tg + 1) * 128]
                            S.add("act", lambda e: e.activation(out=outv, in_=p[:].rearrange("p (c n) -> p c n", c=4), func=AF.Copy),
                                  reads=[pr], writes=[hT.r(tg // 4)])
                        th.append(trp)
                    th.append(lambda: state_update(0, ti))
                    if last and grp == 0:
                        th.append(lambda: out_state(0, sidx))
                    return th

                pending = []
                for tix, (sidx, ti, first, last) in enumerate(tiles_seq):
                    yacc = yaccs[tix % 2]
                    tcol = slice(ti * 128, (ti + 1) * 128)
                    cb_ = cbs[ti % 2]
                    for g in range(2):
                        p, pr = PS(7) if g == 0 else rotP.next()
                        mm(p[:, 0:128], BT[g * 64:(g + 1) * 64, tcol], CT[g * 64:(g + 1) * 64, tcol], True, True, [BT.r(), CT.r()], [pr])
                        S.add("act", lambda e, p=p, cb_=cb_, g=g: e.activation(out=cb_[:, g, :], in_=p[:, 0:128], func=AF.Copy), reads=[pr], writes=[cb_.r()])
                    ydp, ydr = PS(0)
                    munits = [(hp, d) for hp in range(8) for d in range(2)]
                    mts = {}

                    def emit_pdc(u, ti=ti, cb_=cb_, mts=mts, munits=munits):
                        nonlocal nm_
                        hp, d = munits[u]
                        g = hp // 4
                        pd, pdr = rotD.next()

                        def f(e, pd=pd, ti=ti, hp=hp, d=d):
                            for i_ in range(2):
                                hc = d * 16 + 2 * hp + i_
                                o_ = pd[:, i_ * 128:(i_ + 1) * 128]
                                e.matmul(o_, ahi[:, ti, hc:hc + 1].to_broadcast([128, 128]), tri[d], start=True, stop=False)
                                e.matmul(o_, alo[:, ti, hc:hc + 1].to_broadcast([128, 128]), tri[d], start=False, stop=False)
                                ins = e.matmul(o_, nmk[d], identb, start=False, stop=True)
                            return ins
                        S.add("pe", f, reads=[ahi.r(), alo.r(), cb16.r()], writes=[pdr])
                        dc = decb[nm_ % 3]
                        mt = MTb[nm_ % 3]
                        nm_ += 1
                        for i_ in range(2):
                            hc = d * 16 + 2 * hp + i_
                            S.add("act", lambda e, pd=pd, dc=dc, ti=ti, hc=hc, i_=i_: e.activation(out=dc[:, i_, :], in_=pd[:, i_ * 128:(i_ + 1) * 128], func=AF.Exp,
                                                                                                     bias=nacs[:, ti, hc:hc + 1], scale=1.0),
                                  reads=[pdr, nacs.r()], writes=[dc.r()])
                        S.add("pool" if u % 2 == 1 else "dve", lambda e, dc=dc, mt=mt, g=g, cb_=cb_: e.tensor_tensor(
                            out=mt[:], in0=dc[:], in1=cb_[:, g, :].unsqueeze(1).to_broadcast([128, 2, 128]), op=ALU.mult),
                            reads=[dc.r(), cb_.r()], writes=[mt.r()])
                        mts[u] = mt

                    LAm = 3
                    for u in range(LAm):
                        emit_pdc(u)
                    for u, (hp, d) in enumerate(munits):
                        mt = mts.pop(u)

                        def fy(e, mt=mt, hp=hp, d=d, u=u, ti=ti):
                            for i_ in range(2):
                                h = 2 * hp + i_
                                ins = e.matmul(ydp[:, (h % 8) * 64:(h % 8 + 1) * 64], mt[:, i_, :], xs_tok[:, ti, h * 64:(h + 1) * 64],
                                               start=(u % 8 == 0 and i_ == 0), stop=(d == 1), skip_group_check=True)
                            return ins
                        S.add("pe", fy, reads=[mt.r(), xs_tok.r()], writes=[ydr])
                        if u + LAm < len(munits):
                            emit_pdc(u + LAm)
                        if pending:
                            pending.pop(0)()
                        if u % 8 == 7:
                            g = u // 8
                            S.add("act", lambda e, g=g, yacc=yacc: e.activation(out=yacc[:, g * 512:(g + 1) * 512], in_=ydp[:], func=AF.Copy), reads=[ydr], writes=[yacc.r()])
                        if u == 15:
                            while pending:
                                pending.pop(0)()
                    pending = pending + make_tail(sidx, ti, first, last, yacc)
                while pending:
                    pending.pop(0)()
                A.free(xs_tok, z_tok, B_tok, BT, CT, dta, ahi, alo, nacs, eacs, wdec, cdsel, *Sst, hinb, hinf, xw[0], stmp,
                       *cbs, *decb, *MTb, *yaccs, ytmp, ssq)
            for grp in range((2 if SUB == 0 else 1) if STAGE >= 13 else 0):
                ssd_group(grp if SUB < 10 else 1)
            A.free(aneg)
            while wstate["got"] < wstate["woo"]:
                wget()
            rotW = Rot([0, 1, 2, 3])
            for t in range(4):
                w, wr = wget()
                for oc in range(2):
                    o = t * 2 + oc
                    for b in range(3):
                        blk = slice(b * TB, (b + 1) * TB)
                        p, pr = rotW.next()

                        def f(e, p=p, w=w, oc=oc, blk=blk):
                            for k in range(12):
                                rhs = yl[:, k, blk] if k < 4 else hT[:, k - 4, blk]
                                ins = e.matmul(p[:], w[:, k, oc * 128:(oc + 1) * 128], rhs, start=(k == 0), stop=(k == 11))
                            return ins
                        S.add("pe", f, reads=[wr, hT.r(b), yl.r()], writes=[pr])
                        resid(p, pr, l, 2, o, b)
            A.free(yl)

        if STAGE >= 1:
            for t in range(4):
                ada_tile(0, t, 0)
            ada_finish(0, 0)
        if STAGE >= 2:
            even_layer(0)
        if STAGE >= 10:
            ffn(0)
        if NLAYERS > 1:
            odd_layer(1)
            if STAGE >= 20:
                ffn(1)

        yst = [A.alloc("yst%d" % i, [128, D], F32) for i in range(4)]
        rot = Rot([0, 1, 2, 3, 4, 5, 6, 7])
        for t in range(12):
            ys = yst[t % 4]
            for hf in range(2):
                p, pr = rot.next()

                def tr(e, p=p, t=t, hf=hf):
                    for c in range(4):
                        ins = e.transpose(p[:, c * 128:(c + 1) * 128], xT[:, hf * 4 + c, t * 128:(t + 1) * 128], ident)
                    return ins
                S.add("pe", tr, reads=[xT.r(t), cf32.r()], writes=[pr])
                if hf == 0:
                    S.add("act", lambda e, ys=ys, p=p: e.activation(out=ys[:, 0:512], in_=p[:], func=AF.Copy), reads=[pr], writes=[ys.r()])
                else:
                    S.add("dve", lambda e, ys=ys, p=p: e.tensor_copy(out=ys[:, 512:1024], in_=p[:]), reads=[pr], writes=[ys.r()])
            dma("sp", o_y[t * 128:(t + 1) * 128, :], ys[:], reads=[ys.r()])
        assert STAGE < 99 or wstate["got"] == len(wplan), (wstate, len(wplan))
        S.emit(st)
    return nc


_CACHE = {}


def _prep_shared(inp):
    f = np.float32
    sh = {}
    sh["w_ada"] = np.ascontiguousarray(inp["w_ada"], f)
    wie = np.asarray(inp["w_in_even"][0], f)
    fcols = wie[:, 0:256]
    q = wie[:, 256:1024].reshape(D, 12, 64)[:, PERM, :].reshape(D, 768)
    k = wie[:, 1024:1280]
    v = wie[:, 1280:1536]
    sh["w_in_even"] = np.ascontiguousarray(np.concatenate([q, k, fcols, v], 1))
    woe = np.asarray(inp["w_out_even"][0], f)
    att = woe[256:].reshape(12, 64, D)[PERM].reshape(768, D)
    sh["w_out_even"] = np.ascontiguousarray(np.concatenate([woe[:256], att], 0))
    sh["w_in_odd"] = np.ascontiguousarray(inp["w_in_odd"][0], f)
    sh["w_out_odd"] = np.ascontiguousarray(inp["w_out_odd"][0], f)
    sh["ffn_w1"] = np.ascontiguousarray(inp["ffn_w1"], f)
    sh["ffn_w3"] = np.ascontiguousarray(inp["ffn_w3"], f)
    sh["ffn_w2"] = np.ascontiguousarray(inp["ffn_w2"], f)
    bd = np.zeros((2, 2, 4, 128, 128), f)
    for gi, wname in enumerate(("lru_wa", "lru_wx")):
        wsrc = np.asarray(inp[wname][0], f)
        for d in range(2):
            for blk in range(8):
                c, hlf = blk // 2, blk % 2
                bd[gi, d, c, hlf * 64:(hlf + 1) * 64, hlf * 64:(hlf + 1) * 64] = wsrc[d, blk]
    sh["lru_bd"] = bd.reshape(16, 128, 128)
    bvec = np.zeros((1, NBV), f)
    bvec[0, B_NORM:B_NORM + 1024] = inp["ssd_norm"][0]
    bvec[0, B_D:B_D + 16] = inp["ssd_d"][0]
    bvec[0, B_DTB:B_DTB + 32] = np.asarray(inp["ssd_dt_bias"][0]).reshape(32)
    bvec[0, B_ALOG:B_ALOG + 32] = np.asarray(inp["ssd_a_log"][0]).reshape(32)
    sh["bvec"] = bvec
    sh.update(_consts())
    return sh


def _vecs(inp, i):
    f = np.float32
    v = np.zeros((256, 128), f)
    v[R_BADA:R_BADA + 96] = np.asarray(inp["b_ada"], f).reshape(96, 128)
    v[R_CCTX:R_CCTX + 8] = np.asarray(inp["c_ctx"], f).reshape(8, 128)
    v[R_CI:R_CI + 8] = np.asarray(inp["c"][i], f).reshape(8, 128)
    v[R_NMIX:R_NMIX + 16] = np.asarray(inp["norm_mix"], f).reshape(16, 128)
    v[R_NFFN:R_NFFN + 16] = np.asarray(inp["norm_ffn"], f).reshape(16, 128)
    v[R_CLW:R_CLW + 16] = np.asarray(inp["conv_lru_w"][0], f).reshape(16, 128)
    v[R_CLB:R_CLB + 4] = np.asarray(inp["conv_lru_b"][0], f).reshape(4, 128)
    v[R_BA:R_BA + 8] = np.asarray(inp["lru_ba"][0], f).reshape(8, 128)
    v[R_BX:R_BX + 8] = np.asarray(inp["lru_bx"][0], f).reshape(8, 128)
    v[R_LAM:R_LAM + 8] = np.asarray(inp["lru_lambda"][0], f).reshape(8, 128)
    v[R_CSW:R_CSW + 40] = np.asarray(inp["conv_ssd_w"][0], f).reshape(40, 128)
    v[R_CSB:R_CSB + 10] = np.asarray(inp["conv_ssd_b"][0], f).reshape(10, 128)
    v[R_QN] = np.tile(np.asarray(inp["q_norm"][0], f), 2)
    v[R_KN] = np.tile(np.asarray(inp["k_norm"][0], f), 2)
    v[R_LRU0:R_LRU0 + 8] = np.asarray(inp["state_lru"][i, 0], f).reshape(8, 128)
    return v


def kernel(**inp):
    inp = {k: np.asarray(v) for k, v in inp.items()}
    if "nc" not in _CACHE:
        _CACHE["nc"] = build_program()
    nc = _CACHE["nc"]
    sh = _prep_shared(inp)
    in_maps = []
    for i in range(NCORES):
        m = dict(sh)
        m["x"] = np.ascontiguousarray(np.concatenate(
            [inp["x_prompt"][2 * i], inp["x_prompt"][2 * i + 1], inp["x_sample"][i]], 0), np.float32)
        m["vecs"] = _vecs(inp, i)
        m["cache_k"] = np.ascontiguousarray(inp["cache_k"][i, 0].reshape(512, 256), np.float32)
        m["cache_v"] = np.ascontiguousarray(inp["cache_v"][i, 0].reshape(512, 256), np.float32)
        m["state_ssd"] = np.ascontiguousarray(inp["state_ssd"][i, 0].reshape(2, 1024, 64), np.float32)
        in_maps.append(m)
    res = run_bass_kernel_spmd(nc, in_maps[:NRUN], core_ids=list(range(NRUN)))
    R = res.results
    y_prompt = np.zeros((16, 256, D), np.float32)
    y_sample = np.zeros((8, 1024, D), np.float32)
    new_k = np.zeros((16, 1, 256, 4, 64), np.float32)
    new_v = np.zeros((16, 1, 256, 4, 64), np.float32)
    new_lru = np.zeros((16, 1, 2, 512), np.float32)
    new_ssd = np.zeros((16, 1, 2, 16, 64, 64), np.float32)
    for i in range(NRUN):
        y = np.asarray(R[i]["y"])
        y_prompt[2 * i] = y[0:256]
        y_prompt[2 * i + 1] = y[256:512]
        y_sample[i] = y[512:]
        nk = np.asarray(R[i]["newk"]).reshape(2, 256, 4, 64)
        nv = np.asarray(R[i]["newv"]).reshape(2, 256, 4, 64)
        new_k[2 * i:2 * i + 2, 0] = nk
        new_v[2 * i:2 * i + 2, 0] = nv
        new_lru[2 * i:2 * i + 2, 0] = np.asarray(R[i]["newlru"]).reshape(2, 2, 512)
        new_ssd[2 * i:2 * i + 2, 0] = np.asarray(R[i]["newssd"]).reshape(2, 2, 16, 64, 64)
    return (y_prompt, y_sample, new_k, new_v, new_lru, new_ssd)
```
